# Optimizing a Trainium2 kernel written in Bass

```python
import math
import jax, jax.numpy as jnp
from jax import lax
import numpy as np


D_MODEL = 1024
BATCH = 2
SEQ = 8192
DEPTH = 1

MEM_LEN = 256
EPS = 1e-6
ROPE_BASE = 10000.0

RET_HEADS = 8
RET_QK_WIDTH = D_MODEL // 2
RET_V_WIDTH = D_MODEL
RET_DK = RET_QK_WIDTH // RET_HEADS
RET_DV = RET_V_WIDTH // RET_HEADS
CHUNK = 128

S5_WIDTH = D_MODEL
S5_GROUP = 16
S5_GROUPS = S5_WIDTH // S5_GROUP
S5_STATE = 64

D_MIX = RET_V_WIDTH + S5_WIDTH
IN_COLS = 2 * RET_QK_WIDTH + 2 * RET_V_WIDTH + 2 * S5_WIDTH
SPLITS = (RET_QK_WIDTH, 2 * RET_QK_WIDTH, 2 * RET_QK_WIDTH + RET_V_WIDTH,
          2 * RET_QK_WIDTH + 2 * RET_V_WIDTH, 2 * RET_QK_WIDTH + 2 * RET_V_WIDTH + S5_WIDTH)

XA_HEADS = 4
XA_DH = D_MODEL // XA_HEADS

kernel_name = 'hybrid_retention_s5_block'


def rms_norm(x, g):
    xf = x.astype(jnp.float32)
    y = xf * lax.rsqrt(jnp.mean(xf * xf, axis=-1, keepdims=True) + EPS)
    return (y * g.astype(jnp.float32)).astype(x.dtype)


def rotary(x, positions):
    half = x.shape[-1] // 2
    inv = ROPE_BASE ** (-jnp.arange(half, dtype=jnp.float32) / half)
    ang = positions.astype(jnp.float32)[:, :, None, None] * inv
    cos, sin = jnp.cos(ang), jnp.sin(ang)
    x1, x2 = x[..., :half], x[..., half:]
    return jnp.concatenate([x1 * cos - x2 * sin, x1 * sin + x2 * cos], axis=-1)


def retention(q, k, v, gn_g):
    B, L, H, DK = q.shape
    DV = v.shape[-1]
    nC = L // CHUNK
    q = q.astype(jnp.float32)
    k = k.astype(jnp.float32) * DK ** -0.5
    v = v.astype(jnp.float32)
    log_g = jnp.log1p(-jnp.exp2(-5.0 - jnp.arange(H, dtype=jnp.float32)))
    qc = q.reshape(B, nC, CHUNK, H, DK)
    kc = k.reshape(B, nC, CHUNK, H, DK)
    vc = v.reshape(B, nC, CHUNK, H, DV)
    j = jnp.arange(CHUNK, dtype=jnp.float32)
    diff = j[:, None] - j[None, :]
    decay = jnp.where(diff[None] >= 0.0,
                      jnp.exp(log_g[:, None, None] * jnp.maximum(diff, 0.0)[None]), 0.0)
    scores = jnp.einsum('bnihd,bnjhd->bnhij', qc, kc) * decay
    inner = jnp.einsum('bnhij,bnjhe->bnihe', scores, vc)
    k_w = jnp.exp(log_g[None, :] * (CHUNK - 1.0 - j)[:, None])
    kv = jnp.einsum('bnjhd,jh,bnjhe->bnhde', kc, k_w, vc)
    chunk_decay = jnp.exp(log_g * CHUNK)[None, :, None, None]

    def step(R, kv_n):
        return chunk_decay * R + kv_n, R

    _, R_prev = lax.scan(step, jnp.zeros((B, H, DK, DV), jnp.float32), jnp.moveaxis(kv, 1, 0))
    R_prev = jnp.moveaxis(R_prev, 0, 1)
    q_w = jnp.exp(log_g[None, :] * (j + 1.0)[:, None])
    cross = jnp.einsum('bnihd,ih,bnhde->bnihe', qc, q_w, R_prev)
    o = (inner + cross).reshape(B, L, H, DV)
    mu = jnp.mean(o, axis=-1, keepdims=True)
    var = jnp.mean(jnp.square(o - mu), axis=-1, keepdims=True)
    o = ((o - mu) * lax.rsqrt(var + EPS)).reshape(B, L, H * DV)
    return o * gn_g.astype(jnp.float32)


def s5_branch(u, a_re, a_im, log_dt, b_re, b_im, c_re, c_im, d, glu_w, glu_b):
    Bsz, L, W = u.shape
    f32 = jnp.float32
    uf = u.astype(f32)
    ug = uf.reshape(Bsz, L, S5_GROUPS, S5_GROUP)
    dt = jnp.exp(log_dt.astype(f32))[:, None]
    ar, ai = a_re.astype(f32), a_im.astype(f32)
    mag = jnp.exp(ar * dt)
    abar_re = mag * jnp.cos(ai * dt)
    abar_im = mag * jnp.sin(ai * dt)
    den = ar * ar + ai * ai
    nr, ni = abar_re - 1.0, abar_im
    f_re = (nr * ar + ni * ai) / den
    f_im = (ni * ar - nr * ai) / den
    br, bi = b_re.astype(f32), b_im.astype(f32)
    bb_re = f_re[..., None] * br - f_im[..., None] * bi
    bb_im = f_re[..., None] * bi + f_im[..., None] * br
    bu_re = jnp.einsum('blgp,gnp->blgn', ug, bb_re)
    bu_im = jnp.einsum('blgp,gnp->blgn', ug, bb_im)
    a_seq_re = jnp.broadcast_to(abar_re[None, None], (1, L, S5_GROUPS, S5_STATE))
    a_seq_im = jnp.broadcast_to(abar_im[None, None], (1, L, S5_GROUPS, S5_STATE))

    def combine(e1, e2):
        a1r, a1i, b1r, b1i = e1
        a2r, a2i, b2r, b2i = e2
        return (a2r * a1r - a2i * a1i,
                a2r * a1i + a2i * a1r,
                a2r * b1r - a2i * b1i + b2r,
                a2r * b1i + a2i * b1r + b2i)

    _, _, xr, xi = lax.associative_scan(combine, (a_seq_re, a_seq_im, bu_re, bu_im), axis=1)
    y = (jnp.einsum('blgn,gpn->blgp', xr, c_re.astype(f32))
         - jnp.einsum('blgn,gpn->blgp', xi, c_im.astype(f32)))
    y = y.reshape(Bsz, L, W) + d.astype(f32) * uf
    y = jax.nn.gelu(y)
    y = y * jax.nn.sigmoid(y @ glu_w.astype(f32) + glu_b.astype(f32))
    return y


def setup_inputs(seed: int = 0) -> dict:
    key = jax.random.key(seed)
    ks = jax.random.split(key, 24)
    f32 = jnp.float32

    def nrm(k, shape, scale):
        return jax.random.normal(k, shape, f32) * scale

    Ld = DEPTH
    G, N, P = S5_GROUPS, S5_STATE, S5_GROUP
    x = nrm(ks[0], (BATCH, SEQ, D_MODEL), 1.0)
    mem = nrm(ks[1], (BATCH, MEM_LEN, D_MODEL), 1.0)
    positions = jnp.broadcast_to(jnp.arange(SEQ, dtype=jnp.int32)[None, :], (BATCH, SEQ))
    norm1_g = 1.0 + nrm(ks[2], (Ld, D_MODEL), 0.02)
    w_in = nrm(ks[3], (Ld, D_MODEL, IN_COLS), D_MODEL ** -0.5)
    ret_gn_g = 1.0 + nrm(ks[4], (Ld, RET_V_WIDTH), 0.02)
    n_idx = jnp.arange(N, dtype=f32)
    s5_a_re = -0.5 + nrm(ks[5], (Ld, G, N), 0.01)
    s5_a_im = math.pi * n_idx[None, None, :] + nrm(ks[6], (Ld, G, N), 0.01)
    s5_log_dt = jax.random.uniform(ks[7], (Ld, G), f32, math.log(1e-3), math.log(1e-1))
    s5_b_re = nrm(ks[8], (Ld, G, N, P), (2.0 * P) ** -0.5)
    s5_b_im = nrm(ks[9], (Ld, G, N, P), (2.0 * P) ** -0.5)
    s5_c_re = nrm(ks[10], (Ld, G, P, N), N ** -0.5)
    s5_c_im = nrm(ks[11], (Ld, G, P, N), N ** -0.5)
    s5_d = nrm(ks[12], (Ld, S5_WIDTH), 0.5)
    s5_glu_w = nrm(ks[13], (Ld, S5_WIDTH, S5_WIDTH), S5_WIDTH ** -0.5)
    s5_glu_b = nrm(ks[14], (Ld, S5_WIDTH), 0.01)
    w_out = nrm(ks[15], (Ld, D_MIX, D_MODEL), D_MIX ** -0.5)
    norm2_g = 1.0 + nrm(ks[16], (Ld, D_MODEL), 0.02)
    norm_mem_g = 1.0 + nrm(ks[17], (Ld, D_MODEL), 0.02)
    xa_wq = nrm(ks[18], (Ld, D_MODEL, D_MODEL), D_MODEL ** -0.5)
    xa_wk = nrm(ks[19], (Ld, D_MODEL, D_MODEL), D_MODEL ** -0.5)
    xa_wv = nrm(ks[20], (Ld, D_MODEL, D_MODEL), D_MODEL ** -0.5)
    xa_wo = nrm(ks[21], (Ld, D_MODEL, D_MODEL), D_MODEL ** -0.5)
    norm_f_g = 1.0 + nrm(ks[22], (D_MODEL,), 0.02)
    return {'x': x, 'mem': mem, 'positions': positions, 'norm1_g': norm1_g, 'w_in': w_in,
            'ret_gn_g': ret_gn_g, 's5_a_re': s5_a_re, 's5_a_im': s5_a_im, 's5_log_dt': s5_log_dt,
            's5_b_re': s5_b_re, 's5_b_im': s5_b_im, 's5_c_re': s5_c_re, 's5_c_im': s5_c_im,
            's5_d': s5_d, 's5_glu_w': s5_glu_w, 's5_glu_b': s5_glu_b, 'w_out': w_out,
            'norm2_g': norm2_g, 'norm_mem_g': norm_mem_g, 'xa_wq': xa_wq, 'xa_wk': xa_wk,
            'xa_wv': xa_wv, 'xa_wo': xa_wo, 'norm_f_g': norm_f_g}


def reference(x, mem, positions, norm1_g, w_in, ret_gn_g, s5_a_re, s5_a_im, s5_log_dt,
              s5_b_re, s5_b_im, s5_c_re, s5_c_im, s5_d, s5_glu_w, s5_glu_b, w_out,
              norm2_g, norm_mem_g, xa_wq, xa_wk, xa_wv, xa_wo, norm_f_g):
    B, L, _ = x.shape
    M = mem.shape[1]
    for l in range(DEPTH):
        h = rms_norm(x, norm1_g[l])
        proj = h @ w_in[l]
        q, k, v, g_ret, u, g_s5 = jnp.split(proj, SPLITS, axis=-1)
        q = rotary(q.reshape(B, L, RET_HEADS, RET_DK), positions)
        k = rotary(k.reshape(B, L, RET_HEADS, RET_DK), positions)
        v = v.reshape(B, L, RET_HEADS, RET_DV)
        ret = retention(q, k, v, ret_gn_g[l]).astype(x.dtype) * jax.nn.silu(g_ret)
        ssm = s5_branch(u, s5_a_re[l], s5_a_im[l], s5_log_dt[l], s5_b_re[l], s5_b_im[l],
                        s5_c_re[l], s5_c_im[l], s5_d[l], s5_glu_w[l],
                        s5_glu_b[l]).astype(x.dtype) * jax.nn.silu(g_s5)
        x = x + jnp.concatenate([ret, ssm], axis=-1) @ w_out[l]
        h2 = rms_norm(x, norm2_g[l])
        m = rms_norm(mem, norm_mem_g[l])
        qa = (h2 @ xa_wq[l]).reshape(B, L, XA_HEADS, XA_DH)
        ka = (m @ xa_wk[l]).reshape(B, M, XA_HEADS, XA_DH)
        va = (m @ xa_wv[l]).reshape(B, M, XA_HEADS, XA_DH)
        s = jnp.einsum('blhd,bmhd->bhlm', qa, ka).astype(jnp.float32) * XA_DH ** -0.5
        p = jax.nn.softmax(s, axis=-1).astype(va.dtype)
        o = jnp.einsum('bhlm,bmhd->blhd', p, va).reshape(B, L, XA_HEADS * XA_DH)
        x = x + o @ xa_wo[l]
    return rms_norm(x, norm_f_g)
```

```python
import os
import math
from contextlib import ExitStack
import numpy as np
import concourse.bass as bass
import concourse.mybir as mybir
from concourse.bass_utils import run_bass_kernel_spmd

F32 = mybir.dt.float32
BF16 = mybir.dt.bfloat16
I32 = mybir.dt.int32
ALU = mybir.AluOpType
AF = mybir.ActivationFunctionType
AX = mybir.AxisListType

SEG = 6000
NDMA = 24
EPS = 1e-6
NOWN_BLK = 8
MODE = "prefix"
TWO_PI = 2.0 * math.pi
KG = 2.0 * math.sqrt(2.0 / math.pi)


class Prog:
    ENGS = ("pe", "act", "dve", "pool", "sp")

    def __init__(self, nc):
        self.nc = nc
        self.ops = []
        self.last_w = {}
        self.readers = {}
        self.pending_barrier = {e: set() for e in self.ENGS}
        self._bar_at = 0

    def op(self, eng, fn, r=(), w=(), dma=False):
        oid = len(self.ops)
        deps = set()
        for k in r:
            if k in self.last_w:
                deps.add(self.last_w[k])
        for k in w:
            if k in self.last_w:
                deps.add(self.last_w[k])
            for rd in self.readers.get(k, ()):
                deps.add(rd)
        deps |= self.pending_barrier[eng]
        self.pending_barrier[eng] = set()
        deps.discard(oid)
        last_per_eng = {}
        keep = set()
        for d_ in deps:
            od = self.ops[d_]
            if od["dma"]:
                keep.add(d_)
            else:
                e_ = od["eng"]
                if e_ not in last_per_eng or d_ > last_per_eng[e_]:
                    last_per_eng[e_] = d_
        deps = keep | set(last_per_eng.values())
        self.ops.append(dict(eng=eng, fn=fn, deps=deps, dma=dma))
        for k in r:
            self.readers.setdefault(k, []).append(oid)
        for k in w:
            self.last_w[k] = oid
            self.readers[k] = []
        return oid

    def dma(self, fn, r=(), w=(), eng="sp"):
        return self.op(eng, fn, r, w, dma=True)

    def barrier(self):
        last = {}
        for i, o in enumerate(self.ops):
            last[o["eng"]] = i
        dm = [i for i in range(self._bar_at, len(self.ops)) if self.ops[i]["dma"]]
        s = set(last.values()) | set(dm)
        for e in self.ENGS:
            self.pending_barrier[e] |= s
        self._bar_at = len(self.ops)

    def emit(self, stack):
        nc = self.nc
        ops = self.ops
        n = len(ops)
        needed = [False] * n
        for i, o in enumerate(ops):
            nd = set()
            for d in o["deps"]:
                od = ops[d]
                if od["eng"] == "pe" and o["eng"] == "pe":
                    continue
                if od["eng"] == o["eng"] and not od["dma"] and od["fn"] is None:
                    continue
                nd.add(d)
            o["deps"] = nd
            for d in nd:
                needed[d] = True
        cnt = {e: 0 for e in self.ENGS}
        nseg = {e: 0 for e in self.ENGS}
        ndma = {"sp": 0, "pool": 0}
        dma_ops = {"sp": [], "pool": []}
        for i, o in enumerate(ops):
            if o["dma"]:
                q = o["eng"]
                nd = ndma[q]
                o["sig"] = ("dma_" + q, nd % NDMA, 16 * (nd // NDMA + 1), nd)
                dma_ops[q].append(o)
                ndma[q] += 1
            elif needed[i]:
                e = o["eng"]
                c = cnt[e]
                o["sig"] = (e, c // SEG, c % SEG + 1)
                cnt[e] += 1
                nseg[e] = c // SEG + 1
            else:
                o["sig"] = None
        sems = {}
        for e in self.ENGS:
            for s in range(nseg[e]):
                sems[(e, s)] = stack.enter_context(nc.semaphore(f"s_{e}_{s}"))
        for q in ("sp", "pool"):
            for k in range(min(NDMA, ndma[q])):
                sems[("dma_" + q, k)] = stack.enter_context(nc.semaphore(f"s_dma_{q}_{k}"))
        block = stack.enter_context(nc.Block())

        def make(ename):
            def body(eng):
                waited = {}
                for i, o in enumerate(ops):
                    if o["eng"] != ename:
                        continue
                    req = {}
                    for d in o["deps"]:
                        sg = ops[d]["sig"]
                        key = (sg[0], sg[1])
                        req[key] = max(req.get(key, 0), sg[2])
                    if o["dma"]:
                        idx = o["sig"][3]
                        if idx >= NDMA:
                            prev = dma_ops[o["eng"]][idx - NDMA]["sig"]
                            key = (prev[0], prev[1])
                            req[key] = max(req.get(key, 0), prev[2])
                    for key in sorted(req.keys(), key=lambda k: (k[0], k[1])):
                        val = req[key]
                        if waited.get(key, 0) >= val:
                            continue
                        if not key[0].startswith("dma"):
                            if any(k[0] == key[0] and k[1] > key[1] for k in waited):
                                continue
                        eng.wait_ge(sems[key], val)
                        waited[key] = val
                    if o["fn"] is None:
                        continue
                    ins = o["fn"](eng)
                    sg = o["sig"]
                    if sg is not None:
                        if sg[0].startswith("dma"):
                            ins.then_inc(sems[(sg[0], sg[1])], 16)
                        else:
                            ins.then_inc(sems[(sg[0], sg[1])], 1)
            return body

        block.tensor(make("pe"))
        block.scalar(make("act"))
        block.vector(make("dve"))
        block.gpsimd(make("pool"))
        block.sync(make("sp"))


class Arena:
    def __init__(self, t, nwords):
        self.t = t
        self.n = nwords
        self.off = 0

    def reset(self, off=0):
        self.off = off

    def take(self, shape, dtype):
        nel = 1
        for s in shape[1:]:
            nel *= s
        sz = 2 if dtype == BF16 else 4
        nw = (nel * sz + 3) // 4
        nw = (nw + 7) // 8 * 8
        assert self.off + nw <= self.n, f"arena overflow {self.off}+{nw}>{self.n}"
        ap = self.t[:, self.off:self.off + nw]
        self.off += nw
        if dtype != F32:
            ap = ap.bitcast(dtype)
        ap = ap[:, 0:nel]
        fd = shape[1:]
        if len(fd) == 2:
            ap = ap.rearrange("p (a b) -> p a b", a=fd[0])
        elif len(fd) == 3:
            ap = ap.rearrange("p (a b c) -> p a b c", a=fd[0], b=fd[1])
        elif len(fd) == 4:
            ap = ap.rearrange("p (a b c d) -> p a b c d", a=fd[0], b=fd[1], c=fd[2])
        return ap


def build_program(dbg=None, stop_after=None, mode=None):
    mode = mode or MODE
    NPRE = 24 if mode == "prefix" else 0
    NT = NPRE * 256 + 2048
    NCH = NT // 128
    FAKE_CC = isinstance(dbg, dict) and bool(dbg.get("fake_cc"))
    nc = bass.Bass("TRN2", target_bir_lowering=False)

    def din(name, shape, dt=F32):
        return nc.dram_tensor(name, list(shape), dt, kind="ExternalInput").ap()

    xT_d = din("xT", [128, 8, NT])
    xown_d = din("xown", [2, 128, 8, 1024])
    pos_d = din("pos", [128, NCH], I32)
    invf_d = din("invf", [128, 32])
    sqk_d = din("sqk", [128, 16])
    cdt_d = din("cdt", [128, 8])
    wtr_d = din("wtr", [128, 8, 4])
    sel5_d = din("sel5", [128, 3, 8])
    maskT_d = din("maskT", [128, 128])
    t0mask_d = din("t0mask", [128, 128])
    ident_d = din("ident", [128, 128])
    w_in_d = din("w_in", [128, 8, 5120])
    g1_d = din("g1", [128, 8])
    gng_d = din("gng", [128, 1024])
    are_d = din("s5are", [128, 32])
    aim_d = din("s5aim", [128, 32])
    ldt_d = din("s5ldt", [128, 32])
    bre_d = din("s5bre", [128, 32, 16])
    bim_d = din("s5bim", [128, 32, 16])
    cre_d = din("s5cre", [128, 32, 16])
    cim_d = din("s5cim", [128, 32, 16])
    dcol_d = din("dcol", [128, 64])
    glub_d = din("glub", [128, 1024])
    g2_d = din("g2rep", [128, 1024])
    gf_d = din("gfrep", [128, 1024])
    gm_d = din("gm", [128, 8])
    gluw_d = din("gluw", [128, 8, 1024])
    wout_d = din("wout", [128, 16, 1024])
    wqT_d = din("wqT", [128, 8, 1024])
    wk_d = din("wk", [128, 8, 1024])
    wv_d = din("wv", [128, 8, 1024])
    wo_d = din("wo", [128, 8, 1024])
    memT_d = din("memT", [128, 8, 256])
    out_d = nc.dram_tensor("out", [2, 128, 8, 1024], F32, kind="ExternalOutput").ap()

    skind = dict(kind="ExternalOutput") if dbg else {}
    u_d = nc.dram_tensor("u_scr", [NT, 1024], BF16, **skind).ap()
    ccR_in = nc.dram_tensor("ccR_in", [128, 512], F32)
    ccR_out = nc.dram_tensor("ccR_out", [1024, 512], F32)
    ccS_in = nc.dram_tensor("ccS_in", [128, 64], F32)
    ccS_out = nc.dram_tensor("ccS_out", [1024, 64], F32)
    g5_d = nc.dram_tensor("g5_scr", [2048, 1024], BF16, **skind).ap()
    retT_d = nc.dram_tensor("retT_scr", [128, 8, 2048], BF16, **skind).ap()
    ssmT_d = nc.dram_tensor("ssmT_scr", [2, 8, 128, 8, 128], BF16, **skind).ap()

    with ExitStack() as st:
        P = Prog(nc)
        sb = lambda name, shape, dt=F32: st.enter_context(nc.sbuf_tensor("sb_" + name, list(shape), dt))
        ARW = (dbg.get("arw") if isinstance(dbg, dict) and dbg.get("arw") else None) or (49200 if mode == "prefix" else 47200)
        arena_t = sb("arena", [128, ARW])
        AR = Arena(arena_t, ARW)
        identf = sb("identf", [128, 128])
        identb = sb("identb", [128, 128], BF16)
        ones_f = sb("ones_f", [128, 128])
        ones_b = sb("ones_b", [128, 128], BF16)
        epsc = sb("epsc", [128, 1])
        maskT = sb("maskT", [128, 128])
        t0mask = sb("t0mask", [128, 128])
        invf = sb("invf", [128, 32])
        sqk = sb("sqk", [128, 16])
        cdt = sb("cdt", [128, 8])
        wtr = sb("wtr", [128, 8, 4])
        sel5 = sb("sel5", [128, 3, 8])
        A2r = sb("A2r", [128, 2, 32])
        A2i = sb("A2i", [128, 2, 32])
        g1 = sb("g1", [128, 8])
        gm = sb("gm", [128, 8])
        Rf = sb("Rf", [128, 8, 128])
        Rbf = sb("Rbf", [128, 8, 128], BF16)
        car_r = sb("car_r", [128, 32])
        car_i = sb("car_i", [128, 32])
        PS = [st.enter_context(nc.psum_tensor(f"PS{i}", [128, 1024], F32))[:] for i in range(4)]

        def dma(out, in_, r=(), w=(), eng="sp"):
            P.dma(lambda e, out=out, in_=in_: e.dma_start(out=out, in_=in_), r=r, w=w, eng=eng)

        def TT(eng, out, in0, in1, op, r, w):
            P.op(eng, lambda e, out=out, in0=in0, in1=in1, op=op: e.tensor_tensor(out=out, in0=in0, in1=in1, op=op), r=r, w=w)

        def TS(eng, out, in0, s1, s2, op0, op1, r, w):
            if op1 is None:
                P.op(eng, lambda e, out=out, in0=in0, s1=s1, op0=op0: e.tensor_scalar(out=out, in0=in0, scalar1=s1, scalar2=None, op0=op0), r=r, w=w)
            else:
                P.op(eng, lambda e, out=out, in0=in0, s1=s1, s2=s2, op0=op0, op1=op1: e.tensor_scalar(out=out, in0=in0, scalar1=s1, scalar2=s2, op0=op0, op1=op1), r=r, w=w)

        def STT(out, in0, scalar, in1, op0, op1, r, w):
            P.op("dve", lambda e, out=out, in0=in0, scalar=scalar, in1=in1, op0=op0, op1=op1: e.scalar_tensor_tensor(out=out, in0=in0, scalar=scalar, in1=in1, op0=op0, op1=op1), r=r, w=w)

        def ACT(out, in_, func, r, w, **kw):
            P.op("act", lambda e, out=out, in_=in_, func=func, kw=kw: e.activation(out=out, in_=in_, func=func, **kw), r=r, w=w)

        def CP(eng, out, in_, r, w):
            if eng == "act":
                ACT(out, in_, AF.Copy, r, w)
            else:
                P.op(eng, lambda e, out=out, in_=in_: e.tensor_copy(out=out, in_=in_), r=r, w=w)

        def RECIP(out, in_, r, w):
            P.op("dve", lambda e, out=out, in_=in_: e.reciprocal(out=out, in_=in_), r=r, w=w)

        def RED(out, in_, op, r, w, axis=AX.X):
            P.op("dve", lambda e, out=out, in_=in_, op=op, axis=axis: e.tensor_reduce(out=out, in_=in_, axis=axis, op=op), r=r, w=w)

        def MM(out, lhsT, rhs, start, stop, r, w):
            P.op("pe", lambda e, out=out, lhsT=lhsT, rhs=rhs, start=start, stop=stop: e.matmul(out, lhsT=lhsT, rhs=rhs, start=start, stop=stop), r=r, w=w)

        def TR(out, in_, ident, r, w):
            P.op("pe", lambda e, out=out, in_=in_, ident=ident: e.transpose(out=out, in_=in_, identity=ident), r=r, w=w)

        def MEMSET(eng, ap, val, w):
            P.op(eng, lambda e, ap=ap, val=val: e.memset(ap, val), w=w)

        dma(identf[:], ident_d, w=["identf"])
        dma(maskT[:], maskT_d, w=["maskT"])
        dma(t0mask[:], t0mask_d, w=["t0mask"])
        dma(invf[:], invf_d, w=["invf"])
        dma(sqk[:], sqk_d, w=["sqk"])
        dma(cdt[:], cdt_d, w=["cdt"])
        dma(wtr[:], wtr_d, w=["wtr"])
        dma(sel5[:], sel5_d, w=["sel5"])
        dma(g1[:], g1_d, w=["g1"])
        dma(gm[:], gm_d, w=["gm"])
        CP("dve", identb[:], identf[:], ["identf"], ["identb"])
        MEMSET("pool", ones_f[:], 1.0, ["ones_f"])
        MEMSET("pool", ones_b[:], 1.0, ["ones_b"])
        MEMSET("pool", epsc[:], EPS, ["epsc"])
        MEMSET("pool", Rf[:], 0.0, ["Rf"])
        MEMSET("pool", Rbf[:], 0.0, ["Rbf"])
        MEMSET("pool", car_r[:], 0.0, ["car_r"])
        MEMSET("pool", car_i[:], 0.0, ["car_i"])

        def sincos(ang, shp, s_out, c_out, tmps, key):
            kf, ki, rr, mm = tmps
            rk = [key]
            TS("dve", kf, ang, 1.0 / TWO_PI, None, ALU.mult, None, rk, [key + "kf"])
            CP("dve", ki, kf, [key + "kf"], [key + "ki"])
            CP("dve", kf, ki, [key + "ki"], [key + "kf"])
            STT(rr, kf, -6.28125, ang, ALU.mult, ALU.add, [key + "kf"] + rk, [key + "rr"])
            STT(rr, kf, -(TWO_PI - 6.28125), rr, ALU.mult, ALU.add, [key + "kf", key + "rr"], [key + "rr"])

            def wrap(t):
                TS("dve", mm, t, -math.pi, None, ALU.is_lt, None, [key + "rr"], [key + "mm"])
                STT(t, mm, TWO_PI, t, ALU.mult, ALU.add, [key + "mm", key + "rr"], [key + "rr"])
                TS("dve", mm, t, math.pi, None, ALU.is_gt, None, [key + "rr"], [key + "mm"])
                STT(t, mm, -TWO_PI, t, ALU.mult, ALU.add, [key + "mm", key + "rr"], [key + "rr"])
            wrap(rr)
            ACT(s_out, rr, AF.Sin, [key + "rr"], [key + "s"])
            TS("dve", rr, rr, math.pi / 2.0, None, ALU.add, None, [key + "rr", key + "s"], [key + "rr"])
            wrap(rr)
            ACT(c_out, rr, AF.Sin, [key + "rr"], [key + "c"])

        def norm_fm(xt, N, gcol, sq, rs1, rs2, hT, psum_ap, kx, ksq, krs, kh, kps, gkey):
            ACT(sq, xt, AF.Square, [kx], [ksq])
            for kt in range(8):
                MM(psum_ap, ones_b[:], sq[:, kt, :], kt == 0, kt == 7, [ksq, "ones_b"], [kps])
            ACT(rs1, psum_ap, AF.Sqrt, [kps], [krs + "1"], bias=EPS, scale=1.0 / 1024.0)
            RECIP(rs2, rs1, [krs + "1"], [krs])
            for kt in range(8):
                STT(hT[:, kt, :], xt[:, kt, :], gcol[:, kt:kt + 1], rs2, ALU.mult, ALU.mult, [kx, krs, gkey], [kh])

        AR.reset()
        w_in = AR.take([128, 8, 5120], BF16)
        cosT = AR.take([128, NCH, 32], F32)
        sinT = AR.take([128, NCH, 32], F32)
        gng = AR.take([128, 1024], F32)
        xTb = [AR.take([128, 8, 256], F32) for _ in range(2)]
        sq = AR.take([128, 8, 256], BF16)
        posi = AR.take([128, NCH], I32)
        posf = AR.take([128, NCH], F32)
        Rc = AR.take([128, 4, 128], F32)
        mark = AR.off
        tki = AR.take([128, NCH * 32], I32)
        tmm = AR.take([128, NCH * 32], F32)
        trr_t = AR.take([128, NCH * 32], F32)
        AR.reset(mark)
        Gt = AR.take([128, 8, 4, 128], F32)
        AR.reset(mark)
        hTb = [AR.take([128, 8, 256], BF16) for _ in range(2)]
        rs1 = AR.take([128, 256], F32)
        rs2 = AR.take([128, 256], F32)
        qk = AR.take([128, 16, 2, 32], F32)
        rt = [AR.take([128, 16, 32], F32) for _ in range(4)]
        qkt = [AR.take([128, 16, 64], BF16) for _ in range(2)]
        v_bf = [AR.take([128, 1024], BF16) for _ in range(2)]
        gr_bf = [AR.take([128, 1024], BF16) for _ in range(2)]
        g5_bf = [AR.take([128, 1024], BF16) for _ in range(2)]
        u_bf = [AR.take([128, 1024], BF16) for _ in range(2)]
        kT = [AR.take([128, 4, 128], BF16) for _ in range(2)]
        qm = [AR.take([128, 8, 128], BF16) for _ in range(2)]
        PT = AR.take([128, 8, 128], BF16)
        o_sb = AR.take([128, 8, 128], F32)
        osq = AR.take([128, 8, 128], F32)
        ret = [AR.take([128, 1024], BF16) for _ in range(2)]
        retT_blk = AR.take([128, 8, 256], BF16)
        stt = AR.take([128, 48], F32)
        ang = xTb[0].rearrange("p a b -> p (a b)")[:, 0:NCH * 32]
        tkf = xTb[1].rearrange("p a b -> p (a b)")[:, 0:NCH * 32]
        trr = trr_t[:, 0:NCH * 32]

        dma(gng, gng_d, w=["gng"])
        MEMSET("pool", qm[0], 0.0, ["qm0"])
        MEMSET("pool", qm[1], 0.0, ["qm1"])
        dma(posi, pos_d, w=["posi"])
        CP("dve", posf, posi, ["posi"], ["posf"])
        TT("dve", ang.rearrange("p (c i) -> p c i", i=32), posf.unsqueeze(2).to_broadcast([128, NCH, 32]),
           invf[:].unsqueeze(1).to_broadcast([128, NCH, 32]), ALU.mult, ["posf", "invf", "xT0"], ["xT0"])
        sincos(ang, None, sinT.rearrange("p c i -> p (c i)"), cosT.rearrange("p c i -> p (c i)"),
               (tkf, tki, trr, tmm), "xT0")
        P.barrier()
        for cb in (1, 2, 3, 6, 7, 0, 4, 5, 8, 9):
            cs_ = slice(cb * 512, (cb + 1) * 512)
            dma(w_in[:, :, cs_], w_in_d[:, :, cs_], w=[f"w_in{cb}"], eng="pool")
            TT("dve", w_in[:, :, cs_], w_in[:, :, cs_], g1[:].unsqueeze(2).to_broadcast([128, 8, 512]), ALU.mult, [f"w_in{cb}", "g1"], [f"w_in{cb}"])

        qk2 = qk.rearrange("p h t i -> p (h t i)")
        qkt2 = [q_.rearrange("p h d -> p (h d)") for q_ in qkt]
        o2 = o_sb.rearrange("p h e -> p (h e)")
        PSUM_R2 = (mode == "coll") and not (isinstance(dbg, dict) and dbg.get("psum_new"))
        if PSUM_R2:
            KVp = PS[3].rearrange("p (h e) -> p h e", h=8)
            TRb = PS[0][:, 512:1024].bitcast(BF16)
            kvi = lambda h: h
            kvs = lambda par: slice(par, 8, 2)
        else:
            KVp = PS[3][:, 0:512].rearrange("p (b e) -> p b e", b=4)
            TRb = PS[3][:, 512:1024].bitcast(BF16)
            kvi = lambda h: h // 2
            kvs = lambda par: slice(0, 4, 1)
        SCp = PS[1]
        Op = PS[2]
        pj_i = [0]

        def pj():
            if PSUM_R2:
                return PS[0][:, 0:512], "PJ0"
            i = pj_i[0] % 2
            pj_i[0] += 1
            return PS[0][:, i * 512:(i + 1) * 512], f"PJ{i}"

        nblk = NOWN_BLK
        LVL = dbg.get('own_lvl', 9) if isinstance(dbg, dict) else 9
        if mode == "coll":
            sa_iter = [("light", b_) for b_ in range(nblk)] + [("xchg", -1)] + [("full", b_) for b_ in range(nblk)]
            light_cbs = [1, 2, 3]
        else:
            sa_iter = [("light", b_) for b_ in range(NPRE)] + [("full", NPRE + b_) for b_ in range(nblk)]
            light_cbs = [1, 2, 3, 6, 7]
        def do_xchg():
            P.barrier()
            dma(ccR_in.ap()[0:64, :].rearrange("p (b e) -> p b e", b=4), Rf[0:64, 0::2, :], r=["Rf"], w=["ccR_in"])
            dma(ccR_in.ap()[64:128, :].rearrange("p (b e) -> p b e", b=4), Rf[64:128, 1::2, :], r=["Rf"], w=["ccR_in"])
            if FAKE_CC:
                dma(ccR_out.ap()[0:128, :], ccR_in.ap(), r=["ccR_in"], w=["ccR_out"])
            else:
                P.op("pool", lambda e: e.collective_compute("AllGather", ALU.bypass, replica_groups=[list(range(8))],
                                                            ins=[ccR_in.ap().opt()], outs=[ccR_out.ap().opt()]),
                     r=["ccR_in"], w=["ccR_out"])
            dma(Gt.rearrange("p r b e -> p r (b e)"), ccR_out.ap().rearrange("(r p) c -> p r c", p=128), r=["ccR_out"], w=["Gt"])
            TT("dve", Gt, Gt, wtr[:].unsqueeze(3).to_broadcast([128, 8, 4, 128]), ALU.mult, ["Gt", "wtr"], ["Gt"])
            RED(Rc, Gt.rearrange("p r b e -> p b e r"), ALU.add, ["Gt"], ["Rc"])
            CP("act", Rf[0:64, 0::2, :], Rc[0:64, :, :], ["Rc", "Rf"], ["Rf"])
            CP("act", Rf[64:128, 1::2, :], Rc[64:128, :, :], ["Rc", "Rf"], ["Rf"])
            CP("pool", Rbf[:], Rf[:], ["Rf"], ["Rbf"])
            P.barrier()


        def load_block(blk):
            dma(xTb[blk % 2], xT_d[:, :, blk * 256:(blk + 1) * 256], w=[f"xT{blk % 2}"])

        def norm_block(blk):
            xb = xTb[blk % 2]
            kx = f"xT{blk % 2}"
            bb = blk % 2
            ACT(sq, xb, AF.Square, [kx], ["sq"])
            ACT(hTb[bb], xb, AF.Copy, [kx], [f"hT{bb}"])
            pa, pk = pj()
            for ci in range(2):
                for kt in range(8):
                    MM(pa[:, ci:ci + 1], sq[:, kt, ci * 128:(ci + 1) * 128], ones_b[:, 0:1], kt == 0, kt == 7, ["sq", "ones_b"], [pk])
            ACT(rs1[:, 2 * bb:2 * bb + 2], pa[:, 0:2], AF.Sqrt, [pk], [f"rsa{bb}"], bias=EPS, scale=1.0 / 1024.0)
            RECIP(rs2[:, 2 * bb:2 * bb + 2], rs1[:, 2 * bb:2 * bb + 2], [f"rsa{bb}"], [f"rsb{bb}"])

        def phaseA(pas, blk, ci):
            own = pas == "full"
            c = blk * 2 + ci
            d = c % 2
            hT = hTb[blk % 2]
            hk = f"hT{blk % 2}"
            hs = lambda kt: hT[:, kt, ci * 128:(ci + 1) * 128]
            cbs = [0, 1, 2, 3, 4, 5, 6, 7, 8, 9] if own else light_cbs
            for cb in cbs:
                pa, pk = pj()
                for kt in range(8):
                    MM(pa, hs(kt), w_in[:, kt, cb * 512:(cb + 1) * 512], kt == 0, kt == 7, [hk, f"w_in{cb}"], [pk])
                rcol = rs2[:, 2 * (blk % 2) + ci:2 * (blk % 2) + ci + 1]
                rk = f"rsb{blk % 2}"
                if cb == 0:
                    ACT(qk2[:, 0:512], pa, AF.Copy, [pk, rk], ["qk_q"], scale=rcol)
                elif cb == 1:
                    ACT(qk2[:, 512:1024], pa, AF.Copy, [pk, rk], ["qk_k"], scale=rcol)
                elif cb in (2, 3):
                    TS("dve", v_bf[d][:, (cb - 2) * 512:(cb - 1) * 512], pa, rcol, None, ALU.mult, None, [pk, rk], [f"v_bf{d}"])
                elif cb in (4, 5):
                    ACT(gr_bf[d][:, (cb - 4) * 512:(cb - 3) * 512], pa, AF.Silu, [pk, rk], [f"gr_bf{d}"], scale=rcol)
                elif cb in (6, 7):
                    TS("dve", u_bf[d][:, (cb - 6) * 512:(cb - 5) * 512], pa, rcol, None, ALU.mult, None, [pk, rk], [f"u_bf{d}"])
                else:
                    ACT(g5_bf[d][:, (cb - 8) * 512:(cb - 7) * 512], pa, AF.Silu, [pk, rk], [f"g5_bf{d}"], scale=rcol)
            if own or 6 in light_cbs:
                dma(u_d[c * 128:(c + 1) * 128, :], u_bf[d], r=[f"u_bf{d}"])
            if own:
                oc = c - 2 * NPRE
                dma(g5_d[oc * 128:(oc + 1) * 128, :], g5_bf[d], r=[f"g5_bf{d}"])
            h0 = 0 if own else 8
            nh = 16 - h0
            cosb = cosT[:, c, :].unsqueeze(1).to_broadcast([128, nh, 32])
            sinb = sinT[:, c, :].unsqueeze(1).to_broadcast([128, nh, 32])
            x1 = qk[:, h0:16, 0, :]
            x2 = qk[:, h0:16, 1, :]
            rq = ["qk_q", "qk_k"] if own else ["qk_k"]
            qk3 = qk[:, h0:16, :, :].rearrange("p h t i -> p h (t i)")
            TT("pool", qk3, qk3, sqk[:, h0:16].unsqueeze(2).to_broadcast([128, nh, 64]), ALU.mult, rq + ["sqk"], rq)
            q4 = qkt[d].rearrange("p h (t i) -> p h t i", t=2)
            TT("dve", rt[0][:, h0:16, :], x1, cosb, ALU.mult, rq, ["rt0"])
            TT("dve", rt[1][:, h0:16, :], x2, sinb, ALU.mult, rq, ["rt1"])
            TT("dve", q4[:, h0:16, 0, :], rt[0][:, h0:16, :], rt[1][:, h0:16, :], ALU.subtract, ["rt0", "rt1"], [f"qkt{d}"])
            TT("pool", rt[2][:, h0:16, :], x1, sinb, ALU.mult, rq, ["rt2"])
            TT("pool", rt[3][:, h0:16, :], x2, cosb, ALU.mult, rq, ["rt3"])
            TT("pool", q4[:, h0:16, 1, :], rt[2][:, h0:16, :], rt[3][:, h0:16, :], ALU.add, ["rt2", "rt3"], [f"qkt{d}"])
            if own:
                for b in range(8):
                    TR(TRb[:, b * 128:(b + 1) * 128], qkt2[d][:, b * 128:(b + 1) * 128], identb[:], [f"qkt{d}", "identb"], ["TRb"])
                CP("act", kT[d].rearrange("p b t -> p (b t)"), TRb[:, 512:1024], ["TRb"], [f"kT{d}"])
                CP("act", qm[d][0:64, 0::2, :], TRb[0:64, 0:512].rearrange("p (b t) -> p b t", b=4), ["TRb"], [f"qm{d}"])
                CP("act", qm[d][64:128, 1::2, :], TRb[64:128, 0:512].rearrange("p (b t) -> p b t", b=4), ["TRb"], [f"qm{d}"])

        def phaseB(pas, blk, ci):
            own = pas == "full"
            c = blk * 2 + ci
            d = c % 2
            for h in range(8):
                MM(KVp[(h % 2) * 64:(h % 2) * 64 + 64, kvi(h), :], qkt2[d][:, 512 + h * 64:512 + (h + 1) * 64],
                   v_bf[d][:, h * 128:(h + 1) * 128], True, True, [f"qkt{d}", f"v_bf{d}"], ["KVp"])
            if own:
                for h in range(8):
                    MM(SCp[:, h * 128:(h + 1) * 128], kT[d][:, h // 2, :], qm[d][:, h, :], True, True, [f"kT{d}", f"qm{d}"], ["SCp"])
                TT("dve", PT, SCp.rearrange("p (h i) -> p h i", h=8), maskT[:].unsqueeze(1).to_broadcast([128, 8, 128]),
                   ALU.mult, ["SCp", "maskT"], ["PT"])
                for h in range(8):
                    MM(Op[:, h * 128:(h + 1) * 128], PT[:, h, :], v_bf[d][:, h * 128:(h + 1) * 128], True, False, ["PT", f"v_bf{d}"], ["Op"])
                    MM(Op[:, h * 128:(h + 1) * 128], qm[d][:, h, :], Rbf[:, h, :], False, True, [f"qm{d}", "Rbf"], ["Op"])
            for par in range(2):
                rows = slice(par * 64, par * 64 + 64)
                TT("dve", Rf[rows, par::2, :], KVp[rows, kvs(par), :], Rf[rows, par::2, :], ALU.add, ["KVp", "Rf"], ["Rf"])
                TT("dve", Rf[rows, par::2, :], Rf[rows, par::2, :], cdt[rows, par::2].unsqueeze(2).to_broadcast([64, 4, 128]), ALU.mult, ["Rf", "cdt"], ["Rf"])
                CP("act", Rbf[rows, par::2, :], Rf[rows, par::2, :], ["Rf"], ["Rbf"])
            if own:
                CP("act", o2, Op, ["Op"], ["o_sb"])
                ACT(osq.rearrange("p h e -> p (h e)"), Op, AF.Square, ["Op"], ["osq"])
                RED(stt[:, 0:8], o_sb, ALU.add, ["o_sb"], ["st_s"])
                RED(stt[:, 8:16], osq, ALU.add, ["osq"], ["st_q"])
                TS("dve", stt[:, 16:24], stt[:, 0:8], 1.0 / 128.0, None, ALU.mult, None, ["st_s"], ["st_m"])
                TT("dve", stt[:, 24:32], stt[:, 16:24], stt[:, 16:24], ALU.mult, ["st_m"], ["st_m2"])
                STT(stt[:, 32:40], stt[:, 8:16], 1.0 / 128.0, stt[:, 24:32], ALU.mult, ALU.subtract, ["st_q", "st_m2"], ["st_v"])
                ACT(stt[:, 40:48], stt[:, 32:40], AF.Sqrt, ["st_v"], ["st_sd"], bias=EPS, scale=1.0)
                RECIP(stt[:, 32:40], stt[:, 40:48], ["st_sd"], ["st_rs"])
                TT("pool", o_sb, o_sb, stt[:, 16:24].unsqueeze(2).to_broadcast([128, 8, 128]), ALU.subtract, ["o_sb", "st_m"], ["o_sb"])
                TT("pool", o_sb, o_sb, stt[:, 32:40].unsqueeze(2).to_broadcast([128, 8, 128]), ALU.mult, ["o_sb", "st_rs"], ["o_sb"])
                TT("dve", o2, o2, gng, ALU.mult, ["o_sb", "gng"], ["o_sb"])
                TT("dve", ret[d], o2, gr_bf[d], ALU.mult, ["o_sb", f"gr_bf{d}"], [f"ret{d}"])

        def phaseC(pas, blk, ci):
            if pas != "full":
                return
            c = blk * 2 + ci
            d = c % 2
            for kt in range(8):
                TR(TRb[:, kt * 128:(kt + 1) * 128], ret[d][:, kt * 128:(kt + 1) * 128], identb[:], [f"ret{d}", "identb"], ["TRb"])
            CP("act", retT_blk[:, :, ci * 128:(ci + 1) * 128], TRb.rearrange("p (k t) -> p k t", k=8), ["TRb"], ["retT_blk"])
            if ci == 1:
                ob = blk - NPRE
                dma(retT_d[:, :, ob * 256:(ob + 1) * 256], retT_blk, r=["retT_blk"])

        pendB = None
        pendC = None
        real = [(p_, b_) for p_, b_ in sa_iter if p_ != "xchg"]
        has_x = any(p_ == "xchg" for p_, _ in sa_iter)
        load_block(real[0][1])
        norm_block(real[0][1])
        ri = 0
        for pas, blk in sa_iter:
            if pas == "xchg":
                if pendB is not None:
                    phaseB(*pendB)
                    pendB = None
                do_xchg()
                load_block(real[ri][1])
                norm_block(real[ri][1])
                continue
            nxt = real[ri + 1] if ri + 1 < len(real) else None
            if nxt is not None and has_x and nxt[0] != pas:
                nxt = None
            for ci in range(2):
                if ci == 0 and nxt is not None:
                    load_block(nxt[1])
                phaseA(pas, blk, ci)
                if ci == 0 and nxt is not None:
                    norm_block(nxt[1])
                if pendC is not None:
                    phaseC(*pendC)
                    pendC = None
                if pendB is not None:
                    phaseB(*pendB)
                    pendC = pendB
                pendB = (pas, blk, ci)
            ri += 1
        if pendC is not None:
            phaseC(*pendC)
        if pendB is not None:
            phaseB(*pendB)
            phaseC(*pendB)
        P.barrier()
        if stop_after == "SA":
            return finish(nc, P, st, out_d)

        AR.reset()
        T0 = AR.take([128, 64, 128], BF16)
        I2Sre = AR.take([128, 64, 64], BF16)
        I2Sim = AR.take([128, 64, 64], BF16)
        S2Ore = AR.take([128, 32, 128], BF16)
        S2Oim = AR.take([128, 32, 128], BF16)
        TIre = AR.take([128, 32, 16], F32)
        TIim = AR.take([128, 32, 16], F32)
        TOre = AR.take([128, 32, 8], F32)
        TOim = AR.take([128, 32, 8], F32)
        P8 = AR.take([128, 32], F32)
        Q8 = AR.take([128, 32], F32)
        P128 = AR.take([128, 32], F32)
        Q128 = AR.take([128, 32], F32)
        base_sb = AR.off
        are = AR.take([128, 32], F32)
        aim = AR.take([128, 32], F32)
        ldt = AR.take([128, 32], F32)
        bre = AR.take([128, 32, 16], F32)
        bim = AR.take([128, 32, 16], F32)
        cre = AR.take([128, 32, 16], F32)
        cim = AR.take([128, 32, 16], F32)
        dtt = AR.take([128, 32], F32)
        tA = [AR.take([128, 32], F32) for _ in range(12)]
        tI = AR.take([128, 32], I32)
        PWr = AR.take([128, 9, 32], F32)
        PWi = AR.take([128, 9, 32], F32)
        NWr = AR.take([128, 9, 32], F32)
        NWi = AR.take([128, 9, 32], F32)
        E8r = AR.take([128, 17, 32], F32)
        E8i = AR.take([128, 17, 32], F32)
        E16r = AR.take([128, 9, 32], F32)
        E16i = AR.take([128, 9, 32], F32)
        Bbr = AR.take([128, 32, 16], F32)
        Bbi = AR.take([128, 32, 16], F32)
        Gre = AR.take([128, 32, 8, 16], F32)
        Gim = AR.take([128, 32, 8, 16], F32)
        Sre = AR.take([128, 32, 8, 16], F32)
        Sim = AR.take([128, 32, 8, 16], F32)
        Ire = AR.take([128, 32, 8, 16], F32)
        Iim = AR.take([128, 32, 8, 16], F32)
        tB = [AR.take([128, 32, 16], F32) for _ in range(2)]
        T0f = AR.take([128, 4, 128], F32)
        dcol = AR.take([128, 64], F32)

        K0 = ["S0B"]
        dma(dcol, dcol_d, w=K0)
        for t_, d_ in ((are, are_d), (aim, aim_d), (ldt, ldt_d), (bre, bre_d), (bim, bim_d), (cre, cre_d), (cim, cim_d)):
            dma(t_, d_, w=K0)
        P.barrier()
        V = lambda out, in0, in1, op: TT("dve", out, in0, in1, op, K0, K0)
        VS = lambda out, in0, s1, op: TS("dve", out, in0, s1, None, op, None, K0, K0)
        VC = lambda out, in_: CP("dve", out, in_, K0, K0)
        VM = lambda ap, val: MEMSET("dve", ap, val, K0)
        ACT(dtt, ldt, AF.Exp, K0, K0)
        V(tA[0], are, dtt, ALU.mult)
        ACT(tA[1], tA[0], AF.Exp, K0, K0)
        ACT(tA[2], tA[0], AF.Exp, K0, K0, scale=-1.0)
        V(tA[3], aim, dtt, ALU.mult)
        sincos(tA[3], None, tA[4], tA[5], (tA[6], tI, tA[7], tA[8]), "S0B")
        P.barrier()

        def cmul(orr, oi, xr, xi, yr, yi, tmp):
            u1, u2 = tmp
            V(u1, xr, yr, ALU.mult)
            V(u2, xi, yi, ALU.mult)
            V(orr, u1, u2, ALU.subtract)
            V(u1, xr, yi, ALU.mult)
            V(u2, xi, yr, ALU.mult)
            V(oi, u1, u2, ALU.add)

        tsm = (tA[10], tA[11])
        VM(PWr[:, 0, :], 1.0)
        VM(PWi[:, 0, :], 0.0)
        V(PWr[:, 1, :], tA[1], tA[5], ALU.mult)
        V(PWi[:, 1, :], tA[1], tA[4], ALU.mult)
        VM(NWr[:, 0, :], 1.0)
        VM(NWi[:, 0, :], 0.0)
        V(NWr[:, 1, :], tA[2], tA[5], ALU.mult)
        V(tA[9], tA[2], tA[4], ALU.mult)
        VS(NWi[:, 1, :], tA[9], -1.0, ALU.mult)
        for k in range(1, 8):
            cmul(PWr[:, k + 1, :], PWi[:, k + 1, :], PWr[:, k, :], PWi[:, k, :], PWr[:, 1, :], PWi[:, 1, :], tsm)
            cmul(NWr[:, k + 1, :], NWi[:, k + 1, :], NWr[:, k, :], NWi[:, k, :], NWr[:, 1, :], NWi[:, 1, :], tsm)
        VM(E8r[:, 0, :], 1.0)
        VM(E8i[:, 0, :], 0.0)
        VC(E8r[:, 1, :], PWr[:, 8, :])
        VC(E8i[:, 1, :], PWi[:, 8, :])
        for k in range(1, 16):
            cmul(E8r[:, k + 1, :], E8i[:, k + 1, :], E8r[:, k, :], E8i[:, k, :], E8r[:, 1, :], E8i[:, 1, :], tsm)
        VM(E16r[:, 0, :], 1.0)
        VM(E16i[:, 0, :], 0.0)
        VC(E16r[:, 1, :], E8r[:, 16, :])
        VC(E16i[:, 1, :], E8i[:, 16, :])
        for k in range(1, 8):
            cmul(E16r[:, k + 1, :], E16i[:, k + 1, :], E16r[:, k, :], E16i[:, k, :], E16r[:, 1, :], E16i[:, 1, :], tsm)
        VC(P8, E8r[:, 1, :])
        VC(Q8, E8i[:, 1, :])
        VC(P128, E16r[:, 8, :])
        VC(Q128, E16i[:, 8, :])
        cmul(A2r[:, 0, :], A2i[:, 0, :], P128, Q128, P128, Q128, tsm)
        cmul(A2r[:, 1, :], A2i[:, 1, :], A2r[:, 0, :], A2i[:, 0, :], A2r[:, 0, :], A2i[:, 0, :], tsm)
        for r_ in range(16):
            VC(TIre[:, :, r_], E8r[:, 15 - r_, :])
            VC(TIim[:, :, r_], E8i[:, 15 - r_, :])
        for q_ in range(8):
            VC(TOre[:, :, q_], E16r[:, 7 - q_, :])
            VC(TOim[:, :, q_], E16i[:, 7 - q_, :])
        V(tA[0], are, are, ALU.mult)
        V(tA[1], aim, aim, ALU.mult)
        V(tA[0], tA[0], tA[1], ALU.add)
        RECIP(tA[0], tA[0], K0, K0)
        VS(tA[1], PWr[:, 1, :], -1.0, ALU.add)
        V(tA[2], tA[1], are, ALU.mult)
        V(tA[3], PWi[:, 1, :], aim, ALU.mult)
        V(tA[2], tA[2], tA[3], ALU.add)
        V(tA[2], tA[2], tA[0], ALU.mult)
        V(tA[3], PWi[:, 1, :], are, ALU.mult)
        V(tA[4], tA[1], aim, ALU.mult)
        V(tA[3], tA[3], tA[4], ALU.subtract)
        V(tA[3], tA[3], tA[0], ALU.mult)
        bc16 = lambda ap: ap.unsqueeze(2).to_broadcast([128, 32, 16])
        cmul(Bbr, Bbi, bre, bim, bc16(tA[2]), bc16(tA[3]), (tB[0], tB[1]))
        for s_ in range(8):
            cmul(Gre[:, :, s_, :], Gim[:, :, s_, :], Bbr, Bbi, bc16(NWr[:, s_ + 1, :]), bc16(NWi[:, s_ + 1, :]), (tB[0], tB[1]))
            cmul(Ire[:, :, s_, :], Iim[:, :, s_, :], Bbr, Bbi, bc16(PWr[:, 7 - s_, :]), bc16(PWi[:, 7 - s_, :]), (tB[0], tB[1]))
            cmul(Sre[:, :, s_, :], Sim[:, :, s_, :], cre, cim, bc16(PWr[:, s_ + 1, :]), bc16(PWi[:, s_ + 1, :]), (tB[0], tB[1]))
        Sim2 = Sim.rearrange("p g t j -> p (g t j)")
        VS(Sim2, Sim2, -1.0, ALU.mult)
        P.barrier()
        CP("act", S2Ore.rearrange("p g x -> p (g x)"), Sre.rearrange("p g t j -> p (g t j)"), K0, ["S2O"])
        CP("act", S2Oim.rearrange("p g x -> p (g x)"), Sim2, K0, ["S2O"])
        Gre3 = Gre.rearrange("p g s j -> p g (s j)")
        Gim3 = Gim.rearrange("p g s j -> p g (s j)")
        Sre3 = Sre.rearrange("p g t j -> p g (t j)")
        Sim3 = Sim.rearrange("p g t j -> p g (t j)")
        Ire3 = Ire.rearrange("p g s j -> p g (s j)")
        Iim3 = Iim.rearrange("p g s j -> p g (s j)")
        for gb in range(16):
            bank = PS[gb % 2][:, 0:512]
            bk = f"PSb{gb % 2}"
            for i in range(4):
                g = gb * 4 + i
                gh, gp = g // 32, g % 32
                pb = gh * 64
                MM(bank[:, i * 128:(i + 1) * 128], Gre3[pb:pb + 64, gp, :], Sre3[pb:pb + 64, gp, :], True, False, K0, [bk])
                MM(bank[:, i * 128:(i + 1) * 128], Gim3[pb:pb + 64, gp, :], Sim3[pb:pb + 64, gp, :], False, True, K0, [bk])
            TT("dve", T0f, bank.rearrange("p (g x) -> p g x", g=4),
               t0mask[:].unsqueeze(1).to_broadcast([128, 4, 128]), ALU.mult, [bk, "t0mask"] + K0, K0)
            for i in range(4):
                g = gb * 4 + i
                STT(T0[:, g, :], identf[:], dcol[:, g:g + 1], T0f[:, i, :], ALU.mult, ALU.add, K0 + ["identf"], K0 + ["T0"])
        for src3, dst, nm in ((Ire3, I2Sre, "I2Sre"), (Iim3, I2Sim, "I2Sim")):
            for gb in range(8):
                bank = PS[2 + gb % 2][:, 0:512]
                bk = f"PSc{gb % 2}"
                for i in range(8):
                    g = gb * 8 + i
                    gh, gp = g // 32, g % 32
                    pb = gh * 64
                    TR(bank[:, i * 64:(i + 1) * 64], src3[pb:pb + 64, gp, :], identf[pb:pb + 64, pb:pb + 64], ["identf", "S0B"], [bk])
                CP("act", dst[:, gb * 8:(gb + 1) * 8, :], bank.rearrange("p (g n) -> p g n", g=8), [bk], [nm])
        P.barrier()
        if stop_after == "S0b":
            return finish(nc, P, st, out_d)

        AR.reset(base_sb)
        u_raw = AR.take([128, 8, 1024], BF16)
        U = u_raw.rearrange("p s c -> p (s c)").rearrange("p (g m) -> p g m", g=64)
        u_tm2 = AR.take([128, 64, 8, 16], BF16)
        inc_raw = AR.take([128, 4096], F32)
        Incre = inc_raw.rearrange("p (g m) -> p g m", g=32)
        yg_bf = inc_raw.bitcast(BF16).rearrange("p (s c) -> p s c", s=8)
        Incim = AR.take([128, 32, 128], F32)
        gluw = AR.take([128, 8, 1024], BF16)
        glub = AR.take([128, 1024], F32)
        g5s = [AR.take([128, 1024], BF16) for _ in range(2)]
        ct = [AR.take([128, 32], F32) for _ in range(12)]
        un_base = AR.off
        LA = AR.take([128, 32, 8, 16], F32)
        LB = AR.take([128, 32, 8, 16], F32)
        LS = [AR.take([128, 32, 8], F32) for _ in range(4)]
        LE = [AR.take([128, 32], F32) for _ in range(4)]
        AR.reset(un_base)
        Xre_bf = AR.take([128, 32, 128], BF16)
        Xim_bf = AR.take([128, 32, 128], BF16)
        Ysb = [AR.take([128, 8, 128], F32) for _ in range(2)]
        gt1 = [AR.take([128, 1024], F32) for _ in range(2)]
        gt2 = [AR.take([128, 1024], F32) for _ in range(2)]
        AR.reset(un_base)
        ygT = [AR.take([128, 8, 128], BF16) for _ in range(2)]
        zb = [AR.take([128, 1024], F32) for _ in range(2)]
        gate = [AR.take([128, 1024], F32) for _ in range(2)]
        tt_ = [AR.take([128, 1024], F32) for _ in range(2)]
        ssm = [AR.take([128, 1024], BF16) for _ in range(2)]
        ssmT = [AR.take([128, 8, 128], BF16) for _ in range(2)]
        AR.reset(un_base)

        for kt in range(8):
            dma(gluw[:, kt, :], gluw_d[:, kt, :], w=["gluw"], eng="pool")
        dma(glub, glub_d, w=["glub"])

        TRU = PS[0][:, 0:512].bitcast(BF16)
        PIre = PS[1][:, 0:512]
        PIim = PS[1][:, 512:1024]
        TRU2 = [PS[0][:, 0:512].bitcast(BF16), PS[0][:, 512:1024].bitcast(BF16)]
        PI2 = [(PS[1][:, 0:512], PS[1][:, 512:1024]), (PS[2][:, 0:512], PS[2][:, 512:1024])]
        PY = PS[2]
        PYT = PS[3]
        IRK = [f"IR{m}" for m in range(128)]
        IIK = [f"II{m}" for m in range(128)]

        Gs = AR.take([128, 8, 64], F32)
        Gs2 = AR.take([128, 8, 64], F32)
        Sd = AR.take([128, 3, 64], F32)
        xt_ = [AR.take([128, 32], F32) for _ in range(4)]
        NPS = NPRE // 4
        if mode == "coll":
            sb_iter = [("light", 0), ("light", 1), ("xchg", -1), ("full", 0), ("full", 1)]
        else:
            sb_iter = [("light", sc_) for sc_ in range(NPS)] + [("full", NPS), ("full", NPS + 1)]
        for pas, sc in sb_iter:
            if pas == "xchg":
                P.barrier()
                dma(ccS_in.ap()[:, 0:32], car_r[:], r=["car_r"], w=["ccS_in"])
                dma(ccS_in.ap()[:, 32:64], car_i[:], r=["car_i"], w=["ccS_in"])
                if FAKE_CC:
                    dma(ccS_out.ap()[0:128, :], ccS_in.ap(), r=["ccS_in"], w=["ccS_out"])
                else:
                    P.op("pool", lambda e: e.collective_compute("AllGather", ALU.bypass, replica_groups=[list(range(8))],
                                                                ins=[ccS_in.ap().opt()], outs=[ccS_out.ap().opt()]),
                         r=["ccS_in"], w=["ccS_out"])
                dma(Gs, ccS_out.ap().rearrange("(r p) c -> p r c", p=128), r=["ccS_out"], w=["Gs"])
                KX = ["XCH"]
                for d_ in range(3):
                    TT("dve", Gs2, Gs, sel5[:, d_, :].unsqueeze(2).to_broadcast([128, 8, 64]), ALU.mult, ["Gs", "sel5"] + KX, KX)
                    RED(Sd[:, d_, :], Gs2.rearrange("p r c -> p c r"), ALU.add, KX, KX)
                XV = lambda out, in0, in1, op: TT("dve", out, in0, in1, op, KX + ["car_r", "car_i"], KX)
                XV(car_r[:], Sd[:, 0, 0:32], Sd[:, 0, 0:32], ALU.bypass) if False else CP("dve", car_r[:], Sd[:, 0, 0:32], KX + ["car_r"], KX + ["car_r"])
                CP("dve", car_i[:], Sd[:, 0, 32:64], KX + ["car_i"], KX + ["car_i"])
                for d_ in (1, 2):
                    sr, si = Sd[:, d_, 0:32], Sd[:, d_, 32:64]
                    ar_, ai_ = A2r[:, d_ - 1, :], A2i[:, d_ - 1, :]
                    XV(xt_[0], sr, ar_, ALU.mult)
                    XV(xt_[1], si, ai_, ALU.mult)
                    XV(xt_[0], xt_[0], xt_[1], ALU.subtract)
                    TT("dve", car_r[:], car_r[:], xt_[0], ALU.add, KX + ["car_r"], KX + ["car_r"])
                    XV(xt_[2], sr, ai_, ALU.mult)
                    XV(xt_[3], si, ar_, ALU.mult)
                    XV(xt_[2], xt_[2], xt_[3], ALU.add)
                    TT("dve", car_i[:], car_i[:], xt_[2], ALU.add, KX + ["car_i"], KX + ["car_i"])
                P.barrier()
                continue
            own = pas == "full"
            if own:
                P.barrier()
            dma(u_raw, u_d[sc * 1024:(sc + 1) * 1024, :].rearrange("(m s) c -> m s c", s=8), w=["u_raw", "U"])
            src4 = u_raw.rearrange("p s (g j) -> p g s j", j=16)
            CP("dve", u_tm2[:, 0:32, :, :], src4[:, 0:32, :, :], ["u_raw"], ["u_tm2a"])
            CP("act", u_tm2[:, 32:64, :, :], src4[:, 32:64, :, :], ["u_raw"], ["u_tm2b"])
            u3 = u_tm2.rearrange("p g s j -> p g (s j)")
            for gb in range(8):
                tru = TRU2[gb % 2]
                tk = f"TRU{gb % 2}"
                for i in range(8):
                    g = gb * 8 + i
                    TR(tru[:, i * 128:(i + 1) * 128], u3[:, g, :], identb[:], ["u_tm2a", "u_tm2b", "identb"], [tk])
                CP("act", U[:, gb * 8:(gb + 1) * 8, :], tru.rearrange("p (g m) -> p g m", g=8),
                   [tk], ["U", "u_raw"])
            for b4 in range(8):
                pre, pim = PI2[b4 % 2]
                kre, kim = f"PIre{b4 % 2}", f"PIim{b4 % 2}"
                for i in range(4):
                    gp = b4 * 4 + i
                    for gh in range(2):
                        g = gh * 32 + gp
                        MM(pre[gh * 64:(gh + 1) * 64, i * 128:(i + 1) * 128], I2Sre[:, g, :], U[:, g, :], True, True, ["U", "I2Sre"], [kre])
                        MM(pim[gh * 64:(gh + 1) * 64, i * 128:(i + 1) * 128], I2Sim[:, g, :], U[:, g, :], True, True, ["U", "I2Sim"], [kim])
                CP("act", Incre[:, b4 * 4:(b4 + 1) * 4, :], pre.rearrange("p (g m) -> p g m", g=4), [kre], IRK)
                CP("dve", Incim[:, b4 * 4:(b4 + 1) * 4, :], pim.rearrange("p (g m) -> p g m", g=4), [kim], IIK)
            if not own:
                I4r = Incre.rearrange("p g (q r) -> p g q r", r=16)
                I4i = Incim.rearrange("p g (q r) -> p g q r", r=16)
                tir = TIre.unsqueeze(2).to_broadcast([128, 32, 8, 16])
                tii = TIim.unsqueeze(2).to_broadcast([128, 32, 8, 16])
                TT("dve", LA, I4r, tir, ALU.mult, IRK + ["TI"], ["LA"])
                TT("dve", LB, I4i, tii, ALU.mult, IIK + ["TI"], ["LB"])
                TT("dve", LA, LA, LB, ALU.subtract, ["LA", "LB"], ["LA"])
                RED(LS[0], LA, ALU.add, ["LA"], ["LS0"])
                TT("dve", LA, I4i, tir, ALU.mult, IIK + ["TI", "LS0"], ["LA"])
                TT("dve", LB, I4r, tii, ALU.mult, IRK + ["TI", "LA"], ["LB"])
                TT("dve", LA, LA, LB, ALU.add, ["LA", "LB"], ["LA"])
                RED(LS[1], LA, ALU.add, ["LA"], ["LS1"])
                TT("dve", LS[2], LS[0], TOre, ALU.mult, ["LS0", "TO"], ["LS2"])
                TT("dve", LS[3], LS[1], TOim, ALU.mult, ["LS1", "TO"], ["LS3"])
                TT("dve", LS[2], LS[2], LS[3], ALU.subtract, ["LS2", "LS3"], ["LS2"])
                RED(LE[0], LS[2], ALU.add, ["LS2"], ["LE0"])
                TT("dve", LS[2], LS[1], TOre, ALU.mult, ["LS1", "TO", "LE0"], ["LS2"])
                TT("dve", LS[3], LS[0], TOim, ALU.mult, ["LS0", "TO", "LS2"], ["LS3"])
                TT("dve", LS[2], LS[2], LS[3], ALU.add, ["LS2", "LS3"], ["LS2"])
                RED(LE[1], LS[2], ALU.add, ["LS2"], ["LE1"])
                TT("dve", LE[2], car_r[:], P128, ALU.mult, ["car_r", "P128"], ["LE2"])
                TT("dve", LE[3], car_i[:], Q128, ALU.mult, ["car_i", "Q128"], ["LE3"])
                TT("dve", LE[2], LE[2], LE[3], ALU.subtract, ["LE2", "LE3"], ["LE2"])
                TT("dve", LE[0], LE[0], LE[2], ALU.add, ["LE0", "LE2"], ["LE0"])
                TT("dve", LE[2], car_i[:], P128, ALU.mult, ["car_i", "P128", "LE0"], ["LE2"])
                TT("dve", LE[3], car_r[:], Q128, ALU.mult, ["car_r", "Q128", "LE0"], ["LE3"])
                TT("dve", LE[2], LE[2], LE[3], ALU.add, ["LE2", "LE3"], ["LE2"])
                TT("dve", car_i[:], LE[1], LE[2], ALU.add, ["LE1", "LE2"], ["car_i"])
                CP("dve", car_r[:], LE[0], ["LE0"], ["car_r"])
                continue
            CP("dve", Xre_bf[:, :, 0], car_r[:], ["car_r"], ["Xre0"])
            CP("dve", Xim_bf[:, :, 0], car_i[:], ["car_i"], ["Xim0"])
            for m in range(128):
                if m == 0:
                    xr, xi, kr, ki = car_r[:], car_i[:], "car_r", "car_i"
                else:
                    xr, xi, kr, ki = Incre[:, :, m - 1], Incim[:, :, m - 1], IRK[m - 1], IIK[m - 1]
                pp = (m % 2) * 6
                c0, c1, c2, c3, c4, c5 = ct[pp:pp + 6]
                n0 = [f"ct{pp + j}" for j in range(6)]
                TT("dve", c0, xr, P8, ALU.mult, [kr, "P8"], [n0[0]])
                TT("dve", c1, xi, Q8, ALU.mult, [ki, "Q8"], [n0[1]])
                TT("dve", c3, xi, P8, ALU.mult, [ki, "P8"], [n0[3]])
                TT("dve", c4, xr, Q8, ALU.mult, [kr, "Q8"], [n0[4]])
                TT("dve", c2, c0, c1, ALU.subtract, [n0[0], n0[1]], [n0[2]])
                TT("dve", c5, c3, c4, ALU.add, [n0[3], n0[4]], [n0[5]])
                TT("dve", Incre[:, :, m], c2, Incre[:, :, m], ALU.add, [n0[2], IRK[m]], [IRK[m]])
                TT("dve", Incim[:, :, m], c5, Incim[:, :, m], ALU.add, [n0[5], IIK[m]], [IIK[m]])
            CP("dve", car_r[:], Incre[:, :, 127], [IRK[127], "Xre0"], ["car_r"])
            CP("dve", car_i[:], Incim[:, :, 127], [IIK[127], "Xim0"], ["car_i"])
            CP("act", Xre_bf[:, :, 1:128], Incre[:, :, 0:127], IRK, ["Xre"])
            CP("dve", Xim_bf[:, :, 1:128], Incim[:, :, 0:127], IIK, ["Xim"])
            PYb = [PS[1], PS[2]]
            PYTb = [PS[3], PS[0]]

            def y_mm(gb):
                bq = gb % 2
                for i in range(8):
                    g = gb * 8 + i
                    gh, gp = g // 32, g % 32
                    pb = gh * 64
                    oy = PYb[bq][:, i * 128:(i + 1) * 128]
                    MM(oy, T0[:, g, :], U[:, g, :], True, False, ["T0", "U"], [f"PY{bq}"])
                    MM(oy, S2Ore[pb:pb + 64, gp, :], Xre_bf[pb:pb + 64, gp, :], False, False, ["S2O", "Xre", "Xre0"], [f"PY{bq}"])
                    MM(oy, S2Oim[pb:pb + 64, gp, :], Xim_bf[pb:pb + 64, gp, :], False, True, ["S2O", "Xim", "Xim0"], [f"PY{bq}"])

            def y_ew(gb):
                bq = gb % 2
                pyt = PYTb[bq]
                ky, kt_ = f"PY{bq}", f"PYT{bq}"
                CP("act", Ysb[bq].rearrange("p g m -> p (g m)"), PYb[bq], [ky], [f"Ysb{bq}"])
                for i in range(8):
                    TR(pyt[:, i * 128:(i + 1) * 128], Ysb[bq][:, i, :], identf[:], [f"Ysb{bq}", "identf"], [kt_])
                ACT(gt1[bq], pyt, AF.Square, [kt_], [f"gt1{bq}"])
                TS("dve", gt1[bq], gt1[bq], KG * 0.044715, KG, ALU.mult, ALU.add, [f"gt1{bq}"], [f"gt1{bq}"])
                TT("dve", gt1[bq], gt1[bq], pyt, ALU.mult, [f"gt1{bq}", kt_], [f"gt1{bq}"])
                ACT(gt2[bq], gt1[bq], AF.Sigmoid, [f"gt1{bq}"], [f"gt2{bq}"])
                TT("dve", yg_bf[:, :, gb * 128:(gb + 1) * 128].rearrange("p t (g j) -> p g t j", j=16),
                   pyt.rearrange("p (g t j) -> p g t j", g=8, t=8), gt2[bq].rearrange("p (g t j) -> p g t j", g=8, t=8),
                   ALU.mult, [kt_, f"gt2{bq}"], ["yg_bf"])

            y_mm(0)
            for gb in range(8):
                if gb + 1 < 8:
                    y_mm(gb + 1)
                y_ew(gb)
            P.barrier()
            oc = sc - NPS
            g5v = g5_d[oc * 1024:(oc + 1) * 1024, :].rearrange("(m s) c -> m s c", s=8)
            TRa = [PS[0][:, 0:512].bitcast(BF16), PS[0][:, 512:1024].bitcast(BF16)]
            TRs = [PS[3][:, 0:512].bitcast(BF16), PS[3][:, 512:1024].bitcast(BF16)]
            PZb = [PS[1], PS[2]]

            def tail1(s_):
                bq = s_ % 2
                dma(g5s[bq], g5v[:, s_, :], w=[f"g5s{bq}"])
                for kt in range(8):
                    TR(TRa[bq][:, kt * 128:(kt + 1) * 128], yg_bf[:, s_, kt * 128:(kt + 1) * 128], identb[:], ["yg_bf", "identb"], [f"TRa{bq}"])
                CP("act", ygT[bq].rearrange("p k m -> p (k m)"), TRa[bq], [f"TRa{bq}"], [f"ygT{bq}"])
                for cb in range(2):
                    pz = PZb[bq][:, cb * 512:(cb + 1) * 512]
                    for kt in range(8):
                        MM(pz, ygT[bq][:, kt, :], gluw[:, kt, cb * 512:(cb + 1) * 512], kt == 0, kt == 7, [f"ygT{bq}", "gluw"], [f"PZ{bq}"])

            def tail2(s_):
                bq = s_ % 2
                TT("dve", zb[bq], PZb[bq], glub, ALU.add, [f"PZ{bq}", "glub"], [f"zb{bq}"])
                ACT(gate[bq], zb[bq], AF.Sigmoid, [f"zb{bq}"], [f"gate{bq}"])
                TT("dve", tt_[bq], yg_bf[:, s_, :], gate[bq], ALU.mult, ["yg_bf", f"gate{bq}"], [f"tt_{bq}"])
                TT("pool", ssm[bq], tt_[bq], g5s[bq], ALU.mult, [f"tt_{bq}", f"g5s{bq}"], [f"ssm{bq}"])
                for kt in range(8):
                    TR(TRs[bq][:, kt * 128:(kt + 1) * 128], ssm[bq][:, kt * 128:(kt + 1) * 128], identb[:], [f"ssm{bq}", "identb"], [f"TRs{bq}"])
                CP("act", ssmT[bq].rearrange("p k m -> p (k m)"), TRs[bq], [f"TRs{bq}"], [f"ssmT{bq}"])
                dma(ssmT_d[oc, s_], ssmT[bq], r=[f"ssmT{bq}"])

            tail1(0)
            for s_ in range(8):
                if s_ + 1 < 8:
                    tail1(s_ + 1)
                tail2(s_)
        P.barrier()
        if stop_after == "SB":
            return finish(nc, P, st, out_d)

        AR.reset()
        Wqk = AR.take([128, 8, 1024], BF16)
        VO = AR.take([128, 8, 1024], BF16)
        wout = AR.take([128, 16, 1024], BF16)
        g2r = AR.take([128, 1024], F32)
        gfr = AR.take([128, 1024], F32)
        base_sc = AR.off
        wqT = AR.take([128, 8, 1024], BF16)
        wk = AR.take([128, 8, 1024], BF16)
        wv = AR.take([128, 8, 1024], BF16)
        wo = AR.take([128, 8, 1024], BF16)
        memT = AR.take([128, 8, 256], F32)
        msq = AR.take([128, 8, 256], BF16)
        mrs1 = AR.take([128, 256], F32)
        mrs2 = AR.take([128, 256], F32)
        mnT = AR.take([128, 8, 256], BF16)
        kaT = AR.take([128, 8, 256], BF16)
        vaT = AR.take([128, 8, 256], BF16)
        for t_, d_, k_ in ((wqT, wqT_d, "wqT"), (wk, wk_d, "wk"), (wv, wv_d, "wv"), (wo, wo_d, "wo")):
            for kt in range(8):
                dma(t_[:, kt, :], d_[:, kt, :], w=[k_], eng="pool")
        for kt in range(16):
            dma(wout[:, kt, :], wout_d[:, kt, :], w=["wout"], eng="pool")
        dma(g2r, g2_d, w=["g2r"])
        dma(gfr, gf_d, w=["gfr"])
        dma(memT, memT_d, w=["memT"])
        norm_fm(memT, 256, gm[:], msq, mrs1, mrs2, mnT, PS[0][:, 0:256], "memT", "msq", "mrs", "mnT", "PSm", "gm")
        for r_ in range(8):
            for wt, dst, nm, pi in ((wk, kaT, "kaT", 1), (wv, vaT, "vaT", 2)):
                pp_ = PS[pi][:, (r_ % 2) * 512:(r_ % 2) * 512 + 256]
                pk = f"PSkv{pi}_{r_ % 2}"
                for kt in range(8):
                    MM(pp_, wt[:, kt, r_ * 128:(r_ + 1) * 128], mnT[:, kt, :], kt == 0, kt == 7, [nm[:2] if False else ("wk" if pi == 1 else "wv"), "mnT"], [pk])
                CP("act" if pi == 1 else "dve", dst[:, r_, :], pp_, [pk], [nm])
        for dt_ in range(8):
            pw_ = PS[dt_ % 2]
            pk = f"PSw{dt_ % 2}"
            for h in range(4):
                for hf in range(2):
                    MM(pw_[:, h * 256:(h + 1) * 256], wqT[:, 2 * h + hf, dt_ * 128:(dt_ + 1) * 128], kaT[:, 2 * h + hf, :],
                       hf == 0, hf == 1, ["wqT", "kaT"], [pk])
            ACT(Wqk[:, dt_, :], pw_, AF.Copy, [pk], ["Wqk"], scale=1.0 / 16.0)
        for h in range(4):
            for mh in range(2):
                idx = 2 * h + mh
                pw_ = PS[2 + idx % 2]
                pk = f"PSv{idx % 2}"
                for cb in range(2):
                    for hf in range(2):
                        MM(pw_[:, cb * 512:(cb + 1) * 512], vaT[:, 2 * h + hf, mh * 128:(mh + 1) * 128], wo[:, 2 * h + hf, cb * 512:(cb + 1) * 512],
                           hf == 0, hf == 1, ["vaT", "wo"], [pk])
                CP("dve", VO[:, idx, :], pw_, [pk], ["VO"])
        P.barrier()

        AR.reset(base_sc)
        retT_sc = [AR.take([128, 8, 1024], BF16) for _ in range(2)]
        ssmT_s = [AR.take([128, 8, 128], BF16) for _ in range(3)]
        x_t = [AR.take([128, 1024], F32) for _ in range(3)]
        x1 = [AR.take([128, 1024], F32) for _ in range(3)]
        junk = AR.take([128, 1024], F32)
        h2 = [AR.take([128, 1024], BF16) for _ in range(2)]
        h2T = AR.take([128, 8, 128], BF16)
        Pex = AR.take([128, 4, 256], BF16)
        Pn = [AR.take([128, 4, 256], BF16) for _ in range(2)]
        PTt = AR.take([128, 8, 128], BF16)
        x2 = AR.take([128, 1024], F32)
        osb = [AR.take([128, 1024], F32) for _ in range(2)]
        sst = [AR.take([128, 32], F32) for _ in range(3)]
        TRc = PS[3][:, 0:512].bitcast(BF16)
        TRd = PS[3][:, 512:1024].bitcast(BF16)
        tiles = [(sco, s_) for sco in range(2) for s_ in range(8)]
        dma(retT_sc[0], retT_d[:, :, 0:1024], w=["retT_sc0"])
        dma(retT_sc[1], retT_d[:, :, 1024:2048], w=["retT_sc1"])

        def sc_load(t):
            sco, s_ = tiles[t]
            b3 = t % 3
            dma(ssmT_s[b3], ssmT_d[sco, s_], w=[f"ssmT_s{b3}"])
            dma(x_t[b3], xown_d[sco, :, s_, :], w=[f"x_t{b3}"])

        def sc_s1(t):
            sco, s_ = tiles[t]
            b3 = t % 3
            st_ = sst[t % 3]
            sk = f"sst{t % 3}"
            for cb in range(2):
                px = PS[0][:, cb * 512:(cb + 1) * 512]
                for kt in range(8):
                    MM(px, retT_sc[sco][:, kt, s_::8], wout[:, kt, cb * 512:(cb + 1) * 512], kt == 0, False, [f"retT_sc{sco}", "wout"], [f"PX{cb}"])
                for kt in range(8):
                    MM(px, ssmT_s[b3][:, kt, :], wout[:, 8 + kt, cb * 512:(cb + 1) * 512], False, kt == 7, [f"ssmT_s{b3}", "wout"], [f"PX{cb}"])
            TT("dve", x1[b3], PS[0], x_t[b3], ALU.add, ["PX0", "PX1", f"x_t{b3}"], [f"x1_{b3}"])
            ACT(junk, x1[b3], AF.Square, [f"x1_{b3}"], ["junk", sk + "a"], accum_out=st_[:, 0:1])
            ACT(st_[:, 1:2], st_[:, 0:1], AF.Ln, [sk + "a"], [sk + "b"], bias=epsc[:, 0:1], scale=1.0 / 1024.0)
            ACT(st_[:, 2:3], st_[:, 1:2], AF.Exp, [sk + "b"], [sk + "c"], scale=-0.5)
            STT(h2[t % 2], x1[b3], st_[:, 2:3], g2r, ALU.mult, ALU.mult, [f"x1_{b3}", sk + "c", "g2r"], [f"h2_{t % 2}"])

        def sc_s2(t):
            st_ = sst[t % 3]
            sk = f"sst{t % 3}"
            hh = h2[t % 2]
            for kt in range(8):
                TR(TRc[:, kt * 128:(kt + 1) * 128], hh[:, kt * 128:(kt + 1) * 128], identb[:], [f"h2_{t % 2}", "identb"], ["TRc"])
            CP("act", h2T.rearrange("p k t -> p (k t)"), TRc, ["TRc"], ["h2T"])
            for cb in range(2):
                pq = PS[1][:, cb * 512:(cb + 1) * 512]
                for kt in range(8):
                    MM(pq, h2T[:, kt, :], Wqk[:, kt, cb * 512:(cb + 1) * 512], kt == 0, kt == 7, ["h2T", "Wqk"], [f"PQ{cb}"])
            RED(st_[:, 4:8], PS[1].rearrange("p (h m) -> p h m", h=4), ALU.max, ["PQ0", "PQ1"], [sk + "mx"])
            TS("dve", st_[:, 8:12], st_[:, 4:8], -1.0, None, ALU.mult, None, [sk + "mx"], [sk + "nmx"])
            for h in range(4):
                ACT(Pex[:, h, :], PS[1][:, h * 256:(h + 1) * 256], AF.Exp, ["PQ0", "PQ1", sk + "nmx"], ["Pex", sk + f"sm{h}"],
                    bias=st_[:, 8 + h:9 + h], scale=1.0, accum_out=st_[:, 12 + h:13 + h])
            RECIP(st_[:, 16:20], st_[:, 12:16], [sk + f"sm{h}" for h in range(4)], [sk + "rsm"])
            TT("pool", Pn[t % 2], Pex, st_[:, 16:20].unsqueeze(2).to_broadcast([128, 4, 256]), ALU.mult, ["Pex", sk + "rsm"], [f"Pn{t % 2}"])

        def sc_s3(t):
            sco, s_ = tiles[t]
            b3 = t % 3
            st_ = sst[t % 3]
            sk = f"sst{t % 3}"
            b = t % 2
            Pn2 = Pn[b].rearrange("p h m -> p (h m)")
            for idx in range(8):
                TR(TRd[:, idx * 128:(idx + 1) * 128], Pn2[:, idx * 128:(idx + 1) * 128], identb[:], [f"Pn{b}", "identb"], ["TRd"])
            CP("act", PTt.rearrange("p k t -> p (k t)"), TRd, ["TRd"], ["PTt"])
            for cb in range(2):
                po = PS[2][:, cb * 512:(cb + 1) * 512]
                for idx in range(8):
                    MM(po, PTt[:, idx, :], VO[:, idx, cb * 512:(cb + 1) * 512], idx == 0, idx == 7, ["PTt", "VO"], [f"PO{cb}"])
            TT("dve", x2, PS[2], x1[b3], ALU.add, ["PO0", "PO1", f"x1_{b3}"], ["x2"])
            ACT(junk, x2, AF.Square, ["x2"], ["junk", sk + "d"], accum_out=st_[:, 20:21])
            ACT(st_[:, 21:22], st_[:, 20:21], AF.Ln, [sk + "d"], [sk + "e"], bias=epsc[:, 0:1], scale=1.0 / 1024.0)
            ACT(st_[:, 22:23], st_[:, 21:22], AF.Exp, [sk + "e"], [sk + "f"], scale=-0.5)
            STT(osb[b], x2, st_[:, 22:23], gfr, ALU.mult, ALU.mult, ["x2", sk + "f", "gfr"], [f"osb{b}"])
            dma(out_d[sco, :, s_, :], osb[b], r=[f"osb{b}"])

        NTL = len(tiles)
        sc_load(0)
        for i in range(NTL + 2):
            if i + 1 < NTL:
                sc_load(i + 1)
            if i < NTL:
                sc_s1(i)
            if 0 <= i - 2 < NTL:
                sc_s3(i - 2)
            if 0 <= i - 1 < NTL:
                sc_s2(i - 1)
        return finish(nc, P, st, out_d)


def finish(nc, P, st, out_d):
    P.barrier()
    P.op("sp", None)
    P.emit(st)
    return nc


def _const_tables():
    H = 8
    log_g = np.log1p(-np.exp2(-5.0 - np.arange(H, dtype=np.float64)))
    i = np.arange(128, dtype=np.float64)
    sqk = np.zeros((128, 16), np.float64)
    sqk[:, 0:8] = np.exp(log_g[None, :] * (i[:, None] + 1.0))
    sqk[:, 8:16] = np.exp(-log_g[None, :] * (i[:, None] + 1.0)) * 0.125
    cdt = np.broadcast_to(np.exp(log_g * 128.0)[None, :], (128, 8)).copy()
    j = np.arange(128)
    maskT = (j[None, :] >= j[:, None]).astype(np.float32)
    s_idx = np.arange(128) // 16
    t0mask = (s_idx[None, :] >= s_idx[:, None]).astype(np.float32)
    invf = (10000.0 ** (-np.arange(32, dtype=np.float32) / np.float32(32))).astype(np.float32)
    invf = np.broadcast_to(invf[None, :], (128, 32)).copy()
    return (sqk.astype(np.float32), cdt.astype(np.float32), maskT, t0mask, invf, np.eye(128, dtype=np.float32))


def _kt(w):
    K, N = w.shape
    return np.ascontiguousarray(w.reshape(K // 128, 128, N).transpose(1, 0, 2))


def make_in_maps(inputs, mode=None):
    mode = mode or MODE
    NPRE_T = 6144 if mode == "prefix" else 0
    NT = NPRE_T + 2048
    f = lambda a: np.asarray(a, dtype=np.float32)
    x = f(inputs["x"])
    mem = f(inputs["mem"])
    pos = np.asarray(inputs["positions"]).astype(np.int32)
    sqk, cdt, maskT, t0mask, invf, ident = _const_tables()
    rep = lambda v: np.ascontiguousarray(np.broadcast_to(f(v).reshape(1, -1), (128, f(v).size)))
    cols = lambda v: np.ascontiguousarray(f(v).reshape(8, 128).T)

    def s5lay(a):
        a = f(a)
        rest = a.shape[2:]
        a = a.reshape((2, 32, 64) + rest)
        a = np.moveaxis(a, 2, 1)
        return np.ascontiguousarray(a.reshape((128, 32) + rest))

    a_re = s5lay(inputs["s5_a_re"][0])
    a_im = s5lay(inputs["s5_a_im"][0])
    ldt = s5lay(np.broadcast_to(f(inputs["s5_log_dt"][0])[:, None], (64, 64)))
    b_re = s5lay(inputs["s5_b_re"][0])
    b_im = s5lay(inputs["s5_b_im"][0])
    c_re = s5lay(np.transpose(f(inputs["s5_c_re"][0]), (0, 2, 1)))
    c_im = s5lay(np.transpose(f(inputs["s5_c_im"][0]), (0, 2, 1)))
    shared = dict(
        invf=invf, sqk=sqk, cdt=cdt, maskT=maskT, t0mask=t0mask, ident=ident,
        w_in=_kt(f(inputs["w_in"][0])), g1=cols(inputs["norm1_g"][0]), gng=rep(inputs["ret_gn_g"][0]),
        s5are=a_re, s5aim=a_im, s5ldt=ldt, s5bre=b_re, s5bim=b_im, s5cre=c_re, s5cim=c_im,
        dcol=np.ascontiguousarray(f(inputs["s5_d"][0]).reshape(64, 16).T[np.arange(128) % 16, :]), glub=rep(inputs["s5_glu_b"][0]), g2rep=rep(inputs["norm2_g"][0]),
        gfrep=rep(inputs["norm_f_g"]), gm=cols(inputs["norm_mem_g"][0]),
        gluw=_kt(f(inputs["s5_glu_w"][0])), wout=_kt(f(inputs["w_out"][0])),
        wqT=_kt(np.ascontiguousarray(f(inputs["xa_wq"][0]).T)), wk=_kt(f(inputs["xa_wk"][0])),
        wv=_kt(f(inputs["xa_wv"][0])), wo=_kt(f(inputs["xa_wo"][0])),
    )
    log_g = np.log1p(-np.exp2(-5.0 - np.arange(8, dtype=np.float64)))
    maps = []
    for core in range(8):
        b, j = core // 4, core % 4
        m = dict(shared)
        t1 = (j + 1) * 2048
        t0 = t1 - NT
        lo = max(t0, 0)
        xl = np.zeros((NT, 1024), np.float32)
        pl = np.zeros((NT,), np.int32)
        xl[lo - t0:, :] = x[b, lo:t1, :]
        pl[lo - t0:] = pos[b, lo:t1]
        m["xT"] = _kt(np.ascontiguousarray(xl.T))
        m["xown"] = np.ascontiguousarray(x[b, j * 2048:(j + 1) * 2048, :].reshape(2, 128, 8, 1024))
        m["pos"] = np.ascontiguousarray(pl.reshape(NT // 128, 128).T)
        m["memT"] = _kt(np.ascontiguousarray(mem[b].T))
        wtr = np.zeros((128, 8, 4), np.float64)
        sel5 = np.zeros((128, 3, 8), np.float32)
        for rank in range(8):
            rb, ri = rank // 4, rank % 4
            if rb == b and ri < j:
                dist = j - ri
                sel5[:, dist - 1, rank] = 1.0
                for blk in range(4):
                    for par in range(2):
                        h = 2 * blk + par
                        wtr[par * 64:(par + 1) * 64, rank, blk] = np.exp(log_g[h] * 2048.0 * (dist - 1))
        m["wtr"] = wtr.astype(np.float32)
        m["sel5"] = sel5
        maps.append(m)
    return maps


def kernel(**inputs):
    nc = build_program()
    maps = make_in_maps(inputs)
    res = run_bass_kernel_spmd(nc, maps, core_ids=list(range(8)))
    out = np.zeros((2, 8192, 1024), np.float32)
    for core in range(8):
        b, j = core // 4, core % 4
        out[b, j * 2048:(j + 1) * 2048, :] = np.asarray(res.results[core]["out"]).reshape(2048, 1024)
    return out
```

```python
import os
import math
from contextlib import ExitStack
import numpy as np
import concourse.bass as bass
import concourse.mybir as mybir
from concourse.bass_utils import run_bass_kernel_spmd

F32 = mybir.dt.float32
BF16 = mybir.dt.bfloat16
I32 = mybir.dt.int32
ALU = mybir.AluOpType
AF = mybir.ActivationFunctionType
AX = mybir.AxisListType

SEG = 6000
NDMA = 24
EPS = 1e-6
NOWN_BLK = 8
MODE = "prefix"
TWO_PI = 2.0 * math.pi
KG = 2.0 * math.sqrt(2.0 / math.pi)


class Prog:
    ENGS = ("pe", "act", "dve", "pool", "sp")

    def __init__(self, nc):
        self.nc = nc
        self.ops = []
        self.last_w = {}
        self.readers = {}
        self.pending_barrier = {e: set() for e in self.ENGS}
        self._bar_at = 0

    def op(self, eng, fn, r=(), w=(), dma=False):
        oid = len(self.ops)
        deps = set()
        for k in r:
            if k in self.last_w:
                deps.add(self.last_w[k])
        for k in w:
            if k in self.last_w:
                deps.add(self.last_w[k])
            for rd in self.readers.get(k, ()):
                deps.add(rd)
        deps |= self.pending_barrier[eng]
        self.pending_barrier[eng] = set()
        deps.discard(oid)
        last_per_eng = {}
        keep = set()
        for d_ in deps:
            od = self.ops[d_]
            if od["dma"]:
                keep.add(d_)
            else:
                e_ = od["eng"]
                if e_ not in last_per_eng or d_ > last_per_eng[e_]:
                    last_per_eng[e_] = d_
        deps = keep | set(last_per_eng.values())
        self.ops.append(dict(eng=eng, fn=fn, deps=deps, dma=dma))
        for k in r:
            self.readers.setdefault(k, []).append(oid)
        for k in w:
            self.last_w[k] = oid
            self.readers[k] = []
        return oid

    def dma(self, fn, r=(), w=(), eng="sp"):
        return self.op(eng, fn, r, w, dma=True)

    def barrier(self):
        last = {}
        for i, o in enumerate(self.ops):
            last[o["eng"]] = i
        dm = [i for i in range(self._bar_at, len(self.ops)) if self.ops[i]["dma"]]
        s = set(last.values()) | set(dm)
        for e in self.ENGS:
            self.pending_barrier[e] |= s
        self._bar_at = len(self.ops)

    def emit(self, stack):
        nc = self.nc
        ops = self.ops
        n = len(ops)
        needed = [False] * n
        for i, o in enumerate(ops):
            nd = set()
            for d in o["deps"]:
                od = ops[d]
                if od["eng"] == "pe" and o["eng"] == "pe":
                    continue
                if od["eng"] == o["eng"] and not od["dma"] and od["fn"] is None:
                    continue
                nd.add(d)
            o["deps"] = nd
            for d in nd:
                needed[d] = True
        cnt = {e: 0 for e in self.ENGS}
        nseg = {e: 0 for e in self.ENGS}
        ndma = {"sp": 0, "pool": 0}
        dma_ops = {"sp": [], "pool": []}
        for i, o in enumerate(ops):
            if o["dma"]:
                q = o["eng"]
                nd = ndma[q]
                o["sig"] = ("dma_" + q, nd % NDMA, 16 * (nd // NDMA + 1), nd)
                dma_ops[q].append(o)
                ndma[q] += 1
            elif needed[i]:
                e = o["eng"]
                c = cnt[e]
                o["sig"] = (e, c // SEG, c % SEG + 1)
                cnt[e] += 1
                nseg[e] = c // SEG + 1
            else:
                o["sig"] = None
        sems = {}
        for e in self.ENGS:
            for s in range(nseg[e]):
                sems[(e, s)] = stack.enter_context(nc.semaphore(f"s_{e}_{s}"))
        for q in ("sp", "pool"):
            for k in range(min(NDMA, ndma[q])):
                sems[("dma_" + q, k)] = stack.enter_context(nc.semaphore(f"s_dma_{q}_{k}"))
        block = stack.enter_context(nc.Block())

        def make(ename):
            def body(eng):
                waited = {}
                for i, o in enumerate(ops):
                    if o["eng"] != ename:
                        continue
                    req = {}
                    for d in o["deps"]:
                        sg = ops[d]["sig"]
                        key = (sg[0], sg[1])
                        req[key] = max(req.get(key, 0), sg[2])
                    if o["dma"]:
                        idx = o["sig"][3]
                        if idx >= NDMA:
                            prev = dma_ops[o["eng"]][idx - NDMA]["sig"]
                            key = (prev[0], prev[1])
                            req[key] = max(req.get(key, 0), prev[2])
                    for key in sorted(req.keys(), key=lambda k: (k[0], k[1])):
                        val = req[key]
                        if waited.get(key, 0) >= val:
                            continue
                        if not key[0].startswith("dma"):
                            if any(k[0] == key[0] and k[1] > key[1] for k in waited):
                                continue
                        eng.wait_ge(sems[key], val)
                        waited[key] = val
                    if o["fn"] is None:
                        continue
                    ins = o["fn"](eng)
                    sg = o["sig"]
                    if sg is not None:
                        if sg[0].startswith("dma"):
                            ins.then_inc(sems[(sg[0], sg[1])], 16)
                        else:
                            ins.then_inc(sems[(sg[0], sg[1])], 1)
            return body

        block.tensor(make("pe"))
        block.scalar(make("act"))
        block.vector(make("dve"))
        block.gpsimd(make("pool"))
        block.sync(make("sp"))


class Arena:
    def __init__(self, t, nwords):
        self.t = t
        self.n = nwords
        self.off = 0

    def reset(self, off=0):
        self.off = off

    def take(self, shape, dtype):
        nel = 1
        for s in shape[1:]:
            nel *= s
        sz = 2 if dtype == BF16 else 4
        nw = (nel * sz + 3) // 4
        nw = (nw + 7) // 8 * 8
        assert self.off + nw <= self.n, f"arena overflow {self.off}+{nw}>{self.n}"
        ap = self.t[:, self.off:self.off + nw]
        self.off += nw
        if dtype != F32:
            ap = ap.bitcast(dtype)
        ap = ap[:, 0:nel]
        fd = shape[1:]
        if len(fd) == 2:
            ap = ap.rearrange("p (a b) -> p a b", a=fd[0])
        elif len(fd) == 3:
            ap = ap.rearrange("p (a b c) -> p a b c", a=fd[0], b=fd[1])
        elif len(fd) == 4:
            ap = ap.rearrange("p (a b c d) -> p a b c d", a=fd[0], b=fd[1], c=fd[2])
        return ap


def build_program(dbg=None, stop_after=None, mode=None):
    mode = mode or MODE
    NPRE = 24 if mode == "prefix" else 0
    NT = NPRE * 256 + 2048
    NCH = NT // 128
    FAKE_CC = isinstance(dbg, dict) and bool(dbg.get("fake_cc"))
    nc = bass.Bass("TRN2", target_bir_lowering=False)

    def din(name, shape, dt=F32):
        return nc.dram_tensor(name, list(shape), dt, kind="ExternalInput").ap()

    xT_d = din("xT", [128, 8, NT])
    xown_d = din("xown", [2, 128, 8, 1024])
    pos_d = din("pos", [128, NCH], I32)
    invf_d = din("invf", [128, 32])
    sqk_d = din("sqk", [128, 16])
    cdt_d = din("cdt", [128, 8])
    wtr_d = din("wtr", [128, 8, 4])
    sel5_d = din("sel5", [128, 3, 8])
    maskT_d = din("maskT", [128, 128])
    t0mask_d = din("t0mask", [128, 128])
    ident_d = din("ident", [128, 128])
    w_in_d = din("w_in", [128, 8, 5120])
    g1_d = din("g1", [128, 8])
    gng_d = din("gng", [128, 1024])
    are_d = din("s5are", [128, 32])
    aim_d = din("s5aim", [128, 32])
    ldt_d = din("s5ldt", [128, 32])
    bre_d = din("s5bre", [128, 32, 16])
    bim_d = din("s5bim", [128, 32, 16])
    cre_d = din("s5cre", [128, 32, 16])
    cim_d = din("s5cim", [128, 32, 16])
    dcol_d = din("dcol", [128, 64])
    glub_d = din("glub", [128, 1024])
    g2_d = din("g2rep", [128, 1024])
    gf_d = din("gfrep", [128, 1024])
    gm_d = din("gm", [128, 8])
    gluw_d = din("gluw", [128, 8, 1024])
    wout_d = din("wout", [128, 16, 1024])
    wqT_d = din("wqT", [128, 8, 1024])
    wk_d = din("wk", [128, 8, 1024])
    wv_d = din("wv", [128, 8, 1024])
    wo_d = din("wo", [128, 8, 1024])
    memT_d = din("memT", [128, 8, 256])
    out_d = nc.dram_tensor("out", [2, 128, 8, 1024], F32, kind="ExternalOutput").ap()

    skind = dict(kind="ExternalOutput") if dbg else {}
    u_d = nc.dram_tensor("u_scr", [NT, 1024], BF16, **skind).ap()
    ccR_in = nc.dram_tensor("ccR_in", [128, 512], F32)
    ccR_out = nc.dram_tensor("ccR_out", [1024, 512], F32)
    ccS_in = nc.dram_tensor("ccS_in", [128, 64], F32)
    ccS_out = nc.dram_tensor("ccS_out", [1024, 64], F32)
    g5_d = nc.dram_tensor("g5_scr", [2048, 1024], BF16, **skind).ap()
    retT_d = nc.dram_tensor("retT_scr", [128, 8, 2048], BF16, **skind).ap()
    ssmT_d = nc.dram_tensor("ssmT_scr", [2, 8, 128, 8, 128], BF16, **skind).ap()

    with ExitStack() as st:
        P = Prog(nc)
        sb = lambda name, shape, dt=F32: st.enter_context(nc.sbuf_tensor("sb_" + name, list(shape), dt))
        ARW = (dbg.get("arw") if isinstance(dbg, dict) and dbg.get("arw") else None) or (49200 if mode == "prefix" else 47200)
        arena_t = sb("arena", [128, ARW])
        AR = Arena(arena_t, ARW)
        identf = sb("identf", [128, 128])
        identb = sb("identb", [128, 128], BF16)
        ones_f = sb("ones_f", [128, 128])
        ones_b = sb("ones_b", [128, 128], BF16)
        epsc = sb("epsc", [128, 1])
        maskT = sb("maskT", [128, 128])
        t0mask = sb("t0mask", [128, 128])
        invf = sb("invf", [128, 32])
        sqk = sb("sqk", [128, 16])
        cdt = sb("cdt", [128, 8])
        wtr = sb("wtr", [128, 8, 4])
        sel5 = sb("sel5", [128, 3, 8])
        A2r = sb("A2r", [128, 2, 32])
        A2i = sb("A2i", [128, 2, 32])
        g1 = sb("g1", [128, 8])
        gm = sb("gm", [128, 8])
        Rf = sb("Rf", [128, 8, 128])
        Rbf = sb("Rbf", [128, 8, 128], BF16)
        car_r = sb("car_r", [128, 32])
        car_i = sb("car_i", [128, 32])
        PS = [st.enter_context(nc.psum_tensor(f"PS{i}", [128, 1024], F32))[:] for i in range(4)]

        def dma(out, in_, r=(), w=(), eng="sp"):
            P.dma(lambda e, out=out, in_=in_: e.dma_start(out=out, in_=in_), r=r, w=w, eng=eng)

        def TT(eng, out, in0, in1, op, r, w):
            P.op(eng, lambda e, out=out, in0=in0, in1=in1, op=op: e.tensor_tensor(out=out, in0=in0, in1=in1, op=op), r=r, w=w)

        def TS(eng, out, in0, s1, s2, op0, op1, r, w):
            if op1 is None:
                P.op(eng, lambda e, out=out, in0=in0, s1=s1, op0=op0: e.tensor_scalar(out=out, in0=in0, scalar1=s1, scalar2=None, op0=op0), r=r, w=w)
            else:
                P.op(eng, lambda e, out=out, in0=in0, s1=s1, s2=s2, op0=op0, op1=op1: e.tensor_scalar(out=out, in0=in0, scalar1=s1, scalar2=s2, op0=op0, op1=op1), r=r, w=w)

        def STT(out, in0, scalar, in1, op0, op1, r, w):
            P.op("dve", lambda e, out=out, in0=in0, scalar=scalar, in1=in1, op0=op0, op1=op1: e.scalar_tensor_tensor(out=out, in0=in0, scalar=scalar, in1=in1, op0=op0, op1=op1), r=r, w=w)

        def ACT(out, in_, func, r, w, **kw):
            P.op("act", lambda e, out=out, in_=in_, func=func, kw=kw: e.activation(out=out, in_=in_, func=func, **kw), r=r, w=w)

        def CP(eng, out, in_, r, w):
            if eng == "act":
                ACT(out, in_, AF.Copy, r, w)
            else:
                P.op(eng, lambda e, out=out, in_=in_: e.tensor_copy(out=out, in_=in_), r=r, w=w)

        def RECIP(out, in_, r, w):
            P.op("dve", lambda e, out=out, in_=in_: e.reciprocal(out=out, in_=in_), r=r, w=w)

        def RED(out, in_, op, r, w, axis=AX.X):
            P.op("dve", lambda e, out=out, in_=in_, op=op, axis=axis: e.tensor_reduce(out=out, in_=in_, axis=axis, op=op), r=r, w=w)

        def MM(out, lhsT, rhs, start, stop, r, w):
            P.op("pe", lambda e, out=out, lhsT=lhsT, rhs=rhs, start=start, stop=stop: e.matmul(out, lhsT=lhsT, rhs=rhs, start=start, stop=stop), r=r, w=w)

        def TR(out, in_, ident, r, w):
            P.op("pe", lambda e, out=out, in_=in_, ident=ident: e.transpose(out=out, in_=in_, identity=ident), r=r, w=w)

        def MEMSET(eng, ap, val, w):
            P.op(eng, lambda e, ap=ap, val=val: e.memset(ap, val), w=w)

        dma(identf[:], ident_d, w=["identf"])
        dma(maskT[:], maskT_d, w=["maskT"])
        dma(t0mask[:], t0mask_d, w=["t0mask"])
        dma(invf[:], invf_d, w=["invf"])
        dma(sqk[:], sqk_d, w=["sqk"])
        dma(cdt[:], cdt_d, w=["cdt"])
        dma(wtr[:], wtr_d, w=["wtr"])
        dma(sel5[:], sel5_d, w=["sel5"])
        dma(g1[:], g1_d, w=["g1"])
        dma(gm[:], gm_d, w=["gm"])
        CP("dve", identb[:], identf[:], ["identf"], ["identb"])
        MEMSET("pool", ones_f[:], 1.0, ["ones_f"])
        MEMSET("pool", ones_b[:], 1.0, ["ones_b"])
        MEMSET("pool", epsc[:], EPS, ["epsc"])
        MEMSET("pool", Rf[:], 0.0, ["Rf"])
        MEMSET("pool", Rbf[:], 0.0, ["Rbf"])
        MEMSET("pool", car_r[:], 0.0, ["car_r"])
        MEMSET("pool", car_i[:], 0.0, ["car_i"])

        def sincos(ang, shp, s_out, c_out, tmps, key):
            kf, ki, rr, mm = tmps
            rk = [key]
            TS("dve", kf, ang, 1.0 / TWO_PI, None, ALU.mult, None, rk, [key + "kf"])
            CP("dve", ki, kf, [key + "kf"], [key + "ki"])
            CP("dve", kf, ki, [key + "ki"], [key + "kf"])
            STT(rr, kf, -6.28125, ang, ALU.mult, ALU.add, [key + "kf"] + rk, [key + "rr"])
            STT(rr, kf, -(TWO_PI - 6.28125), rr, ALU.mult, ALU.add, [key + "kf", key + "rr"], [key + "rr"])

            def wrap(t):
                TS("dve", mm, t, -math.pi, None, ALU.is_lt, None, [key + "rr"], [key + "mm"])
                STT(t, mm, TWO_PI, t, ALU.mult, ALU.add, [key + "mm", key + "rr"], [key + "rr"])
                TS("dve", mm, t, math.pi, None, ALU.is_gt, None, [key + "rr"], [key + "mm"])
                STT(t, mm, -TWO_PI, t, ALU.mult, ALU.add, [key + "mm", key + "rr"], [key + "rr"])
            wrap(rr)
            ACT(s_out, rr, AF.Sin, [key + "rr"], [key + "s"])
            TS("dve", rr, rr, math.pi / 2.0, None, ALU.add, None, [key + "rr", key + "s"], [key + "rr"])
            wrap(rr)
            ACT(c_out, rr, AF.Sin, [key + "rr"], [key + "c"])

        def norm_fm(xt, N, gcol, sq, rs1, rs2, hT, psum_ap, kx, ksq, krs, kh, kps, gkey):
            ACT(sq, xt, AF.Square, [kx], [ksq])
            for kt in range(8):
                MM(psum_ap, ones_b[:], sq[:, kt, :], kt == 0, kt == 7, [ksq, "ones_b"], [kps])
            ACT(rs1, psum_ap, AF.Sqrt, [kps], [krs + "1"], bias=EPS, scale=1.0 / 1024.0)
            RECIP(rs2, rs1, [krs + "1"], [krs])
            for kt in range(8):
                STT(hT[:, kt, :], xt[:, kt, :], gcol[:, kt:kt + 1], rs2, ALU.mult, ALU.mult, [kx, krs, gkey], [kh])

        AR.reset()
        w_in = AR.take([128, 8, 5120], BF16)
        cosT = AR.take([128, NCH, 32], F32)
        sinT = AR.take([128, NCH, 32], F32)
        gng = AR.take([128, 1024], F32)
        xTb = [AR.take([128, 8, 256], F32) for _ in range(2)]
        sq = AR.take([128, 8, 256], BF16)
        posi = AR.take([128, NCH], I32)
        posf = AR.take([128, NCH], F32)
        Rc = AR.take([128, 4, 128], F32)
        mark = AR.off
        tki = AR.take([128, NCH * 32], I32)
        tmm = AR.take([128, NCH * 32], F32)
        trr_t = AR.take([128, NCH * 32], F32)
        AR.reset(mark)
        Gt = AR.take([128, 8, 4, 128], F32)
        AR.reset(mark)
        hTb = [AR.take([128, 8, 256], BF16) for _ in range(2)]
        rs1 = AR.take([128, 256], F32)
        rs2 = AR.take([128, 256], F32)
        qk = AR.take([128, 16, 2, 32], F32)
        rt = [AR.take([128, 16, 32], F32) for _ in range(4)]
        qkt = [AR.take([128, 16, 64], BF16) for _ in range(2)]
        v_bf = [AR.take([128, 1024], BF16) for _ in range(2)]
        gr_bf = [AR.take([128, 1024], BF16) for _ in range(2)]
        g5_bf = [AR.take([128, 1024], BF16) for _ in range(2)]
        u_bf = [AR.take([128, 1024], BF16) for _ in range(2)]
        kT = [AR.take([128, 4, 128], BF16) for _ in range(2)]
        qm = [AR.take([128, 8, 128], BF16) for _ in range(2)]
        PT = AR.take([128, 8, 128], BF16)
        o_sb = AR.take([128, 8, 128], F32)
        osq = AR.take([128, 8, 128], F32)
        ret = [AR.take([128, 1024], BF16) for _ in range(2)]
        retT_blk = AR.take([128, 8, 256], BF16)
        stt = AR.take([128, 48], F32)
        ang = xTb[0].rearrange("p a b -> p (a b)")[:, 0:NCH * 32]
        tkf = xTb[1].rearrange("p a b -> p (a b)")[:, 0:NCH * 32]
        trr = trr_t[:, 0:NCH * 32]

        dma(gng, gng_d, w=["gng"])
        MEMSET("pool", qm[0], 0.0, ["qm0"])
        MEMSET("pool", qm[1], 0.0, ["qm1"])
        dma(posi, pos_d, w=["posi"])
        CP("dve", posf, posi, ["posi"], ["posf"])
        TT("dve", ang.rearrange("p (c i) -> p c i", i=32), posf.unsqueeze(2).to_broadcast([128, NCH, 32]),
           invf[:].unsqueeze(1).to_broadcast([128, NCH, 32]), ALU.mult, ["posf", "invf", "xT0"], ["xT0"])
        sincos(ang, None, sinT.rearrange("p c i -> p (c i)"), cosT.rearrange("p c i -> p (c i)"),
               (tkf, tki, trr, tmm), "xT0")
        P.barrier()
        for cb in (1, 2, 3, 6, 7, 0, 4, 5, 8, 9):
            cs_ = slice(cb * 512, (cb + 1) * 512)
            dma(w_in[:, :, cs_], w_in_d[:, :, cs_], w=[f"w_in{cb}"], eng="pool")
            TT("dve", w_in[:, :, cs_], w_in[:, :, cs_], g1[:].unsqueeze(2).to_broadcast([128, 8, 512]), ALU.mult, [f"w_in{cb}", "g1"], [f"w_in{cb}"])

        qk2 = qk.rearrange("p h t i -> p (h t i)")
        qkt2 = [q_.rearrange("p h d -> p (h d)") for q_ in qkt]
        o2 = o_sb.rearrange("p h e -> p (h e)")
        PSUM_R2 = (mode == "coll") and not (isinstance(dbg, dict) and dbg.get("psum_new"))
        if PSUM_R2:
            KVp = PS[3].rearrange("p (h e) -> p h e", h=8)
            TRb = PS[0][:, 512:1024].bitcast(BF16)
            kvi = lambda h: h
            kvs = lambda par: slice(par, 8, 2)
        else:
            KVp = PS[3][:, 0:512].rearrange("p (b e) -> p b e", b=4)
            TRb = PS[3][:, 512:1024].bitcast(BF16)
            kvi = lambda h: h // 2
            kvs = lambda par: slice(0, 4, 1)
        SCp = PS[1]
        Op = PS[2]
        pj_i = [0]

        pj_n = [2]

        def pj():
            if PSUM_R2:
                return PS[0][:, 0:512], "PJ0"
            i = pj_i[0] % pj_n[0]
            pj_i[0] += 1
            return PS[i // 2][:, (i % 2) * 512:(i % 2 + 1) * 512], f"PJ{i}"

        nblk = NOWN_BLK
        LVL = dbg.get('own_lvl', 9) if isinstance(dbg, dict) else 9
        if mode == "coll":
            sa_iter = [("light", b_) for b_ in range(nblk)] + [("xchg", -1)] + [("full", b_) for b_ in range(nblk)]
            light_cbs = [1, 2, 3]
        else:
            sa_iter = [("light", b_) for b_ in range(NPRE)] + [("full", NPRE + b_) for b_ in range(nblk)]
            light_cbs = [1, 2, 3, 6, 7]
        def do_xchg():
            P.barrier()
            dma(ccR_in.ap()[0:64, :].rearrange("p (b e) -> p b e", b=4), Rf[0:64, 0::2, :], r=["Rf"], w=["ccR_in"])
            dma(ccR_in.ap()[64:128, :].rearrange("p (b e) -> p b e", b=4), Rf[64:128, 1::2, :], r=["Rf"], w=["ccR_in"])
            if FAKE_CC:
                dma(ccR_out.ap()[0:128, :], ccR_in.ap(), r=["ccR_in"], w=["ccR_out"])
            else:
                P.op("pool", lambda e: e.collective_compute("AllGather", ALU.bypass, replica_groups=[list(range(8))],
                                                            ins=[ccR_in.ap().opt()], outs=[ccR_out.ap().opt()]),
                     r=["ccR_in"], w=["ccR_out"])
            dma(Gt.rearrange("p r b e -> p r (b e)"), ccR_out.ap().rearrange("(r p) c -> p r c", p=128), r=["ccR_out"], w=["Gt"])
            TT("dve", Gt, Gt, wtr[:].unsqueeze(3).to_broadcast([128, 8, 4, 128]), ALU.mult, ["Gt", "wtr"], ["Gt"])
            RED(Rc, Gt.rearrange("p r b e -> p b e r"), ALU.add, ["Gt"], ["Rc"])
            CP("act", Rf[0:64, 0::2, :], Rc[0:64, :, :], ["Rc", "Rf"], ["Rf"])
            CP("act", Rf[64:128, 1::2, :], Rc[64:128, :, :], ["Rc", "Rf"], ["Rf"])
            CP("pool", Rbf[:], Rf[:], ["Rf"], ["Rbf"])
            P.barrier()


        def load_block(blk):
            dma(xTb[blk % 2], xT_d[:, :, blk * 256:(blk + 1) * 256], w=[f"xT{blk % 2}"])

        def norm_block(blk):
            xb = xTb[blk % 2]
            kx = f"xT{blk % 2}"
            bb = blk % 2
            ACT(sq, xb, AF.Square, [kx], ["sq"])
            ACT(hTb[bb], xb, AF.Copy, [kx], [f"hT{bb}"])
            pa, pk = pj()
            for ci in range(2):
                for kt in range(8):
                    MM(pa[:, ci:ci + 1], sq[:, kt, ci * 128:(ci + 1) * 128], ones_b[:, 0:1], kt == 0, kt == 7, ["sq", "ones_b"], [pk])
            ACT(rs1[:, 2 * bb:2 * bb + 2], pa[:, 0:2], AF.Sqrt, [pk], [f"rsa{bb}"], bias=EPS, scale=1.0 / 1024.0)
            RECIP(rs2[:, 2 * bb:2 * bb + 2], rs1[:, 2 * bb:2 * bb + 2], [f"rsa{bb}"], [f"rsb{bb}"])

        def phaseA(pas, blk, ci):
            own = pas == "full"
            c = blk * 2 + ci
            d = c % 2
            hT = hTb[blk % 2]
            hk = f"hT{blk % 2}"
            hs = lambda kt: hT[:, kt, ci * 128:(ci + 1) * 128]
            cbs = [0, 1, 2, 3, 4, 5, 6, 7, 8, 9] if own else light_cbs
            for cb in cbs:
                pa, pk = pj()
                for kt in range(8):
                    MM(pa, hs(kt), w_in[:, kt, cb * 512:(cb + 1) * 512], kt == 0, kt == 7, [hk, f"w_in{cb}"], [pk])
                rcol = rs2[:, 2 * (blk % 2) + ci:2 * (blk % 2) + ci + 1]
                rk = f"rsb{blk % 2}"
                if cb == 0:
                    ACT(qk2[:, 0:512], pa, AF.Copy, [pk, rk], ["qk_q"], scale=rcol)
                elif cb == 1:
                    ACT(qk2[:, 512:1024], pa, AF.Copy, [pk, rk], ["qk_k"], scale=rcol)
                elif cb in (2, 3):
                    TS("dve", v_bf[d][:, (cb - 2) * 512:(cb - 1) * 512], pa, rcol, None, ALU.mult, None, [pk, rk], [f"v_bf{d}"])
                elif cb in (4, 5):
                    ACT(gr_bf[d][:, (cb - 4) * 512:(cb - 3) * 512], pa, AF.Silu, [pk, rk], [f"gr_bf{d}"], scale=rcol)
                elif cb in (6, 7):
                    TS("dve", u_bf[d][:, (cb - 6) * 512:(cb - 5) * 512], pa, rcol, None, ALU.mult, None, [pk, rk], [f"u_bf{d}"])
                else:
                    ACT(g5_bf[d][:, (cb - 8) * 512:(cb - 7) * 512], pa, AF.Silu, [pk, rk], [f"g5_bf{d}"], scale=rcol)
            if own or 6 in light_cbs:
                dma(u_d[c * 128:(c + 1) * 128, :], u_bf[d], r=[f"u_bf{d}"])
            if own:
                oc = c - 2 * NPRE
                dma(g5_d[oc * 128:(oc + 1) * 128, :], g5_bf[d], r=[f"g5_bf{d}"])
            h0 = 0 if own else 8
            nh = 16 - h0
            cosb = cosT[:, c, :].unsqueeze(1).to_broadcast([128, nh, 32])
            sinb = sinT[:, c, :].unsqueeze(1).to_broadcast([128, nh, 32])
            x1 = qk[:, h0:16, 0, :]
            x2 = qk[:, h0:16, 1, :]
            rq = ["qk_q", "qk_k"] if own else ["qk_k"]
            qk3 = qk[:, h0:16, :, :].rearrange("p h t i -> p h (t i)")
            TT("pool", qk3, qk3, sqk[:, h0:16].unsqueeze(2).to_broadcast([128, nh, 64]), ALU.mult, rq + ["sqk"], rq)
            q4 = qkt[d].rearrange("p h (t i) -> p h t i", t=2)
            TT("dve", rt[0][:, h0:16, :], x1, cosb, ALU.mult, rq, ["rt0"])
            TT("dve", rt[1][:, h0:16, :], x2, sinb, ALU.mult, rq, ["rt1"])
            TT("dve", q4[:, h0:16, 0, :], rt[0][:, h0:16, :], rt[1][:, h0:16, :], ALU.subtract, ["rt0", "rt1"], [f"qkt{d}"])
            TT("pool", rt[2][:, h0:16, :], x1, sinb, ALU.mult, rq, ["rt2"])
            TT("pool", rt[3][:, h0:16, :], x2, cosb, ALU.mult, rq, ["rt3"])
            TT("pool", q4[:, h0:16, 1, :], rt[2][:, h0:16, :], rt[3][:, h0:16, :], ALU.add, ["rt2", "rt3"], [f"qkt{d}"])
            if own:
                for b in range(8):
                    TR(TRb[:, b * 128:(b + 1) * 128], qkt2[d][:, b * 128:(b + 1) * 128], identb[:], [f"qkt{d}", "identb"], ["TRb"])
                CP("act", kT[d].rearrange("p b t -> p (b t)"), TRb[:, 512:1024], ["TRb"], [f"kT{d}"])
                CP("act", qm[d][0:64, 0::2, :], TRb[0:64, 0:512].rearrange("p (b t) -> p b t", b=4), ["TRb"], [f"qm{d}"])
                CP("act", qm[d][64:128, 1::2, :], TRb[64:128, 0:512].rearrange("p (b t) -> p b t", b=4), ["TRb"], [f"qm{d}"])

        def phaseB(pas, blk, ci):
            own = pas == "full"
            c = blk * 2 + ci
            d = c % 2
            for h in range(8):
                MM(KVp[(h % 2) * 64:(h % 2) * 64 + 64, kvi(h), :], qkt2[d][:, 512 + h * 64:512 + (h + 1) * 64],
                   v_bf[d][:, h * 128:(h + 1) * 128], True, True, [f"qkt{d}", f"v_bf{d}"], ["KVp"])
            if own:
                for h in range(8):
                    MM(SCp[:, h * 128:(h + 1) * 128], kT[d][:, h // 2, :], qm[d][:, h, :], True, True, [f"kT{d}", f"qm{d}"], ["SCp"])
                TT("dve", PT, SCp.rearrange("p (h i) -> p h i", h=8), maskT[:].unsqueeze(1).to_broadcast([128, 8, 128]),
                   ALU.mult, ["SCp", "maskT"], ["PT"])
                for h in range(8):
                    MM(Op[:, h * 128:(h + 1) * 128], PT[:, h, :], v_bf[d][:, h * 128:(h + 1) * 128], True, False, ["PT", f"v_bf{d}"], ["Op"])
                    MM(Op[:, h * 128:(h + 1) * 128], qm[d][:, h, :], Rbf[:, h, :], False, True, [f"qm{d}", "Rbf"], ["Op"])
            for par in range(2):
                rows = slice(par * 64, par * 64 + 64)
                TT("dve", Rf[rows, par::2, :], KVp[rows, kvs(par), :], Rf[rows, par::2, :], ALU.add, ["KVp", "Rf"], ["Rf"])
                TT("dve", Rf[rows, par::2, :], Rf[rows, par::2, :], cdt[rows, par::2].unsqueeze(2).to_broadcast([64, 4, 128]), ALU.mult, ["Rf", "cdt"], ["Rf"])
                CP("act", Rbf[rows, par::2, :], Rf[rows, par::2, :], ["Rf"], ["Rbf"])
            if own:
                CP("act", o2, Op, ["Op"], ["o_sb"])
                ACT(osq.rearrange("p h e -> p (h e)"), Op, AF.Square, ["Op"], ["osq"])
                RED(stt[:, 0:8], o_sb, ALU.add, ["o_sb"], ["st_s"])
                RED(stt[:, 8:16], osq, ALU.add, ["osq"], ["st_q"])
                TS("dve", stt[:, 16:24], stt[:, 0:8], 1.0 / 128.0, None, ALU.mult, None, ["st_s"], ["st_m"])
                TT("dve", stt[:, 24:32], stt[:, 16:24], stt[:, 16:24], ALU.mult, ["st_m"], ["st_m2"])
                STT(stt[:, 32:40], stt[:, 8:16], 1.0 / 128.0, stt[:, 24:32], ALU.mult, ALU.subtract, ["st_q", "st_m2"], ["st_v"])
                ACT(stt[:, 40:48], stt[:, 32:40], AF.Sqrt, ["st_v"], ["st_sd"], bias=EPS, scale=1.0)
                RECIP(stt[:, 32:40], stt[:, 40:48], ["st_sd"], ["st_rs"])
                TT("pool", o_sb, o_sb, stt[:, 16:24].unsqueeze(2).to_broadcast([128, 8, 128]), ALU.subtract, ["o_sb", "st_m"], ["o_sb"])
                TT("pool", o_sb, o_sb, stt[:, 32:40].unsqueeze(2).to_broadcast([128, 8, 128]), ALU.mult, ["o_sb", "st_rs"], ["o_sb"])
                TT("dve", o2, o2, gng, ALU.mult, ["o_sb", "gng"], ["o_sb"])
                TT("dve", ret[d], o2, gr_bf[d], ALU.mult, ["o_sb", f"gr_bf{d}"], [f"ret{d}"])

        def phaseC(pas, blk, ci):
            if pas != "full":
                return
            c = blk * 2 + ci
            d = c % 2
            for kt in range(8):
                TR(TRb[:, kt * 128:(kt + 1) * 128], ret[d][:, kt * 128:(kt + 1) * 128], identb[:], [f"ret{d}", "identb"], ["TRb"])
            CP("act", retT_blk[:, :, ci * 128:(ci + 1) * 128], TRb.rearrange("p (k t) -> p k t", k=8), ["TRb"], ["retT_blk"])
            if ci == 1:
                ob = blk - NPRE
                dma(retT_d[:, :, ob * 256:(ob + 1) * 256], retT_blk, r=["retT_blk"])

        pendB = None
        pendC = None
        real = [(p_, b_) for p_, b_ in sa_iter if p_ != "xchg"]
        has_x = any(p_ == "xchg" for p_, _ in sa_iter)
        load_block(real[0][1])
        norm_block(real[0][1])
        ri = 0
        prev_pas = None
        for pas, blk in sa_iter:
            if pas == "light" and not PSUM_R2:
                pj_n[0] = 4
            if pas == "full" and prev_pas == "light" and mode != "coll":
                if pendB is not None:
                    phaseB(*pendB)
                    pendB = None
                P.barrier()
            if pas == "full":
                pj_n[0] = 2
            prev_pas = pas
            if pas == "xchg":
                if pendB is not None:
                    phaseB(*pendB)
                    pendB = None
                do_xchg()
                load_block(real[ri][1])
                norm_block(real[ri][1])
                continue
            nxt = real[ri + 1] if ri + 1 < len(real) else None
            if nxt is not None and has_x and nxt[0] != pas:
                nxt = None
            for ci in range(2):
                if ci == 0 and nxt is not None:
                    load_block(nxt[1])
                phaseA(pas, blk, ci)
                if ci == 0 and nxt is not None:
                    norm_block(nxt[1])
                if pendC is not None:
                    phaseC(*pendC)
                    pendC = None
                if pendB is not None:
                    phaseB(*pendB)
                    pendC = pendB
                pendB = (pas, blk, ci)
            ri += 1
        if pendC is not None:
            phaseC(*pendC)
        if pendB is not None:
            phaseB(*pendB)
            phaseC(*pendB)
        P.barrier()
        if stop_after == "SA":
            return finish(nc, P, st, out_d)

        AR.reset()
        T0 = AR.take([128, 64, 128], BF16)
        I2Sre = AR.take([128, 64, 64], BF16)
        I2Sim = AR.take([128, 64, 64], BF16)
        S2Ore = AR.take([128, 32, 128], BF16)
        S2Oim = AR.take([128, 32, 128], BF16)
        TIre = AR.take([128, 32, 16], F32)
        TIim = AR.take([128, 32, 16], F32)
        TOre = AR.take([128, 32, 8], F32)
        TOim = AR.take([128, 32, 8], F32)
        P8 = AR.take([128, 32], F32)
        Q8 = AR.take([128, 32], F32)
        P128 = AR.take([128, 32], F32)
        Q128 = AR.take([128, 32], F32)
        base_sb = AR.off
        are = AR.take([128, 32], F32)
        aim = AR.take([128, 32], F32)
        ldt = AR.take([128, 32], F32)
        bre = AR.take([128, 32, 16], F32)
        bim = AR.take([128, 32, 16], F32)
        cre = AR.take([128, 32, 16], F32)
        cim = AR.take([128, 32, 16], F32)
        dtt = AR.take([128, 32], F32)
        tA = [AR.take([128, 32], F32) for _ in range(12)]
        tI = AR.take([128, 32], I32)
        PWr = AR.take([128, 9, 32], F32)
        PWi = AR.take([128, 9, 32], F32)
        NWr = AR.take([128, 9, 32], F32)
        NWi = AR.take([128, 9, 32], F32)
        E8r = AR.take([128, 17, 32], F32)
        E8i = AR.take([128, 17, 32], F32)
        E16r = AR.take([128, 9, 32], F32)
        E16i = AR.take([128, 9, 32], F32)
        Bbr = AR.take([128, 32, 16], F32)
        Bbi = AR.take([128, 32, 16], F32)
        Gre = AR.take([128, 32, 8, 16], F32)
        Gim = AR.take([128, 32, 8, 16], F32)
        Sre = AR.take([128, 32, 8, 16], F32)
        Sim = AR.take([128, 32, 8, 16], F32)
        Ire = AR.take([128, 32, 8, 16], F32)
        Iim = AR.take([128, 32, 8, 16], F32)
        tB = [AR.take([128, 32, 16], F32) for _ in range(2)]
        T0f = AR.take([128, 4, 128], F32)
        dcol = AR.take([128, 64], F32)

        K0 = ["S0B"]
        dma(dcol, dcol_d, w=K0)
        for t_, d_ in ((are, are_d), (aim, aim_d), (ldt, ldt_d), (bre, bre_d), (bim, bim_d), (cre, cre_d), (cim, cim_d)):
            dma(t_, d_, w=K0)
        P.barrier()
        V = lambda out, in0, in1, op: TT("dve", out, in0, in1, op, K0, K0)
        VS = lambda out, in0, s1, op: TS("dve", out, in0, s1, None, op, None, K0, K0)
        VC = lambda out, in_: CP("dve", out, in_, K0, K0)
        VM = lambda ap, val: MEMSET("dve", ap, val, K0)
        ACT(dtt, ldt, AF.Exp, K0, K0)
        V(tA[0], are, dtt, ALU.mult)
        ACT(tA[1], tA[0], AF.Exp, K0, K0)
        ACT(tA[2], tA[0], AF.Exp, K0, K0, scale=-1.0)
        V(tA[3], aim, dtt, ALU.mult)
        sincos(tA[3], None, tA[4], tA[5], (tA[6], tI, tA[7], tA[8]), "S0B")
        P.barrier()

        def cmul(orr, oi, xr, xi, yr, yi, tmp):
            u1, u2 = tmp
            V(u1, xr, yr, ALU.mult)
            V(u2, xi, yi, ALU.mult)
            V(orr, u1, u2, ALU.subtract)
            V(u1, xr, yi, ALU.mult)
            V(u2, xi, yr, ALU.mult)
            V(oi, u1, u2, ALU.add)

        tsm = (tA[10], tA[11])
        VM(PWr[:, 0, :], 1.0)
        VM(PWi[:, 0, :], 0.0)
        V(PWr[:, 1, :], tA[1], tA[5], ALU.mult)
        V(PWi[:, 1, :], tA[1], tA[4], ALU.mult)
        VM(NWr[:, 0, :], 1.0)
        VM(NWi[:, 0, :], 0.0)
        V(NWr[:, 1, :], tA[2], tA[5], ALU.mult)
        V(tA[9], tA[2], tA[4], ALU.mult)
        VS(NWi[:, 1, :], tA[9], -1.0, ALU.mult)
        for k in range(1, 8):
            cmul(PWr[:, k + 1, :], PWi[:, k + 1, :], PWr[:, k, :], PWi[:, k, :], PWr[:, 1, :], PWi[:, 1, :], tsm)
            cmul(NWr[:, k + 1, :], NWi[:, k + 1, :], NWr[:, k, :], NWi[:, k, :], NWr[:, 1, :], NWi[:, 1, :], tsm)
        VM(E8r[:, 0, :], 1.0)
        VM(E8i[:, 0, :], 0.0)
        VC(E8r[:, 1, :], PWr[:, 8, :])
        VC(E8i[:, 1, :], PWi[:, 8, :])
        for k in range(1, 16):
            cmul(E8r[:, k + 1, :], E8i[:, k + 1, :], E8r[:, k, :], E8i[:, k, :], E8r[:, 1, :], E8i[:, 1, :], tsm)
        VM(E16r[:, 0, :], 1.0)
        VM(E16i[:, 0, :], 0.0)
        VC(E16r[:, 1, :], E8r[:, 16, :])
        VC(E16i[:, 1, :], E8i[:, 16, :])
        for k in range(1, 8):
            cmul(E16r[:, k + 1, :], E16i[:, k + 1, :], E16r[:, k, :], E16i[:, k, :], E16r[:, 1, :], E16i[:, 1, :], tsm)
        VC(P8, E8r[:, 1, :])
        VC(Q8, E8i[:, 1, :])
        VC(P128, E16r[:, 8, :])
        VC(Q128, E16i[:, 8, :])
        cmul(A2r[:, 0, :], A2i[:, 0, :], P128, Q128, P128, Q128, tsm)
        cmul(A2r[:, 1, :], A2i[:, 1, :], A2r[:, 0, :], A2i[:, 0, :], A2r[:, 0, :], A2i[:, 0, :], tsm)
        for r_ in range(16):
            VC(TIre[:, :, r_], E8r[:, 15 - r_, :])
            VC(TIim[:, :, r_], E8i[:, 15 - r_, :])
        for q_ in range(8):
            VC(TOre[:, :, q_], E16r[:, 7 - q_, :])
            VC(TOim[:, :, q_], E16i[:, 7 - q_, :])
        V(tA[0], are, are, ALU.mult)
        V(tA[1], aim, aim, ALU.mult)
        V(tA[0], tA[0], tA[1], ALU.add)
        RECIP(tA[0], tA[0], K0, K0)
        VS(tA[1], PWr[:, 1, :], -1.0, ALU.add)
        V(tA[2], tA[1], are, ALU.mult)
        V(tA[3], PWi[:, 1, :], aim, ALU.mult)
        V(tA[2], tA[2], tA[3], ALU.add)
        V(tA[2], tA[2], tA[0], ALU.mult)
        V(tA[3], PWi[:, 1, :], are, ALU.mult)
        V(tA[4], tA[1], aim, ALU.mult)
        V(tA[3], tA[3], tA[4], ALU.subtract)
        V(tA[3], tA[3], tA[0], ALU.mult)
        bc16 = lambda ap: ap.unsqueeze(2).to_broadcast([128, 32, 16])
        cmul(Bbr, Bbi, bre, bim, bc16(tA[2]), bc16(tA[3]), (tB[0], tB[1]))
        for s_ in range(8):
            cmul(Gre[:, :, s_, :], Gim[:, :, s_, :], Bbr, Bbi, bc16(NWr[:, s_ + 1, :]), bc16(NWi[:, s_ + 1, :]), (tB[0], tB[1]))
            cmul(Ire[:, :, s_, :], Iim[:, :, s_, :], Bbr, Bbi, bc16(PWr[:, 7 - s_, :]), bc16(PWi[:, 7 - s_, :]), (tB[0], tB[1]))
            cmul(Sre[:, :, s_, :], Sim[:, :, s_, :], cre, cim, bc16(PWr[:, s_ + 1, :]), bc16(PWi[:, s_ + 1, :]), (tB[0], tB[1]))
        Sim2 = Sim.rearrange("p g t j -> p (g t j)")
        VS(Sim2, Sim2, -1.0, ALU.mult)
        P.barrier()
        CP("act", S2Ore.rearrange("p g x -> p (g x)"), Sre.rearrange("p g t j -> p (g t j)"), K0, ["S2O"])
        CP("act", S2Oim.rearrange("p g x -> p (g x)"), Sim2, K0, ["S2O"])
        Gre3 = Gre.rearrange("p g s j -> p g (s j)")
        Gim3 = Gim.rearrange("p g s j -> p g (s j)")
        Sre3 = Sre.rearrange("p g t j -> p g (t j)")
        Sim3 = Sim.rearrange("p g t j -> p g (t j)")
        Ire3 = Ire.rearrange("p g s j -> p g (s j)")
        Iim3 = Iim.rearrange("p g s j -> p g (s j)")
        for gb in range(16):
            bank = PS[gb % 2][:, 0:512]
            bk = f"PSb{gb % 2}"
            for i in range(4):
                g = gb * 4 + i
                gh, gp = g // 32, g % 32
                pb = gh * 64
                MM(bank[:, i * 128:(i + 1) * 128], Gre3[pb:pb + 64, gp, :], Sre3[pb:pb + 64, gp, :], True, False, K0, [bk])
                MM(bank[:, i * 128:(i + 1) * 128], Gim3[pb:pb + 64, gp, :], Sim3[pb:pb + 64, gp, :], False, True, K0, [bk])
            TT("dve", T0f, bank.rearrange("p (g x) -> p g x", g=4),
               t0mask[:].unsqueeze(1).to_broadcast([128, 4, 128]), ALU.mult, [bk, "t0mask"] + K0, K0)
            for i in range(4):
                g = gb * 4 + i
                STT(T0[:, g, :], identf[:], dcol[:, g:g + 1], T0f[:, i, :], ALU.mult, ALU.add, K0 + ["identf"], K0 + ["T0"])
        for src3, dst, nm in ((Ire3, I2Sre, "I2Sre"), (Iim3, I2Sim, "I2Sim")):
            for gb in range(8):
                bank = PS[2 + gb % 2][:, 0:512]
                bk = f"PSc{gb % 2}"
                for i in range(8):
                    g = gb * 8 + i
                    gh, gp = g // 32, g % 32
                    pb = gh * 64
                    TR(bank[:, i * 64:(i + 1) * 64], src3[pb:pb + 64, gp, :], identf[pb:pb + 64, pb:pb + 64], ["identf", "S0B"], [bk])
                CP("act", dst[:, gb * 8:(gb + 1) * 8, :], bank.rearrange("p (g n) -> p g n", g=8), [bk], [nm])
        P.barrier()
        if stop_after == "S0b":
            return finish(nc, P, st, out_d)

        AR.reset(base_sb)
        u_raw = AR.take([128, 8, 1024], BF16)
        U = u_raw.rearrange("p s c -> p (s c)").rearrange("p (g m) -> p g m", g=64)
        u_tm2 = AR.take([128, 64, 8, 16], BF16)
        inc_raw = AR.take([128, 4096], F32)
        Incre = inc_raw.rearrange("p (g m) -> p g m", g=32)
        yg_bf = inc_raw.bitcast(BF16).rearrange("p (s c) -> p s c", s=8)
        Incim = AR.take([128, 32, 128], F32)
        gluw = AR.take([128, 8, 1024], BF16)
        glub = AR.take([128, 1024], F32)
        g5s = [AR.take([128, 1024], BF16) for _ in range(2)]
        ct = [AR.take([128, 32], F32) for _ in range(12)]
        un_base = AR.off
        LA = AR.take([128, 32, 8, 16], F32)
        LB = AR.take([128, 32, 8, 16], F32)
        LS = [AR.take([128, 32, 8], F32) for _ in range(4)]
        LE = [AR.take([128, 32], F32) for _ in range(4)]
        AR.reset(un_base)
        Xre_bf = AR.take([128, 32, 128], BF16)
        Xim_bf = AR.take([128, 32, 128], BF16)
        Ysb = [AR.take([128, 8, 128], F32) for _ in range(2)]
        gt1 = [AR.take([128, 1024], F32) for _ in range(2)]
        gt2 = [AR.take([128, 1024], F32) for _ in range(2)]
        AR.reset(un_base)
        ygT = [AR.take([128, 8, 128], BF16) for _ in range(2)]
        zb = [AR.take([128, 1024], F32) for _ in range(2)]
        gate = [AR.take([128, 1024], F32) for _ in range(2)]
        tt_ = [AR.take([128, 1024], F32) for _ in range(2)]
        ssm = [AR.take([128, 1024], BF16) for _ in range(2)]
        ssmT = [AR.take([128, 8, 128], BF16) for _ in range(2)]
        AR.reset(un_base)

        for kt in range(8):
            dma(gluw[:, kt, :], gluw_d[:, kt, :], w=["gluw"], eng="pool")
        dma(glub, glub_d, w=["glub"])

        TRU = PS[0][:, 0:512].bitcast(BF16)
        PIre = PS[1][:, 0:512]
        PIim = PS[1][:, 512:1024]
        TRU2 = [PS[0][:, 0:512].bitcast(BF16), PS[0][:, 512:1024].bitcast(BF16)]
        PI2 = [(PS[1][:, 0:512], PS[1][:, 512:1024]), (PS[2][:, 0:512], PS[2][:, 512:1024])]
        PY = PS[2]
        PYT = PS[3]
        IRK = [f"IR{m}" for m in range(128)]
        IIK = [f"II{m}" for m in range(128)]

        Gs = AR.take([128, 8, 64], F32)
        Gs2 = AR.take([128, 8, 64], F32)
        Sd = AR.take([128, 3, 64], F32)
        xt_ = [AR.take([128, 32], F32) for _ in range(4)]
        NPS = NPRE // 4
        if mode == "coll":
            sb_iter = [("light", 0), ("light", 1), ("xchg", -1), ("full", 0), ("full", 1)]
        else:
            sb_iter = [("light", sc_) for sc_ in range(NPS)] + [("full", NPS), ("full", NPS + 1)]
        for pas, sc in sb_iter:
            if pas == "xchg":
                P.barrier()
                dma(ccS_in.ap()[:, 0:32], car_r[:], r=["car_r"], w=["ccS_in"])
                dma(ccS_in.ap()[:, 32:64], car_i[:], r=["car_i"], w=["ccS_in"])
                if FAKE_CC:
                    dma(ccS_out.ap()[0:128, :], ccS_in.ap(), r=["ccS_in"], w=["ccS_out"])
                else:
                    P.op("pool", lambda e: e.collective_compute("AllGather", ALU.bypass, replica_groups=[list(range(8))],
                                                                ins=[ccS_in.ap().opt()], outs=[ccS_out.ap().opt()]),
                         r=["ccS_in"], w=["ccS_out"])
                dma(Gs, ccS_out.ap().rearrange("(r p) c -> p r c", p=128), r=["ccS_out"], w=["Gs"])
                KX = ["XCH"]
                for d_ in range(3):
                    TT("dve", Gs2, Gs, sel5[:, d_, :].unsqueeze(2).to_broadcast([128, 8, 64]), ALU.mult, ["Gs", "sel5"] + KX, KX)
                    RED(Sd[:, d_, :], Gs2.rearrange("p r c -> p c r"), ALU.add, KX, KX)
                XV = lambda out, in0, in1, op: TT("dve", out, in0, in1, op, KX + ["car_r", "car_i"], KX)
                XV(car_r[:], Sd[:, 0, 0:32], Sd[:, 0, 0:32], ALU.bypass) if False else CP("dve", car_r[:], Sd[:, 0, 0:32], KX + ["car_r"], KX + ["car_r"])
                CP("dve", car_i[:], Sd[:, 0, 32:64], KX + ["car_i"], KX + ["car_i"])
                for d_ in (1, 2):
                    sr, si = Sd[:, d_, 0:32], Sd[:, d_, 32:64]
                    ar_, ai_ = A2r[:, d_ - 1, :], A2i[:, d_ - 1, :]
                    XV(xt_[0], sr, ar_, ALU.mult)
                    XV(xt_[1], si, ai_, ALU.mult)
                    XV(xt_[0], xt_[0], xt_[1], ALU.subtract)
                    TT("dve", car_r[:], car_r[:], xt_[0], ALU.add, KX + ["car_r"], KX + ["car_r"])
                    XV(xt_[2], sr, ai_, ALU.mult)
                    XV(xt_[3], si, ar_, ALU.mult)
                    XV(xt_[2], xt_[2], xt_[3], ALU.add)
                    TT("dve", car_i[:], car_i[:], xt_[2], ALU.add, KX + ["car_i"], KX + ["car_i"])
                P.barrier()
                continue
            own = pas == "full"
            if own:
                P.barrier()
            dma(u_raw, u_d[sc * 1024:(sc + 1) * 1024, :].rearrange("(m s) c -> m s c", s=8), w=["u_raw", "U"])
            src4 = u_raw.rearrange("p s (g j) -> p g s j", j=16)
            CP("dve", u_tm2[:, 0:32, :, :], src4[:, 0:32, :, :], ["u_raw"], ["u_tm2a"])
            CP("act", u_tm2[:, 32:64, :, :], src4[:, 32:64, :, :], ["u_raw"], ["u_tm2b"])
            u3 = u_tm2.rearrange("p g s j -> p g (s j)")
            for gb in range(8):
                tru = TRU2[gb % 2]
                tk = f"TRU{gb % 2}"
                for i in range(8):
                    g = gb * 8 + i
                    TR(tru[:, i * 128:(i + 1) * 128], u3[:, g, :], identb[:], ["u_tm2a", "u_tm2b", "identb"], [tk])
                CP("act", U[:, gb * 8:(gb + 1) * 8, :], tru.rearrange("p (g m) -> p g m", g=8),
                   [tk], ["U", "u_raw"])
            for b4 in range(8):
                pre, pim = PI2[b4 % 2]
                kre, kim = f"PIre{b4 % 2}", f"PIim{b4 % 2}"
                for i in range(4):
                    gp = b4 * 4 + i
                    for gh in range(2):
                        g = gh * 32 + gp
                        MM(pre[gh * 64:(gh + 1) * 64, i * 128:(i + 1) * 128], I2Sre[:, g, :], U[:, g, :], True, True, ["U", "I2Sre"], [kre])
                        MM(pim[gh * 64:(gh + 1) * 64, i * 128:(i + 1) * 128], I2Sim[:, g, :], U[:, g, :], True, True, ["U", "I2Sim"], [kim])
                CP("act", Incre[:, b4 * 4:(b4 + 1) * 4, :], pre.rearrange("p (g m) -> p g m", g=4), [kre], IRK)
                CP("dve", Incim[:, b4 * 4:(b4 + 1) * 4, :], pim.rearrange("p (g m) -> p g m", g=4), [kim], IIK)
            if not own:
                I4r = Incre.rearrange("p g (q r) -> p g q r", r=16)
                I4i = Incim.rearrange("p g (q r) -> p g q r", r=16)
                tir = TIre.unsqueeze(2).to_broadcast([128, 32, 8, 16])
                tii = TIim.unsqueeze(2).to_broadcast([128, 32, 8, 16])
                TT("dve", LA, I4r, tir, ALU.mult, IRK + ["TI"], ["LA"])
                TT("dve", LB, I4i, tii, ALU.mult, IIK + ["TI"], ["LB"])
                TT("dve", LA, LA, LB, ALU.subtract, ["LA", "LB"], ["LA"])
                RED(LS[0], LA, ALU.add, ["LA"], ["LS0"])
                TT("dve", LA, I4i, tir, ALU.mult, IIK + ["TI", "LS0"], ["LA"])
                TT("dve", LB, I4r, tii, ALU.mult, IRK + ["TI", "LA"], ["LB"])
                TT("dve", LA, LA, LB, ALU.add, ["LA", "LB"], ["LA"])
                RED(LS[1], LA, ALU.add, ["LA"], ["LS1"])
                TT("dve", LS[2], LS[0], TOre, ALU.mult, ["LS0", "TO"], ["LS2"])
                TT("dve", LS[3], LS[1], TOim, ALU.mult, ["LS1", "TO"], ["LS3"])
                TT("dve", LS[2], LS[2], LS[3], ALU.subtract, ["LS2", "LS3"], ["LS2"])
                RED(LE[0], LS[2], ALU.add, ["LS2"], ["LE0"])
                TT("dve", LS[2], LS[1], TOre, ALU.mult, ["LS1", "TO", "LE0"], ["LS2"])
                TT("dve", LS[3], LS[0], TOim, ALU.mult, ["LS0", "TO", "LS2"], ["LS3"])
                TT("dve", LS[2], LS[2], LS[3], ALU.add, ["LS2", "LS3"], ["LS2"])
                RED(LE[1], LS[2], ALU.add, ["LS2"], ["LE1"])
                TT("dve", LE[2], car_r[:], P128, ALU.mult, ["car_r", "P128"], ["LE2"])
                TT("dve", LE[3], car_i[:], Q128, ALU.mult, ["car_i", "Q128"], ["LE3"])
                TT("dve", LE[2], LE[2], LE[3], ALU.subtract, ["LE2", "LE3"], ["LE2"])
                TT("dve", LE[0], LE[0], LE[2], ALU.add, ["LE0", "LE2"], ["LE0"])
                TT("dve", LE[2], car_i[:], P128, ALU.mult, ["car_i", "P128", "LE0"], ["LE2"])
                TT("dve", LE[3], car_r[:], Q128, ALU.mult, ["car_r", "Q128", "LE0"], ["LE3"])
                TT("dve", LE[2], LE[2], LE[3], ALU.add, ["LE2", "LE3"], ["LE2"])
                TT("dve", car_i[:], LE[1], LE[2], ALU.add, ["LE1", "LE2"], ["car_i"])
                CP("dve", car_r[:], LE[0], ["LE0"], ["car_r"])
                continue
            CP("dve", Xre_bf[:, :, 0], car_r[:], ["car_r"], ["Xre0"])
            CP("dve", Xim_bf[:, :, 0], car_i[:], ["car_i"], ["Xim0"])
            for m in range(128):
                if m == 0:
                    xr, xi, kr, ki = car_r[:], car_i[:], "car_r", "car_i"
                else:
                    xr, xi, kr, ki = Incre[:, :, m - 1], Incim[:, :, m - 1], IRK[m - 1], IIK[m - 1]
                pp = (m % 2) * 6
                c0, c1, c2, c3, c4, c5 = ct[pp:pp + 6]
                n0 = [f"ct{pp + j}" for j in range(6)]
                TT("dve", c0, xr, P8, ALU.mult, [kr, "P8"], [n0[0]])
                TT("dve", c1, xi, Q8, ALU.mult, [ki, "Q8"], [n0[1]])
                TT("dve", c3, xi, P8, ALU.mult, [ki, "P8"], [n0[3]])
                TT("dve", c4, xr, Q8, ALU.mult, [kr, "Q8"], [n0[4]])
                TT("dve", c2, c0, c1, ALU.subtract, [n0[0], n0[1]], [n0[2]])
                TT("dve", c5, c3, c4, ALU.add, [n0[3], n0[4]], [n0[5]])
                TT("dve", Incre[:, :, m], c2, Incre[:, :, m], ALU.add, [n0[2], IRK[m]], [IRK[m]])
                TT("dve", Incim[:, :, m], c5, Incim[:, :, m], ALU.add, [n0[5], IIK[m]], [IIK[m]])
            CP("dve", car_r[:], Incre[:, :, 127], [IRK[127], "Xre0"], ["car_r"])
            CP("dve", car_i[:], Incim[:, :, 127], [IIK[127], "Xim0"], ["car_i"])
            CP("act", Xre_bf[:, :, 1:128], Incre[:, :, 0:127], IRK, ["Xre"])
            CP("dve", Xim_bf[:, :, 1:128], Incim[:, :, 0:127], IIK, ["Xim"])
            PYb = [PS[1], PS[2]]
            PYTb = [PS[3], PS[0]]

            def y_mm(gb):
                bq = gb % 2
                for i in range(8):
                    g = gb * 8 + i
                    gh, gp = g // 32, g % 32
                    pb = gh * 64
                    oy = PYb[bq][:, i * 128:(i + 1) * 128]
                    MM(oy, T0[:, g, :], U[:, g, :], True, False, ["T0", "U"], [f"PY{bq}"])
                    MM(oy, S2Ore[pb:pb + 64, gp, :], Xre_bf[pb:pb + 64, gp, :], False, False, ["S2O", "Xre", "Xre0"], [f"PY{bq}"])
                    MM(oy, S2Oim[pb:pb + 64, gp, :], Xim_bf[pb:pb + 64, gp, :], False, True, ["S2O", "Xim", "Xim0"], [f"PY{bq}"])

            def y_ew(gb):
                bq = gb % 2
                pyt = PYTb[bq]
                ky, kt_ = f"PY{bq}", f"PYT{bq}"
                CP("act", Ysb[bq].rearrange("p g m -> p (g m)"), PYb[bq], [ky], [f"Ysb{bq}"])
                for i in range(8):
                    TR(pyt[:, i * 128:(i + 1) * 128], Ysb[bq][:, i, :], identf[:], [f"Ysb{bq}", "identf"], [kt_])
                ACT(gt1[bq], pyt, AF.Square, [kt_], [f"gt1{bq}"])
                TS("dve", gt1[bq], gt1[bq], KG * 0.044715, KG, ALU.mult, ALU.add, [f"gt1{bq}"], [f"gt1{bq}"])
                TT("dve", gt1[bq], gt1[bq], pyt, ALU.mult, [f"gt1{bq}", kt_], [f"gt1{bq}"])
                ACT(gt2[bq], gt1[bq], AF.Sigmoid, [f"gt1{bq}"], [f"gt2{bq}"])
                TT("dve", yg_bf[:, :, gb * 128:(gb + 1) * 128].rearrange("p t (g j) -> p g t j", j=16),
                   pyt.rearrange("p (g t j) -> p g t j", g=8, t=8), gt2[bq].rearrange("p (g t j) -> p g t j", g=8, t=8),
                   ALU.mult, [kt_, f"gt2{bq}"], ["yg_bf"])

            y_mm(0)
            for gb in range(8):
                if gb + 1 < 8:
                    y_mm(gb + 1)
                y_ew(gb)
            P.barrier()
            oc = sc - NPS
            g5v = g5_d[oc * 1024:(oc + 1) * 1024, :].rearrange("(m s) c -> m s c", s=8)
            TRa = [PS[0][:, 0:512].bitcast(BF16), PS[0][:, 512:1024].bitcast(BF16)]
            TRs = [PS[3][:, 0:512].bitcast(BF16), PS[3][:, 512:1024].bitcast(BF16)]
            PZb = [PS[1], PS[2]]

            def tail1(s_):
                bq = s_ % 2
                dma(g5s[bq], g5v[:, s_, :], w=[f"g5s{bq}"])
                for kt in range(8):
                    TR(TRa[bq][:, kt * 128:(kt + 1) * 128], yg_bf[:, s_, kt * 128:(kt + 1) * 128], identb[:], ["yg_bf", "identb"], [f"TRa{bq}"])
                CP("act", ygT[bq].rearrange("p k m -> p (k m)"), TRa[bq], [f"TRa{bq}"], [f"ygT{bq}"])
                for cb in range(2):
                    pz = PZb[bq][:, cb * 512:(cb + 1) * 512]
                    for kt in range(8):
                        MM(pz, ygT[bq][:, kt, :], gluw[:, kt, cb * 512:(cb + 1) * 512], kt == 0, kt == 7, [f"ygT{bq}", "gluw"], [f"PZ{bq}"])

            def tail2(s_):
                bq = s_ % 2
                TT("dve", zb[bq], PZb[bq], glub, ALU.add, [f"PZ{bq}", "glub"], [f"zb{bq}"])
                ACT(gate[bq], zb[bq], AF.Sigmoid, [f"zb{bq}"], [f"gate{bq}"])
                TT("dve", tt_[bq], yg_bf[:, s_, :], gate[bq], ALU.mult, ["yg_bf", f"gate{bq}"], [f"tt_{bq}"])
                TT("pool", ssm[bq], tt_[bq], g5s[bq], ALU.mult, [f"tt_{bq}", f"g5s{bq}"], [f"ssm{bq}"])
                for kt in range(8):
                    TR(TRs[bq][:, kt * 128:(kt + 1) * 128], ssm[bq][:, kt * 128:(kt + 1) * 128], identb[:], [f"ssm{bq}", "identb"], [f"TRs{bq}"])
                CP("act", ssmT[bq].rearrange("p k m -> p (k m)"), TRs[bq], [f"TRs{bq}"], [f"ssmT{bq}"])
                dma(ssmT_d[oc, s_], ssmT[bq], r=[f"ssmT{bq}"])

            tail1(0)
            for s_ in range(8):
                if s_ + 1 < 8:
                    tail1(s_ + 1)
                tail2(s_)
        P.barrier()
        if stop_after == "SB":
            return finish(nc, P, st, out_d)

        AR.reset()
        Wqk = AR.take([128, 8, 1024], BF16)
        VO = AR.take([128, 8, 1024], BF16)
        wout = AR.take([128, 16, 1024], BF16)
        g2r = AR.take([128, 1024], F32)
        gfr = AR.take([128, 1024], F32)
        base_sc = AR.off
        wqT = AR.take([128, 8, 1024], BF16)
        wk = AR.take([128, 8, 1024], BF16)
        wv = AR.take([128, 8, 1024], BF16)
        wo = AR.take([128, 8, 1024], BF16)
        memT = AR.take([128, 8, 256], F32)
        msq = AR.take([128, 8, 256], BF16)
        mrs1 = AR.take([128, 256], F32)
        mrs2 = AR.take([128, 256], F32)
        mnT = AR.take([128, 8, 256], BF16)
        kaT = AR.take([128, 8, 256], BF16)
        vaT = AR.take([128, 8, 256], BF16)
        for t_, d_, k_ in ((wqT, wqT_d, "wqT"), (wk, wk_d, "wk"), (wv, wv_d, "wv"), (wo, wo_d, "wo")):
            for kt in range(8):
                dma(t_[:, kt, :], d_[:, kt, :], w=[k_], eng="pool")
        for kt in range(16):
            dma(wout[:, kt, :], wout_d[:, kt, :], w=["wout"], eng="pool")
        dma(g2r, g2_d, w=["g2r"])
        dma(gfr, gf_d, w=["gfr"])
        dma(memT, memT_d, w=["memT"])
        norm_fm(memT, 256, gm[:], msq, mrs1, mrs2, mnT, PS[0][:, 0:256], "memT", "msq", "mrs", "mnT", "PSm", "gm")
        for r_ in range(8):
            for wt, dst, nm, pi in ((wk, kaT, "kaT", 1), (wv, vaT, "vaT", 2)):
                pp_ = PS[pi][:, (r_ % 2) * 512:(r_ % 2) * 512 + 256]
                pk = f"PSkv{pi}_{r_ % 2}"
                for kt in range(8):
                    MM(pp_, wt[:, kt, r_ * 128:(r_ + 1) * 128], mnT[:, kt, :], kt == 0, kt == 7, [nm[:2] if False else ("wk" if pi == 1 else "wv"), "mnT"], [pk])
                CP("act" if pi == 1 else "dve", dst[:, r_, :], pp_, [pk], [nm])
        for dt_ in range(8):
            pw_ = PS[dt_ % 2]
            pk = f"PSw{dt_ % 2}"
            for h in range(4):
                for hf in range(2):
                    MM(pw_[:, h * 256:(h + 1) * 256], wqT[:, 2 * h + hf, dt_ * 128:(dt_ + 1) * 128], kaT[:, 2 * h + hf, :],
                       hf == 0, hf == 1, ["wqT", "kaT"], [pk])
            ACT(Wqk[:, dt_, :], pw_, AF.Copy, [pk], ["Wqk"], scale=1.0 / 16.0)
        for h in range(4):
            for mh in range(2):
                idx = 2 * h + mh
                pw_ = PS[2 + idx % 2]
                pk = f"PSv{idx % 2}"
                for cb in range(2):
                    for hf in range(2):
                        MM(pw_[:, cb * 512:(cb + 1) * 512], vaT[:, 2 * h + hf, mh * 128:(mh + 1) * 128], wo[:, 2 * h + hf, cb * 512:(cb + 1) * 512],
                           hf == 0, hf == 1, ["vaT", "wo"], [pk])
                CP("dve", VO[:, idx, :], pw_, [pk], ["VO"])
        P.barrier()

        AR.reset(base_sc)
        retT_sc = [AR.take([128, 8, 1024], BF16) for _ in range(2)]
        ssmT_s = [AR.take([128, 8, 128], BF16) for _ in range(3)]
        x_t = [AR.take([128, 1024], F32) for _ in range(3)]
        x1 = [AR.take([128, 1024], F32) for _ in range(3)]
        junk = AR.take([128, 1024], F32)
        h2 = [AR.take([128, 1024], BF16) for _ in range(2)]
        h2T = AR.take([128, 8, 128], BF16)
        Pex = AR.take([128, 4, 256], BF16)
        Pn = [AR.take([128, 4, 256], BF16) for _ in range(2)]
        PTt = AR.take([128, 8, 128], BF16)
        x2 = AR.take([128, 1024], F32)
        osb = [AR.take([128, 1024], F32) for _ in range(2)]
        sst = [AR.take([128, 32], F32) for _ in range(3)]
        TRc = PS[3][:, 0:512].bitcast(BF16)
        TRd = PS[3][:, 512:1024].bitcast(BF16)
        tiles = [(sco, s_) for sco in range(2) for s_ in range(8)]
        dma(retT_sc[0], retT_d[:, :, 0:1024], w=["retT_sc0"])
        dma(retT_sc[1], retT_d[:, :, 1024:2048], w=["retT_sc1"])

        def sc_load(t):
            sco, s_ = tiles[t]
            b3 = t % 3
            dma(ssmT_s[b3], ssmT_d[sco, s_], w=[f"ssmT_s{b3}"])
            dma(x_t[b3], xown_d[sco, :, s_, :], w=[f"x_t{b3}"])

        def sc_s1(t):
            sco, s_ = tiles[t]
            b3 = t % 3
            st_ = sst[t % 3]
            sk = f"sst{t % 3}"
            for cb in range(2):
                px = PS[0][:, cb * 512:(cb + 1) * 512]
                for kt in range(8):
                    MM(px, retT_sc[sco][:, kt, s_::8], wout[:, kt, cb * 512:(cb + 1) * 512], kt == 0, False, [f"retT_sc{sco}", "wout"], [f"PX{cb}"])
                for kt in range(8):
                    MM(px, ssmT_s[b3][:, kt, :], wout[:, 8 + kt, cb * 512:(cb + 1) * 512], False, kt == 7, [f"ssmT_s{b3}", "wout"], [f"PX{cb}"])
            TT("dve", x1[b3], PS[0], x_t[b3], ALU.add, ["PX0", "PX1", f"x_t{b3}"], [f"x1_{b3}"])
            ACT(junk, x1[b3], AF.Square, [f"x1_{b3}"], ["junk", sk + "a"], accum_out=st_[:, 0:1])
            ACT(st_[:, 1:2], st_[:, 0:1], AF.Ln, [sk + "a"], [sk + "b"], bias=epsc[:, 0:1], scale=1.0 / 1024.0)
            ACT(st_[:, 2:3], st_[:, 1:2], AF.Exp, [sk + "b"], [sk + "c"], scale=-0.5)
            STT(h2[t % 2], x1[b3], st_[:, 2:3], g2r, ALU.mult, ALU.mult, [f"x1_{b3}", sk + "c", "g2r"], [f"h2_{t % 2}"])

        def sc_s2(t):
            st_ = sst[t % 3]
            sk = f"sst{t % 3}"
            hh = h2[t % 2]
            for kt in range(8):
                TR(TRc[:, kt * 128:(kt + 1) * 128], hh[:, kt * 128:(kt + 1) * 128], identb[:], [f"h2_{t % 2}", "identb"], ["TRc"])
            CP("act", h2T.rearrange("p k t -> p (k t)"), TRc, ["TRc"], ["h2T"])
            for cb in range(2):
                pq = PS[1][:, cb * 512:(cb + 1) * 512]
                for kt in range(8):
                    MM(pq, h2T[:, kt, :], Wqk[:, kt, cb * 512:(cb + 1) * 512], kt == 0, kt == 7, ["h2T", "Wqk"], [f"PQ{cb}"])
            RED(st_[:, 4:8], PS[1].rearrange("p (h m) -> p h m", h=4), ALU.max, ["PQ0", "PQ1"], [sk + "mx"])
            TS("dve", st_[:, 8:12], st_[:, 4:8], -1.0, None, ALU.mult, None, [sk + "mx"], [sk + "nmx"])
            for h in range(4):
                ACT(Pex[:, h, :], PS[1][:, h * 256:(h + 1) * 256], AF.Exp, ["PQ0", "PQ1", sk + "nmx"], ["Pex", sk + f"sm{h}"],
                    bias=st_[:, 8 + h:9 + h], scale=1.0, accum_out=st_[:, 12 + h:13 + h])
            RECIP(st_[:, 16:20], st_[:, 12:16], [sk + f"sm{h}" for h in range(4)], [sk + "rsm"])
            TT("pool", Pn[t % 2], Pex, st_[:, 16:20].unsqueeze(2).to_broadcast([128, 4, 256]), ALU.mult, ["Pex", sk + "rsm"], [f"Pn{t % 2}"])

        def sc_s3(t):
            sco, s_ = tiles[t]
            b3 = t % 3
            st_ = sst[t % 3]
            sk = f"sst{t % 3}"
            b = t % 2
            Pn2 = Pn[b].rearrange("p h m -> p (h m)")
            for idx in range(8):
                TR(TRd[:, idx * 128:(idx + 1) * 128], Pn2[:, idx * 128:(idx + 1) * 128], identb[:], [f"Pn{b}", "identb"], ["TRd"])
            CP("act", PTt.rearrange("p k t -> p (k t)"), TRd, ["TRd"], ["PTt"])
            for cb in range(2):
                po = PS[2][:, cb * 512:(cb + 1) * 512]
                for idx in range(8):
                    MM(po, PTt[:, idx, :], VO[:, idx, cb * 512:(cb + 1) * 512], idx == 0, idx == 7, ["PTt", "VO"], [f"PO{cb}"])
            TT("dve", x2, PS[2], x1[b3], ALU.add, ["PO0", "PO1", f"x1_{b3}"], ["x2"])
            ACT(junk, x2, AF.Square, ["x2"], ["junk", sk + "d"], accum_out=st_[:, 20:21])
            ACT(st_[:, 21:22], st_[:, 20:21], AF.Ln, [sk + "d"], [sk + "e"], bias=epsc[:, 0:1], scale=1.0 / 1024.0)
            ACT(st_[:, 22:23], st_[:, 21:22], AF.Exp, [sk + "e"], [sk + "f"], scale=-0.5)
            STT(osb[b], x2, st_[:, 22:23], gfr, ALU.mult, ALU.mult, ["x2", sk + "f", "gfr"], [f"osb{b}"])
            dma(out_d[sco, :, s_, :], osb[b], r=[f"osb{b}"])

        NTL = len(tiles)
        sc_load(0)
        for i in range(NTL + 2):
            if i + 1 < NTL:
                sc_load(i + 1)
            if i < NTL:
                sc_s1(i)
            if 0 <= i - 2 < NTL:
                sc_s3(i - 2)
            if 0 <= i - 1 < NTL:
                sc_s2(i - 1)
        return finish(nc, P, st, out_d)


def finish(nc, P, st, out_d):
    P.barrier()
    P.op("sp", None)
    P.emit(st)
    return nc


def _const_tables():
    H = 8
    log_g = np.log1p(-np.exp2(-5.0 - np.arange(H, dtype=np.float64)))
    i = np.arange(128, dtype=np.float64)
    sqk = np.zeros((128, 16), np.float64)
    sqk[:, 0:8] = np.exp(log_g[None, :] * (i[:, None] + 1.0))
    sqk[:, 8:16] = np.exp(-log_g[None, :] * (i[:, None] + 1.0)) * 0.125
    cdt = np.broadcast_to(np.exp(log_g * 128.0)[None, :], (128, 8)).copy()
    j = np.arange(128)
    maskT = (j[None, :] >= j[:, None]).astype(np.float32)
    s_idx = np.arange(128) // 16
    t0mask = (s_idx[None, :] >= s_idx[:, None]).astype(np.float32)
    invf = (10000.0 ** (-np.arange(32, dtype=np.float32) / np.float32(32))).astype(np.float32)
    invf = np.broadcast_to(invf[None, :], (128, 32)).copy()
    return (sqk.astype(np.float32), cdt.astype(np.float32), maskT, t0mask, invf, np.eye(128, dtype=np.float32))


def _kt(w):
    K, N = w.shape
    return np.ascontiguousarray(w.reshape(K // 128, 128, N).transpose(1, 0, 2))


def make_in_maps(inputs, mode=None):
    mode = mode or MODE
    NPRE_T = 6144 if mode == "prefix" else 0
    NT = NPRE_T + 2048
    f = lambda a: np.asarray(a, dtype=np.float32)
    x = f(inputs["x"])
    mem = f(inputs["mem"])
    pos = np.asarray(inputs["positions"]).astype(np.int32)
    sqk, cdt, maskT, t0mask, invf, ident = _const_tables()
    rep = lambda v: np.ascontiguousarray(np.broadcast_to(f(v).reshape(1, -1), (128, f(v).size)))
    cols = lambda v: np.ascontiguousarray(f(v).reshape(8, 128).T)

    def s5lay(a):
        a = f(a)
        rest = a.shape[2:]
        a = a.reshape((2, 32, 64) + rest)
        a = np.moveaxis(a, 2, 1)
        return np.ascontiguousarray(a.reshape((128, 32) + rest))

    a_re = s5lay(inputs["s5_a_re"][0])
    a_im = s5lay(inputs["s5_a_im"][0])
    ldt = s5lay(np.broadcast_to(f(inputs["s5_log_dt"][0])[:, None], (64, 64)))
    b_re = s5lay(inputs["s5_b_re"][0])
    b_im = s5lay(inputs["s5_b_im"][0])
    c_re = s5lay(np.transpose(f(inputs["s5_c_re"][0]), (0, 2, 1)))
    c_im = s5lay(np.transpose(f(inputs["s5_c_im"][0]), (0, 2, 1)))
    shared = dict(
        invf=invf, sqk=sqk, cdt=cdt, maskT=maskT, t0mask=t0mask, ident=ident,
        w_in=_kt(f(inputs["w_in"][0])), g1=cols(inputs["norm1_g"][0]), gng=rep(inputs["ret_gn_g"][0]),
        s5are=a_re, s5aim=a_im, s5ldt=ldt, s5bre=b_re, s5bim=b_im, s5cre=c_re, s5cim=c_im,
        dcol=np.ascontiguousarray(f(inputs["s5_d"][0]).reshape(64, 16).T[np.arange(128) % 16, :]), glub=rep(inputs["s5_glu_b"][0]), g2rep=rep(inputs["norm2_g"][0]),
        gfrep=rep(inputs["norm_f_g"]), gm=cols(inputs["norm_mem_g"][0]),
        gluw=_kt(f(inputs["s5_glu_w"][0])), wout=_kt(f(inputs["w_out"][0])),
        wqT=_kt(np.ascontiguousarray(f(inputs["xa_wq"][0]).T)), wk=_kt(f(inputs["xa_wk"][0])),
        wv=_kt(f(inputs["xa_wv"][0])), wo=_kt(f(inputs["xa_wo"][0])),
    )
    log_g = np.log1p(-np.exp2(-5.0 - np.arange(8, dtype=np.float64)))
    maps = []
    for core in range(8):
        b, j = core // 4, core % 4
        m = dict(shared)
        t1 = (j + 1) * 2048
        t0 = t1 - NT
        lo = max(t0, 0)
        xl = np.zeros((NT, 1024), np.float32)
        pl = np.zeros((NT,), np.int32)
        xl[lo - t0:, :] = x[b, lo:t1, :]
        pl[lo - t0:] = pos[b, lo:t1]
        m["xT"] = _kt(np.ascontiguousarray(xl.T))
        m["xown"] = np.ascontiguousarray(x[b, j * 2048:(j + 1) * 2048, :].reshape(2, 128, 8, 1024))
        m["pos"] = np.ascontiguousarray(pl.reshape(NT // 128, 128).T)
        m["memT"] = _kt(np.ascontiguousarray(mem[b].T))
        wtr = np.zeros((128, 8, 4), np.float64)
        sel5 = np.zeros((128, 3, 8), np.float32)
        for rank in range(8):
            rb, ri = rank // 4, rank % 4
            if rb == b and ri < j:
                dist = j - ri
                sel5[:, dist - 1, rank] = 1.0
                for blk in range(4):
                    for par in range(2):
                        h = 2 * blk + par
                        wtr[par * 64:(par + 1) * 64, rank, blk] = np.exp(log_g[h] * 2048.0 * (dist - 1))
        m["wtr"] = wtr.astype(np.float32)
        m["sel5"] = sel5
        maps.append(m)
    return maps


def kernel(**inputs):
    nc = build_program()
    maps = make_in_maps(inputs)
    res = run_bass_kernel_spmd(nc, maps, core_ids=list(range(8)))
    out = np.zeros((2, 8192, 1024), np.float32)
    for core in range(8):
        b, j = core // 4, core % 4
        out[b, j * 2048:(j + 1) * 2048, :] = np.asarray(res.results[core]["out"]).reshape(2048, 1024)
    return out
```

```python
import os
import math
from contextlib import ExitStack
import numpy as np
import concourse.bass as bass
import concourse.mybir as mybir
from concourse.bass_utils import run_bass_kernel_spmd

F32 = mybir.dt.float32
BF16 = mybir.dt.bfloat16
I32 = mybir.dt.int32
ALU = mybir.AluOpType
AF = mybir.ActivationFunctionType
AX = mybir.AxisListType

SEG = 6000
NDMA = 24
EPS = 1e-6
NOWN_BLK = 8
MODE = "prefix"
TWO_PI = 2.0 * math.pi
KG = 2.0 * math.sqrt(2.0 / math.pi)


class Prog:
    ENGS = ("pe", "act", "dve", "pool", "sp")

    def __init__(self, nc):
        self.nc = nc
        self.ops = []
        self.last_w = {}
        self.readers = {}
        self.pending_barrier = {e: set() for e in self.ENGS}
        self._bar_at = 0

    def op(self, eng, fn, r=(), w=(), dma=False):
        oid = len(self.ops)
        deps = set()
        for k in r:
            if k in self.last_w:
                deps.add(self.last_w[k])
        for k in w:
            if k in self.last_w:
                deps.add(self.last_w[k])
            for rd in self.readers.get(k, ()):
                deps.add(rd)
        deps |= self.pending_barrier[eng]
        self.pending_barrier[eng] = set()
        deps.discard(oid)
        last_per_eng = {}
        keep = set()
        for d_ in deps:
            od = self.ops[d_]
            if od["dma"]:
                keep.add(d_)
            else:
                e_ = od["eng"]
                if e_ not in last_per_eng or d_ > last_per_eng[e_]:
                    last_per_eng[e_] = d_
        deps = keep | set(last_per_eng.values())
        self.ops.append(dict(eng=eng, fn=fn, deps=deps, dma=dma))
        for k in r:
            self.readers.setdefault(k, []).append(oid)
        for k in w:
            self.last_w[k] = oid
            self.readers[k] = []
        return oid

    def dma(self, fn, r=(), w=(), eng="sp"):
        return self.op(eng, fn, r, w, dma=True)

    def barrier(self):
        last = {}
        for i, o in enumerate(self.ops):
            last[o["eng"]] = i
        dm = [i for i in range(self._bar_at, len(self.ops)) if self.ops[i]["dma"]]
        s = set(last.values()) | set(dm)
        for e in self.ENGS:
            self.pending_barrier[e] |= s
        self._bar_at = len(self.ops)

    def emit(self, stack):
        nc = self.nc
        ops = self.ops
        n = len(ops)
        needed = [False] * n
        for i, o in enumerate(ops):
            nd = set()
            for d in o["deps"]:
                od = ops[d]
                if od["eng"] == "pe" and o["eng"] == "pe":
                    continue
                if od["eng"] == o["eng"] and not od["dma"] and od["fn"] is None:
                    continue
                nd.add(d)
            o["deps"] = nd
            for d in nd:
                needed[d] = True
        cnt = {e: 0 for e in self.ENGS}
        nseg = {e: 0 for e in self.ENGS}
        ndma = {"sp": 0, "pool": 0}
        dma_ops = {"sp": [], "pool": []}
        for i, o in enumerate(ops):
            if o["dma"]:
                q = o["eng"]
                nd = ndma[q]
                o["sig"] = ("dma_" + q, nd % NDMA, 16 * (nd // NDMA + 1), nd)
                dma_ops[q].append(o)
                ndma[q] += 1
            elif needed[i]:
                e = o["eng"]
                c = cnt[e]
                o["sig"] = (e, c // SEG, c % SEG + 1)
                cnt[e] += 1
                nseg[e] = c // SEG + 1
            else:
                o["sig"] = None
        sems = {}
        for e in self.ENGS:
            for s in range(nseg[e]):
                sems[(e, s)] = stack.enter_context(nc.semaphore(f"s_{e}_{s}"))
        for q in ("sp", "pool"):
            for k in range(min(NDMA, ndma[q])):
                sems[("dma_" + q, k)] = stack.enter_context(nc.semaphore(f"s_dma_{q}_{k}"))
        block = stack.enter_context(nc.Block())

        def make(ename):
            def body(eng):
                waited = {}
                for i, o in enumerate(ops):
                    if o["eng"] != ename:
                        continue
                    req = {}
                    for d in o["deps"]:
                        sg = ops[d]["sig"]
                        key = (sg[0], sg[1])
                        req[key] = max(req.get(key, 0), sg[2])
                    if o["dma"]:
                        idx = o["sig"][3]
                        if idx >= NDMA:
                            prev = dma_ops[o["eng"]][idx - NDMA]["sig"]
                            key = (prev[0], prev[1])
                            req[key] = max(req.get(key, 0), prev[2])
                    for key in sorted(req.keys(), key=lambda k: (k[0], k[1])):
                        val = req[key]
                        if waited.get(key, 0) >= val:
                            continue
                        if not key[0].startswith("dma"):
                            if any(k[0] == key[0] and k[1] > key[1] for k in waited):
                                continue
                        eng.wait_ge(sems[key], val)
                        waited[key] = val
                    if o["fn"] is None:
                        continue
                    ins = o["fn"](eng)
                    sg = o["sig"]
                    if sg is not None:
                        if sg[0].startswith("dma"):
                            ins.then_inc(sems[(sg[0], sg[1])], 16)
                        else:
                            ins.then_inc(sems[(sg[0], sg[1])], 1)
            return body

        block.tensor(make("pe"))
        block.scalar(make("act"))
        block.vector(make("dve"))
        block.gpsimd(make("pool"))
        block.sync(make("sp"))


class Arena:
    def __init__(self, t, nwords):
        self.t = t
        self.n = nwords
        self.off = 0

    def reset(self, off=0):
        self.off = off

    def take(self, shape, dtype):
        nel = 1
        for s in shape[1:]:
            nel *= s
        sz = 2 if dtype == BF16 else 4
        nw = (nel * sz + 3) // 4
        nw = (nw + 7) // 8 * 8
        assert self.off + nw <= self.n, f"arena overflow {self.off}+{nw}>{self.n}"
        ap = self.t[:, self.off:self.off + nw]
        self.off += nw
        if dtype != F32:
            ap = ap.bitcast(dtype)
        ap = ap[:, 0:nel]
        fd = shape[1:]
        if len(fd) == 2:
            ap = ap.rearrange("p (a b) -> p a b", a=fd[0])
        elif len(fd) == 3:
            ap = ap.rearrange("p (a b c) -> p a b c", a=fd[0], b=fd[1])
        elif len(fd) == 4:
            ap = ap.rearrange("p (a b c d) -> p a b c d", a=fd[0], b=fd[1], c=fd[2])
        return ap


def build_program(dbg=None, stop_after=None, mode=None):
    mode = mode or MODE
    NPRE = 24 if mode == "prefix" else 0
    NT = NPRE * 256 + 2048
    NCH = NT // 128
    FAKE_CC = isinstance(dbg, dict) and bool(dbg.get("fake_cc"))
    nc = bass.Bass("TRN2", target_bir_lowering=False)

    def din(name, shape, dt=F32):
        return nc.dram_tensor(name, list(shape), dt, kind="ExternalInput").ap()

    xT_d = din("xT", [128, 8, NT])
    xown_d = din("xown", [2, 128, 8, 1024])
    pos_d = din("pos", [128, NCH], I32)
    invf_d = din("invf", [128, 32])
    sqk_d = din("sqk", [128, 16])
    cdt_d = din("cdt", [128, 8])
    wtr_d = din("wtr", [128, 8, 4])
    sel5_d = din("sel5", [128, 3, 8])
    maskT_d = din("maskT", [128, 128])
    t0mask_d = din("t0mask", [128, 128])
    ident_d = din("ident", [128, 128])
    w_in_d = din("w_in", [128, 8, 5120])
    g1_d = din("g1", [128, 8])
    gng_d = din("gng", [128, 1024])
    are_d = din("s5are", [128, 32])
    aim_d = din("s5aim", [128, 32])
    ldt_d = din("s5ldt", [128, 32])
    bre_d = din("s5bre", [128, 32, 16])
    bim_d = din("s5bim", [128, 32, 16])
    cre_d = din("s5cre", [128, 32, 16])
    cim_d = din("s5cim", [128, 32, 16])
    dcol_d = din("dcol", [128, 64])
    glub_d = din("glub", [128, 1024])
    g2_d = din("g2rep", [128, 1024])
    gf_d = din("gfrep", [128, 1024])
    gm_d = din("gm", [128, 8])
    gluw_d = din("gluw", [128, 8, 1024])
    wout_d = din("wout", [128, 16, 1024])
    wqT_d = din("wqT", [128, 8, 1024])
    wk_d = din("wk", [128, 8, 1024])
    wv_d = din("wv", [128, 8, 1024])
    wo_d = din("wo", [128, 8, 1024])
    memT_d = din("memT", [128, 8, 256])
    out_d = nc.dram_tensor("out", [2, 128, 8, 1024], F32, kind="ExternalOutput").ap()

    skind = dict(kind="ExternalOutput") if dbg else {}
    u_d = nc.dram_tensor("u_scr", [NT, 1024], BF16, **skind).ap()
    ccR_in = nc.dram_tensor("ccR_in", [128, 512], F32)
    ccR_out = nc.dram_tensor("ccR_out", [1024, 512], F32)
    ccS_in = nc.dram_tensor("ccS_in", [128, 64], F32)
    ccS_out = nc.dram_tensor("ccS_out", [1024, 64], F32)
    g5_d = nc.dram_tensor("g5_scr", [2048, 1024], BF16, **skind).ap()
    retT_d = nc.dram_tensor("retT_scr", [128, 8, 2048], BF16, **skind).ap()
    ssmT_d = nc.dram_tensor("ssmT_scr", [2, 8, 128, 8, 128], BF16, **skind).ap()

    with ExitStack() as st:
        P = Prog(nc)
        sb = lambda name, shape, dt=F32: st.enter_context(nc.sbuf_tensor("sb_" + name, list(shape), dt))
        ARW = (dbg.get("arw") if isinstance(dbg, dict) and dbg.get("arw") else None) or (49200 if mode == "prefix" else 47200)
        arena_t = sb("arena", [128, ARW])
        AR = Arena(arena_t, ARW)
        identf = sb("identf", [128, 128])
        identb = sb("identb", [128, 128], BF16)
        ones_f = sb("ones_f", [128, 128])
        ones_b = sb("ones_b", [128, 128], BF16)
        epsc = sb("epsc", [128, 1])
        maskT = sb("maskT", [128, 128])
        t0mask = sb("t0mask", [128, 128])
        invf = sb("invf", [128, 32])
        sqk = sb("sqk", [128, 16])
        cdt = sb("cdt", [128, 8])
        wtr = sb("wtr", [128, 8, 4])
        sel5 = sb("sel5", [128, 3, 8])
        A2r = sb("A2r", [128, 2, 32])
        A2i = sb("A2i", [128, 2, 32])
        g1 = sb("g1", [128, 8])
        gm = sb("gm", [128, 8])
        Rf = sb("Rf", [128, 8, 128])
        Rbf = sb("Rbf", [128, 8, 128], BF16)
        car_r = sb("car_r", [128, 32])
        car_i = sb("car_i", [128, 32])
        PS = [st.enter_context(nc.psum_tensor(f"PS{i}", [128, 1024], F32))[:] for i in range(4)]

        def dma(out, in_, r=(), w=(), eng="sp"):
            P.dma(lambda e, out=out, in_=in_: e.dma_start(out=out, in_=in_), r=r, w=w, eng=eng)

        def TT(eng, out, in0, in1, op, r, w):
            P.op(eng, lambda e, out=out, in0=in0, in1=in1, op=op: e.tensor_tensor(out=out, in0=in0, in1=in1, op=op), r=r, w=w)

        def TS(eng, out, in0, s1, s2, op0, op1, r, w):
            if op1 is None:
                P.op(eng, lambda e, out=out, in0=in0, s1=s1, op0=op0: e.tensor_scalar(out=out, in0=in0, scalar1=s1, scalar2=None, op0=op0), r=r, w=w)
            else:
                P.op(eng, lambda e, out=out, in0=in0, s1=s1, s2=s2, op0=op0, op1=op1: e.tensor_scalar(out=out, in0=in0, scalar1=s1, scalar2=s2, op0=op0, op1=op1), r=r, w=w)

        def STT(out, in0, scalar, in1, op0, op1, r, w):
            P.op("dve", lambda e, out=out, in0=in0, scalar=scalar, in1=in1, op0=op0, op1=op1: e.scalar_tensor_tensor(out=out, in0=in0, scalar=scalar, in1=in1, op0=op0, op1=op1), r=r, w=w)

        def ACT(out, in_, func, r, w, **kw):
            P.op("act", lambda e, out=out, in_=in_, func=func, kw=kw: e.activation(out=out, in_=in_, func=func, **kw), r=r, w=w)

        def CP(eng, out, in_, r, w):
            if eng == "act":
                ACT(out, in_, AF.Copy, r, w)
            else:
                P.op(eng, lambda e, out=out, in_=in_: e.tensor_copy(out=out, in_=in_), r=r, w=w)

        def RECIP(out, in_, r, w):
            P.op("dve", lambda e, out=out, in_=in_: e.reciprocal(out=out, in_=in_), r=r, w=w)

        def RED(out, in_, op, r, w, axis=AX.X):
            P.op("dve", lambda e, out=out, in_=in_, op=op, axis=axis: e.tensor_reduce(out=out, in_=in_, axis=axis, op=op), r=r, w=w)

        def MM(out, lhsT, rhs, start, stop, r, w):
            P.op("pe", lambda e, out=out, lhsT=lhsT, rhs=rhs, start=start, stop=stop: e.matmul(out, lhsT=lhsT, rhs=rhs, start=start, stop=stop), r=r, w=w)

        def TR(out, in_, ident, r, w):
            P.op("pe", lambda e, out=out, in_=in_, ident=ident: e.transpose(out=out, in_=in_, identity=ident), r=r, w=w)

        def MEMSET(eng, ap, val, w):
            P.op(eng, lambda e, ap=ap, val=val: e.memset(ap, val), w=w)

        dma(identf[:], ident_d, w=["identf"])
        dma(maskT[:], maskT_d, w=["maskT"])
        dma(t0mask[:], t0mask_d, w=["t0mask"])
        dma(invf[:], invf_d, w=["invf"])
        dma(sqk[:], sqk_d, w=["sqk"])
        dma(cdt[:], cdt_d, w=["cdt"])
        dma(wtr[:], wtr_d, w=["wtr"])
        dma(sel5[:], sel5_d, w=["sel5"])
        dma(g1[:], g1_d, w=["g1"])
        dma(gm[:], gm_d, w=["gm"])
        CP("dve", identb[:], identf[:], ["identf"], ["identb"])
        MEMSET("pool", ones_f[:], 1.0, ["ones_f"])
        MEMSET("pool", ones_b[:], 1.0, ["ones_b"])
        MEMSET("pool", epsc[:], EPS, ["epsc"])
        MEMSET("pool", Rf[:], 0.0, ["Rf"])
        MEMSET("pool", Rbf[:], 0.0, ["Rbf"])
        MEMSET("pool", car_r[:], 0.0, ["car_r"])
        MEMSET("pool", car_i[:], 0.0, ["car_i"])

        def sincos(ang, shp, s_out, c_out, tmps, key):
            kf, ki, rr, mm = tmps
            rk = [key]
            TS("dve", kf, ang, 1.0 / TWO_PI, None, ALU.mult, None, rk, [key + "kf"])
            CP("dve", ki, kf, [key + "kf"], [key + "ki"])
            CP("dve", kf, ki, [key + "ki"], [key + "kf"])
            STT(rr, kf, -6.28125, ang, ALU.mult, ALU.add, [key + "kf"] + rk, [key + "rr"])
            STT(rr, kf, -(TWO_PI - 6.28125), rr, ALU.mult, ALU.add, [key + "kf", key + "rr"], [key + "rr"])

            def wrap(t):
                TS("dve", mm, t, -math.pi, None, ALU.is_lt, None, [key + "rr"], [key + "mm"])
                STT(t, mm, TWO_PI, t, ALU.mult, ALU.add, [key + "mm", key + "rr"], [key + "rr"])
                TS("dve", mm, t, math.pi, None, ALU.is_gt, None, [key + "rr"], [key + "mm"])
                STT(t, mm, -TWO_PI, t, ALU.mult, ALU.add, [key + "mm", key + "rr"], [key + "rr"])
            wrap(rr)
            ACT(s_out, rr, AF.Sin, [key + "rr"], [key + "s"])
            TS("dve", rr, rr, math.pi / 2.0, None, ALU.add, None, [key + "rr", key + "s"], [key + "rr"])
            wrap(rr)
            ACT(c_out, rr, AF.Sin, [key + "rr"], [key + "c"])

        def norm_fm(xt, N, gcol, sq, rs1, rs2, hT, psum_ap, kx, ksq, krs, kh, kps, gkey):
            ACT(sq, xt, AF.Square, [kx], [ksq])
            for kt in range(8):
                MM(psum_ap, ones_b[:], sq[:, kt, :], kt == 0, kt == 7, [ksq, "ones_b"], [kps])
            ACT(rs1, psum_ap, AF.Sqrt, [kps], [krs + "1"], bias=EPS, scale=1.0 / 1024.0)
            RECIP(rs2, rs1, [krs + "1"], [krs])
            for kt in range(8):
                STT(hT[:, kt, :], xt[:, kt, :], gcol[:, kt:kt + 1], rs2, ALU.mult, ALU.mult, [kx, krs, gkey], [kh])

        AR.reset()
        w_in = AR.take([128, 8, 5120], BF16)
        cosT = AR.take([128, NCH, 32], F32)
        sinT = AR.take([128, NCH, 32], F32)
        gng = AR.take([128, 1024], F32)
        xTb = [AR.take([128, 8, 256], F32) for _ in range(2)]
        sq = AR.take([128, 8, 256], BF16)
        posi = AR.take([128, NCH], I32)
        posf = AR.take([128, NCH], F32)
        Rc = AR.take([128, 4, 128], F32)
        mark = AR.off
        tki = AR.take([128, NCH * 32], I32)
        tmm = AR.take([128, NCH * 32], F32)
        trr_t = AR.take([128, NCH * 32], F32)
        AR.reset(mark)
        Gt = AR.take([128, 8, 4, 128], F32)
        AR.reset(mark)
        hTb = [AR.take([128, 8, 256], BF16) for _ in range(2)]
        rs1 = AR.take([128, 256], F32)
        rs2 = AR.take([128, 256], F32)
        qk = AR.take([128, 16, 2, 32], F32)
        rt = [AR.take([128, 16, 32], F32) for _ in range(4)]
        qkt = [AR.take([128, 16, 64], BF16) for _ in range(2)]
        v_bf = [AR.take([128, 1024], BF16) for _ in range(2)]
        gr_bf = [AR.take([128, 1024], BF16) for _ in range(2)]
        g5_bf = [AR.take([128, 1024], BF16) for _ in range(2)]
        u_bf = [AR.take([128, 1024], BF16) for _ in range(2)]
        kT = [AR.take([128, 4, 128], BF16) for _ in range(2)]
        qm = [AR.take([128, 8, 128], BF16) for _ in range(2)]
        PT = AR.take([128, 8, 128], BF16)
        o_sb = AR.take([128, 8, 128], F32)
        osq = AR.take([128, 8, 128], F32)
        ret = [AR.take([128, 1024], BF16) for _ in range(2)]
        retT_blk = AR.take([128, 8, 256], BF16)
        stt = AR.take([128, 48], F32)
        ang = xTb[0].rearrange("p a b -> p (a b)")[:, 0:NCH * 32]
        tkf = xTb[1].rearrange("p a b -> p (a b)")[:, 0:NCH * 32]
        trr = trr_t[:, 0:NCH * 32]

        dma(gng, gng_d, w=["gng"])
        MEMSET("pool", qm[0], 0.0, ["qm0"])
        MEMSET("pool", qm[1], 0.0, ["qm1"])
        dma(posi, pos_d, w=["posi"])
        CP("dve", posf, posi, ["posi"], ["posf"])
        TT("dve", ang.rearrange("p (c i) -> p c i", i=32), posf.unsqueeze(2).to_broadcast([128, NCH, 32]),
           invf[:].unsqueeze(1).to_broadcast([128, NCH, 32]), ALU.mult, ["posf", "invf", "xT0"], ["xT0"])
        sincos(ang, None, sinT.rearrange("p c i -> p (c i)"), cosT.rearrange("p c i -> p (c i)"),
               (tkf, tki, trr, tmm), "xT0")
        P.barrier()
        for cb in (1, 2, 3, 6, 7, 0, 4, 5, 8, 9):
            cs_ = slice(cb * 512, (cb + 1) * 512)
            dma(w_in[:, :, cs_], w_in_d[:, :, cs_], w=[f"w_in{cb}"], eng="pool")
            TT("dve", w_in[:, :, cs_], w_in[:, :, cs_], g1[:].unsqueeze(2).to_broadcast([128, 8, 512]), ALU.mult, [f"w_in{cb}", "g1"], [f"w_in{cb}"])

        qk2 = qk.rearrange("p h t i -> p (h t i)")
        qkt2 = [q_.rearrange("p h d -> p (h d)") for q_ in qkt]
        o2 = o_sb.rearrange("p h e -> p (h e)")
        PSUM_R2 = (mode == "coll") and not (isinstance(dbg, dict) and dbg.get("psum_new"))
        if PSUM_R2:
            KVp = PS[3].rearrange("p (h e) -> p h e", h=8)
            TRb = PS[0][:, 512:1024].bitcast(BF16)
            kvi = lambda h: h
            kvs = lambda par: slice(par, 8, 2)
        else:
            KVp = PS[3][:, 0:512].rearrange("p (b e) -> p b e", b=4)
            TRb = PS[3][:, 512:1024].bitcast(BF16)
            kvi = lambda h: h // 2
            kvs = lambda par: slice(0, 4, 1)
        SCp = PS[2][:, 0:512]
        Op = PS[2][:, 512:1024]
        pj_i = [0]

        pj_n = [4]

        def pj():
            if PSUM_R2:
                return PS[0][:, 0:512], "PJ0"
            i = pj_i[0] % pj_n[0]
            pj_i[0] += 1
            return PS[i // 2][:, (i % 2) * 512:(i % 2 + 1) * 512], f"PJ{i}"

        nblk = NOWN_BLK
        LVL = dbg.get('own_lvl', 9) if isinstance(dbg, dict) else 9
        if mode == "coll":
            sa_iter = [("light", b_) for b_ in range(nblk)] + [("xchg", -1)] + [("full", b_) for b_ in range(nblk)]
            light_cbs = [1, 2, 3]
        else:
            sa_iter = [("light", b_) for b_ in range(NPRE)] + [("full", NPRE + b_) for b_ in range(nblk)]
            light_cbs = [1, 2, 3, 6, 7]
        def do_xchg():
            P.barrier()
            dma(ccR_in.ap()[0:64, :].rearrange("p (b e) -> p b e", b=4), Rf[0:64, 0::2, :], r=["Rf"], w=["ccR_in"])
            dma(ccR_in.ap()[64:128, :].rearrange("p (b e) -> p b e", b=4), Rf[64:128, 1::2, :], r=["Rf"], w=["ccR_in"])
            if FAKE_CC:
                dma(ccR_out.ap()[0:128, :], ccR_in.ap(), r=["ccR_in"], w=["ccR_out"])
            else:
                P.op("pool", lambda e: e.collective_compute("AllGather", ALU.bypass, replica_groups=[list(range(8))],
                                                            ins=[ccR_in.ap().opt()], outs=[ccR_out.ap().opt()]),
                     r=["ccR_in"], w=["ccR_out"])
            dma(Gt.rearrange("p r b e -> p r (b e)"), ccR_out.ap().rearrange("(r p) c -> p r c", p=128), r=["ccR_out"], w=["Gt"])
            TT("dve", Gt, Gt, wtr[:].unsqueeze(3).to_broadcast([128, 8, 4, 128]), ALU.mult, ["Gt", "wtr"], ["Gt"])
            RED(Rc, Gt.rearrange("p r b e -> p b e r"), ALU.add, ["Gt"], ["Rc"])
            CP("act", Rf[0:64, 0::2, :], Rc[0:64, :, :], ["Rc", "Rf"], ["Rf"])
            CP("act", Rf[64:128, 1::2, :], Rc[64:128, :, :], ["Rc", "Rf"], ["Rf"])
            CP("pool", Rbf[:], Rf[:], ["Rf"], ["Rbf"])
            P.barrier()


        def load_block(blk):
            dma(xTb[blk % 2], xT_d[:, :, blk * 256:(blk + 1) * 256], w=[f"xT{blk % 2}"])

        def norm_block(blk):
            xb = xTb[blk % 2]
            kx = f"xT{blk % 2}"
            bb = blk % 2
            ACT(sq, xb, AF.Square, [kx], ["sq"])
            ACT(hTb[bb], xb, AF.Copy, [kx], [f"hT{bb}"])
            pa, pk = pj()
            for ci in range(2):
                for kt in range(8):
                    MM(pa[:, ci:ci + 1], sq[:, kt, ci * 128:(ci + 1) * 128], ones_b[:, 0:1], kt == 0, kt == 7, ["sq", "ones_b"], [pk])
            ACT(rs1[:, 2 * bb:2 * bb + 2], pa[:, 0:2], AF.Sqrt, [pk], [f"rsa{bb}"], bias=EPS, scale=1.0 / 1024.0)
            RECIP(rs2[:, 2 * bb:2 * bb + 2], rs1[:, 2 * bb:2 * bb + 2], [f"rsa{bb}"], [f"rsb{bb}"])

        def phaseA(pas, blk, ci):
            own = pas == "full"
            c = blk * 2 + ci
            d = c % 2
            hT = hTb[blk % 2]
            hk = f"hT{blk % 2}"
            hs = lambda kt: hT[:, kt, ci * 128:(ci + 1) * 128]
            cbs = [0, 1, 2, 3, 4, 5, 6, 7, 8, 9] if own else light_cbs
            for cb in cbs:
                pa, pk = pj()
                for kt in range(8):
                    MM(pa, hs(kt), w_in[:, kt, cb * 512:(cb + 1) * 512], kt == 0, kt == 7, [hk, f"w_in{cb}"], [pk])
                rcol = rs2[:, 2 * (blk % 2) + ci:2 * (blk % 2) + ci + 1]
                rk = f"rsb{blk % 2}"
                if cb == 0:
                    ACT(qk2[:, 0:512], pa, AF.Copy, [pk, rk], ["qk_q"], scale=rcol)
                elif cb == 1:
                    ACT(qk2[:, 512:1024], pa, AF.Copy, [pk, rk], ["qk_k"], scale=rcol)
                elif cb in (2, 3):
                    TS("dve", v_bf[d][:, (cb - 2) * 512:(cb - 1) * 512], pa, rcol, None, ALU.mult, None, [pk, rk], [f"v_bf{d}"])
                elif cb in (4, 5):
                    ACT(gr_bf[d][:, (cb - 4) * 512:(cb - 3) * 512], pa, AF.Silu, [pk, rk], [f"gr_bf{d}"], scale=rcol)
                elif cb in (6, 7):
                    TS("dve", u_bf[d][:, (cb - 6) * 512:(cb - 5) * 512], pa, rcol, None, ALU.mult, None, [pk, rk], [f"u_bf{d}"])
                else:
                    ACT(g5_bf[d][:, (cb - 8) * 512:(cb - 7) * 512], pa, AF.Silu, [pk, rk], [f"g5_bf{d}"], scale=rcol)
            if own or 6 in light_cbs:
                dma(u_d[c * 128:(c + 1) * 128, :], u_bf[d], r=[f"u_bf{d}"])
            if own:
                oc = c - 2 * NPRE
                dma(g5_d[oc * 128:(oc + 1) * 128, :], g5_bf[d], r=[f"g5_bf{d}"])
            h0 = 0 if own else 8
            nh = 16 - h0
            cosb = cosT[:, c, :].unsqueeze(1).to_broadcast([128, nh, 32])
            sinb = sinT[:, c, :].unsqueeze(1).to_broadcast([128, nh, 32])
            x1 = qk[:, h0:16, 0, :]
            x2 = qk[:, h0:16, 1, :]
            rq = ["qk_q", "qk_k"] if own else ["qk_k"]
            qk3 = qk[:, h0:16, :, :].rearrange("p h t i -> p h (t i)")
            TT("pool", qk3, qk3, sqk[:, h0:16].unsqueeze(2).to_broadcast([128, nh, 64]), ALU.mult, rq + ["sqk"], rq)
            q4 = qkt[d].rearrange("p h (t i) -> p h t i", t=2)
            TT("dve", rt[0][:, h0:16, :], x1, cosb, ALU.mult, rq, ["rt0"])
            TT("dve", rt[1][:, h0:16, :], x2, sinb, ALU.mult, rq, ["rt1"])
            TT("dve", q4[:, h0:16, 0, :], rt[0][:, h0:16, :], rt[1][:, h0:16, :], ALU.subtract, ["rt0", "rt1"], [f"qkt{d}"])
            TT("pool", rt[2][:, h0:16, :], x1, sinb, ALU.mult, rq, ["rt2"])
            TT("pool", rt[3][:, h0:16, :], x2, cosb, ALU.mult, rq, ["rt3"])
            TT("pool", q4[:, h0:16, 1, :], rt[2][:, h0:16, :], rt[3][:, h0:16, :], ALU.add, ["rt2", "rt3"], [f"qkt{d}"])
            if own:
                for b in range(8):
                    TR(TRb[:, b * 128:(b + 1) * 128], qkt2[d][:, b * 128:(b + 1) * 128], identb[:], [f"qkt{d}", "identb"], ["TRb"])
                CP("act", kT[d].rearrange("p b t -> p (b t)"), TRb[:, 512:1024], ["TRb"], [f"kT{d}"])
                CP("act", qm[d][0:64, 0::2, :], TRb[0:64, 0:512].rearrange("p (b t) -> p b t", b=4), ["TRb"], [f"qm{d}"])
                CP("act", qm[d][64:128, 1::2, :], TRb[64:128, 0:512].rearrange("p (b t) -> p b t", b=4), ["TRb"], [f"qm{d}"])

        def phaseB(pas, blk, ci):
            own = pas == "full"
            c = blk * 2 + ci
            d = c % 2
            for h in range(8):
                MM(KVp[(h % 2) * 64:(h % 2) * 64 + 64, kvi(h), :], qkt2[d][:, 512 + h * 64:512 + (h + 1) * 64],
                   v_bf[d][:, h * 128:(h + 1) * 128], True, True, [f"qkt{d}", f"v_bf{d}"], ["KVp"])
            if own:
                for hf in range(2):
                    for h in range(4 * hf, 4 * hf + 4):
                        MM(SCp[:, (h % 4) * 128:(h % 4 + 1) * 128], kT[d][:, h // 2, :], qm[d][:, h, :], True, True, [f"kT{d}", f"qm{d}"], ["SCp"])
                    TT("dve", PT[:, 4 * hf:4 * hf + 4, :], SCp.rearrange("p (h i) -> p h i", h=4), maskT[:].unsqueeze(1).to_broadcast([128, 4, 128]),
                       ALU.mult, ["SCp", "maskT"], [f"PT{hf}"])
                for hf in range(2):
                    for h in range(4 * hf, 4 * hf + 4):
                        oo = Op[:, (h % 4) * 128:(h % 4 + 1) * 128]
                        MM(oo, PT[:, h, :], v_bf[d][:, h * 128:(h + 1) * 128], True, False, [f"PT{hf}", f"v_bf{d}"], ["Op"])
                        MM(oo, qm[d][:, h, :], Rbf[:, h, :], False, True, [f"qm{d}", "Rbf"], ["Op"])
                    CP("act", o2[:, hf * 512:(hf + 1) * 512], Op, ["Op"], [f"o_sb{hf}", "o_sb"])
                    ACT(osq.rearrange("p h e -> p (h e)")[:, hf * 512:(hf + 1) * 512], Op, AF.Square, ["Op"], [f"osq{hf}"])
            for par in range(2):
                rows = slice(par * 64, par * 64 + 64)
                TT("dve", Rf[rows, par::2, :], KVp[rows, kvs(par), :], Rf[rows, par::2, :], ALU.add, ["KVp", "Rf"], ["Rf"])
                TT("dve", Rf[rows, par::2, :], Rf[rows, par::2, :], cdt[rows, par::2].unsqueeze(2).to_broadcast([64, 4, 128]), ALU.mult, ["Rf", "cdt"], ["Rf"])
                CP("act", Rbf[rows, par::2, :], Rf[rows, par::2, :], ["Rf"], ["Rbf"])
            if own:
                RED(stt[:, 0:8], o_sb, ALU.add, ["o_sb0", "o_sb1"], ["st_s"])
                RED(stt[:, 8:16], osq, ALU.add, ["osq0", "osq1"], ["st_q"])
                TS("dve", stt[:, 16:24], stt[:, 0:8], 1.0 / 128.0, None, ALU.mult, None, ["st_s"], ["st_m"])
                TT("dve", stt[:, 24:32], stt[:, 16:24], stt[:, 16:24], ALU.mult, ["st_m"], ["st_m2"])
                STT(stt[:, 32:40], stt[:, 8:16], 1.0 / 128.0, stt[:, 24:32], ALU.mult, ALU.subtract, ["st_q", "st_m2"], ["st_v"])
                ACT(stt[:, 40:48], stt[:, 32:40], AF.Sqrt, ["st_v"], ["st_sd"], bias=EPS, scale=1.0)
                RECIP(stt[:, 32:40], stt[:, 40:48], ["st_sd"], ["st_rs"])
                TT("pool", o_sb, o_sb, stt[:, 16:24].unsqueeze(2).to_broadcast([128, 8, 128]), ALU.subtract, ["o_sb0", "o_sb1", "st_m"], ["o_sb0", "o_sb1", "o_sb"])
                TT("pool", o_sb, o_sb, stt[:, 32:40].unsqueeze(2).to_broadcast([128, 8, 128]), ALU.mult, ["o_sb", "st_rs"], ["o_sb"])
                TT("dve", o2, o2, gng, ALU.mult, ["o_sb", "gng"], ["o_sb"])
                TT("dve", ret[d], o2, gr_bf[d], ALU.mult, ["o_sb", f"gr_bf{d}"], [f"ret{d}"])

        def phaseC(pas, blk, ci):
            if pas != "full":
                return
            c = blk * 2 + ci
            d = c % 2
            for kt in range(8):
                TR(TRb[:, kt * 128:(kt + 1) * 128], ret[d][:, kt * 128:(kt + 1) * 128], identb[:], [f"ret{d}", "identb"], ["TRb"])
            CP("act", retT_blk[:, :, ci * 128:(ci + 1) * 128], TRb.rearrange("p (k t) -> p k t", k=8), ["TRb"], ["retT_blk"])
            if ci == 1:
                ob = blk - NPRE
                dma(retT_d[:, :, ob * 256:(ob + 1) * 256], retT_blk, r=["retT_blk"])

        pendB = None
        pendC = None
        real = [(p_, b_) for p_, b_ in sa_iter if p_ != "xchg"]
        has_x = any(p_ == "xchg" for p_, _ in sa_iter)
        load_block(real[0][1])
        norm_block(real[0][1])
        ri = 0
        prev_pas = None
        for pas, blk in sa_iter:
            if pas == "light" and not PSUM_R2:
                pj_n[0] = 4
            if pas == "full" and prev_pas == "light" and mode != "coll":
                if pendB is not None:
                    phaseB(*pendB)
                    pendB = None
                P.barrier()
            prev_pas = pas
            if pas == "xchg":
                if pendB is not None:
                    phaseB(*pendB)
                    pendB = None
                do_xchg()
                load_block(real[ri][1])
                norm_block(real[ri][1])
                continue
            nxt = real[ri + 1] if ri + 1 < len(real) else None
            if nxt is not None and has_x and nxt[0] != pas:
                nxt = None
            for ci in range(2):
                if ci == 0 and nxt is not None:
                    load_block(nxt[1])
                phaseA(pas, blk, ci)
                if ci == 0 and nxt is not None:
                    norm_block(nxt[1])
                if pendC is not None:
                    phaseC(*pendC)
                    pendC = None
                if pendB is not None:
                    phaseB(*pendB)
                    pendC = pendB
                pendB = (pas, blk, ci)
            ri += 1
        if pendC is not None:
            phaseC(*pendC)
        if pendB is not None:
            phaseB(*pendB)
            phaseC(*pendB)
        P.barrier()
        if stop_after == "SA":
            return finish(nc, P, st, out_d)

        AR.reset()
        T0 = AR.take([128, 64, 128], BF16)
        I2Sre = AR.take([128, 64, 64], BF16)
        I2Sim = AR.take([128, 64, 64], BF16)
        S2Ore = AR.take([128, 32, 128], BF16)
        S2Oim = AR.take([128, 32, 128], BF16)
        TIre = AR.take([128, 32, 16], F32)
        TIim = AR.take([128, 32, 16], F32)
        TOre = AR.take([128, 32, 8], F32)
        TOim = AR.take([128, 32, 8], F32)
        P8 = AR.take([128, 32], F32)
        Q8 = AR.take([128, 32], F32)
        P128 = AR.take([128, 32], F32)
        Q128 = AR.take([128, 32], F32)
        base_sb = AR.off
        are = AR.take([128, 32], F32)
        aim = AR.take([128, 32], F32)
        ldt = AR.take([128, 32], F32)
        bre = AR.take([128, 32, 16], F32)
        bim = AR.take([128, 32, 16], F32)
        cre = AR.take([128, 32, 16], F32)
        cim = AR.take([128, 32, 16], F32)
        dtt = AR.take([128, 32], F32)
        tA = [AR.take([128, 32], F32) for _ in range(12)]
        tI = AR.take([128, 32], I32)
        PWr = AR.take([128, 9, 32], F32)
        PWi = AR.take([128, 9, 32], F32)
        NWr = AR.take([128, 9, 32], F32)
        NWi = AR.take([128, 9, 32], F32)
        E8r = AR.take([128, 17, 32], F32)
        E8i = AR.take([128, 17, 32], F32)
        E16r = AR.take([128, 9, 32], F32)
        E16i = AR.take([128, 9, 32], F32)
        Bbr = AR.take([128, 32, 16], F32)
        Bbi = AR.take([128, 32, 16], F32)
        Gre = AR.take([128, 32, 8, 16], F32)
        Gim = AR.take([128, 32, 8, 16], F32)
        Sre = AR.take([128, 32, 8, 16], F32)
        Sim = AR.take([128, 32, 8, 16], F32)
        Ire = AR.take([128, 32, 8, 16], F32)
        Iim = AR.take([128, 32, 8, 16], F32)
        tB = [AR.take([128, 32, 16], F32) for _ in range(2)]
        T0f = AR.take([128, 4, 128], F32)
        dcol = AR.take([128, 64], F32)

        K0 = ["S0B"]
        dma(dcol, dcol_d, w=K0)
        for t_, d_ in ((are, are_d), (aim, aim_d), (ldt, ldt_d), (bre, bre_d), (bim, bim_d), (cre, cre_d), (cim, cim_d)):
            dma(t_, d_, w=K0)
        P.barrier()
        V = lambda out, in0, in1, op: TT("dve", out, in0, in1, op, K0, K0)
        VS = lambda out, in0, s1, op: TS("dve", out, in0, s1, None, op, None, K0, K0)
        VC = lambda out, in_: CP("dve", out, in_, K0, K0)
        VM = lambda ap, val: MEMSET("dve", ap, val, K0)
        ACT(dtt, ldt, AF.Exp, K0, K0)
        V(tA[0], are, dtt, ALU.mult)
        ACT(tA[1], tA[0], AF.Exp, K0, K0)
        ACT(tA[2], tA[0], AF.Exp, K0, K0, scale=-1.0)
        V(tA[3], aim, dtt, ALU.mult)
        sincos(tA[3], None, tA[4], tA[5], (tA[6], tI, tA[7], tA[8]), "S0B")
        P.barrier()

        def cmul(orr, oi, xr, xi, yr, yi, tmp):
            u1, u2 = tmp
            V(u1, xr, yr, ALU.mult)
            V(u2, xi, yi, ALU.mult)
            V(orr, u1, u2, ALU.subtract)
            V(u1, xr, yi, ALU.mult)
            V(u2, xi, yr, ALU.mult)
            V(oi, u1, u2, ALU.add)

        tsm = (tA[10], tA[11])
        VM(PWr[:, 0, :], 1.0)
        VM(PWi[:, 0, :], 0.0)
        V(PWr[:, 1, :], tA[1], tA[5], ALU.mult)
        V(PWi[:, 1, :], tA[1], tA[4], ALU.mult)
        VM(NWr[:, 0, :], 1.0)
        VM(NWi[:, 0, :], 0.0)
        V(NWr[:, 1, :], tA[2], tA[5], ALU.mult)
        V(tA[9], tA[2], tA[4], ALU.mult)
        VS(NWi[:, 1, :], tA[9], -1.0, ALU.mult)
        for k in range(1, 8):
            cmul(PWr[:, k + 1, :], PWi[:, k + 1, :], PWr[:, k, :], PWi[:, k, :], PWr[:, 1, :], PWi[:, 1, :], tsm)
            cmul(NWr[:, k + 1, :], NWi[:, k + 1, :], NWr[:, k, :], NWi[:, k, :], NWr[:, 1, :], NWi[:, 1, :], tsm)
        VM(E8r[:, 0, :], 1.0)
        VM(E8i[:, 0, :], 0.0)
        VC(E8r[:, 1, :], PWr[:, 8, :])
        VC(E8i[:, 1, :], PWi[:, 8, :])
        for k in range(1, 16):
            cmul(E8r[:, k + 1, :], E8i[:, k + 1, :], E8r[:, k, :], E8i[:, k, :], E8r[:, 1, :], E8i[:, 1, :], tsm)
        VM(E16r[:, 0, :], 1.0)
        VM(E16i[:, 0, :], 0.0)
        VC(E16r[:, 1, :], E8r[:, 16, :])
        VC(E16i[:, 1, :], E8i[:, 16, :])
        for k in range(1, 8):
            cmul(E16r[:, k + 1, :], E16i[:, k + 1, :], E16r[:, k, :], E16i[:, k, :], E16r[:, 1, :], E16i[:, 1, :], tsm)
        VC(P8, E8r[:, 1, :])
        VC(Q8, E8i[:, 1, :])
        VC(P128, E16r[:, 8, :])
        VC(Q128, E16i[:, 8, :])
        cmul(A2r[:, 0, :], A2i[:, 0, :], P128, Q128, P128, Q128, tsm)
        cmul(A2r[:, 1, :], A2i[:, 1, :], A2r[:, 0, :], A2i[:, 0, :], A2r[:, 0, :], A2i[:, 0, :], tsm)
        for r_ in range(16):
            VC(TIre[:, :, r_], E8r[:, 15 - r_, :])
            VC(TIim[:, :, r_], E8i[:, 15 - r_, :])
        for q_ in range(8):
            VC(TOre[:, :, q_], E16r[:, 7 - q_, :])
            VC(TOim[:, :, q_], E16i[:, 7 - q_, :])
        V(tA[0], are, are, ALU.mult)
        V(tA[1], aim, aim, ALU.mult)
        V(tA[0], tA[0], tA[1], ALU.add)
        RECIP(tA[0], tA[0], K0, K0)
        VS(tA[1], PWr[:, 1, :], -1.0, ALU.add)
        V(tA[2], tA[1], are, ALU.mult)
        V(tA[3], PWi[:, 1, :], aim, ALU.mult)
        V(tA[2], tA[2], tA[3], ALU.add)
        V(tA[2], tA[2], tA[0], ALU.mult)
        V(tA[3], PWi[:, 1, :], are, ALU.mult)
        V(tA[4], tA[1], aim, ALU.mult)
        V(tA[3], tA[3], tA[4], ALU.subtract)
        V(tA[3], tA[3], tA[0], ALU.mult)
        bc16 = lambda ap: ap.unsqueeze(2).to_broadcast([128, 32, 16])
        cmul(Bbr, Bbi, bre, bim, bc16(tA[2]), bc16(tA[3]), (tB[0], tB[1]))
        for s_ in range(8):
            cmul(Gre[:, :, s_, :], Gim[:, :, s_, :], Bbr, Bbi, bc16(NWr[:, s_ + 1, :]), bc16(NWi[:, s_ + 1, :]), (tB[0], tB[1]))
            cmul(Ire[:, :, s_, :], Iim[:, :, s_, :], Bbr, Bbi, bc16(PWr[:, 7 - s_, :]), bc16(PWi[:, 7 - s_, :]), (tB[0], tB[1]))
            cmul(Sre[:, :, s_, :], Sim[:, :, s_, :], cre, cim, bc16(PWr[:, s_ + 1, :]), bc16(PWi[:, s_ + 1, :]), (tB[0], tB[1]))
        Sim2 = Sim.rearrange("p g t j -> p (g t j)")
        VS(Sim2, Sim2, -1.0, ALU.mult)
        P.barrier()
        CP("act", S2Ore.rearrange("p g x -> p (g x)"), Sre.rearrange("p g t j -> p (g t j)"), K0, ["S2O"])
        CP("act", S2Oim.rearrange("p g x -> p (g x)"), Sim2, K0, ["S2O"])
        Gre3 = Gre.rearrange("p g s j -> p g (s j)")
        Gim3 = Gim.rearrange("p g s j -> p g (s j)")
        Sre3 = Sre.rearrange("p g t j -> p g (t j)")
        Sim3 = Sim.rearrange("p g t j -> p g (t j)")
        Ire3 = Ire.rearrange("p g s j -> p g (s j)")
        Iim3 = Iim.rearrange("p g s j -> p g (s j)")
        for gb in range(16):
            bank = PS[gb % 2][:, 0:512]
            bk = f"PSb{gb % 2}"
            for i in range(4):
                g = gb * 4 + i
                gh, gp = g // 32, g % 32
                pb = gh * 64
                MM(bank[:, i * 128:(i + 1) * 128], Gre3[pb:pb + 64, gp, :], Sre3[pb:pb + 64, gp, :], True, False, K0, [bk])
                MM(bank[:, i * 128:(i + 1) * 128], Gim3[pb:pb + 64, gp, :], Sim3[pb:pb + 64, gp, :], False, True, K0, [bk])
            TT("dve", T0f, bank.rearrange("p (g x) -> p g x", g=4),
               t0mask[:].unsqueeze(1).to_broadcast([128, 4, 128]), ALU.mult, [bk, "t0mask"] + K0, K0)
            for i in range(4):
                g = gb * 4 + i
                STT(T0[:, g, :], identf[:], dcol[:, g:g + 1], T0f[:, i, :], ALU.mult, ALU.add, K0 + ["identf"], K0 + ["T0"])
        for src3, dst, nm in ((Ire3, I2Sre, "I2Sre"), (Iim3, I2Sim, "I2Sim")):
            for gb in range(8):
                bank = PS[2 + gb % 2][:, 0:512]
                bk = f"PSc{gb % 2}"
                for i in range(8):
                    g = gb * 8 + i
                    gh, gp = g // 32, g % 32
                    pb = gh * 64
                    TR(bank[:, i * 64:(i + 1) * 64], src3[pb:pb + 64, gp, :], identf[pb:pb + 64, pb:pb + 64], ["identf", "S0B"], [bk])
                CP("act", dst[:, gb * 8:(gb + 1) * 8, :], bank.rearrange("p (g n) -> p g n", g=8), [bk], [nm])
        P.barrier()
        if stop_after == "S0b":
            return finish(nc, P, st, out_d)

        AR.reset(base_sb)
        u_raw = AR.take([128, 8, 1024], BF16)
        U = u_raw.rearrange("p s c -> p (s c)").rearrange("p (g m) -> p g m", g=64)
        u_tm2 = AR.take([128, 64, 8, 16], BF16)
        inc_raw = AR.take([128, 4096], F32)
        Incre = inc_raw.rearrange("p (g m) -> p g m", g=32)
        yg_bf = inc_raw.bitcast(BF16).rearrange("p (s c) -> p s c", s=8)
        Incim = AR.take([128, 32, 128], F32)
        gluw = AR.take([128, 8, 1024], BF16)
        glub = AR.take([128, 1024], F32)
        g5s = [AR.take([128, 1024], BF16) for _ in range(2)]
        ct = [AR.take([128, 32], F32) for _ in range(12)]
        un_base = AR.off
        LA = AR.take([128, 32, 8, 16], F32)
        LB = AR.take([128, 32, 8, 16], F32)
        LS = [AR.take([128, 32, 8], F32) for _ in range(4)]
        LE = [AR.take([128, 32], F32) for _ in range(4)]
        AR.reset(un_base)
        Xre_bf = AR.take([128, 32, 128], BF16)
        Xim_bf = AR.take([128, 32, 128], BF16)
        Ysb = [AR.take([128, 8, 128], F32) for _ in range(2)]
        gt1 = [AR.take([128, 1024], F32) for _ in range(2)]
        gt2 = [AR.take([128, 1024], F32) for _ in range(2)]
        AR.reset(un_base)
        ygT = [AR.take([128, 8, 128], BF16) for _ in range(2)]
        zb = [AR.take([128, 1024], F32) for _ in range(2)]
        gate = [AR.take([128, 1024], F32) for _ in range(2)]
        tt_ = [AR.take([128, 1024], F32) for _ in range(2)]
        ssm = [AR.take([128, 1024], BF16) for _ in range(2)]
        ssmT = [AR.take([128, 8, 128], BF16) for _ in range(2)]
        AR.reset(un_base)

        for kt in range(8):
            dma(gluw[:, kt, :], gluw_d[:, kt, :], w=["gluw"], eng="pool")
        dma(glub, glub_d, w=["glub"])

        TRU = PS[0][:, 0:512].bitcast(BF16)
        PIre = PS[1][:, 0:512]
        PIim = PS[1][:, 512:1024]
        TRU2 = [PS[0][:, 0:512].bitcast(BF16), PS[0][:, 512:1024].bitcast(BF16)]
        PI2 = [(PS[1][:, 0:512], PS[1][:, 512:1024]), (PS[2][:, 0:512], PS[2][:, 512:1024])]
        PY = PS[2]
        PYT = PS[3]
        IRK = [f"IR{m}" for m in range(128)]
        IIK = [f"II{m}" for m in range(128)]

        Gs = AR.take([128, 8, 64], F32)
        Gs2 = AR.take([128, 8, 64], F32)
        Sd = AR.take([128, 3, 64], F32)
        xt_ = [AR.take([128, 32], F32) for _ in range(4)]
        NPS = NPRE // 4
        if mode == "coll":
            sb_iter = [("light", 0), ("light", 1), ("xchg", -1), ("full", 0), ("full", 1)]
        else:
            sb_iter = [("light", sc_) for sc_ in range(NPS)] + [("full", NPS), ("full", NPS + 1)]
        for pas, sc in sb_iter:
            if pas == "xchg":
                P.barrier()
                dma(ccS_in.ap()[:, 0:32], car_r[:], r=["car_r"], w=["ccS_in"])
                dma(ccS_in.ap()[:, 32:64], car_i[:], r=["car_i"], w=["ccS_in"])
                if FAKE_CC:
                    dma(ccS_out.ap()[0:128, :], ccS_in.ap(), r=["ccS_in"], w=["ccS_out"])
                else:
                    P.op("pool", lambda e: e.collective_compute("AllGather", ALU.bypass, replica_groups=[list(range(8))],
                                                                ins=[ccS_in.ap().opt()], outs=[ccS_out.ap().opt()]),
                         r=["ccS_in"], w=["ccS_out"])
                dma(Gs, ccS_out.ap().rearrange("(r p) c -> p r c", p=128), r=["ccS_out"], w=["Gs"])
                KX = ["XCH"]
                for d_ in range(3):
                    TT("dve", Gs2, Gs, sel5[:, d_, :].unsqueeze(2).to_broadcast([128, 8, 64]), ALU.mult, ["Gs", "sel5"] + KX, KX)
                    RED(Sd[:, d_, :], Gs2.rearrange("p r c -> p c r"), ALU.add, KX, KX)
                XV = lambda out, in0, in1, op: TT("dve", out, in0, in1, op, KX + ["car_r", "car_i"], KX)
                XV(car_r[:], Sd[:, 0, 0:32], Sd[:, 0, 0:32], ALU.bypass) if False else CP("dve", car_r[:], Sd[:, 0, 0:32], KX + ["car_r"], KX + ["car_r"])
                CP("dve", car_i[:], Sd[:, 0, 32:64], KX + ["car_i"], KX + ["car_i"])
                for d_ in (1, 2):
                    sr, si = Sd[:, d_, 0:32], Sd[:, d_, 32:64]
                    ar_, ai_ = A2r[:, d_ - 1, :], A2i[:, d_ - 1, :]
                    XV(xt_[0], sr, ar_, ALU.mult)
                    XV(xt_[1], si, ai_, ALU.mult)
                    XV(xt_[0], xt_[0], xt_[1], ALU.subtract)
                    TT("dve", car_r[:], car_r[:], xt_[0], ALU.add, KX + ["car_r"], KX + ["car_r"])
                    XV(xt_[2], sr, ai_, ALU.mult)
                    XV(xt_[3], si, ar_, ALU.mult)
                    XV(xt_[2], xt_[2], xt_[3], ALU.add)
                    TT("dve", car_i[:], car_i[:], xt_[2], ALU.add, KX + ["car_i"], KX + ["car_i"])
                P.barrier()
                continue
            own = pas == "full"
            if own:
                P.barrier()
            dma(u_raw, u_d[sc * 1024:(sc + 1) * 1024, :].rearrange("(m s) c -> m s c", s=8), w=["u_raw", "U"])
            src4 = u_raw.rearrange("p s (g j) -> p g s j", j=16)
            CP("dve", u_tm2[:, 0:32, :, :], src4[:, 0:32, :, :], ["u_raw"], ["u_tm2a"])
            CP("act", u_tm2[:, 32:64, :, :], src4[:, 32:64, :, :], ["u_raw"], ["u_tm2b"])
            u3 = u_tm2.rearrange("p g s j -> p g (s j)")
            for gb in range(8):
                tru = TRU2[gb % 2]
                tk = f"TRU{gb % 2}"
                for i in range(8):
                    g = gb * 8 + i
                    TR(tru[:, i * 128:(i + 1) * 128], u3[:, g, :], identb[:], ["u_tm2a", "u_tm2b", "identb"], [tk])
                CP("act", U[:, gb * 8:(gb + 1) * 8, :], tru.rearrange("p (g m) -> p g m", g=8),
                   [tk], ["U", "u_raw"])
            for b4 in range(8):
                pre, pim = PI2[b4 % 2]
                kre, kim = f"PIre{b4 % 2}", f"PIim{b4 % 2}"
                for i in range(4):
                    gp = b4 * 4 + i
                    for gh in range(2):
                        g = gh * 32 + gp
                        MM(pre[gh * 64:(gh + 1) * 64, i * 128:(i + 1) * 128], I2Sre[:, g, :], U[:, g, :], True, True, ["U", "I2Sre"], [kre])
                        MM(pim[gh * 64:(gh + 1) * 64, i * 128:(i + 1) * 128], I2Sim[:, g, :], U[:, g, :], True, True, ["U", "I2Sim"], [kim])
                CP("act", Incre[:, b4 * 4:(b4 + 1) * 4, :], pre.rearrange("p (g m) -> p g m", g=4), [kre], IRK)
                CP("dve", Incim[:, b4 * 4:(b4 + 1) * 4, :], pim.rearrange("p (g m) -> p g m", g=4), [kim], IIK)
            if not own:
                I4r = Incre.rearrange("p g (q r) -> p g q r", r=16)
                I4i = Incim.rearrange("p g (q r) -> p g q r", r=16)
                tir = TIre.unsqueeze(2).to_broadcast([128, 32, 8, 16])
                tii = TIim.unsqueeze(2).to_broadcast([128, 32, 8, 16])
                TT("dve", LA, I4r, tir, ALU.mult, IRK + ["TI"], ["LA"])
                TT("dve", LB, I4i, tii, ALU.mult, IIK + ["TI"], ["LB"])
                TT("dve", LA, LA, LB, ALU.subtract, ["LA", "LB"], ["LA"])
                RED(LS[0], LA, ALU.add, ["LA"], ["LS0"])
                TT("dve", LA, I4i, tir, ALU.mult, IIK + ["TI", "LS0"], ["LA"])
                TT("dve", LB, I4r, tii, ALU.mult, IRK + ["TI", "LA"], ["LB"])
                TT("dve", LA, LA, LB, ALU.add, ["LA", "LB"], ["LA"])
                RED(LS[1], LA, ALU.add, ["LA"], ["LS1"])
                TT("dve", LS[2], LS[0], TOre, ALU.mult, ["LS0", "TO"], ["LS2"])
                TT("dve", LS[3], LS[1], TOim, ALU.mult, ["LS1", "TO"], ["LS3"])
                TT("dve", LS[2], LS[2], LS[3], ALU.subtract, ["LS2", "LS3"], ["LS2"])
                RED(LE[0], LS[2], ALU.add, ["LS2"], ["LE0"])
                TT("dve", LS[2], LS[1], TOre, ALU.mult, ["LS1", "TO", "LE0"], ["LS2"])
                TT("dve", LS[3], LS[0], TOim, ALU.mult, ["LS0", "TO", "LS2"], ["LS3"])
                TT("dve", LS[2], LS[2], LS[3], ALU.add, ["LS2", "LS3"], ["LS2"])
                RED(LE[1], LS[2], ALU.add, ["LS2"], ["LE1"])
                TT("dve", LE[2], car_r[:], P128, ALU.mult, ["car_r", "P128"], ["LE2"])
                TT("dve", LE[3], car_i[:], Q128, ALU.mult, ["car_i", "Q128"], ["LE3"])
                TT("dve", LE[2], LE[2], LE[3], ALU.subtract, ["LE2", "LE3"], ["LE2"])
                TT("dve", LE[0], LE[0], LE[2], ALU.add, ["LE0", "LE2"], ["LE0"])
                TT("dve", LE[2], car_i[:], P128, ALU.mult, ["car_i", "P128", "LE0"], ["LE2"])
                TT("dve", LE[3], car_r[:], Q128, ALU.mult, ["car_r", "Q128", "LE0"], ["LE3"])
                TT("dve", LE[2], LE[2], LE[3], ALU.add, ["LE2", "LE3"], ["LE2"])
                TT("dve", car_i[:], LE[1], LE[2], ALU.add, ["LE1", "LE2"], ["car_i"])
                CP("dve", car_r[:], LE[0], ["LE0"], ["car_r"])
                continue
            CP("dve", Xre_bf[:, :, 0], car_r[:], ["car_r"], ["Xre0"])
            CP("dve", Xim_bf[:, :, 0], car_i[:], ["car_i"], ["Xim0"])
            for m in range(128):
                if m == 0:
                    xr, xi, kr, ki = car_r[:], car_i[:], "car_r", "car_i"
                else:
                    xr, xi, kr, ki = Incre[:, :, m - 1], Incim[:, :, m - 1], IRK[m - 1], IIK[m - 1]
                pp = (m % 2) * 6
                c0, c1, c2, c3, c4, c5 = ct[pp:pp + 6]
                n0 = [f"ct{pp + j}" for j in range(6)]
                TT("dve", c0, xr, P8, ALU.mult, [kr, "P8"], [n0[0]])
                TT("dve", c1, xi, Q8, ALU.mult, [ki, "Q8"], [n0[1]])
                TT("dve", c3, xi, P8, ALU.mult, [ki, "P8"], [n0[3]])
                TT("dve", c4, xr, Q8, ALU.mult, [kr, "Q8"], [n0[4]])
                TT("dve", c2, c0, c1, ALU.subtract, [n0[0], n0[1]], [n0[2]])
                TT("dve", c5, c3, c4, ALU.add, [n0[3], n0[4]], [n0[5]])
                TT("dve", Incre[:, :, m], c2, Incre[:, :, m], ALU.add, [n0[2], IRK[m]], [IRK[m]])
                TT("dve", Incim[:, :, m], c5, Incim[:, :, m], ALU.add, [n0[5], IIK[m]], [IIK[m]])
            CP("dve", car_r[:], Incre[:, :, 127], [IRK[127], "Xre0"], ["car_r"])
            CP("dve", car_i[:], Incim[:, :, 127], [IIK[127], "Xim0"], ["car_i"])
            CP("act", Xre_bf[:, :, 1:128], Incre[:, :, 0:127], IRK, ["Xre"])
            CP("dve", Xim_bf[:, :, 1:128], Incim[:, :, 0:127], IIK, ["Xim"])
            PYb = [PS[1], PS[2]]
            PYTb = [PS[3], PS[0]]

            def y_mm(gb):
                bq = gb % 2
                for i in range(8):
                    g = gb * 8 + i
                    gh, gp = g // 32, g % 32
                    pb = gh * 64
                    oy = PYb[bq][:, i * 128:(i + 1) * 128]
                    MM(oy, T0[:, g, :], U[:, g, :], True, False, ["T0", "U"], [f"PY{bq}"])
                    MM(oy, S2Ore[pb:pb + 64, gp, :], Xre_bf[pb:pb + 64, gp, :], False, False, ["S2O", "Xre", "Xre0"], [f"PY{bq}"])
                    MM(oy, S2Oim[pb:pb + 64, gp, :], Xim_bf[pb:pb + 64, gp, :], False, True, ["S2O", "Xim", "Xim0"], [f"PY{bq}"])

            def y_ew(gb):
                bq = gb % 2
                pyt = PYTb[bq]
                ky, kt_ = f"PY{bq}", f"PYT{bq}"
                CP("act", Ysb[bq].rearrange("p g m -> p (g m)"), PYb[bq], [ky], [f"Ysb{bq}"])
                for i in range(8):
                    TR(pyt[:, i * 128:(i + 1) * 128], Ysb[bq][:, i, :], identf[:], [f"Ysb{bq}", "identf"], [kt_])
                ACT(gt1[bq], pyt, AF.Square, [kt_], [f"gt1{bq}"])
                TS("dve", gt1[bq], gt1[bq], KG * 0.044715, KG, ALU.mult, ALU.add, [f"gt1{bq}"], [f"gt1{bq}"])
                TT("dve", gt1[bq], gt1[bq], pyt, ALU.mult, [f"gt1{bq}", kt_], [f"gt1{bq}"])
                ACT(gt2[bq], gt1[bq], AF.Sigmoid, [f"gt1{bq}"], [f"gt2{bq}"])
                TT("dve", yg_bf[:, :, gb * 128:(gb + 1) * 128].rearrange("p t (g j) -> p g t j", j=16),
                   pyt.rearrange("p (g t j) -> p g t j", g=8, t=8), gt2[bq].rearrange("p (g t j) -> p g t j", g=8, t=8),
                   ALU.mult, [kt_, f"gt2{bq}"], ["yg_bf"])

            y_mm(0)
            for gb in range(8):
                if gb + 1 < 8:
                    y_mm(gb + 1)
                y_ew(gb)
            P.barrier()
            oc = sc - NPS
            g5v = g5_d[oc * 1024:(oc + 1) * 1024, :].rearrange("(m s) c -> m s c", s=8)
            TRa = [PS[0][:, 0:512].bitcast(BF16), PS[0][:, 512:1024].bitcast(BF16)]
            TRs = [PS[3][:, 0:512].bitcast(BF16), PS[3][:, 512:1024].bitcast(BF16)]
            PZb = [PS[1], PS[2]]

            def tail1(s_):
                bq = s_ % 2
                dma(g5s[bq], g5v[:, s_, :], w=[f"g5s{bq}"])
                for kt in range(8):
                    TR(TRa[bq][:, kt * 128:(kt + 1) * 128], yg_bf[:, s_, kt * 128:(kt + 1) * 128], identb[:], ["yg_bf", "identb"], [f"TRa{bq}"])
                CP("act", ygT[bq].rearrange("p k m -> p (k m)"), TRa[bq], [f"TRa{bq}"], [f"ygT{bq}"])
                for cb in range(2):
                    pz = PZb[bq][:, cb * 512:(cb + 1) * 512]
                    for kt in range(8):
                        MM(pz, ygT[bq][:, kt, :], gluw[:, kt, cb * 512:(cb + 1) * 512], kt == 0, kt == 7, [f"ygT{bq}", "gluw"], [f"PZ{bq}"])

            def tail2(s_):
                bq = s_ % 2
                TT("dve", zb[bq], PZb[bq], glub, ALU.add, [f"PZ{bq}", "glub"], [f"zb{bq}"])
                ACT(gate[bq], zb[bq], AF.Sigmoid, [f"zb{bq}"], [f"gate{bq}"])
                TT("dve", tt_[bq], yg_bf[:, s_, :], gate[bq], ALU.mult, ["yg_bf", f"gate{bq}"], [f"tt_{bq}"])
                TT("pool", ssm[bq], tt_[bq], g5s[bq], ALU.mult, [f"tt_{bq}", f"g5s{bq}"], [f"ssm{bq}"])
                for kt in range(8):
                    TR(TRs[bq][:, kt * 128:(kt + 1) * 128], ssm[bq][:, kt * 128:(kt + 1) * 128], identb[:], [f"ssm{bq}", "identb"], [f"TRs{bq}"])
                CP("act", ssmT[bq].rearrange("p k m -> p (k m)"), TRs[bq], [f"TRs{bq}"], [f"ssmT{bq}"])
                dma(ssmT_d[oc, s_], ssmT[bq], r=[f"ssmT{bq}"])

            tail1(0)
            for s_ in range(8):
                if s_ + 1 < 8:
                    tail1(s_ + 1)
                tail2(s_)
        P.barrier()
        if stop_after == "SB":
            return finish(nc, P, st, out_d)

        AR.reset()
        Wqk = AR.take([128, 8, 1024], BF16)
        VO = AR.take([128, 8, 1024], BF16)
        wout = AR.take([128, 16, 1024], BF16)
        g2r = AR.take([128, 1024], F32)
        gfr = AR.take([128, 1024], F32)
        base_sc = AR.off
        wqT = AR.take([128, 8, 1024], BF16)
        wk = AR.take([128, 8, 1024], BF16)
        wv = AR.take([128, 8, 1024], BF16)
        wo = AR.take([128, 8, 1024], BF16)
        memT = AR.take([128, 8, 256], F32)
        msq = AR.take([128, 8, 256], BF16)
        mrs1 = AR.take([128, 256], F32)
        mrs2 = AR.take([128, 256], F32)
        mnT = AR.take([128, 8, 256], BF16)
        kaT = AR.take([128, 8, 256], BF16)
        vaT = AR.take([128, 8, 256], BF16)
        for t_, d_, k_ in ((wqT, wqT_d, "wqT"), (wk, wk_d, "wk"), (wv, wv_d, "wv"), (wo, wo_d, "wo")):
            for kt in range(8):
                dma(t_[:, kt, :], d_[:, kt, :], w=[k_], eng="pool")
        for kt in range(16):
            dma(wout[:, kt, :], wout_d[:, kt, :], w=["wout"], eng="pool")
        dma(g2r, g2_d, w=["g2r"])
        dma(gfr, gf_d, w=["gfr"])
        dma(memT, memT_d, w=["memT"])
        norm_fm(memT, 256, gm[:], msq, mrs1, mrs2, mnT, PS[0][:, 0:256], "memT", "msq", "mrs", "mnT", "PSm", "gm")
        for r_ in range(8):
            for wt, dst, nm, pi in ((wk, kaT, "kaT", 1), (wv, vaT, "vaT", 2)):
                pp_ = PS[pi][:, (r_ % 2) * 512:(r_ % 2) * 512 + 256]
                pk = f"PSkv{pi}_{r_ % 2}"
                for kt in range(8):
                    MM(pp_, wt[:, kt, r_ * 128:(r_ + 1) * 128], mnT[:, kt, :], kt == 0, kt == 7, [nm[:2] if False else ("wk" if pi == 1 else "wv"), "mnT"], [pk])
                CP("act" if pi == 1 else "dve", dst[:, r_, :], pp_, [pk], [nm])
        for dt_ in range(8):
            pw_ = PS[dt_ % 2]
            pk = f"PSw{dt_ % 2}"
            for h in range(4):
                for hf in range(2):
                    MM(pw_[:, h * 256:(h + 1) * 256], wqT[:, 2 * h + hf, dt_ * 128:(dt_ + 1) * 128], kaT[:, 2 * h + hf, :],
                       hf == 0, hf == 1, ["wqT", "kaT"], [pk])
            ACT(Wqk[:, dt_, :], pw_, AF.Copy, [pk], ["Wqk"], scale=1.0 / 16.0)
        for h in range(4):
            for mh in range(2):
                idx = 2 * h + mh
                pw_ = PS[2 + idx % 2]
                pk = f"PSv{idx % 2}"
                for cb in range(2):
                    for hf in range(2):
                        MM(pw_[:, cb * 512:(cb + 1) * 512], vaT[:, 2 * h + hf, mh * 128:(mh + 1) * 128], wo[:, 2 * h + hf, cb * 512:(cb + 1) * 512],
                           hf == 0, hf == 1, ["vaT", "wo"], [pk])
                CP("dve", VO[:, idx, :], pw_, [pk], ["VO"])
        P.barrier()

        AR.reset(base_sc)
        retT_sc = [AR.take([128, 8, 1024], BF16) for _ in range(2)]
        ssmT_s = [AR.take([128, 8, 128], BF16) for _ in range(3)]
        x_t = [AR.take([128, 1024], F32) for _ in range(3)]
        x1 = [AR.take([128, 1024], F32) for _ in range(3)]
        junk = AR.take([128, 1024], F32)
        h2 = [AR.take([128, 1024], BF16) for _ in range(2)]
        h2T = AR.take([128, 8, 128], BF16)
        Pex = AR.take([128, 4, 256], BF16)
        Pn = [AR.take([128, 4, 256], BF16) for _ in range(2)]
        PTt = AR.take([128, 8, 128], BF16)
        x2 = AR.take([128, 1024], F32)
        osb = [AR.take([128, 1024], F32) for _ in range(2)]
        sst = [AR.take([128, 32], F32) for _ in range(3)]
        TRc = PS[3][:, 0:512].bitcast(BF16)
        TRd = PS[3][:, 512:1024].bitcast(BF16)
        tiles = [(sco, s_) for sco in range(2) for s_ in range(8)]
        dma(retT_sc[0], retT_d[:, :, 0:1024], w=["retT_sc0"])
        dma(retT_sc[1], retT_d[:, :, 1024:2048], w=["retT_sc1"])

        def sc_load(t):
            sco, s_ = tiles[t]
            b3 = t % 3
            dma(ssmT_s[b3], ssmT_d[sco, s_], w=[f"ssmT_s{b3}"])
            dma(x_t[b3], xown_d[sco, :, s_, :], w=[f"x_t{b3}"])

        def sc_s1(t):
            sco, s_ = tiles[t]
            b3 = t % 3
            st_ = sst[t % 3]
            sk = f"sst{t % 3}"
            for cb in range(2):
                px = PS[0][:, cb * 512:(cb + 1) * 512]
                for kt in range(8):
                    MM(px, retT_sc[sco][:, kt, s_::8], wout[:, kt, cb * 512:(cb + 1) * 512], kt == 0, False, [f"retT_sc{sco}", "wout"], [f"PX{cb}"])
                for kt in range(8):
                    MM(px, ssmT_s[b3][:, kt, :], wout[:, 8 + kt, cb * 512:(cb + 1) * 512], False, kt == 7, [f"ssmT_s{b3}", "wout"], [f"PX{cb}"])
            TT("dve", x1[b3], PS[0], x_t[b3], ALU.add, ["PX0", "PX1", f"x_t{b3}"], [f"x1_{b3}"])
            ACT(junk, x1[b3], AF.Square, [f"x1_{b3}"], ["junk", sk + "a"], accum_out=st_[:, 0:1])
            ACT(st_[:, 1:2], st_[:, 0:1], AF.Ln, [sk + "a"], [sk + "b"], bias=epsc[:, 0:1], scale=1.0 / 1024.0)
            ACT(st_[:, 2:3], st_[:, 1:2], AF.Exp, [sk + "b"], [sk + "c"], scale=-0.5)
            STT(h2[t % 2], x1[b3], st_[:, 2:3], g2r, ALU.mult, ALU.mult, [f"x1_{b3}", sk + "c", "g2r"], [f"h2_{t % 2}"])

        def sc_s2(t):
            st_ = sst[t % 3]
            sk = f"sst{t % 3}"
            hh = h2[t % 2]
            for kt in range(8):
                TR(TRc[:, kt * 128:(kt + 1) * 128], hh[:, kt * 128:(kt + 1) * 128], identb[:], [f"h2_{t % 2}", "identb"], ["TRc"])
            CP("act", h2T.rearrange("p k t -> p (k t)"), TRc, ["TRc"], ["h2T"])
            for cb in range(2):
                pq = PS[1][:, cb * 512:(cb + 1) * 512]
                for kt in range(8):
                    MM(pq, h2T[:, kt, :], Wqk[:, kt, cb * 512:(cb + 1) * 512], kt == 0, kt == 7, ["h2T", "Wqk"], [f"PQ{cb}"])
            RED(st_[:, 4:8], PS[1].rearrange("p (h m) -> p h m", h=4), ALU.max, ["PQ0", "PQ1"], [sk + "mx"])
            TS("dve", st_[:, 8:12], st_[:, 4:8], -1.0, None, ALU.mult, None, [sk + "mx"], [sk + "nmx"])
            for h in range(4):
                ACT(Pex[:, h, :], PS[1][:, h * 256:(h + 1) * 256], AF.Exp, ["PQ0", "PQ1", sk + "nmx"], ["Pex", sk + f"sm{h}"],
                    bias=st_[:, 8 + h:9 + h], scale=1.0, accum_out=st_[:, 12 + h:13 + h])
            RECIP(st_[:, 16:20], st_[:, 12:16], [sk + f"sm{h}" for h in range(4)], [sk + "rsm"])
            TT("pool", Pn[t % 2], Pex, st_[:, 16:20].unsqueeze(2).to_broadcast([128, 4, 256]), ALU.mult, ["Pex", sk + "rsm"], [f"Pn{t % 2}"])

        def sc_s3(t):
            sco, s_ = tiles[t]
            b3 = t % 3
            st_ = sst[t % 3]
            sk = f"sst{t % 3}"
            b = t % 2
            Pn2 = Pn[b].rearrange("p h m -> p (h m)")
            for idx in range(8):
                TR(TRd[:, idx * 128:(idx + 1) * 128], Pn2[:, idx * 128:(idx + 1) * 128], identb[:], [f"Pn{b}", "identb"], ["TRd"])
            CP("act", PTt.rearrange("p k t -> p (k t)"), TRd, ["TRd"], ["PTt"])
            for cb in range(2):
                po = PS[2][:, cb * 512:(cb + 1) * 512]
                for idx in range(8):
                    MM(po, PTt[:, idx, :], VO[:, idx, cb * 512:(cb + 1) * 512], idx == 0, idx == 7, ["PTt", "VO"], [f"PO{cb}"])
            TT("dve", x2, PS[2], x1[b3], ALU.add, ["PO0", "PO1", f"x1_{b3}"], ["x2"])
            ACT(junk, x2, AF.Square, ["x2"], ["junk", sk + "d"], accum_out=st_[:, 20:21])
            ACT(st_[:, 21:22], st_[:, 20:21], AF.Ln, [sk + "d"], [sk + "e"], bias=epsc[:, 0:1], scale=1.0 / 1024.0)
            ACT(st_[:, 22:23], st_[:, 21:22], AF.Exp, [sk + "e"], [sk + "f"], scale=-0.5)
            STT(osb[b], x2, st_[:, 22:23], gfr, ALU.mult, ALU.mult, ["x2", sk + "f", "gfr"], [f"osb{b}"])
            dma(out_d[sco, :, s_, :], osb[b], r=[f"osb{b}"])

        NTL = len(tiles)
        sc_load(0)
        for i in range(NTL + 2):
            if i + 1 < NTL:
                sc_load(i + 1)
            if i < NTL:
                sc_s1(i)
            if 0 <= i - 2 < NTL:
                sc_s3(i - 2)
            if 0 <= i - 1 < NTL:
                sc_s2(i - 1)
        return finish(nc, P, st, out_d)


def finish(nc, P, st, out_d):
    P.barrier()
    P.op("sp", None)
    P.emit(st)
    return nc


def _const_tables():
    H = 8
    log_g = np.log1p(-np.exp2(-5.0 - np.arange(H, dtype=np.float64)))
    i = np.arange(128, dtype=np.float64)
    sqk = np.zeros((128, 16), np.float64)
    sqk[:, 0:8] = np.exp(log_g[None, :] * (i[:, None] + 1.0))
    sqk[:, 8:16] = np.exp(-log_g[None, :] * (i[:, None] + 1.0)) * 0.125
    cdt = np.broadcast_to(np.exp(log_g * 128.0)[None, :], (128, 8)).copy()
    j = np.arange(128)
    maskT = (j[None, :] >= j[:, None]).astype(np.float32)
    s_idx = np.arange(128) // 16
    t0mask = (s_idx[None, :] >= s_idx[:, None]).astype(np.float32)
    invf = (10000.0 ** (-np.arange(32, dtype=np.float32) / np.float32(32))).astype(np.float32)
    invf = np.broadcast_to(invf[None, :], (128, 32)).copy()
    return (sqk.astype(np.float32), cdt.astype(np.float32), maskT, t0mask, invf, np.eye(128, dtype=np.float32))


def _kt(w):
    K, N = w.shape
    return np.ascontiguousarray(w.reshape(K // 128, 128, N).transpose(1, 0, 2))


def make_in_maps(inputs, mode=None):
    mode = mode or MODE
    NPRE_T = 6144 if mode == "prefix" else 0
    NT = NPRE_T + 2048
    f = lambda a: np.asarray(a, dtype=np.float32)
    x = f(inputs["x"])
    mem = f(inputs["mem"])
    pos = np.asarray(inputs["positions"]).astype(np.int32)
    sqk, cdt, maskT, t0mask, invf, ident = _const_tables()
    rep = lambda v: np.ascontiguousarray(np.broadcast_to(f(v).reshape(1, -1), (128, f(v).size)))
    cols = lambda v: np.ascontiguousarray(f(v).reshape(8, 128).T)

    def s5lay(a):
        a = f(a)
        rest = a.shape[2:]
        a = a.reshape((2, 32, 64) + rest)
        a = np.moveaxis(a, 2, 1)
        return np.ascontiguousarray(a.reshape((128, 32) + rest))

    a_re = s5lay(inputs["s5_a_re"][0])
    a_im = s5lay(inputs["s5_a_im"][0])
    ldt = s5lay(np.broadcast_to(f(inputs["s5_log_dt"][0])[:, None], (64, 64)))
    b_re = s5lay(inputs["s5_b_re"][0])
    b_im = s5lay(inputs["s5_b_im"][0])
    c_re = s5lay(np.transpose(f(inputs["s5_c_re"][0]), (0, 2, 1)))
    c_im = s5lay(np.transpose(f(inputs["s5_c_im"][0]), (0, 2, 1)))
    shared = dict(
        invf=invf, sqk=sqk, cdt=cdt, maskT=maskT, t0mask=t0mask, ident=ident,
        w_in=_kt(f(inputs["w_in"][0])), g1=cols(inputs["norm1_g"][0]), gng=rep(inputs["ret_gn_g"][0]),
        s5are=a_re, s5aim=a_im, s5ldt=ldt, s5bre=b_re, s5bim=b_im, s5cre=c_re, s5cim=c_im,
        dcol=np.ascontiguousarray(f(inputs["s5_d"][0]).reshape(64, 16).T[np.arange(128) % 16, :]), glub=rep(inputs["s5_glu_b"][0]), g2rep=rep(inputs["norm2_g"][0]),
        gfrep=rep(inputs["norm_f_g"]), gm=cols(inputs["norm_mem_g"][0]),
        gluw=_kt(f(inputs["s5_glu_w"][0])), wout=_kt(f(inputs["w_out"][0])),
        wqT=_kt(np.ascontiguousarray(f(inputs["xa_wq"][0]).T)), wk=_kt(f(inputs["xa_wk"][0])),
        wv=_kt(f(inputs["xa_wv"][0])), wo=_kt(f(inputs["xa_wo"][0])),
    )
    log_g = np.log1p(-np.exp2(-5.0 - np.arange(8, dtype=np.float64)))
    maps = []
    for core in range(8):
        b, j = core // 4, core % 4
        m = dict(shared)
        t1 = (j + 1) * 2048
        t0 = t1 - NT
        lo = max(t0, 0)
        xl = np.zeros((NT, 1024), np.float32)
        pl = np.zeros((NT,), np.int32)
        xl[lo - t0:, :] = x[b, lo:t1, :]
        pl[lo - t0:] = pos[b, lo:t1]
        m["xT"] = _kt(np.ascontiguousarray(xl.T))
        m["xown"] = np.ascontiguousarray(x[b, j * 2048:(j + 1) * 2048, :].reshape(2, 128, 8, 1024))
        m["pos"] = np.ascontiguousarray(pl.reshape(NT // 128, 128).T)
        m["memT"] = _kt(np.ascontiguousarray(mem[b].T))
        wtr = np.zeros((128, 8, 4), np.float64)
        sel5 = np.zeros((128, 3, 8), np.float32)
        for rank in range(8):
            rb, ri = rank // 4, rank % 4
            if rb == b and ri < j:
                dist = j - ri
                sel5[:, dist - 1, rank] = 1.0
                for blk in range(4):
                    for par in range(2):
                        h = 2 * blk + par
                        wtr[par * 64:(par + 1) * 64, rank, blk] = np.exp(log_g[h] * 2048.0 * (dist - 1))
        m["wtr"] = wtr.astype(np.float32)
        m["sel5"] = sel5
        maps.append(m)
    return maps


def kernel(**inputs):
    nc = build_program()
    maps = make_in_maps(inputs)
    res = run_bass_kernel_spmd(nc, maps, core_ids=list(range(8)))
    out = np.zeros((2, 8192, 1024), np.float32)
    for core in range(8):
        b, j = core // 4, core % 4
        out[b, j * 2048:(j + 1) * 2048, :] = np.asarray(res.results[core]["out"]).reshape(2048, 1024)
    return out
```

```python
import os
import math
from contextlib import ExitStack
import numpy as np
import concourse.bass as bass
import concourse.mybir as mybir
from concourse.bass_utils import run_bass_kernel_spmd

F32 = mybir.dt.float32
BF16 = mybir.dt.bfloat16
I32 = mybir.dt.int32
ALU = mybir.AluOpType
AF = mybir.ActivationFunctionType
AX = mybir.AxisListType

SEG = 6000
NDMA = 24
EPS = 1e-6
NOWN_BLK = 8
MODE = "prefix"
TWO_PI = 2.0 * math.pi
KG = 2.0 * math.sqrt(2.0 / math.pi)


class Prog:
    ENGS = ("pe", "act", "dve", "pool", "sp")

    def __init__(self, nc):
        self.nc = nc
        self.ops = []
        self.last_w = {}
        self.readers = {}
        self.pending_barrier = {e: set() for e in self.ENGS}
        self._bar_at = 0

    def op(self, eng, fn, r=(), w=(), dma=False):
        oid = len(self.ops)
        deps = set()
        for k in r:
            if k in self.last_w:
                deps.add(self.last_w[k])
        for k in w:
            if k in self.last_w:
                deps.add(self.last_w[k])
            for rd in self.readers.get(k, ()):
                deps.add(rd)
        deps |= self.pending_barrier[eng]
        self.pending_barrier[eng] = set()
        deps.discard(oid)
        last_per_eng = {}
        keep = set()
        for d_ in deps:
            od = self.ops[d_]
            if od["dma"]:
                keep.add(d_)
            else:
                e_ = od["eng"]
                if e_ not in last_per_eng or d_ > last_per_eng[e_]:
                    last_per_eng[e_] = d_
        deps = keep | set(last_per_eng.values())
        self.ops.append(dict(eng=eng, fn=fn, deps=deps, dma=dma))
        for k in r:
            self.readers.setdefault(k, []).append(oid)
        for k in w:
            self.last_w[k] = oid
            self.readers[k] = []
        return oid

    def dma(self, fn, r=(), w=(), eng="sp"):
        return self.op(eng, fn, r, w, dma=True)

    def barrier(self):
        last = {}
        for i, o in enumerate(self.ops):
            last[o["eng"]] = i
        dm = [i for i in range(self._bar_at, len(self.ops)) if self.ops[i]["dma"]]
        s = set(last.values()) | set(dm)
        for e in self.ENGS:
            self.pending_barrier[e] |= s
        self._bar_at = len(self.ops)

    def emit(self, stack):
        nc = self.nc
        ops = self.ops
        n = len(ops)
        needed = [False] * n
        for i, o in enumerate(ops):
            nd = set()
            for d in o["deps"]:
                od = ops[d]
                if od["eng"] == "pe" and o["eng"] == "pe":
                    continue
                if od["eng"] == o["eng"] and not od["dma"] and od["fn"] is None:
                    continue
                nd.add(d)
            o["deps"] = nd
            for d in nd:
                needed[d] = True
        cnt = {e: 0 for e in self.ENGS}
        nseg = {e: 0 for e in self.ENGS}
        ndma = {"sp": 0, "pool": 0}
        dma_ops = {"sp": [], "pool": []}
        for i, o in enumerate(ops):
            if o["dma"]:
                q = o["eng"]
                nd = ndma[q]
                o["sig"] = ("dma_" + q, nd % NDMA, 16 * (nd // NDMA + 1), nd)
                dma_ops[q].append(o)
                ndma[q] += 1
            elif needed[i]:
                e = o["eng"]
                c = cnt[e]
                o["sig"] = (e, c // SEG, c % SEG + 1)
                cnt[e] += 1
                nseg[e] = c // SEG + 1
            else:
                o["sig"] = None
        sems = {}
        for e in self.ENGS:
            for s in range(nseg[e]):
                sems[(e, s)] = stack.enter_context(nc.semaphore(f"s_{e}_{s}"))
        for q in ("sp", "pool"):
            for k in range(min(NDMA, ndma[q])):
                sems[("dma_" + q, k)] = stack.enter_context(nc.semaphore(f"s_dma_{q}_{k}"))
        block = stack.enter_context(nc.Block())

        def make(ename):
            def body(eng):
                waited = {}
                for i, o in enumerate(ops):
                    if o["eng"] != ename:
                        continue
                    req = {}
                    for d in o["deps"]:
                        sg = ops[d]["sig"]
                        key = (sg[0], sg[1])
                        req[key] = max(req.get(key, 0), sg[2])
                    if o["dma"]:
                        idx = o["sig"][3]
                        if idx >= NDMA:
                            prev = dma_ops[o["eng"]][idx - NDMA]["sig"]
                            key = (prev[0], prev[1])
                            req[key] = max(req.get(key, 0), prev[2])
                    for key in sorted(req.keys(), key=lambda k: (k[0], k[1])):
                        val = req[key]
                        if waited.get(key, 0) >= val:
                            continue
                        if not key[0].startswith("dma"):
                            if any(k[0] == key[0] and k[1] > key[1] for k in waited):
                                continue
                        eng.wait_ge(sems[key], val)
                        waited[key] = val
                    if o["fn"] is None:
                        continue
                    ins = o["fn"](eng)
                    sg = o["sig"]
                    if sg is not None:
                        if sg[0].startswith("dma"):
                            ins.then_inc(sems[(sg[0], sg[1])], 16)
                        else:
                            ins.then_inc(sems[(sg[0], sg[1])], 1)
            return body

        block.tensor(make("pe"))
        block.scalar(make("act"))
        block.vector(make("dve"))
        block.gpsimd(make("pool"))
        block.sync(make("sp"))


class Arena:
    def __init__(self, t, nwords):
        self.t = t
        self.n = nwords
        self.off = 0

    def reset(self, off=0):
        self.off = off

    def take(self, shape, dtype):
        nel = 1
        for s in shape[1:]:
            nel *= s
        sz = 2 if dtype == BF16 else 4
        nw = (nel * sz + 3) // 4
        nw = (nw + 7) // 8 * 8
        assert self.off + nw <= self.n, f"arena overflow {self.off}+{nw}>{self.n}"
        ap = self.t[:, self.off:self.off + nw]
        self.off += nw
        if dtype != F32:
            ap = ap.bitcast(dtype)
        ap = ap[:, 0:nel]
        fd = shape[1:]
        if len(fd) == 2:
            ap = ap.rearrange("p (a b) -> p a b", a=fd[0])
        elif len(fd) == 3:
            ap = ap.rearrange("p (a b c) -> p a b c", a=fd[0], b=fd[1])
        elif len(fd) == 4:
            ap = ap.rearrange("p (a b c d) -> p a b c d", a=fd[0], b=fd[1], c=fd[2])
        return ap


def build_program(dbg=None, stop_after=None, mode=None):
    mode = mode or MODE
    NPRE = 24 if mode == "prefix" else 0
    NT = NPRE * 256 + 2048
    NCH = NT // 128
    FAKE_CC = isinstance(dbg, dict) and bool(dbg.get("fake_cc"))
    nc = bass.Bass("TRN2", target_bir_lowering=False)

    def din(name, shape, dt=F32):
        return nc.dram_tensor(name, list(shape), dt, kind="ExternalInput").ap()

    xT_d = din("xT", [128, 8, NT])
    xown_d = din("xown", [2, 128, 8, 1024])
    pos_d = din("pos", [128, NCH], I32)
    invf_d = din("invf", [128, 32])
    sqk_d = din("sqk", [128, 16])
    cdt_d = din("cdt", [128, 8])
    wtr_d = din("wtr", [128, 8, 4])
    sel5_d = din("sel5", [128, 3, 8])
    maskT_d = din("maskT", [128, 128])
    t0mask_d = din("t0mask", [128, 128])
    ident_d = din("ident", [128, 128])
    w_in_d = din("w_in", [128, 8, 5120])
    g1_d = din("g1", [128, 8])
    gng_d = din("gng", [128, 1024])
    are_d = din("s5are", [128, 32])
    aim_d = din("s5aim", [128, 32])
    ldt_d = din("s5ldt", [128, 32])
    bre_d = din("s5bre", [128, 32, 16])
    bim_d = din("s5bim", [128, 32, 16])
    cre_d = din("s5cre", [128, 32, 16])
    cim_d = din("s5cim", [128, 32, 16])
    dcol_d = din("dcol", [128, 64])
    glub_d = din("glub", [128, 1024])
    g2_d = din("g2rep", [128, 1024])
    gf_d = din("gfrep", [128, 1024])
    gm_d = din("gm", [128, 8])
    gluw_d = din("gluw", [128, 8, 1024])
    wout_d = din("wout", [128, 16, 1024])
    wqT_d = din("wqT", [128, 8, 1024])
    wk_d = din("wk", [128, 8, 1024])
    wv_d = din("wv", [128, 8, 1024])
    wo_d = din("wo", [128, 8, 1024])
    memT_d = din("memT", [128, 8, 256])
    out_d = nc.dram_tensor("out", [2, 128, 8, 1024], F32, kind="ExternalOutput").ap()

    skind = dict(kind="ExternalOutput") if dbg else {}
    u_d = nc.dram_tensor("u_scr", [NT, 1024], BF16, **skind).ap()
    ccR_in = nc.dram_tensor("ccR_in", [128, 512], F32)
    ccR_out = nc.dram_tensor("ccR_out", [1024, 512], F32)
    ccS_in = nc.dram_tensor("ccS_in", [128, 64], F32)
    ccS_out = nc.dram_tensor("ccS_out", [1024, 64], F32)
    g5_d = nc.dram_tensor("g5_scr", [2048, 1024], BF16, **skind).ap()
    retT_d = nc.dram_tensor("retT_scr", [128, 8, 2048], BF16, **skind).ap()
    ssmT_d = nc.dram_tensor("ssmT_scr", [2, 8, 128, 8, 128], BF16, **skind).ap()

    with ExitStack() as st:
        P = Prog(nc)
        sb = lambda name, shape, dt=F32: st.enter_context(nc.sbuf_tensor("sb_" + name, list(shape), dt))
        ARW = (dbg.get("arw") if isinstance(dbg, dict) and dbg.get("arw") else None) or (49200 if mode == "prefix" else 47200)
        arena_t = sb("arena", [128, ARW])
        AR = Arena(arena_t, ARW)
        identf = sb("identf", [128, 128])
        identb = sb("identb", [128, 128], BF16)
        ones_f = sb("ones_f", [128, 128])
        ones_b = sb("ones_b", [128, 128], BF16)
        epsc = sb("epsc", [128, 1])
        maskT = sb("maskT", [128, 128])
        t0mask = sb("t0mask", [128, 128])
        invf = sb("invf", [128, 32])
        sqk = sb("sqk", [128, 16])
        cdt = sb("cdt", [128, 8])
        wtr = sb("wtr", [128, 8, 4])
        sel5 = sb("sel5", [128, 3, 8])
        A2r = sb("A2r", [128, 2, 32])
        A2i = sb("A2i", [128, 2, 32])
        g1 = sb("g1", [128, 8])
        gm = sb("gm", [128, 8])
        Rf = sb("Rf", [128, 8, 128])
        Rbf = sb("Rbf", [128, 8, 128], BF16)
        car_r = sb("car_r", [128, 32])
        car_i = sb("car_i", [128, 32])
        PS = [st.enter_context(nc.psum_tensor(f"PS{i}", [128, 1024], F32))[:] for i in range(4)]

        def dma(out, in_, r=(), w=(), eng="sp"):
            P.dma(lambda e, out=out, in_=in_: e.dma_start(out=out, in_=in_), r=r, w=w, eng=eng)

        def TT(eng, out, in0, in1, op, r, w):
            P.op(eng, lambda e, out=out, in0=in0, in1=in1, op=op: e.tensor_tensor(out=out, in0=in0, in1=in1, op=op), r=r, w=w)

        def TS(eng, out, in0, s1, s2, op0, op1, r, w):
            if op1 is None:
                P.op(eng, lambda e, out=out, in0=in0, s1=s1, op0=op0: e.tensor_scalar(out=out, in0=in0, scalar1=s1, scalar2=None, op0=op0), r=r, w=w)
            else:
                P.op(eng, lambda e, out=out, in0=in0, s1=s1, s2=s2, op0=op0, op1=op1: e.tensor_scalar(out=out, in0=in0, scalar1=s1, scalar2=s2, op0=op0, op1=op1), r=r, w=w)

        def STT(out, in0, scalar, in1, op0, op1, r, w):
            P.op("dve", lambda e, out=out, in0=in0, scalar=scalar, in1=in1, op0=op0, op1=op1: e.scalar_tensor_tensor(out=out, in0=in0, scalar=scalar, in1=in1, op0=op0, op1=op1), r=r, w=w)

        def ACT(out, in_, func, r, w, **kw):
            P.op("act", lambda e, out=out, in_=in_, func=func, kw=kw: e.activation(out=out, in_=in_, func=func, **kw), r=r, w=w)

        def CP(eng, out, in_, r, w):
            if eng == "act":
                ACT(out, in_, AF.Copy, r, w)
            else:
                P.op(eng, lambda e, out=out, in_=in_: e.tensor_copy(out=out, in_=in_), r=r, w=w)

        def RECIP(out, in_, r, w):
            P.op("dve", lambda e, out=out, in_=in_: e.reciprocal(out=out, in_=in_), r=r, w=w)

        def RED(out, in_, op, r, w, axis=AX.X):
            P.op("dve", lambda e, out=out, in_=in_, op=op, axis=axis: e.tensor_reduce(out=out, in_=in_, axis=axis, op=op), r=r, w=w)

        def MM(out, lhsT, rhs, start, stop, r, w):
            P.op("pe", lambda e, out=out, lhsT=lhsT, rhs=rhs, start=start, stop=stop: e.matmul(out, lhsT=lhsT, rhs=rhs, start=start, stop=stop), r=r, w=w)

        def TR(out, in_, ident, r, w):
            P.op("pe", lambda e, out=out, in_=in_, ident=ident: e.transpose(out=out, in_=in_, identity=ident), r=r, w=w)

        def MEMSET(eng, ap, val, w):
            P.op(eng, lambda e, ap=ap, val=val: e.memset(ap, val), w=w)

        dma(identf[:], ident_d, w=["identf"])
        dma(maskT[:], maskT_d, w=["maskT"])
        dma(t0mask[:], t0mask_d, w=["t0mask"])
        dma(invf[:], invf_d, w=["invf"])
        dma(sqk[:], sqk_d, w=["sqk"])
        dma(cdt[:], cdt_d, w=["cdt"])
        dma(wtr[:], wtr_d, w=["wtr"])
        dma(sel5[:], sel5_d, w=["sel5"])
        dma(g1[:], g1_d, w=["g1"])
        dma(gm[:], gm_d, w=["gm"])
        CP("dve", identb[:], identf[:], ["identf"], ["identb"])
        MEMSET("pool", ones_f[:], 1.0, ["ones_f"])
        MEMSET("pool", ones_b[:], 1.0, ["ones_b"])
        MEMSET("pool", epsc[:], EPS, ["epsc"])
        MEMSET("pool", Rf[:], 0.0, ["Rf"])
        MEMSET("pool", Rbf[:], 0.0, ["Rbf"])
        MEMSET("pool", car_r[:], 0.0, ["car_r"])
        MEMSET("pool", car_i[:], 0.0, ["car_i"])

        def sincos(ang, shp, s_out, c_out, tmps, key):
            kf, ki, rr, mm = tmps
            rk = [key]
            TS("dve", kf, ang, 1.0 / TWO_PI, None, ALU.mult, None, rk, [key + "kf"])
            CP("dve", ki, kf, [key + "kf"], [key + "ki"])
            CP("dve", kf, ki, [key + "ki"], [key + "kf"])
            STT(rr, kf, -6.28125, ang, ALU.mult, ALU.add, [key + "kf"] + rk, [key + "rr"])
            STT(rr, kf, -(TWO_PI - 6.28125), rr, ALU.mult, ALU.add, [key + "kf", key + "rr"], [key + "rr"])

            def wrap(t):
                TS("dve", mm, t, -math.pi, None, ALU.is_lt, None, [key + "rr"], [key + "mm"])
                STT(t, mm, TWO_PI, t, ALU.mult, ALU.add, [key + "mm", key + "rr"], [key + "rr"])
                TS("dve", mm, t, math.pi, None, ALU.is_gt, None, [key + "rr"], [key + "mm"])
                STT(t, mm, -TWO_PI, t, ALU.mult, ALU.add, [key + "mm", key + "rr"], [key + "rr"])
            wrap(rr)
            ACT(s_out, rr, AF.Sin, [key + "rr"], [key + "s"])
            TS("dve", rr, rr, math.pi / 2.0, None, ALU.add, None, [key + "rr", key + "s"], [key + "rr"])
            wrap(rr)
            ACT(c_out, rr, AF.Sin, [key + "rr"], [key + "c"])

        def norm_fm(xt, N, gcol, sq, rs1, rs2, hT, psum_ap, kx, ksq, krs, kh, kps, gkey):
            ACT(sq, xt, AF.Square, [kx], [ksq])
            for kt in range(8):
                MM(psum_ap, ones_b[:], sq[:, kt, :], kt == 0, kt == 7, [ksq, "ones_b"], [kps])
            ACT(rs1, psum_ap, AF.Sqrt, [kps], [krs + "1"], bias=EPS, scale=1.0 / 1024.0)
            RECIP(rs2, rs1, [krs + "1"], [krs])
            for kt in range(8):
                STT(hT[:, kt, :], xt[:, kt, :], gcol[:, kt:kt + 1], rs2, ALU.mult, ALU.mult, [kx, krs, gkey], [kh])

        AR.reset()
        w_in = AR.take([128, 8, 5120], BF16)
        cosT = AR.take([128, NCH, 32], F32)
        sinT = AR.take([128, NCH, 32], F32)
        gng = AR.take([128, 1024], F32)
        xTb = [AR.take([128, 8, 256], F32) for _ in range(2)]
        sq = AR.take([128, 8, 256], BF16)
        posi = AR.take([128, NCH], I32)
        posf = AR.take([128, NCH], F32)
        Rc = AR.take([128, 4, 128], F32)
        mark = AR.off
        tki = AR.take([128, NCH * 32], I32)
        tmm = AR.take([128, NCH * 32], F32)
        trr_t = AR.take([128, NCH * 32], F32)
        AR.reset(mark)
        Gt = AR.take([128, 8, 4, 128], F32)
        AR.reset(mark)
        hTb = [AR.take([128, 8, 256], BF16) for _ in range(2)]
        rs1 = AR.take([128, 256], F32)
        rs2 = AR.take([128, 256], F32)
        qk = AR.take([128, 16, 2, 32], F32)
        rt = [AR.take([128, 16, 32], F32) for _ in range(4)]
        qkt = [AR.take([128, 16, 64], BF16) for _ in range(2)]
        v_bf = [AR.take([128, 1024], BF16) for _ in range(2)]
        gr_bf = [AR.take([128, 1024], BF16) for _ in range(2)]
        g5_bf = [AR.take([128, 1024], BF16) for _ in range(2)]
        u_bf = [AR.take([128, 1024], BF16) for _ in range(2)]
        kT = [AR.take([128, 4, 128], BF16) for _ in range(2)]
        qm = [AR.take([128, 8, 128], BF16) for _ in range(2)]
        PT = AR.take([128, 8, 128], BF16)
        o_sb = AR.take([128, 8, 128], F32)
        osq = AR.take([128, 8, 128], F32)
        ret = [AR.take([128, 1024], BF16) for _ in range(2)]
        retT_blk = AR.take([128, 8, 256], BF16)
        stt = AR.take([128, 48], F32)
        ang = xTb[0].rearrange("p a b -> p (a b)")[:, 0:NCH * 32]
        tkf = xTb[1].rearrange("p a b -> p (a b)")[:, 0:NCH * 32]
        trr = trr_t[:, 0:NCH * 32]

        dma(gng, gng_d, w=["gng"])
        MEMSET("pool", qm[0], 0.0, ["qm0"])
        MEMSET("pool", qm[1], 0.0, ["qm1"])
        dma(posi, pos_d, w=["posi"])
        CP("dve", posf, posi, ["posi"], ["posf"])
        TT("dve", ang.rearrange("p (c i) -> p c i", i=32), posf.unsqueeze(2).to_broadcast([128, NCH, 32]),
           invf[:].unsqueeze(1).to_broadcast([128, NCH, 32]), ALU.mult, ["posf", "invf", "xT0"], ["xT0"])
        sincos(ang, None, sinT.rearrange("p c i -> p (c i)"), cosT.rearrange("p c i -> p (c i)"),
               (tkf, tki, trr, tmm), "xT0")
        P.barrier()
        for cb in (1, 2, 3, 6, 7, 0, 4, 5, 8, 9):
            cs_ = slice(cb * 512, (cb + 1) * 512)
            dma(w_in[:, :, cs_], w_in_d[:, :, cs_], w=[f"w_in{cb}"], eng="pool")
            TT("dve", w_in[:, :, cs_], w_in[:, :, cs_], g1[:].unsqueeze(2).to_broadcast([128, 8, 512]), ALU.mult, [f"w_in{cb}", "g1"], [f"w_in{cb}"])

        qk2 = qk.rearrange("p h t i -> p (h t i)")
        qkt2 = [q_.rearrange("p h d -> p (h d)") for q_ in qkt]
        o2 = o_sb.rearrange("p h e -> p (h e)")
        PSUM_R2 = (mode == "coll") and not (isinstance(dbg, dict) and dbg.get("psum_new"))
        if PSUM_R2:
            KVp = PS[3].rearrange("p (h e) -> p h e", h=8)
            TRb = PS[0][:, 512:1024].bitcast(BF16)
            kvi = lambda h: h
            kvs = lambda par: slice(par, 8, 2)
        else:
            KVp = PS[3][:, 0:512].rearrange("p (b e) -> p b e", b=4)
            TRb = PS[3][:, 512:1024].bitcast(BF16)
            kvi = lambda h: h // 2
            kvs = lambda par: slice(0, 4, 1)
        SCp = PS[2][:, 0:512]
        Op = PS[2][:, 512:1024]
        pj_i = [0]

        pj_n = [4]

        def pj():
            if PSUM_R2:
                return PS[0][:, 0:512], "PJ0"
            i = pj_i[0] % pj_n[0]
            pj_i[0] += 1
            return PS[i // 2][:, (i % 2) * 512:(i % 2 + 1) * 512], f"PJ{i}"

        nblk = NOWN_BLK
        LVL = dbg.get('own_lvl', 9) if isinstance(dbg, dict) else 9
        if mode == "coll":
            sa_iter = [("light", b_) for b_ in range(nblk)] + [("xchg", -1)] + [("full", b_) for b_ in range(nblk)]
            light_cbs = [1, 2, 3]
        else:
            sa_iter = [("light", b_) for b_ in range(NPRE)] + [("full", NPRE + b_) for b_ in range(nblk)]
            light_cbs = [1, 2, 3, 6, 7]
        def do_xchg():
            P.barrier()
            dma(ccR_in.ap()[0:64, :].rearrange("p (b e) -> p b e", b=4), Rf[0:64, 0::2, :], r=["Rf"], w=["ccR_in"])
            dma(ccR_in.ap()[64:128, :].rearrange("p (b e) -> p b e", b=4), Rf[64:128, 1::2, :], r=["Rf"], w=["ccR_in"])
            if FAKE_CC:
                dma(ccR_out.ap()[0:128, :], ccR_in.ap(), r=["ccR_in"], w=["ccR_out"])
            else:
                P.op("pool", lambda e: e.collective_compute("AllGather", ALU.bypass, replica_groups=[list(range(8))],
                                                            ins=[ccR_in.ap().opt()], outs=[ccR_out.ap().opt()]),
                     r=["ccR_in"], w=["ccR_out"])
            dma(Gt.rearrange("p r b e -> p r (b e)"), ccR_out.ap().rearrange("(r p) c -> p r c", p=128), r=["ccR_out"], w=["Gt"])
            TT("dve", Gt, Gt, wtr[:].unsqueeze(3).to_broadcast([128, 8, 4, 128]), ALU.mult, ["Gt", "wtr"], ["Gt"])
            RED(Rc, Gt.rearrange("p r b e -> p b e r"), ALU.add, ["Gt"], ["Rc"])
            CP("act", Rf[0:64, 0::2, :], Rc[0:64, :, :], ["Rc", "Rf"], ["Rf"])
            CP("act", Rf[64:128, 1::2, :], Rc[64:128, :, :], ["Rc", "Rf"], ["Rf"])
            CP("pool", Rbf[:], Rf[:], ["Rf"], ["Rbf"])
            P.barrier()


        def load_block(blk):
            dma(xTb[blk % 2], xT_d[:, :, blk * 256:(blk + 1) * 256], w=[f"xT{blk % 2}"])

        def norm_block(blk):
            xb = xTb[blk % 2]
            kx = f"xT{blk % 2}"
            bb = blk % 2
            ACT(sq, xb, AF.Square, [kx], ["sq"])
            ACT(hTb[bb], xb, AF.Copy, [kx], [f"hT{bb}"])
            pa, pk = pj()
            for ci in range(2):
                for kt in range(8):
                    MM(pa[:, ci:ci + 1], sq[:, kt, ci * 128:(ci + 1) * 128], ones_b[:, 0:1], kt == 0, kt == 7, ["sq", "ones_b"], [pk])
            ACT(rs1[:, 2 * bb:2 * bb + 2], pa[:, 0:2], AF.Sqrt, [pk], [f"rsa{bb}"], bias=EPS, scale=1.0 / 1024.0)
            RECIP(rs2[:, 2 * bb:2 * bb + 2], rs1[:, 2 * bb:2 * bb + 2], [f"rsa{bb}"], [f"rsb{bb}"])

        def phaseA(pas, blk, ci):
            own = pas == "full"
            c = blk * 2 + ci
            d = c % 2
            hT = hTb[blk % 2]
            hk = f"hT{blk % 2}"
            hs = lambda kt: hT[:, kt, ci * 128:(ci + 1) * 128]
            cbs = [0, 1, 2, 3, 4, 5, 6, 7, 8, 9] if own else light_cbs
            for cb in cbs:
                pa, pk = pj()
                for kt in range(8):
                    MM(pa, hs(kt), w_in[:, kt, cb * 512:(cb + 1) * 512], kt == 0, kt == 7, [hk, f"w_in{cb}"], [pk])
                rcol = rs2[:, 2 * (blk % 2) + ci:2 * (blk % 2) + ci + 1]
                rk = f"rsb{blk % 2}"
                if cb == 0:
                    ACT(qk2[:, 0:512], pa, AF.Copy, [pk, rk], ["qk_q"], scale=rcol)
                elif cb == 1:
                    ACT(qk2[:, 512:1024], pa, AF.Copy, [pk, rk], ["qk_k"], scale=rcol)
                elif cb in (2, 3):
                    TS("dve", v_bf[d][:, (cb - 2) * 512:(cb - 1) * 512], pa, rcol, None, ALU.mult, None, [pk, rk], [f"v_bf{d}"])
                elif cb in (4, 5):
                    ACT(gr_bf[d][:, (cb - 4) * 512:(cb - 3) * 512], pa, AF.Silu, [pk, rk], [f"gr_bf{d}"], scale=rcol)
                elif cb in (6, 7):
                    TS("dve", u_bf[d][:, (cb - 6) * 512:(cb - 5) * 512], pa, rcol, None, ALU.mult, None, [pk, rk], [f"u_bf{d}"])
                else:
                    ACT(g5_bf[d][:, (cb - 8) * 512:(cb - 7) * 512], pa, AF.Silu, [pk, rk], [f"g5_bf{d}"], scale=rcol)
            if own or 6 in light_cbs:
                dma(u_d[c * 128:(c + 1) * 128, :], u_bf[d], r=[f"u_bf{d}"])
            if own:
                oc = c - 2 * NPRE
                dma(g5_d[oc * 128:(oc + 1) * 128, :], g5_bf[d], r=[f"g5_bf{d}"])
            h0 = 0 if own else 8
            nh = 16 - h0
            cosb = cosT[:, c, :].unsqueeze(1).to_broadcast([128, nh, 32])
            sinb = sinT[:, c, :].unsqueeze(1).to_broadcast([128, nh, 32])
            x1 = qk[:, h0:16, 0, :]
            x2 = qk[:, h0:16, 1, :]
            rq = ["qk_q", "qk_k"] if own else ["qk_k"]
            qk3 = qk[:, h0:16, :, :].rearrange("p h t i -> p h (t i)")
            TT("pool", qk3, qk3, sqk[:, h0:16].unsqueeze(2).to_broadcast([128, nh, 64]), ALU.mult, rq + ["sqk"], rq)
            q4 = qkt[d].rearrange("p h (t i) -> p h t i", t=2)
            TT("dve", rt[0][:, h0:16, :], x1, cosb, ALU.mult, rq, ["rt0"])
            TT("dve", rt[1][:, h0:16, :], x2, sinb, ALU.mult, rq, ["rt1"])
            TT("dve", q4[:, h0:16, 0, :], rt[0][:, h0:16, :], rt[1][:, h0:16, :], ALU.subtract, ["rt0", "rt1"], [f"qkt{d}"])
            TT("pool", rt[2][:, h0:16, :], x1, sinb, ALU.mult, rq, ["rt2"])
            TT("pool", rt[3][:, h0:16, :], x2, cosb, ALU.mult, rq, ["rt3"])
            TT("pool", q4[:, h0:16, 1, :], rt[2][:, h0:16, :], rt[3][:, h0:16, :], ALU.add, ["rt2", "rt3"], [f"qkt{d}"])
            if own:
                for b in range(8):
                    TR(TRb[:, b * 128:(b + 1) * 128], qkt2[d][:, b * 128:(b + 1) * 128], identb[:], [f"qkt{d}", "identb"], ["TRb"])
                CP("act", kT[d].rearrange("p b t -> p (b t)"), TRb[:, 512:1024], ["TRb"], [f"kT{d}"])
                CP("act", qm[d][0:64, 0::2, :], TRb[0:64, 0:512].rearrange("p (b t) -> p b t", b=4), ["TRb"], [f"qm{d}"])
                CP("act", qm[d][64:128, 1::2, :], TRb[64:128, 0:512].rearrange("p (b t) -> p b t", b=4), ["TRb"], [f"qm{d}"])

        def phaseB(pas, blk, ci):
            own = pas == "full"
            c = blk * 2 + ci
            d = c % 2
            for h in range(8):
                MM(KVp[(h % 2) * 64:(h % 2) * 64 + 64, kvi(h), :], qkt2[d][:, 512 + h * 64:512 + (h + 1) * 64],
                   v_bf[d][:, h * 128:(h + 1) * 128], True, True, [f"qkt{d}", f"v_bf{d}"], ["KVp"])
            if own:
                for hf in range(2):
                    for h in range(4 * hf, 4 * hf + 4):
                        MM(SCp[:, (h % 4) * 128:(h % 4 + 1) * 128], kT[d][:, h // 2, :], qm[d][:, h, :], True, True, [f"kT{d}", f"qm{d}"], ["SCp"])
                    TT("dve", PT[:, 4 * hf:4 * hf + 4, :], SCp.rearrange("p (h i) -> p h i", h=4), maskT[:].unsqueeze(1).to_broadcast([128, 4, 128]),
                       ALU.mult, ["SCp", "maskT"], [f"PT{hf}"])
                for hf in range(2):
                    for h in range(4 * hf, 4 * hf + 4):
                        oo = Op[:, (h % 4) * 128:(h % 4 + 1) * 128]
                        MM(oo, PT[:, h, :], v_bf[d][:, h * 128:(h + 1) * 128], True, False, [f"PT{hf}", f"v_bf{d}"], ["Op"])
                        MM(oo, qm[d][:, h, :], Rbf[:, h, :], False, True, [f"qm{d}", "Rbf"], ["Op"])
                    CP("act", o2[:, hf * 512:(hf + 1) * 512], Op, ["Op"], [f"o_sb{hf}", "o_sb"])
                    ACT(osq.rearrange("p h e -> p (h e)")[:, hf * 512:(hf + 1) * 512], Op, AF.Square, ["Op"], [f"osq{hf}"])
            for par in range(2):
                rows = slice(par * 64, par * 64 + 64)
                TT("dve", Rf[rows, par::2, :], KVp[rows, kvs(par), :], Rf[rows, par::2, :], ALU.add, ["KVp", "Rf"], ["Rf"])
                TT("dve", Rf[rows, par::2, :], Rf[rows, par::2, :], cdt[rows, par::2].unsqueeze(2).to_broadcast([64, 4, 128]), ALU.mult, ["Rf", "cdt"], ["Rf"])
                CP("act", Rbf[rows, par::2, :], Rf[rows, par::2, :], ["Rf"], ["Rbf"])
            if own:
                RED(stt[:, 0:8], o_sb, ALU.add, ["o_sb0", "o_sb1"], ["st_s"])
                RED(stt[:, 8:16], osq, ALU.add, ["osq0", "osq1"], ["st_q"])
                TS("dve", stt[:, 16:24], stt[:, 0:8], 1.0 / 128.0, None, ALU.mult, None, ["st_s"], ["st_m"])
                TT("dve", stt[:, 24:32], stt[:, 16:24], stt[:, 16:24], ALU.mult, ["st_m"], ["st_m2"])
                STT(stt[:, 32:40], stt[:, 8:16], 1.0 / 128.0, stt[:, 24:32], ALU.mult, ALU.subtract, ["st_q", "st_m2"], ["st_v"])
                ACT(stt[:, 40:48], stt[:, 32:40], AF.Sqrt, ["st_v"], ["st_sd"], bias=EPS, scale=1.0)
                RECIP(stt[:, 32:40], stt[:, 40:48], ["st_sd"], ["st_rs"])
                TT("pool", o_sb, o_sb, stt[:, 16:24].unsqueeze(2).to_broadcast([128, 8, 128]), ALU.subtract, ["o_sb0", "o_sb1", "st_m"], ["o_sb0", "o_sb1", "o_sb"])
                TT("pool", o_sb, o_sb, stt[:, 32:40].unsqueeze(2).to_broadcast([128, 8, 128]), ALU.mult, ["o_sb", "st_rs"], ["o_sb"])
                TT("dve", o2, o2, gng, ALU.mult, ["o_sb", "gng"], ["o_sb"])
                TT("dve", ret[d], o2, gr_bf[d], ALU.mult, ["o_sb", f"gr_bf{d}"], [f"ret{d}"])

        def phaseC(pas, blk, ci):
            if pas != "full":
                return
            c = blk * 2 + ci
            d = c % 2
            for kt in range(8):
                TR(TRb[:, kt * 128:(kt + 1) * 128], ret[d][:, kt * 128:(kt + 1) * 128], identb[:], [f"ret{d}", "identb"], ["TRb"])
            CP("act", retT_blk[:, :, ci * 128:(ci + 1) * 128], TRb.rearrange("p (k t) -> p k t", k=8), ["TRb"], ["retT_blk"])
            if ci == 1:
                ob = blk - NPRE
                dma(retT_d[:, :, ob * 256:(ob + 1) * 256], retT_blk, r=["retT_blk"])

        pendB = None
        pendC = None
        real = [(p_, b_) for p_, b_ in sa_iter if p_ != "xchg"]
        has_x = any(p_ == "xchg" for p_, _ in sa_iter)
        load_block(real[0][1])
        norm_block(real[0][1])
        ri = 0
        prev_pas = None
        for pas, blk in sa_iter:
            if pas == "light" and not PSUM_R2:
                pj_n[0] = 4
            if pas == "full" and prev_pas == "light" and mode != "coll":
                if pendB is not None:
                    phaseB(*pendB)
                    pendB = None
                P.barrier()
            prev_pas = pas
            if pas == "xchg":
                if pendB is not None:
                    phaseB(*pendB)
                    pendB = None
                do_xchg()
                load_block(real[ri][1])
                norm_block(real[ri][1])
                continue
            nxt = real[ri + 1] if ri + 1 < len(real) else None
            if nxt is not None and has_x and nxt[0] != pas:
                nxt = None
            for ci in range(2):
                if ci == 0 and nxt is not None:
                    load_block(nxt[1])
                phaseA(pas, blk, ci)
                if ci == 0 and nxt is not None:
                    norm_block(nxt[1])
                if pendC is not None:
                    phaseC(*pendC)
                    pendC = None
                if pendB is not None:
                    phaseB(*pendB)
                    pendC = pendB
                pendB = (pas, blk, ci)
            ri += 1
        if pendC is not None:
            phaseC(*pendC)
        if pendB is not None:
            phaseB(*pendB)
            phaseC(*pendB)
        P.barrier()
        if stop_after == "SA":
            return finish(nc, P, st, out_d)

        AR.reset()
        T0 = AR.take([128, 64, 128], BF16)
        I2Sre = AR.take([128, 64, 64], BF16)
        I2Sim = AR.take([128, 64, 64], BF16)
        S2Ore = AR.take([128, 32, 128], BF16)
        S2Oim = AR.take([128, 32, 128], BF16)
        TIre = AR.take([128, 32, 16], F32)
        TIim = AR.take([128, 32, 16], F32)
        TOre = AR.take([128, 32, 8], F32)
        TOim = AR.take([128, 32, 8], F32)
        P8 = AR.take([128, 32], F32)
        Q8 = AR.take([128, 32], F32)
        P128 = AR.take([128, 32], F32)
        Q128 = AR.take([128, 32], F32)
        base_sb = AR.off
        are = AR.take([128, 32], F32)
        aim = AR.take([128, 32], F32)
        ldt = AR.take([128, 32], F32)
        bre = AR.take([128, 32, 16], F32)
        bim = AR.take([128, 32, 16], F32)
        cre = AR.take([128, 32, 16], F32)
        cim = AR.take([128, 32, 16], F32)
        dtt = AR.take([128, 32], F32)
        tA = [AR.take([128, 32], F32) for _ in range(12)]
        tI = AR.take([128, 32], I32)
        PWr = AR.take([128, 9, 32], F32)
        PWi = AR.take([128, 9, 32], F32)
        NWr = AR.take([128, 9, 32], F32)
        NWi = AR.take([128, 9, 32], F32)
        E8r = AR.take([128, 17, 32], F32)
        E8i = AR.take([128, 17, 32], F32)
        E16r = AR.take([128, 9, 32], F32)
        E16i = AR.take([128, 9, 32], F32)
        Bbr = AR.take([128, 32, 16], F32)
        Bbi = AR.take([128, 32, 16], F32)
        Gre = AR.take([128, 32, 8, 16], F32)
        Gim = AR.take([128, 32, 8, 16], F32)
        Sre = AR.take([128, 32, 8, 16], F32)
        Sim = AR.take([128, 32, 8, 16], F32)
        Ire = AR.take([128, 32, 8, 16], F32)
        Iim = AR.take([128, 32, 8, 16], F32)
        tB = [AR.take([128, 32, 16], F32) for _ in range(2)]
        T0f = AR.take([128, 4, 128], F32)
        dcol = AR.take([128, 64], F32)

        K0 = ["S0B"]
        dma(dcol, dcol_d, w=K0)
        for t_, d_ in ((are, are_d), (aim, aim_d), (ldt, ldt_d), (bre, bre_d), (bim, bim_d), (cre, cre_d), (cim, cim_d)):
            dma(t_, d_, w=K0)
        P.barrier()
        V = lambda out, in0, in1, op: TT("dve", out, in0, in1, op, K0, K0)
        VS = lambda out, in0, s1, op: TS("dve", out, in0, s1, None, op, None, K0, K0)
        VC = lambda out, in_: CP("dve", out, in_, K0, K0)
        VM = lambda ap, val: MEMSET("dve", ap, val, K0)
        ACT(dtt, ldt, AF.Exp, K0, K0)
        V(tA[0], are, dtt, ALU.mult)
        ACT(tA[1], tA[0], AF.Exp, K0, K0)
        ACT(tA[2], tA[0], AF.Exp, K0, K0, scale=-1.0)
        V(tA[3], aim, dtt, ALU.mult)
        sincos(tA[3], None, tA[4], tA[5], (tA[6], tI, tA[7], tA[8]), "S0B")
        P.barrier()

        def cmul(orr, oi, xr, xi, yr, yi, tmp):
            u1, u2 = tmp
            V(u1, xr, yr, ALU.mult)
            V(u2, xi, yi, ALU.mult)
            V(orr, u1, u2, ALU.subtract)
            V(u1, xr, yi, ALU.mult)
            V(u2, xi, yr, ALU.mult)
            V(oi, u1, u2, ALU.add)

        tsm = (tA[10], tA[11])
        VM(PWr[:, 0, :], 1.0)
        VM(PWi[:, 0, :], 0.0)
        V(PWr[:, 1, :], tA[1], tA[5], ALU.mult)
        V(PWi[:, 1, :], tA[1], tA[4], ALU.mult)
        VM(NWr[:, 0, :], 1.0)
        VM(NWi[:, 0, :], 0.0)
        V(NWr[:, 1, :], tA[2], tA[5], ALU.mult)
        V(tA[9], tA[2], tA[4], ALU.mult)
        VS(NWi[:, 1, :], tA[9], -1.0, ALU.mult)
        for k in range(1, 8):
            cmul(PWr[:, k + 1, :], PWi[:, k + 1, :], PWr[:, k, :], PWi[:, k, :], PWr[:, 1, :], PWi[:, 1, :], tsm)
            cmul(NWr[:, k + 1, :], NWi[:, k + 1, :], NWr[:, k, :], NWi[:, k, :], NWr[:, 1, :], NWi[:, 1, :], tsm)
        VM(E8r[:, 0, :], 1.0)
        VM(E8i[:, 0, :], 0.0)
        VC(E8r[:, 1, :], PWr[:, 8, :])
        VC(E8i[:, 1, :], PWi[:, 8, :])
        for k in range(1, 16):
            cmul(E8r[:, k + 1, :], E8i[:, k + 1, :], E8r[:, k, :], E8i[:, k, :], E8r[:, 1, :], E8i[:, 1, :], tsm)
        VM(E16r[:, 0, :], 1.0)
        VM(E16i[:, 0, :], 0.0)
        VC(E16r[:, 1, :], E8r[:, 16, :])
        VC(E16i[:, 1, :], E8i[:, 16, :])
        for k in range(1, 8):
            cmul(E16r[:, k + 1, :], E16i[:, k + 1, :], E16r[:, k, :], E16i[:, k, :], E16r[:, 1, :], E16i[:, 1, :], tsm)
        VC(P8, E8r[:, 1, :])
        VC(Q8, E8i[:, 1, :])
        VC(P128, E16r[:, 8, :])
        VC(Q128, E16i[:, 8, :])
        cmul(A2r[:, 0, :], A2i[:, 0, :], P128, Q128, P128, Q128, tsm)
        cmul(A2r[:, 1, :], A2i[:, 1, :], A2r[:, 0, :], A2i[:, 0, :], A2r[:, 0, :], A2i[:, 0, :], tsm)
        for r_ in range(16):
            VC(TIre[:, :, r_], E8r[:, 15 - r_, :])
            VC(TIim[:, :, r_], E8i[:, 15 - r_, :])
        for q_ in range(8):
            VC(TOre[:, :, q_], E16r[:, 7 - q_, :])
            VC(TOim[:, :, q_], E16i[:, 7 - q_, :])
        V(tA[0], are, are, ALU.mult)
        V(tA[1], aim, aim, ALU.mult)
        V(tA[0], tA[0], tA[1], ALU.add)
        RECIP(tA[0], tA[0], K0, K0)
        VS(tA[1], PWr[:, 1, :], -1.0, ALU.add)
        V(tA[2], tA[1], are, ALU.mult)
        V(tA[3], PWi[:, 1, :], aim, ALU.mult)
        V(tA[2], tA[2], tA[3], ALU.add)
        V(tA[2], tA[2], tA[0], ALU.mult)
        V(tA[3], PWi[:, 1, :], are, ALU.mult)
        V(tA[4], tA[1], aim, ALU.mult)
        V(tA[3], tA[3], tA[4], ALU.subtract)
        V(tA[3], tA[3], tA[0], ALU.mult)
        bc16 = lambda ap: ap.unsqueeze(2).to_broadcast([128, 32, 16])
        cmul(Bbr, Bbi, bre, bim, bc16(tA[2]), bc16(tA[3]), (tB[0], tB[1]))
        for s_ in range(8):
            cmul(Gre[:, :, s_, :], Gim[:, :, s_, :], Bbr, Bbi, bc16(NWr[:, s_ + 1, :]), bc16(NWi[:, s_ + 1, :]), (tB[0], tB[1]))
            cmul(Ire[:, :, s_, :], Iim[:, :, s_, :], Bbr, Bbi, bc16(PWr[:, 7 - s_, :]), bc16(PWi[:, 7 - s_, :]), (tB[0], tB[1]))
            cmul(Sre[:, :, s_, :], Sim[:, :, s_, :], cre, cim, bc16(PWr[:, s_ + 1, :]), bc16(PWi[:, s_ + 1, :]), (tB[0], tB[1]))
        Sim2 = Sim.rearrange("p g t j -> p (g t j)")
        VS(Sim2, Sim2, -1.0, ALU.mult)
        P.barrier()
        CP("act", S2Ore.rearrange("p g x -> p (g x)"), Sre.rearrange("p g t j -> p (g t j)"), K0, ["S2O"])
        CP("act", S2Oim.rearrange("p g x -> p (g x)"), Sim2, K0, ["S2O"])
        Gre3 = Gre.rearrange("p g s j -> p g (s j)")
        Gim3 = Gim.rearrange("p g s j -> p g (s j)")
        Sre3 = Sre.rearrange("p g t j -> p g (t j)")
        Sim3 = Sim.rearrange("p g t j -> p g (t j)")
        Ire3 = Ire.rearrange("p g s j -> p g (s j)")
        Iim3 = Iim.rearrange("p g s j -> p g (s j)")
        for gb in range(16):
            bank = PS[gb % 2][:, 0:512]
            bk = f"PSb{gb % 2}"
            for i in range(4):
                g = gb * 4 + i
                gh, gp = g // 32, g % 32
                pb = gh * 64
                MM(bank[:, i * 128:(i + 1) * 128], Gre3[pb:pb + 64, gp, :], Sre3[pb:pb + 64, gp, :], True, False, K0, [bk])
                MM(bank[:, i * 128:(i + 1) * 128], Gim3[pb:pb + 64, gp, :], Sim3[pb:pb + 64, gp, :], False, True, K0, [bk])
            TT("dve", T0f, bank.rearrange("p (g x) -> p g x", g=4),
               t0mask[:].unsqueeze(1).to_broadcast([128, 4, 128]), ALU.mult, [bk, "t0mask"] + K0, K0)
            for i in range(4):
                g = gb * 4 + i
                STT(T0[:, g, :], identf[:], dcol[:, g:g + 1], T0f[:, i, :], ALU.mult, ALU.add, K0 + ["identf"], K0 + ["T0"])
        for src3, dst, nm in ((Ire3, I2Sre, "I2Sre"), (Iim3, I2Sim, "I2Sim")):
            for gb in range(8):
                bank = PS[2 + gb % 2][:, 0:512]
                bk = f"PSc{gb % 2}"
                for i in range(8):
                    g = gb * 8 + i
                    gh, gp = g // 32, g % 32
                    pb = gh * 64
                    TR(bank[:, i * 64:(i + 1) * 64], src3[pb:pb + 64, gp, :], identf[pb:pb + 64, pb:pb + 64], ["identf", "S0B"], [bk])
                CP("act", dst[:, gb * 8:(gb + 1) * 8, :], bank.rearrange("p (g n) -> p g n", g=8), [bk], [nm])
        P.barrier()
        if stop_after == "S0b":
            return finish(nc, P, st, out_d)

        AR.reset(base_sb)
        u_raw = AR.take([128, 8, 1024], BF16)
        U = u_raw.rearrange("p s c -> p (s c)").rearrange("p (g m) -> p g m", g=64)
        u_tm2 = AR.take([128, 64, 8, 16], BF16)
        inc_raw = AR.take([128, 4096], F32)
        Incre = inc_raw.rearrange("p (g m) -> p g m", g=32)
        yg_bf = inc_raw.bitcast(BF16).rearrange("p (s c) -> p s c", s=8)
        Incim = AR.take([128, 32, 128], F32)
        gluw = AR.take([128, 8, 1024], BF16)
        glub = AR.take([128, 1024], F32)
        g5s = [AR.take([128, 1024], BF16) for _ in range(2)]
        ct = [AR.take([128, 32], F32) for _ in range(12)]
        un_base = AR.off
        LA = AR.take([128, 32, 8, 16], F32)
        LB = AR.take([128, 32, 8, 16], F32)
        LS = [AR.take([128, 32, 8], F32) for _ in range(4)]
        LE = [AR.take([128, 32], F32) for _ in range(4)]
        AR.reset(un_base)
        Xre_bf = AR.take([128, 32, 128], BF16)
        Xim_bf = AR.take([128, 32, 128], BF16)
        Ysb = [AR.take([128, 8, 128], F32) for _ in range(2)]
        gt1 = [AR.take([128, 1024], F32) for _ in range(2)]
        gt2 = [AR.take([128, 1024], F32) for _ in range(2)]
        AR.reset(un_base)
        ygT = [AR.take([128, 8, 128], BF16) for _ in range(2)]
        zb = [AR.take([128, 1024], F32) for _ in range(2)]
        gate = [AR.take([128, 1024], F32) for _ in range(2)]
        tt_ = [AR.take([128, 1024], F32) for _ in range(2)]
        ssm = [AR.take([128, 1024], BF16) for _ in range(2)]
        ssmT = [AR.take([128, 8, 128], BF16) for _ in range(2)]
        AR.reset(un_base)

        for kt in range(8):
            dma(gluw[:, kt, :], gluw_d[:, kt, :], w=["gluw"], eng="pool")
        dma(glub, glub_d, w=["glub"])

        TRU = PS[0][:, 0:512].bitcast(BF16)
        PIre = PS[1][:, 0:512]
        PIim = PS[1][:, 512:1024]
        TRU2 = [PS[0][:, 0:512].bitcast(BF16), PS[0][:, 512:1024].bitcast(BF16)]
        PI2 = [(PS[1][:, 0:512], PS[1][:, 512:1024]), (PS[2][:, 0:512], PS[2][:, 512:1024])]
        PY = PS[2]
        PYT = PS[3]
        IRK = [f"IR{m}" for m in range(128)]
        IIK = [f"II{m}" for m in range(128)]

        Gs = AR.take([128, 8, 64], F32)
        Gs2 = AR.take([128, 8, 64], F32)
        Sd = AR.take([128, 3, 64], F32)
        xt_ = [AR.take([128, 32], F32) for _ in range(4)]
        NPS = NPRE // 4
        if mode == "coll":
            sb_iter = [("light", 0), ("light", 1), ("xchg", -1), ("full", 0), ("full", 1)]
        else:
            sb_iter = [("light", sc_) for sc_ in range(NPS)] + [("full", NPS), ("full", NPS + 1)]
        for pas, sc in sb_iter:
            if pas == "xchg":
                P.barrier()
                dma(ccS_in.ap()[:, 0:32], car_r[:], r=["car_r"], w=["ccS_in"])
                dma(ccS_in.ap()[:, 32:64], car_i[:], r=["car_i"], w=["ccS_in"])
                if FAKE_CC:
                    dma(ccS_out.ap()[0:128, :], ccS_in.ap(), r=["ccS_in"], w=["ccS_out"])
                else:
                    P.op("pool", lambda e: e.collective_compute("AllGather", ALU.bypass, replica_groups=[list(range(8))],
                                                                ins=[ccS_in.ap().opt()], outs=[ccS_out.ap().opt()]),
                         r=["ccS_in"], w=["ccS_out"])
                dma(Gs, ccS_out.ap().rearrange("(r p) c -> p r c", p=128), r=["ccS_out"], w=["Gs"])
                KX = ["XCH"]
                for d_ in range(3):
                    TT("dve", Gs2, Gs, sel5[:, d_, :].unsqueeze(2).to_broadcast([128, 8, 64]), ALU.mult, ["Gs", "sel5"] + KX, KX)
                    RED(Sd[:, d_, :], Gs2.rearrange("p r c -> p c r"), ALU.add, KX, KX)
                XV = lambda out, in0, in1, op: TT("dve", out, in0, in1, op, KX + ["car_r", "car_i"], KX)
                XV(car_r[:], Sd[:, 0, 0:32], Sd[:, 0, 0:32], ALU.bypass) if False else CP("dve", car_r[:], Sd[:, 0, 0:32], KX + ["car_r"], KX + ["car_r"])
                CP("dve", car_i[:], Sd[:, 0, 32:64], KX + ["car_i"], KX + ["car_i"])
                for d_ in (1, 2):
                    sr, si = Sd[:, d_, 0:32], Sd[:, d_, 32:64]
                    ar_, ai_ = A2r[:, d_ - 1, :], A2i[:, d_ - 1, :]
                    XV(xt_[0], sr, ar_, ALU.mult)
                    XV(xt_[1], si, ai_, ALU.mult)
                    XV(xt_[0], xt_[0], xt_[1], ALU.subtract)
                    TT("dve", car_r[:], car_r[:], xt_[0], ALU.add, KX + ["car_r"], KX + ["car_r"])
                    XV(xt_[2], sr, ai_, ALU.mult)
                    XV(xt_[3], si, ar_, ALU.mult)
                    XV(xt_[2], xt_[2], xt_[3], ALU.add)
                    TT("dve", car_i[:], car_i[:], xt_[2], ALU.add, KX + ["car_i"], KX + ["car_i"])
                P.barrier()
                continue
            own = pas == "full"
            if own:
                P.barrier()
            dma(u_raw, u_d[sc * 1024:(sc + 1) * 1024, :].rearrange("(m s) c -> m s c", s=8), w=["u_raw", "U"])
            src4 = u_raw.rearrange("p s (g j) -> p g s j", j=16)
            CP("dve", u_tm2[:, 0:32, :, :], src4[:, 0:32, :, :], ["u_raw"], ["u_tm2a"])
            CP("act", u_tm2[:, 32:64, :, :], src4[:, 32:64, :, :], ["u_raw"], ["u_tm2b"])
            u3 = u_tm2.rearrange("p g s j -> p g (s j)")
            for gb in range(8):
                tru = TRU2[gb % 2]
                tk = f"TRU{gb % 2}"
                for i in range(8):
                    g = gb * 8 + i
                    TR(tru[:, i * 128:(i + 1) * 128], u3[:, g, :], identb[:], ["u_tm2a", "u_tm2b", "identb"], [tk])
                CP("act", U[:, gb * 8:(gb + 1) * 8, :], tru.rearrange("p (g m) -> p g m", g=8),
                   [tk], ["U", "u_raw"])
            for b4 in range(8):
                pre, pim = PI2[b4 % 2]
                kre, kim = f"PIre{b4 % 2}", f"PIim{b4 % 2}"
                for i in range(4):
                    gp = b4 * 4 + i
                    for gh in range(2):
                        g = gh * 32 + gp
                        MM(pre[gh * 64:(gh + 1) * 64, i * 128:(i + 1) * 128], I2Sre[:, g, :], U[:, g, :], True, True, ["U", "I2Sre"], [kre])
                        MM(pim[gh * 64:(gh + 1) * 64, i * 128:(i + 1) * 128], I2Sim[:, g, :], U[:, g, :], True, True, ["U", "I2Sim"], [kim])
                CP("act", Incre[:, b4 * 4:(b4 + 1) * 4, :], pre.rearrange("p (g m) -> p g m", g=4), [kre], IRK)
                CP("act", Incim[:, b4 * 4:(b4 + 1) * 4, :], pim.rearrange("p (g m) -> p g m", g=4), [kim], IIK)
            if not own:
                I4r = Incre.rearrange("p g (q r) -> p g q r", r=16)
                I4i = Incim.rearrange("p g (q r) -> p g q r", r=16)
                tir = TIre.unsqueeze(2).to_broadcast([128, 32, 8, 16])
                tii = TIim.unsqueeze(2).to_broadcast([128, 32, 8, 16])
                TT("dve", LA, I4r, tir, ALU.mult, IRK + ["TI"], ["LA"])
                TT("dve", LB, I4i, tii, ALU.mult, IIK + ["TI"], ["LB"])
                TT("dve", LA, LA, LB, ALU.subtract, ["LA", "LB"], ["LA"])
                RED(LS[0], LA, ALU.add, ["LA"], ["LS0"])
                TT("dve", LA, I4i, tir, ALU.mult, IIK + ["TI", "LS0"], ["LA"])
                TT("dve", LB, I4r, tii, ALU.mult, IRK + ["TI", "LA"], ["LB"])
                TT("dve", LA, LA, LB, ALU.add, ["LA", "LB"], ["LA"])
                RED(LS[1], LA, ALU.add, ["LA"], ["LS1"])
                TT("dve", LS[2], LS[0], TOre, ALU.mult, ["LS0", "TO"], ["LS2"])
                TT("dve", LS[3], LS[1], TOim, ALU.mult, ["LS1", "TO"], ["LS3"])
                TT("dve", LS[2], LS[2], LS[3], ALU.subtract, ["LS2", "LS3"], ["LS2"])
                RED(LE[0], LS[2], ALU.add, ["LS2"], ["LE0"])
                TT("dve", LS[2], LS[1], TOre, ALU.mult, ["LS1", "TO", "LE0"], ["LS2"])
                TT("dve", LS[3], LS[0], TOim, ALU.mult, ["LS0", "TO", "LS2"], ["LS3"])
                TT("dve", LS[2], LS[2], LS[3], ALU.add, ["LS2", "LS3"], ["LS2"])
                RED(LE[1], LS[2], ALU.add, ["LS2"], ["LE1"])
                TT("dve", LE[2], car_r[:], P128, ALU.mult, ["car_r", "P128"], ["LE2"])
                TT("dve", LE[3], car_i[:], Q128, ALU.mult, ["car_i", "Q128"], ["LE3"])
                TT("dve", LE[2], LE[2], LE[3], ALU.subtract, ["LE2", "LE3"], ["LE2"])
                TT("dve", LE[0], LE[0], LE[2], ALU.add, ["LE0", "LE2"], ["LE0"])
                TT("dve", LE[2], car_i[:], P128, ALU.mult, ["car_i", "P128", "LE0"], ["LE2"])
                TT("dve", LE[3], car_r[:], Q128, ALU.mult, ["car_r", "Q128", "LE0"], ["LE3"])
                TT("dve", LE[2], LE[2], LE[3], ALU.add, ["LE2", "LE3"], ["LE2"])
                TT("dve", car_i[:], LE[1], LE[2], ALU.add, ["LE1", "LE2"], ["car_i"])
                CP("dve", car_r[:], LE[0], ["LE0"], ["car_r"])
                continue
            CP("dve", Xre_bf[:, :, 0], car_r[:], ["car_r"], ["Xre0"])
            CP("dve", Xim_bf[:, :, 0], car_i[:], ["car_i"], ["Xim0"])
            for m in range(128):
                if m == 0:
                    xr, xi, kr, ki = car_r[:], car_i[:], "car_r", "car_i"
                else:
                    xr, xi, kr, ki = Incre[:, :, m - 1], Incim[:, :, m - 1], IRK[m - 1], IIK[m - 1]
                pp = (m % 2) * 6
                c0, c1, c2, c3, c4, c5 = ct[pp:pp + 6]
                n0 = [f"ct{pp + j}" for j in range(6)]
                TT("dve", c0, xr, P8, ALU.mult, [kr, "P8"], [n0[0]])
                TT("dve", c1, xi, Q8, ALU.mult, [ki, "Q8"], [n0[1]])
                TT("dve", c3, xi, P8, ALU.mult, [ki, "P8"], [n0[3]])
                TT("dve", c4, xr, Q8, ALU.mult, [kr, "Q8"], [n0[4]])
                TT("dve", c2, c0, c1, ALU.subtract, [n0[0], n0[1]], [n0[2]])
                TT("dve", c5, c3, c4, ALU.add, [n0[3], n0[4]], [n0[5]])
                TT("dve", Incre[:, :, m], c2, Incre[:, :, m], ALU.add, [n0[2], IRK[m]], [IRK[m]])
                TT("dve", Incim[:, :, m], c5, Incim[:, :, m], ALU.add, [n0[5], IIK[m]], [IIK[m]])
            CP("dve", car_r[:], Incre[:, :, 127], [IRK[127], "Xre0"], ["car_r"])
            CP("dve", car_i[:], Incim[:, :, 127], [IIK[127], "Xim0"], ["car_i"])
            CP("act", Xre_bf[:, :, 1:128], Incre[:, :, 0:127], IRK, ["Xre"])
            CP("dve", Xim_bf[:, :, 1:128], Incim[:, :, 0:127], IIK, ["Xim"])
            PYb = [PS[1], PS[2]]
            PYTb = [PS[3], PS[0]]

            def y_mm(gb):
                bq = gb % 2
                for i in range(8):
                    g = gb * 8 + i
                    gh, gp = g // 32, g % 32
                    pb = gh * 64
                    oy = PYb[bq][:, i * 128:(i + 1) * 128]
                    MM(oy, T0[:, g, :], U[:, g, :], True, False, ["T0", "U"], [f"PY{bq}"])
                    MM(oy, S2Ore[pb:pb + 64, gp, :], Xre_bf[pb:pb + 64, gp, :], False, False, ["S2O", "Xre", "Xre0"], [f"PY{bq}"])
                    MM(oy, S2Oim[pb:pb + 64, gp, :], Xim_bf[pb:pb + 64, gp, :], False, True, ["S2O", "Xim", "Xim0"], [f"PY{bq}"])

            def y_ew(gb):
                bq = gb % 2
                pyt = PYTb[bq]
                ky, kt_ = f"PY{bq}", f"PYT{bq}"
                CP("act", Ysb[bq].rearrange("p g m -> p (g m)"), PYb[bq], [ky], [f"Ysb{bq}"])
                for i in range(8):
                    TR(pyt[:, i * 128:(i + 1) * 128], Ysb[bq][:, i, :], identf[:], [f"Ysb{bq}", "identf"], [kt_])
                ACT(gt1[bq], pyt, AF.Square, [kt_], [f"gt1{bq}"])
                TS("dve", gt1[bq], gt1[bq], KG * 0.044715, KG, ALU.mult, ALU.add, [f"gt1{bq}"], [f"gt1{bq}"])
                TT("dve", gt1[bq], gt1[bq], pyt, ALU.mult, [f"gt1{bq}", kt_], [f"gt1{bq}"])
                ACT(gt2[bq], gt1[bq], AF.Sigmoid, [f"gt1{bq}"], [f"gt2{bq}"])
                TT("dve", yg_bf[:, :, gb * 128:(gb + 1) * 128].rearrange("p t (g j) -> p g t j", j=16),
                   pyt.rearrange("p (g t j) -> p g t j", g=8, t=8), gt2[bq].rearrange("p (g t j) -> p g t j", g=8, t=8),
                   ALU.mult, [kt_, f"gt2{bq}"], ["yg_bf"])

            y_mm(0)
            for gb in range(8):
                if gb + 1 < 8:
                    y_mm(gb + 1)
                y_ew(gb)
            P.barrier()
            oc = sc - NPS
            g5v = g5_d[oc * 1024:(oc + 1) * 1024, :].rearrange("(m s) c -> m s c", s=8)
            TRa = [PS[0][:, 0:512].bitcast(BF16), PS[0][:, 512:1024].bitcast(BF16)]
            TRs = [PS[3][:, 0:512].bitcast(BF16), PS[3][:, 512:1024].bitcast(BF16)]
            PZb = [PS[1], PS[2]]

            def tail1(s_):
                bq = s_ % 2
                dma(g5s[bq], g5v[:, s_, :], w=[f"g5s{bq}"])
                for kt in range(8):
                    TR(TRa[bq][:, kt * 128:(kt + 1) * 128], yg_bf[:, s_, kt * 128:(kt + 1) * 128], identb[:], ["yg_bf", "identb"], [f"TRa{bq}"])
                CP("act", ygT[bq].rearrange("p k m -> p (k m)"), TRa[bq], [f"TRa{bq}"], [f"ygT{bq}"])
                for cb in range(2):
                    pz = PZb[bq][:, cb * 512:(cb + 1) * 512]
                    for kt in range(8):
                        MM(pz, ygT[bq][:, kt, :], gluw[:, kt, cb * 512:(cb + 1) * 512], kt == 0, kt == 7, [f"ygT{bq}", "gluw"], [f"PZ{bq}"])

            def tail2(s_):
                bq = s_ % 2
                TT("dve", zb[bq], PZb[bq], glub, ALU.add, [f"PZ{bq}", "glub"], [f"zb{bq}"])
                ACT(gate[bq], zb[bq], AF.Sigmoid, [f"zb{bq}"], [f"gate{bq}"])
                TT("dve", tt_[bq], yg_bf[:, s_, :], gate[bq], ALU.mult, ["yg_bf", f"gate{bq}"], [f"tt_{bq}"])
                TT("pool", ssm[bq], tt_[bq], g5s[bq], ALU.mult, [f"tt_{bq}", f"g5s{bq}"], [f"ssm{bq}"])
                for kt in range(8):
                    TR(TRs[bq][:, kt * 128:(kt + 1) * 128], ssm[bq][:, kt * 128:(kt + 1) * 128], identb[:], [f"ssm{bq}", "identb"], [f"TRs{bq}"])
                CP("act", ssmT[bq].rearrange("p k m -> p (k m)"), TRs[bq], [f"TRs{bq}"], [f"ssmT{bq}"])
                dma(ssmT_d[oc, s_], ssmT[bq], r=[f"ssmT{bq}"])

            tail1(0)
            for s_ in range(8):
                if s_ + 1 < 8:
                    tail1(s_ + 1)
                tail2(s_)
        P.barrier()
        if stop_after == "SB":
            return finish(nc, P, st, out_d)

        AR.reset()
        Wqk = AR.take([128, 8, 1024], BF16)
        VO = AR.take([128, 8, 1024], BF16)
        wout = AR.take([128, 16, 1024], BF16)
        g2r = AR.take([128, 1024], F32)
        gfr = AR.take([128, 1024], F32)
        base_sc = AR.off
        wqT = AR.take([128, 8, 1024], BF16)
        wk = AR.take([128, 8, 1024], BF16)
        wv = AR.take([128, 8, 1024], BF16)
        wo = AR.take([128, 8, 1024], BF16)
        memT = AR.take([128, 8, 256], F32)
        msq = AR.take([128, 8, 256], BF16)
        mrs1 = AR.take([128, 256], F32)
        mrs2 = AR.take([128, 256], F32)
        mnT = AR.take([128, 8, 256], BF16)
        kaT = AR.take([128, 8, 256], BF16)
        vaT = AR.take([128, 8, 256], BF16)
        for t_, d_, k_ in ((wqT, wqT_d, "wqT"), (wk, wk_d, "wk"), (wv, wv_d, "wv"), (wo, wo_d, "wo")):
            for kt in range(8):
                dma(t_[:, kt, :], d_[:, kt, :], w=[k_], eng="pool")
        for kt in range(16):
            dma(wout[:, kt, :], wout_d[:, kt, :], w=["wout"], eng="pool")
        dma(g2r, g2_d, w=["g2r"])
        dma(gfr, gf_d, w=["gfr"])
        dma(memT, memT_d, w=["memT"])
        norm_fm(memT, 256, gm[:], msq, mrs1, mrs2, mnT, PS[0][:, 0:256], "memT", "msq", "mrs", "mnT", "PSm", "gm")
        for r_ in range(8):
            for wt, dst, nm, pi in ((wk, kaT, "kaT", 1), (wv, vaT, "vaT", 2)):
                pp_ = PS[pi][:, (r_ % 2) * 512:(r_ % 2) * 512 + 256]
                pk = f"PSkv{pi}_{r_ % 2}"
                for kt in range(8):
                    MM(pp_, wt[:, kt, r_ * 128:(r_ + 1) * 128], mnT[:, kt, :], kt == 0, kt == 7, [nm[:2] if False else ("wk" if pi == 1 else "wv"), "mnT"], [pk])
                CP("act" if pi == 1 else "dve", dst[:, r_, :], pp_, [pk], [nm])
        for dt_ in range(8):
            pw_ = PS[dt_ % 2]
            pk = f"PSw{dt_ % 2}"
            for h in range(4):
                for hf in range(2):
                    MM(pw_[:, h * 256:(h + 1) * 256], wqT[:, 2 * h + hf, dt_ * 128:(dt_ + 1) * 128], kaT[:, 2 * h + hf, :],
                       hf == 0, hf == 1, ["wqT", "kaT"], [pk])
            ACT(Wqk[:, dt_, :], pw_, AF.Copy, [pk], ["Wqk"], scale=1.0 / 16.0)
        for h in range(4):
            for mh in range(2):
                idx = 2 * h + mh
                pw_ = PS[2 + idx % 2]
                pk = f"PSv{idx % 2}"
                for cb in range(2):
                    for hf in range(2):
                        MM(pw_[:, cb * 512:(cb + 1) * 512], vaT[:, 2 * h + hf, mh * 128:(mh + 1) * 128], wo[:, 2 * h + hf, cb * 512:(cb + 1) * 512],
                           hf == 0, hf == 1, ["vaT", "wo"], [pk])
                CP("dve", VO[:, idx, :], pw_, [pk], ["VO"])
        P.barrier()

        AR.reset(base_sc)
        retT_sc = [AR.take([128, 8, 1024], BF16) for _ in range(2)]
        ssmT_s = [AR.take([128, 8, 128], BF16) for _ in range(3)]
        x_t = [AR.take([128, 1024], F32) for _ in range(3)]
        x1 = [AR.take([128, 1024], F32) for _ in range(3)]
        junk = AR.take([128, 1024], F32)
        h2 = [AR.take([128, 1024], BF16) for _ in range(2)]
        h2T = AR.take([128, 8, 128], BF16)
        Pex = AR.take([128, 4, 256], BF16)
        Pn = [AR.take([128, 4, 256], BF16) for _ in range(2)]
        PTt = AR.take([128, 8, 128], BF16)
        x2 = AR.take([128, 1024], F32)
        osb = [AR.take([128, 1024], F32) for _ in range(2)]
        sst = [AR.take([128, 32], F32) for _ in range(3)]
        TRc = PS[3][:, 0:512].bitcast(BF16)
        TRd = PS[3][:, 512:1024].bitcast(BF16)
        tiles = [(sco, s_) for sco in range(2) for s_ in range(8)]
        dma(retT_sc[0], retT_d[:, :, 0:1024], w=["retT_sc0"])
        dma(retT_sc[1], retT_d[:, :, 1024:2048], w=["retT_sc1"])

        def sc_load(t):
            sco, s_ = tiles[t]
            b3 = t % 3
            dma(ssmT_s[b3], ssmT_d[sco, s_], w=[f"ssmT_s{b3}"])
            dma(x_t[b3], xown_d[sco, :, s_, :], w=[f"x_t{b3}"])

        def sc_s1(t):
            sco, s_ = tiles[t]
            b3 = t % 3
            st_ = sst[t % 3]
            sk = f"sst{t % 3}"
            for cb in range(2):
                px = PS[0][:, cb * 512:(cb + 1) * 512]
                for kt in range(8):
                    MM(px, retT_sc[sco][:, kt, s_::8], wout[:, kt, cb * 512:(cb + 1) * 512], kt == 0, False, [f"retT_sc{sco}", "wout"], [f"PX{cb}"])
                for kt in range(8):
                    MM(px, ssmT_s[b3][:, kt, :], wout[:, 8 + kt, cb * 512:(cb + 1) * 512], False, kt == 7, [f"ssmT_s{b3}", "wout"], [f"PX{cb}"])
            TT("dve", x1[b3], PS[0], x_t[b3], ALU.add, ["PX0", "PX1", f"x_t{b3}"], [f"x1_{b3}"])
            ACT(junk, x1[b3], AF.Square, [f"x1_{b3}"], ["junk", sk + "a"], accum_out=st_[:, 0:1])
            ACT(st_[:, 1:2], st_[:, 0:1], AF.Ln, [sk + "a"], [sk + "b"], bias=epsc[:, 0:1], scale=1.0 / 1024.0)
            ACT(st_[:, 2:3], st_[:, 1:2], AF.Exp, [sk + "b"], [sk + "c"], scale=-0.5)
            STT(h2[t % 2], x1[b3], st_[:, 2:3], g2r, ALU.mult, ALU.mult, [f"x1_{b3}", sk + "c", "g2r"], [f"h2_{t % 2}"])

        def sc_s2(t):
            st_ = sst[t % 3]
            sk = f"sst{t % 3}"
            hh = h2[t % 2]
            for kt in range(8):
                TR(TRc[:, kt * 128:(kt + 1) * 128], hh[:, kt * 128:(kt + 1) * 128], identb[:], [f"h2_{t % 2}", "identb"], ["TRc"])
            CP("act", h2T.rearrange("p k t -> p (k t)"), TRc, ["TRc"], ["h2T"])
            for cb in range(2):
                pq = PS[1][:, cb * 512:(cb + 1) * 512]
                for kt in range(8):
                    MM(pq, h2T[:, kt, :], Wqk[:, kt, cb * 512:(cb + 1) * 512], kt == 0, kt == 7, ["h2T", "Wqk"], [f"PQ{cb}"])
            RED(st_[:, 4:8], PS[1].rearrange("p (h m) -> p h m", h=4), ALU.max, ["PQ0", "PQ1"], [sk + "mx"])
            TS("dve", st_[:, 8:12], st_[:, 4:8], -1.0, None, ALU.mult, None, [sk + "mx"], [sk + "nmx"])
            for h in range(4):
                ACT(Pex[:, h, :], PS[1][:, h * 256:(h + 1) * 256], AF.Exp, ["PQ0", "PQ1", sk + "nmx"], ["Pex", sk + f"sm{h}"],
                    bias=st_[:, 8 + h:9 + h], scale=1.0, accum_out=st_[:, 12 + h:13 + h])
            RECIP(st_[:, 16:20], st_[:, 12:16], [sk + f"sm{h}" for h in range(4)], [sk + "rsm"])
            TT("pool", Pn[t % 2], Pex, st_[:, 16:20].unsqueeze(2).to_broadcast([128, 4, 256]), ALU.mult, ["Pex", sk + "rsm"], [f"Pn{t % 2}"])

        def sc_s3(t):
            sco, s_ = tiles[t]
            b3 = t % 3
            st_ = sst[t % 3]
            sk = f"sst{t % 3}"
            b = t % 2
            Pn2 = Pn[b].rearrange("p h m -> p (h m)")
            for idx in range(8):
                TR(TRd[:, idx * 128:(idx + 1) * 128], Pn2[:, idx * 128:(idx + 1) * 128], identb[:], [f"Pn{b}", "identb"], ["TRd"])
            CP("act", PTt.rearrange("p k t -> p (k t)"), TRd, ["TRd"], ["PTt"])
            for cb in range(2):
                po = PS[2][:, cb * 512:(cb + 1) * 512]
                for idx in range(8):
                    MM(po, PTt[:, idx, :], VO[:, idx, cb * 512:(cb + 1) * 512], idx == 0, idx == 7, ["PTt", "VO"], [f"PO{cb}"])
            TT("dve", x2, PS[2], x1[b3], ALU.add, ["PO0", "PO1", f"x1_{b3}"], ["x2"])
            ACT(junk, x2, AF.Square, ["x2"], ["junk", sk + "d"], accum_out=st_[:, 20:21])
            ACT(st_[:, 21:22], st_[:, 20:21], AF.Ln, [sk + "d"], [sk + "e"], bias=epsc[:, 0:1], scale=1.0 / 1024.0)
            ACT(st_[:, 22:23], st_[:, 21:22], AF.Exp, [sk + "e"], [sk + "f"], scale=-0.5)
            STT(osb[b], x2, st_[:, 22:23], gfr, ALU.mult, ALU.mult, ["x2", sk + "f", "gfr"], [f"osb{b}"])
            dma(out_d[sco, :, s_, :], osb[b], r=[f"osb{b}"])

        NTL = len(tiles)
        sc_load(0)
        for i in range(NTL + 2):
            if i + 1 < NTL:
                sc_load(i + 1)
            if i < NTL:
                sc_s1(i)
            if 0 <= i - 2 < NTL:
                sc_s3(i - 2)
            if 0 <= i - 1 < NTL:
                sc_s2(i - 1)
        return finish(nc, P, st, out_d)


def finish(nc, P, st, out_d):
    P.barrier()
    P.op("sp", None)
    P.emit(st)
    return nc


def _const_tables():
    H = 8
    log_g = np.log1p(-np.exp2(-5.0 - np.arange(H, dtype=np.float64)))
    i = np.arange(128, dtype=np.float64)
    sqk = np.zeros((128, 16), np.float64)
    sqk[:, 0:8] = np.exp(log_g[None, :] * (i[:, None] + 1.0))
    sqk[:, 8:16] = np.exp(-log_g[None, :] * (i[:, None] + 1.0)) * 0.125
    cdt = np.broadcast_to(np.exp(log_g * 128.0)[None, :], (128, 8)).copy()
    j = np.arange(128)
    maskT = (j[None, :] >= j[:, None]).astype(np.float32)
    s_idx = np.arange(128) // 16
    t0mask = (s_idx[None, :] >= s_idx[:, None]).astype(np.float32)
    invf = (10000.0 ** (-np.arange(32, dtype=np.float32) / np.float32(32))).astype(np.float32)
    invf = np.broadcast_to(invf[None, :], (128, 32)).copy()
    return (sqk.astype(np.float32), cdt.astype(np.float32), maskT, t0mask, invf, np.eye(128, dtype=np.float32))


def _kt(w):
    K, N = w.shape
    return np.ascontiguousarray(w.reshape(K // 128, 128, N).transpose(1, 0, 2))


def make_in_maps(inputs, mode=None):
    mode = mode or MODE
    NPRE_T = 6144 if mode == "prefix" else 0
    NT = NPRE_T + 2048
    f = lambda a: np.asarray(a, dtype=np.float32)
    x = f(inputs["x"])
    mem = f(inputs["mem"])
    pos = np.asarray(inputs["positions"]).astype(np.int32)
    sqk, cdt, maskT, t0mask, invf, ident = _const_tables()
    rep = lambda v: np.ascontiguousarray(np.broadcast_to(f(v).reshape(1, -1), (128, f(v).size)))
    cols = lambda v: np.ascontiguousarray(f(v).reshape(8, 128).T)

    def s5lay(a):
        a = f(a)
        rest = a.shape[2:]
        a = a.reshape((2, 32, 64) + rest)
        a = np.moveaxis(a, 2, 1)
        return np.ascontiguousarray(a.reshape((128, 32) + rest))

    a_re = s5lay(inputs["s5_a_re"][0])
    a_im = s5lay(inputs["s5_a_im"][0])
    ldt = s5lay(np.broadcast_to(f(inputs["s5_log_dt"][0])[:, None], (64, 64)))
    b_re = s5lay(inputs["s5_b_re"][0])
    b_im = s5lay(inputs["s5_b_im"][0])
    c_re = s5lay(np.transpose(f(inputs["s5_c_re"][0]), (0, 2, 1)))
    c_im = s5lay(np.transpose(f(inputs["s5_c_im"][0]), (0, 2, 1)))
    shared = dict(
        invf=invf, sqk=sqk, cdt=cdt, maskT=maskT, t0mask=t0mask, ident=ident,
        w_in=_kt(f(inputs["w_in"][0])), g1=cols(inputs["norm1_g"][0]), gng=rep(inputs["ret_gn_g"][0]),
        s5are=a_re, s5aim=a_im, s5ldt=ldt, s5bre=b_re, s5bim=b_im, s5cre=c_re, s5cim=c_im,
        dcol=np.ascontiguousarray(f(inputs["s5_d"][0]).reshape(64, 16).T[np.arange(128) % 16, :]), glub=rep(inputs["s5_glu_b"][0]), g2rep=rep(inputs["norm2_g"][0]),
        gfrep=rep(inputs["norm_f_g"]), gm=cols(inputs["norm_mem_g"][0]),
        gluw=_kt(f(inputs["s5_glu_w"][0])), wout=_kt(f(inputs["w_out"][0])),
        wqT=_kt(np.ascontiguousarray(f(inputs["xa_wq"][0]).T)), wk=_kt(f(inputs["xa_wk"][0])),
        wv=_kt(f(inputs["xa_wv"][0])), wo=_kt(f(inputs["xa_wo"][0])),
    )
    log_g = np.log1p(-np.exp2(-5.0 - np.arange(8, dtype=np.float64)))
    maps = []
    for core in range(8):
        b, j = core // 4, core % 4
        m = dict(shared)
        t1 = (j + 1) * 2048
        t0 = t1 - NT
        lo = max(t0, 0)
        xl = np.zeros((NT, 1024), np.float32)
        pl = np.zeros((NT,), np.int32)
        xl[lo - t0:, :] = x[b, lo:t1, :]
        pl[lo - t0:] = pos[b, lo:t1]
        m["xT"] = _kt(np.ascontiguousarray(xl.T))
        m["xown"] = np.ascontiguousarray(x[b, j * 2048:(j + 1) * 2048, :].reshape(2, 128, 8, 1024))
        m["pos"] = np.ascontiguousarray(pl.reshape(NT // 128, 128).T)
        m["memT"] = _kt(np.ascontiguousarray(mem[b].T))
        wtr = np.zeros((128, 8, 4), np.float64)
        sel5 = np.zeros((128, 3, 8), np.float32)
        for rank in range(8):
            rb, ri = rank // 4, rank % 4
            if rb == b and ri < j:
                dist = j - ri
                sel5[:, dist - 1, rank] = 1.0
                for blk in range(4):
                    for par in range(2):
                        h = 2 * blk + par
                        wtr[par * 64:(par + 1) * 64, rank, blk] = np.exp(log_g[h] * 2048.0 * (dist - 1))
        m["wtr"] = wtr.astype(np.float32)
        m["sel5"] = sel5
        maps.append(m)
    return maps


def kernel(**inputs):
    nc = build_program()
    maps = make_in_maps(inputs)
    res = run_bass_kernel_spmd(nc, maps, core_ids=list(range(8)))
    out = np.zeros((2, 8192, 1024), np.float32)
    for core in range(8):
        b, j = core // 4, core % 4
        out[b, j * 2048:(j + 1) * 2048, :] = np.asarray(res.results[core]["out"]).reshape(2048, 1024)
    return out
```

```python
import os
import math
from contextlib import ExitStack
import numpy as np
import concourse.bass as bass
import concourse.mybir as mybir
from concourse.bass_utils import run_bass_kernel_spmd

F32 = mybir.dt.float32
BF16 = mybir.dt.bfloat16
I32 = mybir.dt.int32
ALU = mybir.AluOpType
AF = mybir.ActivationFunctionType
AX = mybir.AxisListType

SEG = 6000
NDMA = 24
EPS = 1e-6
NOWN_BLK = 8
MODE = "prefix"
TWO_PI = 2.0 * math.pi
KG = 2.0 * math.sqrt(2.0 / math.pi)


class Prog:
    ENGS = ("pe", "act", "dve", "pool", "sp")

    def __init__(self, nc):
        self.nc = nc
        self.ops = []
        self.last_w = {}
        self.readers = {}
        self.pending_barrier = {e: set() for e in self.ENGS}
        self._bar_at = 0

    def op(self, eng, fn, r=(), w=(), dma=False):
        oid = len(self.ops)
        deps = set()
        for k in r:
            if k in self.last_w:
                deps.add(self.last_w[k])
        for k in w:
            if k in self.last_w:
                deps.add(self.last_w[k])
            for rd in self.readers.get(k, ()):
                deps.add(rd)
        deps |= self.pending_barrier[eng]
        self.pending_barrier[eng] = set()
        deps.discard(oid)
        last_per_eng = {}
        keep = set()
        for d_ in deps:
            od = self.ops[d_]
            if od["dma"]:
                keep.add(d_)
            else:
                e_ = od["eng"]
                if e_ not in last_per_eng or d_ > last_per_eng[e_]:
                    last_per_eng[e_] = d_
        deps = keep | set(last_per_eng.values())
        self.ops.append(dict(eng=eng, fn=fn, deps=deps, dma=dma))
        for k in r:
            self.readers.setdefault(k, []).append(oid)
        for k in w:
            self.last_w[k] = oid
            self.readers[k] = []
        return oid

    def dma(self, fn, r=(), w=(), eng="sp"):
        return self.op(eng, fn, r, w, dma=True)

    def barrier(self):
        last = {}
        for i, o in enumerate(self.ops):
            last[o["eng"]] = i
        dm = [i for i in range(self._bar_at, len(self.ops)) if self.ops[i]["dma"]]
        s = set(last.values()) | set(dm)
        for e in self.ENGS:
            self.pending_barrier[e] |= s
        self._bar_at = len(self.ops)

    def emit(self, stack):
        nc = self.nc
        ops = self.ops
        n = len(ops)
        needed = [False] * n
        for i, o in enumerate(ops):
            nd = set()
            for d in o["deps"]:
                od = ops[d]
                if od["eng"] == "pe" and o["eng"] == "pe":
                    continue
                if od["eng"] == o["eng"] and not od["dma"] and od["fn"] is None:
                    continue
                nd.add(d)
            o["deps"] = nd
            for d in nd:
                needed[d] = True
        cnt = {e: 0 for e in self.ENGS}
        nseg = {e: 0 for e in self.ENGS}
        ndma = {"sp": 0, "pool": 0}
        dma_ops = {"sp": [], "pool": []}
        for i, o in enumerate(ops):
            if o["dma"]:
                q = o["eng"]
                nd = ndma[q]
                o["sig"] = ("dma_" + q, nd % NDMA, 16 * (nd // NDMA + 1), nd)
                dma_ops[q].append(o)
                ndma[q] += 1
            elif needed[i]:
                e = o["eng"]
                c = cnt[e]
                o["sig"] = (e, c // SEG, c % SEG + 1)
                cnt[e] += 1
                nseg[e] = c // SEG + 1
            else:
                o["sig"] = None
        sems = {}
        for e in self.ENGS:
            for s in range(nseg[e]):
                sems[(e, s)] = stack.enter_context(nc.semaphore(f"s_{e}_{s}"))
        for q in ("sp", "pool"):
            for k in range(min(NDMA, ndma[q])):
                sems[("dma_" + q, k)] = stack.enter_context(nc.semaphore(f"s_dma_{q}_{k}"))
        block = stack.enter_context(nc.Block())

        def make(ename):
            def body(eng):
                waited = {}
                for i, o in enumerate(ops):
                    if o["eng"] != ename:
                        continue
                    req = {}
                    for d in o["deps"]:
                        sg = ops[d]["sig"]
                        key = (sg[0], sg[1])
                        req[key] = max(req.get(key, 0), sg[2])
                    if o["dma"]:
                        idx = o["sig"][3]
                        if idx >= NDMA:
                            prev = dma_ops[o["eng"]][idx - NDMA]["sig"]
                            key = (prev[0], prev[1])
                            req[key] = max(req.get(key, 0), prev[2])
                    for key in sorted(req.keys(), key=lambda k: (k[0], k[1])):
                        val = req[key]
                        if waited.get(key, 0) >= val:
                            continue
                        if not key[0].startswith("dma"):
                            if any(k[0] == key[0] and k[1] > key[1] for k in waited):
                                continue
                        eng.wait_ge(sems[key], val)
                        waited[key] = val
                    if o["fn"] is None:
                        continue
                    ins = o["fn"](eng)
                    sg = o["sig"]
                    if sg is not None:
                        if sg[0].startswith("dma"):
                            ins.then_inc(sems[(sg[0], sg[1])], 16)
                        else:
                            ins.then_inc(sems[(sg[0], sg[1])], 1)
            return body

        block.tensor(make("pe"))
        block.scalar(make("act"))
        block.vector(make("dve"))
        block.gpsimd(make("pool"))
        block.sync(make("sp"))


class Arena:
    def __init__(self, t, nwords):
        self.t = t
        self.n = nwords
        self.off = 0

    def reset(self, off=0):
        self.off = off

    def take(self, shape, dtype):
        nel = 1
        for s in shape[1:]:
            nel *= s
        sz = 2 if dtype == BF16 else 4
        nw = (nel * sz + 3) // 4
        nw = (nw + 7) // 8 * 8
        assert self.off + nw <= self.n, f"arena overflow {self.off}+{nw}>{self.n}"
        ap = self.t[:, self.off:self.off + nw]
        self.off += nw
        if dtype != F32:
            ap = ap.bitcast(dtype)
        ap = ap[:, 0:nel]
        fd = shape[1:]
        if len(fd) == 2:
            ap = ap.rearrange("p (a b) -> p a b", a=fd[0])
        elif len(fd) == 3:
            ap = ap.rearrange("p (a b c) -> p a b c", a=fd[0], b=fd[1])
        elif len(fd) == 4:
            ap = ap.rearrange("p (a b c d) -> p a b c d", a=fd[0], b=fd[1], c=fd[2])
        return ap


def build_program(dbg=None, stop_after=None, mode=None):
    mode = mode or MODE
    NPRE = 24 if mode == "prefix" else 0
    NT = NPRE * 256 + 2048
    NCH = NT // 128
    FAKE_CC = isinstance(dbg, dict) and bool(dbg.get("fake_cc"))
    nc = bass.Bass("TRN2", target_bir_lowering=False)

    def din(name, shape, dt=F32):
        return nc.dram_tensor(name, list(shape), dt, kind="ExternalInput").ap()

    xT_d = din("xT", [128, 8, NT])
    xown_d = din("xown", [2, 128, 8, 1024])
    pos_d = din("pos", [128, NCH], I32)
    invf_d = din("invf", [128, 32])
    sqk_d = din("sqk", [128, 16])
    cdt_d = din("cdt", [128, 8])
    wtr_d = din("wtr", [128, 8, 4])
    sel5_d = din("sel5", [128, 3, 8])
    maskT_d = din("maskT", [128, 128])
    t0mask_d = din("t0mask", [128, 128])
    ident_d = din("ident", [128, 128])
    w_in_d = din("w_in", [128, 8, 5120])
    g1_d = din("g1", [128, 8])
    gng_d = din("gng", [128, 1024])
    are_d = din("s5are", [128, 32])
    aim_d = din("s5aim", [128, 32])
    ldt_d = din("s5ldt", [128, 32])
    bre_d = din("s5bre", [128, 32, 16])
    bim_d = din("s5bim", [128, 32, 16])
    cre_d = din("s5cre", [128, 32, 16])
    cim_d = din("s5cim", [128, 32, 16])
    dcol_d = din("dcol", [128, 64])
    glub_d = din("glub", [128, 1024])
    g2_d = din("g2rep", [128, 1024])
    gf_d = din("gfrep", [128, 1024])
    gm_d = din("gm", [128, 8])
    gluw_d = din("gluw", [128, 8, 1024])
    wout_d = din("wout", [128, 16, 1024])
    wqT_d = din("wqT", [128, 8, 1024])
    wk_d = din("wk", [128, 8, 1024])
    wv_d = din("wv", [128, 8, 1024])
    wo_d = din("wo", [128, 8, 1024])
    memT_d = din("memT", [128, 8, 256])
    out_d = nc.dram_tensor("out", [2, 128, 8, 1024], F32, kind="ExternalOutput").ap()

    skind = dict(kind="ExternalOutput") if dbg else {}
    u_d = nc.dram_tensor("u_scr", [NT, 1024], BF16, **skind).ap()
    ccR_in = nc.dram_tensor("ccR_in", [128, 512], F32)
    ccR_out = nc.dram_tensor("ccR_out", [1024, 512], F32)
    ccS_in = nc.dram_tensor("ccS_in", [128, 64], F32)
    ccS_out = nc.dram_tensor("ccS_out", [1024, 64], F32)
    g5_d = nc.dram_tensor("g5_scr", [2048, 1024], BF16, **skind).ap()
    retT_d = nc.dram_tensor("retT_scr", [128, 8, 2048], BF16, **skind).ap()
    ssmT_d = nc.dram_tensor("ssmT_scr", [2, 8, 128, 8, 128], BF16, **skind).ap()

    with ExitStack() as st:
        P = Prog(nc)
        sb = lambda name, shape, dt=F32: st.enter_context(nc.sbuf_tensor("sb_" + name, list(shape), dt))
        ARW = (dbg.get("arw") if isinstance(dbg, dict) and dbg.get("arw") else None) or (49200 if mode == "prefix" else 47200)
        arena_t = sb("arena", [128, ARW])
        AR = Arena(arena_t, ARW)
        identf = sb("identf", [128, 128])
        identb = sb("identb", [128, 128], BF16)
        ones_f = sb("ones_f", [128, 128])
        ones_b = sb("ones_b", [128, 128], BF16)
        epsc = sb("epsc", [128, 1])
        maskT = sb("maskT", [128, 128])
        t0mask = sb("t0mask", [128, 128])
        invf = sb("invf", [128, 32])
        sqk = sb("sqk", [128, 16])
        cdt = sb("cdt", [128, 8])
        wtr = sb("wtr", [128, 8, 4])
        sel5 = sb("sel5", [128, 3, 8])
        A2r = sb("A2r", [128, 2, 32])
        A2i = sb("A2i", [128, 2, 32])
        g1 = sb("g1", [128, 8])
        gm = sb("gm", [128, 8])
        Rf = sb("Rf", [128, 8, 128])
        Rbf = sb("Rbf", [128, 8, 128], BF16)
        car_r = sb("car_r", [128, 32])
        car_i = sb("car_i", [128, 32])
        PS = [st.enter_context(nc.psum_tensor(f"PS{i}", [128, 1024], F32))[:] for i in range(4)]

        def dma(out, in_, r=(), w=(), eng="sp"):
            P.dma(lambda e, out=out, in_=in_: e.dma_start(out=out, in_=in_), r=r, w=w, eng=eng)

        def TT(eng, out, in0, in1, op, r, w):
            P.op(eng, lambda e, out=out, in0=in0, in1=in1, op=op: e.tensor_tensor(out=out, in0=in0, in1=in1, op=op), r=r, w=w)

        def TS(eng, out, in0, s1, s2, op0, op1, r, w):
            if op1 is None:
                P.op(eng, lambda e, out=out, in0=in0, s1=s1, op0=op0: e.tensor_scalar(out=out, in0=in0, scalar1=s1, scalar2=None, op0=op0), r=r, w=w)
            else:
                P.op(eng, lambda e, out=out, in0=in0, s1=s1, s2=s2, op0=op0, op1=op1: e.tensor_scalar(out=out, in0=in0, scalar1=s1, scalar2=s2, op0=op0, op1=op1), r=r, w=w)

        def STT(out, in0, scalar, in1, op0, op1, r, w):
            P.op("dve", lambda e, out=out, in0=in0, scalar=scalar, in1=in1, op0=op0, op1=op1: e.scalar_tensor_tensor(out=out, in0=in0, scalar=scalar, in1=in1, op0=op0, op1=op1), r=r, w=w)

        def ACT(out, in_, func, r, w, **kw):
            P.op("act", lambda e, out=out, in_=in_, func=func, kw=kw: e.activation(out=out, in_=in_, func=func, **kw), r=r, w=w)

        def CP(eng, out, in_, r, w):
            if eng == "act":
                ACT(out, in_, AF.Copy, r, w)
            else:
                P.op(eng, lambda e, out=out, in_=in_: e.tensor_copy(out=out, in_=in_), r=r, w=w)

        def RECIP(out, in_, r, w):
            P.op("dve", lambda e, out=out, in_=in_: e.reciprocal(out=out, in_=in_), r=r, w=w)

        def RED(out, in_, op, r, w, axis=AX.X):
            P.op("dve", lambda e, out=out, in_=in_, op=op, axis=axis: e.tensor_reduce(out=out, in_=in_, axis=axis, op=op), r=r, w=w)

        def MM(out, lhsT, rhs, start, stop, r, w):
            P.op("pe", lambda e, out=out, lhsT=lhsT, rhs=rhs, start=start, stop=stop: e.matmul(out, lhsT=lhsT, rhs=rhs, start=start, stop=stop), r=r, w=w)

        def TR(out, in_, ident, r, w):
            P.op("pe", lambda e, out=out, in_=in_, ident=ident: e.transpose(out=out, in_=in_, identity=ident), r=r, w=w)

        def MEMSET(eng, ap, val, w):
            P.op(eng, lambda e, ap=ap, val=val: e.memset(ap, val), w=w)

        dma(identf[:], ident_d, w=["identf"])
        dma(maskT[:], maskT_d, w=["maskT"])
        dma(t0mask[:], t0mask_d, w=["t0mask"])
        dma(invf[:], invf_d, w=["invf"])
        dma(sqk[:], sqk_d, w=["sqk"])
        dma(cdt[:], cdt_d, w=["cdt"])
        dma(wtr[:], wtr_d, w=["wtr"])
        dma(sel5[:], sel5_d, w=["sel5"])
        dma(g1[:], g1_d, w=["g1"])
        dma(gm[:], gm_d, w=["gm"])
        CP("dve", identb[:], identf[:], ["identf"], ["identb"])
        MEMSET("pool", ones_f[:], 1.0, ["ones_f"])
        MEMSET("pool", ones_b[:], 1.0, ["ones_b"])
        MEMSET("pool", epsc[:], EPS, ["epsc"])
        MEMSET("pool", Rf[:], 0.0, ["Rf"])
        MEMSET("pool", Rbf[:], 0.0, ["Rbf"])
        MEMSET("pool", car_r[:], 0.0, ["car_r"])
        MEMSET("pool", car_i[:], 0.0, ["car_i"])

        def sincos(ang, shp, s_out, c_out, tmps, key):
            kf, ki, rr, mm = tmps
            rk = [key]
            TS("dve", kf, ang, 1.0 / TWO_PI, None, ALU.mult, None, rk, [key + "kf"])
            CP("dve", ki, kf, [key + "kf"], [key + "ki"])
            CP("dve", kf, ki, [key + "ki"], [key + "kf"])
            STT(rr, kf, -6.28125, ang, ALU.mult, ALU.add, [key + "kf"] + rk, [key + "rr"])
            STT(rr, kf, -(TWO_PI - 6.28125), rr, ALU.mult, ALU.add, [key + "kf", key + "rr"], [key + "rr"])

            def wrap(t):
                TS("dve", mm, t, -math.pi, None, ALU.is_lt, None, [key + "rr"], [key + "mm"])
                STT(t, mm, TWO_PI, t, ALU.mult, ALU.add, [key + "mm", key + "rr"], [key + "rr"])
                TS("dve", mm, t, math.pi, None, ALU.is_gt, None, [key + "rr"], [key + "mm"])
                STT(t, mm, -TWO_PI, t, ALU.mult, ALU.add, [key + "mm", key + "rr"], [key + "rr"])
            wrap(rr)
            ACT(s_out, rr, AF.Sin, [key + "rr"], [key + "s"])
            TS("dve", rr, rr, math.pi / 2.0, None, ALU.add, None, [key + "rr", key + "s"], [key + "rr"])
            wrap(rr)
            ACT(c_out, rr, AF.Sin, [key + "rr"], [key + "c"])

        def norm_fm(xt, N, gcol, sq, rs1, rs2, hT, psum_ap, kx, ksq, krs, kh, kps, gkey):
            ACT(sq, xt, AF.Square, [kx], [ksq])
            for kt in range(8):
                MM(psum_ap, ones_b[:], sq[:, kt, :], kt == 0, kt == 7, [ksq, "ones_b"], [kps])
            ACT(rs1, psum_ap, AF.Sqrt, [kps], [krs + "1"], bias=EPS, scale=1.0 / 1024.0)
            RECIP(rs2, rs1, [krs + "1"], [krs])
            for kt in range(8):
                STT(hT[:, kt, :], xt[:, kt, :], gcol[:, kt:kt + 1], rs2, ALU.mult, ALU.mult, [kx, krs, gkey], [kh])

        AR.reset()
        w_in = AR.take([128, 8, 5120], BF16)
        cosT = AR.take([128, NCH, 32], F32)
        sinT = AR.take([128, NCH, 32], F32)
        gng = AR.take([128, 1024], F32)
        xTb = [AR.take([128, 8, 256], F32) for _ in range(2)]
        sq = AR.take([128, 8, 256], BF16)
        posi = AR.take([128, NCH], I32)
        posf = AR.take([128, NCH], F32)
        Rc = AR.take([128, 4, 128], F32)
        mark = AR.off
        tki = AR.take([128, NCH * 32], I32)
        tmm = AR.take([128, NCH * 32], F32)
        trr_t = AR.take([128, NCH * 32], F32)
        AR.reset(mark)
        Gt = AR.take([128, 8, 4, 128], F32)
        AR.reset(mark)
        hTb = [AR.take([128, 8, 256], BF16) for _ in range(2)]
        rs1 = AR.take([128, 256], F32)
        rs2 = AR.take([128, 256], F32)
        qk = AR.take([128, 16, 2, 32], F32)
        rt = [AR.take([128, 16, 32], F32) for _ in range(4)]
        qkt = [AR.take([128, 16, 64], BF16) for _ in range(2)]
        v_bf = [AR.take([128, 1024], BF16) for _ in range(2)]
        gr_bf = [AR.take([128, 1024], BF16) for _ in range(2)]
        g5_bf = [AR.take([128, 1024], BF16) for _ in range(2)]
        u_bf = [AR.take([128, 1024], BF16) for _ in range(2)]
        kT = [AR.take([128, 4, 128], BF16) for _ in range(2)]
        qm = [AR.take([128, 8, 128], BF16) for _ in range(2)]
        PT = AR.take([128, 8, 128], BF16)
        o_sb = AR.take([128, 8, 128], F32)
        osq = AR.take([128, 8, 128], F32)
        ret = [AR.take([128, 1024], BF16) for _ in range(2)]
        retT_blk = AR.take([128, 8, 256], BF16)
        stt = AR.take([128, 48], F32)
        ang = xTb[0].rearrange("p a b -> p (a b)")[:, 0:NCH * 32]
        tkf = xTb[1].rearrange("p a b -> p (a b)")[:, 0:NCH * 32]
        trr = trr_t[:, 0:NCH * 32]

        dma(gng, gng_d, w=["gng"])
        MEMSET("pool", qm[0], 0.0, ["qm0"])
        MEMSET("pool", qm[1], 0.0, ["qm1"])
        dma(posi, pos_d, w=["posi"])
        CP("dve", posf, posi, ["posi"], ["posf"])
        TT("dve", ang.rearrange("p (c i) -> p c i", i=32), posf.unsqueeze(2).to_broadcast([128, NCH, 32]),
           invf[:].unsqueeze(1).to_broadcast([128, NCH, 32]), ALU.mult, ["posf", "invf", "xT0"], ["xT0"])
        sincos(ang, None, sinT.rearrange("p c i -> p (c i)"), cosT.rearrange("p c i -> p (c i)"),
               (tkf, tki, trr, tmm), "xT0")
        P.barrier()
        for cb in (1, 2, 3, 6, 7, 0, 4, 5, 8, 9):
            cs_ = slice(cb * 512, (cb + 1) * 512)
            dma(w_in[:, :, cs_], w_in_d[:, :, cs_], w=[f"w_in{cb}"], eng="pool")
            TT("dve", w_in[:, :, cs_], w_in[:, :, cs_], g1[:].unsqueeze(2).to_broadcast([128, 8, 512]), ALU.mult, [f"w_in{cb}", "g1"], [f"w_in{cb}"])

        qk2 = qk.rearrange("p h t i -> p (h t i)")
        qkt2 = [q_.rearrange("p h d -> p (h d)") for q_ in qkt]
        o2 = o_sb.rearrange("p h e -> p (h e)")
        PSUM_R2 = (mode == "coll") and not (isinstance(dbg, dict) and dbg.get("psum_new"))
        if PSUM_R2:
            KVp = PS[3].rearrange("p (h e) -> p h e", h=8)
            TRb = PS[0][:, 512:1024].bitcast(BF16)
            kvi = lambda h: h
            kvs = lambda par: slice(par, 8, 2)
        else:
            KVp = PS[3][:, 0:512].rearrange("p (b e) -> p b e", b=4)
            TRb = PS[3][:, 512:1024].bitcast(BF16)
            kvi = lambda h: h // 2
            kvs = lambda par: slice(0, 4, 1)
        SCp = PS[1]
        Op = PS[2]
        pj_i = [0]

        pj_n = [2]

        def pj():
            if PSUM_R2:
                return PS[0][:, 0:512], "PJ0"
            i = pj_i[0] % pj_n[0]
            pj_i[0] += 1
            return PS[i // 2][:, (i % 2) * 512:(i % 2 + 1) * 512], f"PJ{i}"

        nblk = NOWN_BLK
        LVL = dbg.get('own_lvl', 9) if isinstance(dbg, dict) else 9
        if mode == "coll":
            sa_iter = [("light", b_) for b_ in range(nblk)] + [("xchg", -1)] + [("full", b_) for b_ in range(nblk)]
            light_cbs = [1, 2, 3]
        else:
            sa_iter = [("light", b_) for b_ in range(NPRE)] + [("full", NPRE + b_) for b_ in range(nblk)]
            light_cbs = [1, 2, 3, 6, 7]
        def do_xchg():
            P.barrier()
            dma(ccR_in.ap()[0:64, :].rearrange("p (b e) -> p b e", b=4), Rf[0:64, 0::2, :], r=["Rf"], w=["ccR_in"])
            dma(ccR_in.ap()[64:128, :].rearrange("p (b e) -> p b e", b=4), Rf[64:128, 1::2, :], r=["Rf"], w=["ccR_in"])
            if FAKE_CC:
                dma(ccR_out.ap()[0:128, :], ccR_in.ap(), r=["ccR_in"], w=["ccR_out"])
            else:
                P.op("pool", lambda e: e.collective_compute("AllGather", ALU.bypass, replica_groups=[list(range(8))],
                                                            ins=[ccR_in.ap().opt()], outs=[ccR_out.ap().opt()]),
                     r=["ccR_in"], w=["ccR_out"])
            dma(Gt.rearrange("p r b e -> p r (b e)"), ccR_out.ap().rearrange("(r p) c -> p r c", p=128), r=["ccR_out"], w=["Gt"])
            TT("dve", Gt, Gt, wtr[:].unsqueeze(3).to_broadcast([128, 8, 4, 128]), ALU.mult, ["Gt", "wtr"], ["Gt"])
            RED(Rc, Gt.rearrange("p r b e -> p b e r"), ALU.add, ["Gt"], ["Rc"])
            CP("act", Rf[0:64, 0::2, :], Rc[0:64, :, :], ["Rc", "Rf"], ["Rf"])
            CP("act", Rf[64:128, 1::2, :], Rc[64:128, :, :], ["Rc", "Rf"], ["Rf"])
            CP("pool", Rbf[:], Rf[:], ["Rf"], ["Rbf"])
            P.barrier()


        def load_block(blk):
            dma(xTb[blk % 2], xT_d[:, :, blk * 256:(blk + 1) * 256], w=[f"xT{blk % 2}"])

        def norm_block(blk):
            xb = xTb[blk % 2]
            kx = f"xT{blk % 2}"
            bb = blk % 2
            ACT(sq, xb, AF.Square, [kx], ["sq"])
            ACT(hTb[bb], xb, AF.Copy, [kx], [f"hT{bb}"])
            pa, pk = pj()
            for ci in range(2):
                for kt in range(8):
                    MM(pa[:, ci:ci + 1], sq[:, kt, ci * 128:(ci + 1) * 128], ones_b[:, 0:1], kt == 0, kt == 7, ["sq", "ones_b"], [pk])
            ACT(rs1[:, 2 * bb:2 * bb + 2], pa[:, 0:2], AF.Sqrt, [pk], [f"rsa{bb}"], bias=EPS, scale=1.0 / 1024.0)
            RECIP(rs2[:, 2 * bb:2 * bb + 2], rs1[:, 2 * bb:2 * bb + 2], [f"rsa{bb}"], [f"rsb{bb}"])

        def phaseA(pas, blk, ci):
            own = pas == "full"
            c = blk * 2 + ci
            d = c % 2
            hT = hTb[blk % 2]
            hk = f"hT{blk % 2}"
            hs = lambda kt: hT[:, kt, ci * 128:(ci + 1) * 128]
            cbs = [0, 1, 2, 3, 4, 5, 6, 7, 8, 9] if own else light_cbs
            for cb in cbs:
                pa, pk = pj()
                for kt in range(8):
                    MM(pa, hs(kt), w_in[:, kt, cb * 512:(cb + 1) * 512], kt == 0, kt == 7, [hk, f"w_in{cb}"], [pk])
                rcol = rs2[:, 2 * (blk % 2) + ci:2 * (blk % 2) + ci + 1]
                rk = f"rsb{blk % 2}"
                if cb == 0:
                    ACT(qk2[:, 0:512], pa, AF.Copy, [pk, rk], ["qk_q"], scale=rcol)
                elif cb == 1:
                    ACT(qk2[:, 512:1024], pa, AF.Copy, [pk, rk], ["qk_k"], scale=rcol)
                elif cb in (2, 3):
                    TS("dve", v_bf[d][:, (cb - 2) * 512:(cb - 1) * 512], pa, rcol, None, ALU.mult, None, [pk, rk], [f"v_bf{d}"])
                elif cb in (4, 5):
                    ACT(gr_bf[d][:, (cb - 4) * 512:(cb - 3) * 512], pa, AF.Silu, [pk, rk], [f"gr_bf{d}"], scale=rcol)
                elif cb in (6, 7):
                    TS("dve", u_bf[d][:, (cb - 6) * 512:(cb - 5) * 512], pa, rcol, None, ALU.mult, None, [pk, rk], [f"u_bf{d}"])
                else:
                    ACT(g5_bf[d][:, (cb - 8) * 512:(cb - 7) * 512], pa, AF.Silu, [pk, rk], [f"g5_bf{d}"], scale=rcol)
            if own or 6 in light_cbs:
                dma(u_d[c * 128:(c + 1) * 128, :], u_bf[d], r=[f"u_bf{d}"])
            if own:
                oc = c - 2 * NPRE
                dma(g5_d[oc * 128:(oc + 1) * 128, :], g5_bf[d], r=[f"g5_bf{d}"])
            h0 = 0 if own else 8
            nh = 16 - h0
            cosb = cosT[:, c, :].unsqueeze(1).to_broadcast([128, nh, 32])
            sinb = sinT[:, c, :].unsqueeze(1).to_broadcast([128, nh, 32])
            x1 = qk[:, h0:16, 0, :]
            x2 = qk[:, h0:16, 1, :]
            rq = ["qk_q", "qk_k"] if own else ["qk_k"]
            qk3 = qk[:, h0:16, :, :].rearrange("p h t i -> p h (t i)")
            TT("pool", qk3, qk3, sqk[:, h0:16].unsqueeze(2).to_broadcast([128, nh, 64]), ALU.mult, rq + ["sqk"], rq)
            q4 = qkt[d].rearrange("p h (t i) -> p h t i", t=2)
            TT("dve", rt[0][:, h0:16, :], x1, cosb, ALU.mult, rq, ["rt0"])
            TT("dve", rt[1][:, h0:16, :], x2, sinb, ALU.mult, rq, ["rt1"])
            TT("dve", q4[:, h0:16, 0, :], rt[0][:, h0:16, :], rt[1][:, h0:16, :], ALU.subtract, ["rt0", "rt1"], [f"qkt{d}"])
            TT("pool", rt[2][:, h0:16, :], x1, sinb, ALU.mult, rq, ["rt2"])
            TT("pool", rt[3][:, h0:16, :], x2, cosb, ALU.mult, rq, ["rt3"])
            TT("pool", q4[:, h0:16, 1, :], rt[2][:, h0:16, :], rt[3][:, h0:16, :], ALU.add, ["rt2", "rt3"], [f"qkt{d}"])
            if own:
                for b in range(8):
                    TR(TRb[:, b * 128:(b + 1) * 128], qkt2[d][:, b * 128:(b + 1) * 128], identb[:], [f"qkt{d}", "identb"], ["TRb"])
                CP("act", kT[d].rearrange("p b t -> p (b t)"), TRb[:, 512:1024], ["TRb"], [f"kT{d}"])
                CP("act", qm[d][0:64, 0::2, :], TRb[0:64, 0:512].rearrange("p (b t) -> p b t", b=4), ["TRb"], [f"qm{d}"])
                CP("act", qm[d][64:128, 1::2, :], TRb[64:128, 0:512].rearrange("p (b t) -> p b t", b=4), ["TRb"], [f"qm{d}"])

        def phaseB(pas, blk, ci):
            own = pas == "full"
            c = blk * 2 + ci
            d = c % 2
            for h in range(8):
                MM(KVp[(h % 2) * 64:(h % 2) * 64 + 64, kvi(h), :], qkt2[d][:, 512 + h * 64:512 + (h + 1) * 64],
                   v_bf[d][:, h * 128:(h + 1) * 128], True, True, [f"qkt{d}", f"v_bf{d}"], ["KVp"])
            if own:
                for h in range(8):
                    MM(SCp[:, h * 128:(h + 1) * 128], kT[d][:, h // 2, :], qm[d][:, h, :], True, True, [f"kT{d}", f"qm{d}"], ["SCp"])
                TT("dve", PT, SCp.rearrange("p (h i) -> p h i", h=8), maskT[:].unsqueeze(1).to_broadcast([128, 8, 128]),
                   ALU.mult, ["SCp", "maskT"], ["PT"])
                for h in range(8):
                    MM(Op[:, h * 128:(h + 1) * 128], PT[:, h, :], v_bf[d][:, h * 128:(h + 1) * 128], True, False, ["PT", f"v_bf{d}"], ["Op"])
                    MM(Op[:, h * 128:(h + 1) * 128], qm[d][:, h, :], Rbf[:, h, :], False, True, [f"qm{d}", "Rbf"], ["Op"])
            for par in range(2):
                rows = slice(par * 64, par * 64 + 64)
                TT("dve", Rf[rows, par::2, :], KVp[rows, kvs(par), :], Rf[rows, par::2, :], ALU.add, ["KVp", "Rf"], ["Rf"])
                TT("dve", Rf[rows, par::2, :], Rf[rows, par::2, :], cdt[rows, par::2].unsqueeze(2).to_broadcast([64, 4, 128]), ALU.mult, ["Rf", "cdt"], ["Rf"])
                CP("act", Rbf[rows, par::2, :], Rf[rows, par::2, :], ["Rf"], ["Rbf"])
            if own:
                CP("act", o2, Op, ["Op"], ["o_sb"])
                ACT(osq.rearrange("p h e -> p (h e)"), Op, AF.Square, ["Op"], ["osq"])
                RED(stt[:, 0:8], o_sb, ALU.add, ["o_sb"], ["st_s"])
                RED(stt[:, 8:16], osq, ALU.add, ["osq"], ["st_q"])
                TS("dve", stt[:, 16:24], stt[:, 0:8], 1.0 / 128.0, None, ALU.mult, None, ["st_s"], ["st_m"])
                TT("dve", stt[:, 24:32], stt[:, 16:24], stt[:, 16:24], ALU.mult, ["st_m"], ["st_m2"])
                STT(stt[:, 32:40], stt[:, 8:16], 1.0 / 128.0, stt[:, 24:32], ALU.mult, ALU.subtract, ["st_q", "st_m2"], ["st_v"])
                ACT(stt[:, 40:48], stt[:, 32:40], AF.Sqrt, ["st_v"], ["st_sd"], bias=EPS, scale=1.0)
                RECIP(stt[:, 32:40], stt[:, 40:48], ["st_sd"], ["st_rs"])
                TT("pool", o_sb, o_sb, stt[:, 16:24].unsqueeze(2).to_broadcast([128, 8, 128]), ALU.subtract, ["o_sb", "st_m"], ["o_sb"])
                TT("pool", o_sb, o_sb, stt[:, 32:40].unsqueeze(2).to_broadcast([128, 8, 128]), ALU.mult, ["o_sb", "st_rs"], ["o_sb"])
                TT("dve", o2, o2, gng, ALU.mult, ["o_sb", "gng"], ["o_sb"])
                TT("dve", ret[d], o2, gr_bf[d], ALU.mult, ["o_sb", f"gr_bf{d}"], [f"ret{d}"])

        def phaseC(pas, blk, ci):
            if pas != "full":
                return
            c = blk * 2 + ci
            d = c % 2
            for kt in range(8):
                TR(TRb[:, kt * 128:(kt + 1) * 128], ret[d][:, kt * 128:(kt + 1) * 128], identb[:], [f"ret{d}", "identb"], ["TRb"])
            CP("act", retT_blk[:, :, ci * 128:(ci + 1) * 128], TRb.rearrange("p (k t) -> p k t", k=8), ["TRb"], ["retT_blk"])
            if ci == 1:
                ob = blk - NPRE
                dma(retT_d[:, :, ob * 256:(ob + 1) * 256], retT_blk, r=["retT_blk"])

        pendB = None
        pendC = None
        real = [(p_, b_) for p_, b_ in sa_iter if p_ != "xchg"]
        has_x = any(p_ == "xchg" for p_, _ in sa_iter)
        load_block(real[0][1])
        norm_block(real[0][1])
        ri = 0
        prev_pas = None
        for pas, blk in sa_iter:
            if pas == "light" and not PSUM_R2:
                pj_n[0] = 4
            if pas == "full" and prev_pas == "light" and mode != "coll":
                if pendB is not None:
                    phaseB(*pendB)
                    pendB = None
                P.barrier()
            if pas == "full":
                pj_n[0] = 2
            prev_pas = pas
            if pas == "xchg":
                if pendB is not None:
                    phaseB(*pendB)
                    pendB = None
                do_xchg()
                load_block(real[ri][1])
                norm_block(real[ri][1])
                continue
            nxt = real[ri + 1] if ri + 1 < len(real) else None
            if nxt is not None and has_x and nxt[0] != pas:
                nxt = None
            for ci in range(2):
                if ci == 0 and nxt is not None:
                    load_block(nxt[1])
                phaseA(pas, blk, ci)
                if ci == 0 and nxt is not None:
                    norm_block(nxt[1])
                if pendC is not None:
                    phaseC(*pendC)
                    pendC = None
                if pendB is not None:
                    phaseB(*pendB)
                    pendC = pendB
                pendB = (pas, blk, ci)
            ri += 1
        if pendC is not None:
            phaseC(*pendC)
        if pendB is not None:
            phaseB(*pendB)
            phaseC(*pendB)
        P.barrier()
        if stop_after == "SA":
            return finish(nc, P, st, out_d)

        AR.reset()
        T0 = AR.take([128, 64, 128], BF16)
        I2Sre = AR.take([128, 64, 64], BF16)
        I2Sim = AR.take([128, 64, 64], BF16)
        S2Ore = AR.take([128, 32, 128], BF16)
        S2Oim = AR.take([128, 32, 128], BF16)
        TIre = AR.take([128, 32, 16], F32)
        TIim = AR.take([128, 32, 16], F32)
        TOre = AR.take([128, 32, 8], F32)
        TOim = AR.take([128, 32, 8], F32)
        P8 = AR.take([128, 32], F32)
        Q8 = AR.take([128, 32], F32)
        P128 = AR.take([128, 32], F32)
        Q128 = AR.take([128, 32], F32)
        base_sb = AR.off
        are = AR.take([128, 32], F32)
        aim = AR.take([128, 32], F32)
        ldt = AR.take([128, 32], F32)
        bre = AR.take([128, 32, 16], F32)
        bim = AR.take([128, 32, 16], F32)
        cre = AR.take([128, 32, 16], F32)
        cim = AR.take([128, 32, 16], F32)
        dtt = AR.take([128, 32], F32)
        tA = [AR.take([128, 32], F32) for _ in range(12)]
        tI = AR.take([128, 32], I32)
        PWr = AR.take([128, 9, 32], F32)
        PWi = AR.take([128, 9, 32], F32)
        NWr = AR.take([128, 9, 32], F32)
        NWi = AR.take([128, 9, 32], F32)
        E8r = AR.take([128, 17, 32], F32)
        E8i = AR.take([128, 17, 32], F32)
        E16r = AR.take([128, 9, 32], F32)
        E16i = AR.take([128, 9, 32], F32)
        Bbr = AR.take([128, 32, 16], F32)
        Bbi = AR.take([128, 32, 16], F32)
        Gre = AR.take([128, 32, 8, 16], F32)
        Gim = AR.take([128, 32, 8, 16], F32)
        Sre = AR.take([128, 32, 8, 16], F32)
        Sim = AR.take([128, 32, 8, 16], F32)
        Ire = AR.take([128, 32, 8, 16], F32)
        Iim = AR.take([128, 32, 8, 16], F32)
        tB = [AR.take([128, 32, 16], F32) for _ in range(2)]
        T0f = AR.take([128, 4, 128], F32)
        dcol = AR.take([128, 64], F32)

        K0 = ["S0B"]
        dma(dcol, dcol_d, w=K0)
        for t_, d_ in ((are, are_d), (aim, aim_d), (ldt, ldt_d), (bre, bre_d), (bim, bim_d), (cre, cre_d), (cim, cim_d)):
            dma(t_, d_, w=K0)
        P.barrier()
        V = lambda out, in0, in1, op: TT("dve", out, in0, in1, op, K0, K0)
        VS = lambda out, in0, s1, op: TS("dve", out, in0, s1, None, op, None, K0, K0)
        VC = lambda out, in_: CP("dve", out, in_, K0, K0)
        VM = lambda ap, val: MEMSET("dve", ap, val, K0)
        ACT(dtt, ldt, AF.Exp, K0, K0)
        V(tA[0], are, dtt, ALU.mult)
        ACT(tA[1], tA[0], AF.Exp, K0, K0)
        ACT(tA[2], tA[0], AF.Exp, K0, K0, scale=-1.0)
        V(tA[3], aim, dtt, ALU.mult)
        sincos(tA[3], None, tA[4], tA[5], (tA[6], tI, tA[7], tA[8]), "S0B")
        P.barrier()

        def cmul(orr, oi, xr, xi, yr, yi, tmp):
            u1, u2 = tmp
            V(u1, xr, yr, ALU.mult)
            V(u2, xi, yi, ALU.mult)
            V(orr, u1, u2, ALU.subtract)
            V(u1, xr, yi, ALU.mult)
            V(u2, xi, yr, ALU.mult)
            V(oi, u1, u2, ALU.add)

        tsm = (tA[10], tA[11])
        VM(PWr[:, 0, :], 1.0)
        VM(PWi[:, 0, :], 0.0)
        V(PWr[:, 1, :], tA[1], tA[5], ALU.mult)
        V(PWi[:, 1, :], tA[1], tA[4], ALU.mult)
        VM(NWr[:, 0, :], 1.0)
        VM(NWi[:, 0, :], 0.0)
        V(NWr[:, 1, :], tA[2], tA[5], ALU.mult)
        V(tA[9], tA[2], tA[4], ALU.mult)
        VS(NWi[:, 1, :], tA[9], -1.0, ALU.mult)
        for k in range(1, 8):
            cmul(PWr[:, k + 1, :], PWi[:, k + 1, :], PWr[:, k, :], PWi[:, k, :], PWr[:, 1, :], PWi[:, 1, :], tsm)
            cmul(NWr[:, k + 1, :], NWi[:, k + 1, :], NWr[:, k, :], NWi[:, k, :], NWr[:, 1, :], NWi[:, 1, :], tsm)
        VM(E8r[:, 0, :], 1.0)
        VM(E8i[:, 0, :], 0.0)
        VC(E8r[:, 1, :], PWr[:, 8, :])
        VC(E8i[:, 1, :], PWi[:, 8, :])
        for k in range(1, 16):
            cmul(E8r[:, k + 1, :], E8i[:, k + 1, :], E8r[:, k, :], E8i[:, k, :], E8r[:, 1, :], E8i[:, 1, :], tsm)
        VM(E16r[:, 0, :], 1.0)
        VM(E16i[:, 0, :], 0.0)
        VC(E16r[:, 1, :], E8r[:, 16, :])
        VC(E16i[:, 1, :], E8i[:, 16, :])
        for k in range(1, 8):
            cmul(E16r[:, k + 1, :], E16i[:, k + 1, :], E16r[:, k, :], E16i[:, k, :], E16r[:, 1, :], E16i[:, 1, :], tsm)
        VC(P8, E8r[:, 1, :])
        VC(Q8, E8i[:, 1, :])
        VC(P128, E16r[:, 8, :])
        VC(Q128, E16i[:, 8, :])
        cmul(A2r[:, 0, :], A2i[:, 0, :], P128, Q128, P128, Q128, tsm)
        cmul(A2r[:, 1, :], A2i[:, 1, :], A2r[:, 0, :], A2i[:, 0, :], A2r[:, 0, :], A2i[:, 0, :], tsm)
        for r_ in range(16):
            VC(TIre[:, :, r_], E8r[:, 15 - r_, :])
            VC(TIim[:, :, r_], E8i[:, 15 - r_, :])
        for q_ in range(8):
            VC(TOre[:, :, q_], E16r[:, 7 - q_, :])
            VC(TOim[:, :, q_], E16i[:, 7 - q_, :])
        V(tA[0], are, are, ALU.mult)
        V(tA[1], aim, aim, ALU.mult)
        V(tA[0], tA[0], tA[1], ALU.add)
        RECIP(tA[0], tA[0], K0, K0)
        VS(tA[1], PWr[:, 1, :], -1.0, ALU.add)
        V(tA[2], tA[1], are, ALU.mult)
        V(tA[3], PWi[:, 1, :], aim, ALU.mult)
        V(tA[2], tA[2], tA[3], ALU.add)
        V(tA[2], tA[2], tA[0], ALU.mult)
        V(tA[3], PWi[:, 1, :], are, ALU.mult)
        V(tA[4], tA[1], aim, ALU.mult)
        V(tA[3], tA[3], tA[4], ALU.subtract)
        V(tA[3], tA[3], tA[0], ALU.mult)
        bc16 = lambda ap: ap.unsqueeze(2).to_broadcast([128, 32, 16])
        cmul(Bbr, Bbi, bre, bim, bc16(tA[2]), bc16(tA[3]), (tB[0], tB[1]))
        for s_ in range(8):
            cmul(Gre[:, :, s_, :], Gim[:, :, s_, :], Bbr, Bbi, bc16(NWr[:, s_ + 1, :]), bc16(NWi[:, s_ + 1, :]), (tB[0], tB[1]))
            cmul(Ire[:, :, s_, :], Iim[:, :, s_, :], Bbr, Bbi, bc16(PWr[:, 7 - s_, :]), bc16(PWi[:, 7 - s_, :]), (tB[0], tB[1]))
            cmul(Sre[:, :, s_, :], Sim[:, :, s_, :], cre, cim, bc16(PWr[:, s_ + 1, :]), bc16(PWi[:, s_ + 1, :]), (tB[0], tB[1]))
        Sim2 = Sim.rearrange("p g t j -> p (g t j)")
        VS(Sim2, Sim2, -1.0, ALU.mult)
        P.barrier()
        CP("act", S2Ore.rearrange("p g x -> p (g x)"), Sre.rearrange("p g t j -> p (g t j)"), K0, ["S2O"])
        CP("act", S2Oim.rearrange("p g x -> p (g x)"), Sim2, K0, ["S2O"])
        Gre3 = Gre.rearrange("p g s j -> p g (s j)")
        Gim3 = Gim.rearrange("p g s j -> p g (s j)")
        Sre3 = Sre.rearrange("p g t j -> p g (t j)")
        Sim3 = Sim.rearrange("p g t j -> p g (t j)")
        Ire3 = Ire.rearrange("p g s j -> p g (s j)")
        Iim3 = Iim.rearrange("p g s j -> p g (s j)")
        for gb in range(16):
            bank = PS[gb % 2][:, 0:512]
            bk = f"PSb{gb % 2}"
            for i in range(4):
                g = gb * 4 + i
                gh, gp = g // 32, g % 32
                pb = gh * 64
                MM(bank[:, i * 128:(i + 1) * 128], Gre3[pb:pb + 64, gp, :], Sre3[pb:pb + 64, gp, :], True, False, K0, [bk])
                MM(bank[:, i * 128:(i + 1) * 128], Gim3[pb:pb + 64, gp, :], Sim3[pb:pb + 64, gp, :], False, True, K0, [bk])
            TT("dve", T0f, bank.rearrange("p (g x) -> p g x", g=4),
               t0mask[:].unsqueeze(1).to_broadcast([128, 4, 128]), ALU.mult, [bk, "t0mask"] + K0, K0)
            for i in range(4):
                g = gb * 4 + i
                STT(T0[:, g, :], identf[:], dcol[:, g:g + 1], T0f[:, i, :], ALU.mult, ALU.add, K0 + ["identf"], K0 + ["T0"])
        for src3, dst, nm in ((Ire3, I2Sre, "I2Sre"), (Iim3, I2Sim, "I2Sim")):
            for gb in range(8):
                bank = PS[2 + gb % 2][:, 0:512]
                bk = f"PSc{gb % 2}"
                for i in range(8):
                    g = gb * 8 + i
                    gh, gp = g // 32, g % 32
                    pb = gh * 64
                    TR(bank[:, i * 64:(i + 1) * 64], src3[pb:pb + 64, gp, :], identf[pb:pb + 64, pb:pb + 64], ["identf", "S0B"], [bk])
                CP("act", dst[:, gb * 8:(gb + 1) * 8, :], bank.rearrange("p (g n) -> p g n", g=8), [bk], [nm])
        P.barrier()
        if stop_after == "S0b":
            return finish(nc, P, st, out_d)

        AR.reset(base_sb)
        u_raw = AR.take([128, 8, 1024], BF16)
        U = u_raw.rearrange("p s c -> p (s c)").rearrange("p (g m) -> p g m", g=64)
        u_tm2 = AR.take([128, 64, 8, 16], BF16)
        inc_raw = AR.take([128, 4096], F32)
        Incre = inc_raw.rearrange("p (g m) -> p g m", g=32)
        yg_bf = inc_raw.bitcast(BF16).rearrange("p (s c) -> p s c", s=8)
        Incim = AR.take([128, 32, 128], F32)
        gluw = AR.take([128, 8, 1024], BF16)
        glub = AR.take([128, 1024], F32)
        g5s = [AR.take([128, 1024], BF16) for _ in range(2)]
        ct = [AR.take([128, 32], F32) for _ in range(12)]
        un_base = AR.off
        LA = AR.take([128, 32, 8, 16], F32)
        LB = AR.take([128, 32, 8, 16], F32)
        LS = [AR.take([128, 32, 8], F32) for _ in range(4)]
        LE = [AR.take([128, 32], F32) for _ in range(4)]
        AR.reset(un_base)
        Xre_bf = AR.take([128, 32, 128], BF16)
        Xim_bf = AR.take([128, 32, 128], BF16)
        Ysb = [AR.take([128, 8, 128], F32) for _ in range(2)]
        gt1 = [AR.take([128, 1024], F32) for _ in range(2)]
        gt2 = [AR.take([128, 1024], F32) for _ in range(2)]
        AR.reset(un_base)
        ygT = [AR.take([128, 8, 128], BF16) for _ in range(2)]
        zb = [AR.take([128, 1024], F32) for _ in range(2)]
        gate = [AR.take([128, 1024], F32) for _ in range(2)]
        tt_ = [AR.take([128, 1024], F32) for _ in range(2)]
        ssm = [AR.take([128, 1024], BF16) for _ in range(2)]
        ssmT = [AR.take([128, 8, 128], BF16) for _ in range(2)]
        AR.reset(un_base)

        dma(gluw, gluw_d, w=["gluw"], eng="pool")
        dma(glub, glub_d, w=["glub"])

        TRU = PS[0][:, 0:512].bitcast(BF16)
        PIre = PS[1][:, 0:512]
        PIim = PS[1][:, 512:1024]
        TRU2 = [PS[0][:, 0:512].bitcast(BF16), PS[0][:, 512:1024].bitcast(BF16)]
        PI2 = [(PS[1][:, 0:512], PS[1][:, 512:1024]), (PS[2][:, 0:512], PS[2][:, 512:1024])]
        PY = PS[2]
        PYT = PS[3]
        IRK = [f"IR{m}" for m in range(128)]
        IIK = [f"II{m}" for m in range(128)]

        Gs = AR.take([128, 8, 64], F32)
        Gs2 = AR.take([128, 8, 64], F32)
        Sd = AR.take([128, 3, 64], F32)
        xt_ = [AR.take([128, 32], F32) for _ in range(4)]
        NPS = NPRE // 4
        if mode == "coll":
            sb_iter = [("light", 0), ("light", 1), ("xchg", -1), ("full", 0), ("full", 1)]
        else:
            sb_iter = [("light", sc_) for sc_ in range(NPS)] + [("full", NPS), ("full", NPS + 1)]
        for pas, sc in sb_iter:
            if pas == "xchg":
                P.barrier()
                dma(ccS_in.ap()[:, 0:32], car_r[:], r=["car_r"], w=["ccS_in"])
                dma(ccS_in.ap()[:, 32:64], car_i[:], r=["car_i"], w=["ccS_in"])
                if FAKE_CC:
                    dma(ccS_out.ap()[0:128, :], ccS_in.ap(), r=["ccS_in"], w=["ccS_out"])
                else:
                    P.op("pool", lambda e: e.collective_compute("AllGather", ALU.bypass, replica_groups=[list(range(8))],
                                                                ins=[ccS_in.ap().opt()], outs=[ccS_out.ap().opt()]),
                         r=["ccS_in"], w=["ccS_out"])
                dma(Gs, ccS_out.ap().rearrange("(r p) c -> p r c", p=128), r=["ccS_out"], w=["Gs"])
                KX = ["XCH"]
                for d_ in range(3):
                    TT("dve", Gs2, Gs, sel5[:, d_, :].unsqueeze(2).to_broadcast([128, 8, 64]), ALU.mult, ["Gs", "sel5"] + KX, KX)
                    RED(Sd[:, d_, :], Gs2.rearrange("p r c -> p c r"), ALU.add, KX, KX)
                XV = lambda out, in0, in1, op: TT("dve", out, in0, in1, op, KX + ["car_r", "car_i"], KX)
                XV(car_r[:], Sd[:, 0, 0:32], Sd[:, 0, 0:32], ALU.bypass) if False else CP("dve", car_r[:], Sd[:, 0, 0:32], KX + ["car_r"], KX + ["car_r"])
                CP("dve", car_i[:], Sd[:, 0, 32:64], KX + ["car_i"], KX + ["car_i"])
                for d_ in (1, 2):
                    sr, si = Sd[:, d_, 0:32], Sd[:, d_, 32:64]
                    ar_, ai_ = A2r[:, d_ - 1, :], A2i[:, d_ - 1, :]
                    XV(xt_[0], sr, ar_, ALU.mult)
                    XV(xt_[1], si, ai_, ALU.mult)
                    XV(xt_[0], xt_[0], xt_[1], ALU.subtract)
                    TT("dve", car_r[:], car_r[:], xt_[0], ALU.add, KX + ["car_r"], KX + ["car_r"])
                    XV(xt_[2], sr, ai_, ALU.mult)
                    XV(xt_[3], si, ar_, ALU.mult)
                    XV(xt_[2], xt_[2], xt_[3], ALU.add)
                    TT("dve", car_i[:], car_i[:], xt_[2], ALU.add, KX + ["car_i"], KX + ["car_i"])
                P.barrier()
                continue
            own = pas == "full"
            if own:
                P.barrier()
            dma(u_raw, u_d[sc * 1024:(sc + 1) * 1024, :].rearrange("(m s) c -> m s c", s=8), w=["u_raw", "U"])
            src4 = u_raw.rearrange("p s (g j) -> p g s j", j=16)
            CP("dve", u_tm2[:, 0:32, :, :], src4[:, 0:32, :, :], ["u_raw"], ["u_tm2a"])
            CP("act", u_tm2[:, 32:64, :, :], src4[:, 32:64, :, :], ["u_raw"], ["u_tm2b"])
            u3 = u_tm2.rearrange("p g s j -> p g (s j)")
            for gb in range(8):
                tru = TRU2[gb % 2]
                tk = f"TRU{gb % 2}"
                for i in range(8):
                    g = gb * 8 + i
                    TR(tru[:, i * 128:(i + 1) * 128], u3[:, g, :], identb[:], ["u_tm2a", "u_tm2b", "identb"], [tk])
                CP("act", U[:, gb * 8:(gb + 1) * 8, :], tru.rearrange("p (g m) -> p g m", g=8),
                   [tk], ["U", "u_raw"])
            for b4 in range(8):
                pre, pim = PI2[b4 % 2]
                kre, kim = f"PIre{b4 % 2}", f"PIim{b4 % 2}"
                for i in range(4):
                    gp = b4 * 4 + i
                    for gh in range(2):
                        g = gh * 32 + gp
                        MM(pre[gh * 64:(gh + 1) * 64, i * 128:(i + 1) * 128], I2Sre[:, g, :], U[:, g, :], True, True, ["U", "I2Sre"], [kre])
                        MM(pim[gh * 64:(gh + 1) * 64, i * 128:(i + 1) * 128], I2Sim[:, g, :], U[:, g, :], True, True, ["U", "I2Sim"], [kim])
                CP("act", Incre[:, b4 * 4:(b4 + 1) * 4, :], pre.rearrange("p (g m) -> p g m", g=4), [kre], IRK)
                CP("dve", Incim[:, b4 * 4:(b4 + 1) * 4, :], pim.rearrange("p (g m) -> p g m", g=4), [kim], IIK)
            if not own:
                I4r = Incre.rearrange("p g (q r) -> p g q r", r=16)
                I4i = Incim.rearrange("p g (q r) -> p g q r", r=16)
                tir = TIre.unsqueeze(2).to_broadcast([128, 32, 8, 16])
                tii = TIim.unsqueeze(2).to_broadcast([128, 32, 8, 16])
                TT("dve", LA, I4r, tir, ALU.mult, IRK + ["TI"], ["LA"])
                TT("dve", LB, I4i, tii, ALU.mult, IIK + ["TI"], ["LB"])
                TT("dve", LA, LA, LB, ALU.subtract, ["LA", "LB"], ["LA"])
                RED(LS[0], LA, ALU.add, ["LA"], ["LS0"])
                TT("dve", LA, I4i, tir, ALU.mult, IIK + ["TI", "LS0"], ["LA"])
                TT("dve", LB, I4r, tii, ALU.mult, IRK + ["TI", "LA"], ["LB"])
                TT("dve", LA, LA, LB, ALU.add, ["LA", "LB"], ["LA"])
                RED(LS[1], LA, ALU.add, ["LA"], ["LS1"])
                TT("dve", LS[2], LS[0], TOre, ALU.mult, ["LS0", "TO"], ["LS2"])
                TT("dve", LS[3], LS[1], TOim, ALU.mult, ["LS1", "TO"], ["LS3"])
                TT("dve", LS[2], LS[2], LS[3], ALU.subtract, ["LS2", "LS3"], ["LS2"])
                RED(LE[0], LS[2], ALU.add, ["LS2"], ["LE0"])
                TT("dve", LS[2], LS[1], TOre, ALU.mult, ["LS1", "TO", "LE0"], ["LS2"])
                TT("dve", LS[3], LS[0], TOim, ALU.mult, ["LS0", "TO", "LS2"], ["LS3"])
                TT("dve", LS[2], LS[2], LS[3], ALU.add, ["LS2", "LS3"], ["LS2"])
                RED(LE[1], LS[2], ALU.add, ["LS2"], ["LE1"])
                TT("dve", LE[2], car_r[:], P128, ALU.mult, ["car_r", "P128"], ["LE2"])
                TT("dve", LE[3], car_i[:], Q128, ALU.mult, ["car_i", "Q128"], ["LE3"])
                TT("dve", LE[2], LE[2], LE[3], ALU.subtract, ["LE2", "LE3"], ["LE2"])
                TT("dve", LE[0], LE[0], LE[2], ALU.add, ["LE0", "LE2"], ["LE0"])
                TT("dve", LE[2], car_i[:], P128, ALU.mult, ["car_i", "P128", "LE0"], ["LE2"])
                TT("dve", LE[3], car_r[:], Q128, ALU.mult, ["car_r", "Q128", "LE0"], ["LE3"])
                TT("dve", LE[2], LE[2], LE[3], ALU.add, ["LE2", "LE3"], ["LE2"])
                TT("dve", car_i[:], LE[1], LE[2], ALU.add, ["LE1", "LE2"], ["car_i"])
                CP("dve", car_r[:], LE[0], ["LE0"], ["car_r"])
                continue
            CP("dve", Xre_bf[:, :, 0], car_r[:], ["car_r"], ["Xre0"])
            CP("dve", Xim_bf[:, :, 0], car_i[:], ["car_i"], ["Xim0"])
            for m in range(128):
                if m == 0:
                    xr, xi, kr, ki = car_r[:], car_i[:], "car_r", "car_i"
                else:
                    xr, xi, kr, ki = Incre[:, :, m - 1], Incim[:, :, m - 1], IRK[m - 1], IIK[m - 1]
                pp = (m % 2) * 6
                c0, c1, c2, c3, c4, c5 = ct[pp:pp + 6]
                n0 = [f"ct{pp + j}" for j in range(6)]
                TT("dve", c0, xr, P8, ALU.mult, [kr, "P8"], [n0[0]])
                TT("dve", c1, xi, Q8, ALU.mult, [ki, "Q8"], [n0[1]])
                TT("dve", c3, xi, P8, ALU.mult, [ki, "P8"], [n0[3]])
                TT("dve", c4, xr, Q8, ALU.mult, [kr, "Q8"], [n0[4]])
                TT("dve", c2, c0, c1, ALU.subtract, [n0[0], n0[1]], [n0[2]])
                TT("dve", c5, c3, c4, ALU.add, [n0[3], n0[4]], [n0[5]])
                TT("dve", Incre[:, :, m], c2, Incre[:, :, m], ALU.add, [n0[2], IRK[m]], [IRK[m]])
                TT("dve", Incim[:, :, m], c5, Incim[:, :, m], ALU.add, [n0[5], IIK[m]], [IIK[m]])
            CP("dve", car_r[:], Incre[:, :, 127], [IRK[127], "Xre0"], ["car_r"])
            CP("dve", car_i[:], Incim[:, :, 127], [IIK[127], "Xim0"], ["car_i"])
            CP("act", Xre_bf[:, :, 1:128], Incre[:, :, 0:127], IRK, ["Xre"])
            CP("dve", Xim_bf[:, :, 1:128], Incim[:, :, 0:127], IIK, ["Xim"])
            PYb = [PS[1], PS[2]]
            PYTb = [PS[3], PS[0]]

            def y_mm(gb):
                bq = gb % 2
                for i in range(8):
                    g = gb * 8 + i
                    gh, gp = g // 32, g % 32
                    pb = gh * 64
                    oy = PYb[bq][:, i * 128:(i + 1) * 128]
                    MM(oy, T0[:, g, :], U[:, g, :], True, False, ["T0", "U"], [f"PY{bq}"])
                    MM(oy, S2Ore[pb:pb + 64, gp, :], Xre_bf[pb:pb + 64, gp, :], False, False, ["S2O", "Xre", "Xre0"], [f"PY{bq}"])
                    MM(oy, S2Oim[pb:pb + 64, gp, :], Xim_bf[pb:pb + 64, gp, :], False, True, ["S2O", "Xim", "Xim0"], [f"PY{bq}"])

            def y_ew(gb):
                bq = gb % 2
                pyt = PYTb[bq]
                ky, kt_ = f"PY{bq}", f"PYT{bq}"
                CP("act", Ysb[bq].rearrange("p g m -> p (g m)"), PYb[bq], [ky], [f"Ysb{bq}"])
                for i in range(8):
                    TR(pyt[:, i * 128:(i + 1) * 128], Ysb[bq][:, i, :], identf[:], [f"Ysb{bq}", "identf"], [kt_])
                ACT(gt1[bq], pyt, AF.Square, [kt_], [f"gt1{bq}"])
                TS("dve", gt1[bq], gt1[bq], KG * 0.044715, KG, ALU.mult, ALU.add, [f"gt1{bq}"], [f"gt1{bq}"])
                TT("dve", gt1[bq], gt1[bq], pyt, ALU.mult, [f"gt1{bq}", kt_], [f"gt1{bq}"])
                ACT(gt2[bq], gt1[bq], AF.Sigmoid, [f"gt1{bq}"], [f"gt2{bq}"])
                TT("dve", yg_bf[:, :, gb * 128:(gb + 1) * 128].rearrange("p t (g j) -> p g t j", j=16),
                   pyt.rearrange("p (g t j) -> p g t j", g=8, t=8), gt2[bq].rearrange("p (g t j) -> p g t j", g=8, t=8),
                   ALU.mult, [kt_, f"gt2{bq}"], ["yg_bf"])

            y_mm(0)
            for gb in range(8):
                if gb + 1 < 8:
                    y_mm(gb + 1)
                y_ew(gb)
            P.barrier()
            oc = sc - NPS
            g5v = g5_d[oc * 1024:(oc + 1) * 1024, :].rearrange("(m s) c -> m s c", s=8)
            TRa = [PS[0][:, 0:512].bitcast(BF16), PS[0][:, 512:1024].bitcast(BF16)]
            TRs = [PS[3][:, 0:512].bitcast(BF16), PS[3][:, 512:1024].bitcast(BF16)]
            PZb = [PS[1], PS[2]]

            def tail1(s_):
                bq = s_ % 2
                dma(g5s[bq], g5v[:, s_, :], w=[f"g5s{bq}"])
                for kt in range(8):
                    TR(TRa[bq][:, kt * 128:(kt + 1) * 128], yg_bf[:, s_, kt * 128:(kt + 1) * 128], identb[:], ["yg_bf", "identb"], [f"TRa{bq}"])
                CP("act", ygT[bq].rearrange("p k m -> p (k m)"), TRa[bq], [f"TRa{bq}"], [f"ygT{bq}"])
                for cb in range(2):
                    pz = PZb[bq][:, cb * 512:(cb + 1) * 512]
                    for kt in range(8):
                        MM(pz, ygT[bq][:, kt, :], gluw[:, kt, cb * 512:(cb + 1) * 512], kt == 0, kt == 7, [f"ygT{bq}", "gluw"], [f"PZ{bq}"])

            def tail2(s_):
                bq = s_ % 2
                TT("dve", zb[bq], PZb[bq], glub, ALU.add, [f"PZ{bq}", "glub"], [f"zb{bq}"])
                ACT(gate[bq], zb[bq], AF.Sigmoid, [f"zb{bq}"], [f"gate{bq}"])
                TT("dve", tt_[bq], yg_bf[:, s_, :], gate[bq], ALU.mult, ["yg_bf", f"gate{bq}"], [f"tt_{bq}"])
                TT("pool", ssm[bq], tt_[bq], g5s[bq], ALU.mult, [f"tt_{bq}", f"g5s{bq}"], [f"ssm{bq}"])
                for kt in range(8):
                    TR(TRs[bq][:, kt * 128:(kt + 1) * 128], ssm[bq][:, kt * 128:(kt + 1) * 128], identb[:], [f"ssm{bq}", "identb"], [f"TRs{bq}"])
                CP("act", ssmT[bq].rearrange("p k m -> p (k m)"), TRs[bq], [f"TRs{bq}"], [f"ssmT{bq}"])
                dma(ssmT_d[oc, s_], ssmT[bq], r=[f"ssmT{bq}"])

            tail1(0)
            for s_ in range(8):
                if s_ + 1 < 8:
                    tail1(s_ + 1)
                tail2(s_)
        P.barrier()
        if stop_after == "SB":
            return finish(nc, P, st, out_d)

        AR.reset()
        Wqk = AR.take([128, 8, 1024], BF16)
        VO = AR.take([128, 8, 1024], BF16)
        wout = AR.take([128, 16, 1024], BF16)
        g2r = AR.take([128, 1024], F32)
        gfr = AR.take([128, 1024], F32)
        base_sc = AR.off
        wqT = AR.take([128, 8, 1024], BF16)
        wk = AR.take([128, 8, 1024], BF16)
        wv = AR.take([128, 8, 1024], BF16)
        wo = AR.take([128, 8, 1024], BF16)
        memT = AR.take([128, 8, 256], F32)
        msq = AR.take([128, 8, 256], BF16)
        mrs1 = AR.take([128, 256], F32)
        mrs2 = AR.take([128, 256], F32)
        mnT = AR.take([128, 8, 256], BF16)
        kaT = AR.take([128, 8, 256], BF16)
        vaT = AR.take([128, 8, 256], BF16)
        for t_, d_, k_ in ((wk, wk_d, "wk"), (wv, wv_d, "wv"), (wqT, wqT_d, "wqT"), (wo, wo_d, "wo")):
            dma(t_, d_, w=[k_], eng="pool")
        dma(wout[:, 0:8, :], wout_d[:, 0:8, :], w=["wout"], eng="pool")
        dma(wout[:, 8:16, :], wout_d[:, 8:16, :], w=["wout"], eng="pool")
        dma(g2r, g2_d, w=["g2r"])
        dma(gfr, gf_d, w=["gfr"])
        dma(memT, memT_d, w=["memT"])
        norm_fm(memT, 256, gm[:], msq, mrs1, mrs2, mnT, PS[0][:, 0:256], "memT", "msq", "mrs", "mnT", "PSm", "gm")
        for r_ in range(8):
            for wt, dst, nm, pi in ((wk, kaT, "kaT", 1), (wv, vaT, "vaT", 2)):
                pp_ = PS[pi][:, (r_ % 2) * 512:(r_ % 2) * 512 + 256]
                pk = f"PSkv{pi}_{r_ % 2}"
                for kt in range(8):
                    MM(pp_, wt[:, kt, r_ * 128:(r_ + 1) * 128], mnT[:, kt, :], kt == 0, kt == 7, [nm[:2] if False else ("wk" if pi == 1 else "wv"), "mnT"], [pk])
                CP("act" if pi == 1 else "dve", dst[:, r_, :], pp_, [pk], [nm])
        for dt_ in range(8):
            pw_ = PS[dt_ % 2]
            pk = f"PSw{dt_ % 2}"
            for h in range(4):
                for hf in range(2):
                    MM(pw_[:, h * 256:(h + 1) * 256], wqT[:, 2 * h + hf, dt_ * 128:(dt_ + 1) * 128], kaT[:, 2 * h + hf, :],
                       hf == 0, hf == 1, ["wqT", "kaT"], [pk])
            ACT(Wqk[:, dt_, :], pw_, AF.Copy, [pk], ["Wqk"], scale=1.0 / 16.0)
        for h in range(4):
            for mh in range(2):
                idx = 2 * h + mh
                pw_ = PS[2 + idx % 2]
                pk = f"PSv{idx % 2}"
                for cb in range(2):
                    for hf in range(2):
                        MM(pw_[:, cb * 512:(cb + 1) * 512], vaT[:, 2 * h + hf, mh * 128:(mh + 1) * 128], wo[:, 2 * h + hf, cb * 512:(cb + 1) * 512],
                           hf == 0, hf == 1, ["vaT", "wo"], [pk])
                CP("dve", VO[:, idx, :], pw_, [pk], ["VO"])
        P.barrier()

        AR.reset(base_sc)
        retT_sc = [AR.take([128, 8, 1024], BF16) for _ in range(2)]
        ssmT_s = [AR.take([128, 8, 128], BF16) for _ in range(3)]
        x_t = [AR.take([128, 1024], F32) for _ in range(3)]
        x1 = [AR.take([128, 1024], F32) for _ in range(3)]
        junk = AR.take([128, 1024], F32)
        h2 = [AR.take([128, 1024], BF16) for _ in range(2)]
        h2T = AR.take([128, 8, 128], BF16)
        Pex = AR.take([128, 4, 256], BF16)
        Pn = [AR.take([128, 4, 256], BF16) for _ in range(2)]
        PTt = AR.take([128, 8, 128], BF16)
        x2 = AR.take([128, 1024], F32)
        osb = [AR.take([128, 1024], F32) for _ in range(2)]
        sst = [AR.take([128, 32], F32) for _ in range(3)]
        TRc = PS[3][:, 0:512].bitcast(BF16)
        TRd = PS[3][:, 512:1024].bitcast(BF16)
        tiles = [(sco, s_) for sco in range(2) for s_ in range(8)]
        dma(retT_sc[0], retT_d[:, :, 0:1024], w=["retT_sc0"])
        dma(retT_sc[1], retT_d[:, :, 1024:2048], w=["retT_sc1"])

        def sc_load(t):
            sco, s_ = tiles[t]
            b3 = t % 3
            dma(ssmT_s[b3], ssmT_d[sco, s_], w=[f"ssmT_s{b3}"])
            dma(x_t[b3], xown_d[sco, :, s_, :], w=[f"x_t{b3}"])

        def sc_s1(t):
            sco, s_ = tiles[t]
            b3 = t % 3
            st_ = sst[t % 3]
            sk = f"sst{t % 3}"
            for cb in range(2):
                px = PS[0][:, cb * 512:(cb + 1) * 512]
                for kt in range(8):
                    MM(px, retT_sc[sco][:, kt, s_::8], wout[:, kt, cb * 512:(cb + 1) * 512], kt == 0, False, [f"retT_sc{sco}", "wout"], [f"PX{cb}"])
                for kt in range(8):
                    MM(px, ssmT_s[b3][:, kt, :], wout[:, 8 + kt, cb * 512:(cb + 1) * 512], False, kt == 7, [f"ssmT_s{b3}", "wout"], [f"PX{cb}"])
            TT("dve", x1[b3], PS[0], x_t[b3], ALU.add, ["PX0", "PX1", f"x_t{b3}"], [f"x1_{b3}"])
            ACT(junk, x1[b3], AF.Square, [f"x1_{b3}"], ["junk", sk + "a"], accum_out=st_[:, 0:1])
            ACT(st_[:, 1:2], st_[:, 0:1], AF.Ln, [sk + "a"], [sk + "b"], bias=epsc[:, 0:1], scale=1.0 / 1024.0)
            ACT(st_[:, 2:3], st_[:, 1:2], AF.Exp, [sk + "b"], [sk + "c"], scale=-0.5)
            STT(h2[t % 2], x1[b3], st_[:, 2:3], g2r, ALU.mult, ALU.mult, [f"x1_{b3}", sk + "c", "g2r"], [f"h2_{t % 2}"])

        def sc_s2(t):
            st_ = sst[t % 3]
            sk = f"sst{t % 3}"
            hh = h2[t % 2]
            for kt in range(8):
                TR(TRc[:, kt * 128:(kt + 1) * 128], hh[:, kt * 128:(kt + 1) * 128], identb[:], [f"h2_{t % 2}", "identb"], ["TRc"])
            CP("act", h2T.rearrange("p k t -> p (k t)"), TRc, ["TRc"], ["h2T"])
            for cb in range(2):
                pq = PS[1][:, cb * 512:(cb + 1) * 512]
                for kt in range(8):
                    MM(pq, h2T[:, kt, :], Wqk[:, kt, cb * 512:(cb + 1) * 512], kt == 0, kt == 7, ["h2T", "Wqk"], [f"PQ{cb}"])
            RED(st_[:, 4:8], PS[1].rearrange("p (h m) -> p h m", h=4), ALU.max, ["PQ0", "PQ1"], [sk + "mx"])
            TS("dve", st_[:, 8:12], st_[:, 4:8], -1.0, None, ALU.mult, None, [sk + "mx"], [sk + "nmx"])
            for h in range(4):
                ACT(Pex[:, h, :], PS[1][:, h * 256:(h + 1) * 256], AF.Exp, ["PQ0", "PQ1", sk + "nmx"], ["Pex", sk + f"sm{h}"],
                    bias=st_[:, 8 + h:9 + h], scale=1.0, accum_out=st_[:, 12 + h:13 + h])
            RECIP(st_[:, 16:20], st_[:, 12:16], [sk + f"sm{h}" for h in range(4)], [sk + "rsm"])
            TT("pool", Pn[t % 2], Pex, st_[:, 16:20].unsqueeze(2).to_broadcast([128, 4, 256]), ALU.mult, ["Pex", sk + "rsm"], [f"Pn{t % 2}"])

        def sc_s3(t):
            sco, s_ = tiles[t]
            b3 = t % 3
            st_ = sst[t % 3]
            sk = f"sst{t % 3}"
            b = t % 2
            Pn2 = Pn[b].rearrange("p h m -> p (h m)")
            for idx in range(8):
                TR(TRd[:, idx * 128:(idx + 1) * 128], Pn2[:, idx * 128:(idx + 1) * 128], identb[:], [f"Pn{b}", "identb"], ["TRd"])
            CP("act", PTt.rearrange("p k t -> p (k t)"), TRd, ["TRd"], ["PTt"])
            for cb in range(2):
                po = PS[2][:, cb * 512:(cb + 1) * 512]
                for idx in range(8):
                    MM(po, PTt[:, idx, :], VO[:, idx, cb * 512:(cb + 1) * 512], idx == 0, idx == 7, ["PTt", "VO"], [f"PO{cb}"])
            TT("dve", x2, PS[2], x1[b3], ALU.add, ["PO0", "PO1", f"x1_{b3}"], ["x2"])
            ACT(junk, x2, AF.Square, ["x2"], ["junk", sk + "d"], accum_out=st_[:, 20:21])
            ACT(st_[:, 21:22], st_[:, 20:21], AF.Ln, [sk + "d"], [sk + "e"], bias=epsc[:, 0:1], scale=1.0 / 1024.0)
            ACT(st_[:, 22:23], st_[:, 21:22], AF.Exp, [sk + "e"], [sk + "f"], scale=-0.5)
            STT(osb[b], x2, st_[:, 22:23], gfr, ALU.mult, ALU.mult, ["x2", sk + "f", "gfr"], [f"osb{b}"])
            dma(out_d[sco, :, s_, :], osb[b], r=[f"osb{b}"])

        NTL = len(tiles)
        sc_load(0)
        for i in range(NTL + 2):
            if i + 1 < NTL:
                sc_load(i + 1)
            if i < NTL:
                sc_s1(i)
            if 0 <= i - 2 < NTL:
                sc_s3(i - 2)
            if 0 <= i - 1 < NTL:
                sc_s2(i - 1)
        return finish(nc, P, st, out_d)


def finish(nc, P, st, out_d):
    P.barrier()
    P.op("sp", None)
    P.emit(st)
    return nc


def _const_tables():
    H = 8
    log_g = np.log1p(-np.exp2(-5.0 - np.arange(H, dtype=np.float64)))
    i = np.arange(128, dtype=np.float64)
    sqk = np.zeros((128, 16), np.float64)
    sqk[:, 0:8] = np.exp(log_g[None, :] * (i[:, None] + 1.0))
    sqk[:, 8:16] = np.exp(-log_g[None, :] * (i[:, None] + 1.0)) * 0.125
    cdt = np.broadcast_to(np.exp(log_g * 128.0)[None, :], (128, 8)).copy()
    j = np.arange(128)
    maskT = (j[None, :] >= j[:, None]).astype(np.float32)
    s_idx = np.arange(128) // 16
    t0mask = (s_idx[None, :] >= s_idx[:, None]).astype(np.float32)
    invf = (10000.0 ** (-np.arange(32, dtype=np.float32) / np.float32(32))).astype(np.float32)
    invf = np.broadcast_to(invf[None, :], (128, 32)).copy()
    return (sqk.astype(np.float32), cdt.astype(np.float32), maskT, t0mask, invf, np.eye(128, dtype=np.float32))


def _kt(w):
    K, N = w.shape
    return np.ascontiguousarray(w.reshape(K // 128, 128, N).transpose(1, 0, 2))


def make_in_maps(inputs, mode=None):
    mode = mode or MODE
    NPRE_T = 6144 if mode == "prefix" else 0
    NT = NPRE_T + 2048
    f = lambda a: np.asarray(a, dtype=np.float32)
    x = f(inputs["x"])
    mem = f(inputs["mem"])
    pos = np.asarray(inputs["positions"]).astype(np.int32)
    sqk, cdt, maskT, t0mask, invf, ident = _const_tables()
    rep = lambda v: np.ascontiguousarray(np.broadcast_to(f(v).reshape(1, -1), (128, f(v).size)))
    cols = lambda v: np.ascontiguousarray(f(v).reshape(8, 128).T)

    def s5lay(a):
        a = f(a)
        rest = a.shape[2:]
        a = a.reshape((2, 32, 64) + rest)
        a = np.moveaxis(a, 2, 1)
        return np.ascontiguousarray(a.reshape((128, 32) + rest))

    a_re = s5lay(inputs["s5_a_re"][0])
    a_im = s5lay(inputs["s5_a_im"][0])
    ldt = s5lay(np.broadcast_to(f(inputs["s5_log_dt"][0])[:, None], (64, 64)))
    b_re = s5lay(inputs["s5_b_re"][0])
    b_im = s5lay(inputs["s5_b_im"][0])
    c_re = s5lay(np.transpose(f(inputs["s5_c_re"][0]), (0, 2, 1)))
    c_im = s5lay(np.transpose(f(inputs["s5_c_im"][0]), (0, 2, 1)))
    shared = dict(
        invf=invf, sqk=sqk, cdt=cdt, maskT=maskT, t0mask=t0mask, ident=ident,
        w_in=_kt(f(inputs["w_in"][0])), g1=cols(inputs["norm1_g"][0]), gng=rep(inputs["ret_gn_g"][0]),
        s5are=a_re, s5aim=a_im, s5ldt=ldt, s5bre=b_re, s5bim=b_im, s5cre=c_re, s5cim=c_im,
        dcol=np.ascontiguousarray(f(inputs["s5_d"][0]).reshape(64, 16).T[np.arange(128) % 16, :]), glub=rep(inputs["s5_glu_b"][0]), g2rep=rep(inputs["norm2_g"][0]),
        gfrep=rep(inputs["norm_f_g"]), gm=cols(inputs["norm_mem_g"][0]),
        gluw=_kt(f(inputs["s5_glu_w"][0])), wout=_kt(f(inputs["w_out"][0])),
        wqT=_kt(np.ascontiguousarray(f(inputs["xa_wq"][0]).T)), wk=_kt(f(inputs["xa_wk"][0])),
        wv=_kt(f(inputs["xa_wv"][0])), wo=_kt(f(inputs["xa_wo"][0])),
    )
    log_g = np.log1p(-np.exp2(-5.0 - np.arange(8, dtype=np.float64)))
    maps = []
    for core in range(8):
        b, j = core // 4, core % 4
        m = dict(shared)
        t1 = (j + 1) * 2048
        t0 = t1 - NT
        lo = max(t0, 0)
        xl = np.zeros((NT, 1024), np.float32)
        pl = np.zeros((NT,), np.int32)
        xl[lo - t0:, :] = x[b, lo:t1, :]
        pl[lo - t0:] = pos[b, lo:t1]
        m["xT"] = _kt(np.ascontiguousarray(xl.T))
        m["xown"] = np.ascontiguousarray(x[b, j * 2048:(j + 1) * 2048, :].reshape(2, 128, 8, 1024))
        m["pos"] = np.ascontiguousarray(pl.reshape(NT // 128, 128).T)
        m["memT"] = _kt(np.ascontiguousarray(mem[b].T))
        wtr = np.zeros((128, 8, 4), np.float64)
        sel5 = np.zeros((128, 3, 8), np.float32)
        for rank in range(8):
            rb, ri = rank // 4, rank % 4
            if rb == b and ri < j:
                dist = j - ri
                sel5[:, dist - 1, rank] = 1.0
                for blk in range(4):
                    for par in range(2):
                        h = 2 * blk + par
                        wtr[par * 64:(par + 1) * 64, rank, blk] = np.exp(log_g[h] * 2048.0 * (dist - 1))
        m["wtr"] = wtr.astype(np.float32)
        m["sel5"] = sel5
        maps.append(m)
    return maps


def kernel(**inputs):
    nc = build_program()
    maps = make_in_maps(inputs)
    res = run_bass_kernel_spmd(nc, maps, core_ids=list(range(8)))
    out = np.zeros((2, 8192, 1024), np.float32)
    for core in range(8):
        b, j = core // 4, core % 4
        out[b, j * 2048:(j + 1) * 2048, :] = np.asarray(res.results[core]["out"]).reshape(2048, 1024)
    return out
```

```python
import os
import math
from contextlib import ExitStack
import numpy as np
import concourse.bass as bass
import concourse.mybir as mybir
from concourse.bass_utils import run_bass_kernel_spmd

F32 = mybir.dt.float32
BF16 = mybir.dt.bfloat16
I32 = mybir.dt.int32
ALU = mybir.AluOpType
AF = mybir.ActivationFunctionType
AX = mybir.AxisListType

SEG = 6000
NDMA = 24
EPS = 1e-6
NOWN_BLK = 8
MODE = "prefix"
TWO_PI = 2.0 * math.pi
KG = 2.0 * math.sqrt(2.0 / math.pi)


class Prog:
    ENGS = ("pe", "act", "dve", "pool", "sp")

    def __init__(self, nc):
        self.nc = nc
        self.ops = []
        self.last_w = {}
        self.readers = {}
        self.pending_barrier = {e: set() for e in self.ENGS}
        self._bar_at = 0

    def op(self, eng, fn, r=(), w=(), dma=False):
        oid = len(self.ops)
        deps = set()
        for k in r:
            if k in self.last_w:
                deps.add(self.last_w[k])
        for k in w:
            if k in self.last_w:
                deps.add(self.last_w[k])
            for rd in self.readers.get(k, ()):
                deps.add(rd)
        deps |= self.pending_barrier[eng]
        self.pending_barrier[eng] = set()
        deps.discard(oid)
        last_per_eng = {}
        keep = set()
        for d_ in deps:
            od = self.ops[d_]
            if od["dma"]:
                keep.add(d_)
            else:
                e_ = od["eng"]
                if e_ not in last_per_eng or d_ > last_per_eng[e_]:
                    last_per_eng[e_] = d_
        deps = keep | set(last_per_eng.values())
        self.ops.append(dict(eng=eng, fn=fn, deps=deps, dma=dma))
        for k in r:
            self.readers.setdefault(k, []).append(oid)
        for k in w:
            self.last_w[k] = oid
            self.readers[k] = []
        return oid

    def dma(self, fn, r=(), w=(), eng="sp"):
        return self.op(eng, fn, r, w, dma=True)

    def barrier(self):
        last = {}
        for i, o in enumerate(self.ops):
            last[o["eng"]] = i
        dm = [i for i in range(self._bar_at, len(self.ops)) if self.ops[i]["dma"]]
        s = set(last.values()) | set(dm)
        for e in self.ENGS:
            self.pending_barrier[e] |= s
        self._bar_at = len(self.ops)

    def emit(self, stack):
        nc = self.nc
        ops = self.ops
        n = len(ops)
        needed = [False] * n
        for i, o in enumerate(ops):
            nd = set()
            for d in o["deps"]:
                od = ops[d]
                if od["eng"] == "pe" and o["eng"] == "pe":
                    continue
                if od["eng"] == o["eng"] and not od["dma"] and od["fn"] is None:
                    continue
                nd.add(d)
            o["deps"] = nd
            for d in nd:
                needed[d] = True
        cnt = {e: 0 for e in self.ENGS}
        nseg = {e: 0 for e in self.ENGS}
        ndma = {"sp": 0, "pool": 0}
        dma_ops = {"sp": [], "pool": []}
        for i, o in enumerate(ops):
            if o["dma"]:
                q = o["eng"]
                nd = ndma[q]
                o["sig"] = ("dma_" + q, nd % NDMA, 16 * (nd // NDMA + 1), nd)
                dma_ops[q].append(o)
                ndma[q] += 1
            elif needed[i]:
                e = o["eng"]
                c = cnt[e]
                o["sig"] = (e, c // SEG, c % SEG + 1)
                cnt[e] += 1
                nseg[e] = c // SEG + 1
            else:
                o["sig"] = None
        sems = {}
        for e in self.ENGS:
            for s in range(nseg[e]):
                sems[(e, s)] = stack.enter_context(nc.semaphore(f"s_{e}_{s}"))
        for q in ("sp", "pool"):
            for k in range(min(NDMA, ndma[q])):
                sems[("dma_" + q, k)] = stack.enter_context(nc.semaphore(f"s_dma_{q}_{k}"))
        block = stack.enter_context(nc.Block())

        def make(ename):
            def body(eng):
                waited = {}
                for i, o in enumerate(ops):
                    if o["eng"] != ename:
                        continue
                    req = {}
                    for d in o["deps"]:
                        sg = ops[d]["sig"]
                        key = (sg[0], sg[1])
                        req[key] = max(req.get(key, 0), sg[2])
                    if o["dma"]:
                        idx = o["sig"][3]
                        if idx >= NDMA:
                            prev = dma_ops[o["eng"]][idx - NDMA]["sig"]
                            key = (prev[0], prev[1])
                            req[key] = max(req.get(key, 0), prev[2])
                    for key in sorted(req.keys(), key=lambda k: (k[0], k[1])):
                        val = req[key]
                        if waited.get(key, 0) >= val:
                            continue
                        if not key[0].startswith("dma"):
                            if any(k[0] == key[0] and k[1] > key[1] for k in waited):
                                continue
                        eng.wait_ge(sems[key], val)
                        waited[key] = val
                    if o["fn"] is None:
                        continue
                    ins = o["fn"](eng)
                    sg = o["sig"]
                    if sg is not None:
                        if sg[0].startswith("dma"):
                            ins.then_inc(sems[(sg[0], sg[1])], 16)
                        else:
                            ins.then_inc(sems[(sg[0], sg[1])], 1)
            return body

        block.tensor(make("pe"))
        block.scalar(make("act"))
        block.vector(make("dve"))
        block.gpsimd(make("pool"))
        block.sync(make("sp"))


class Arena:
    def __init__(self, t, nwords):
        self.t = t
        self.n = nwords
        self.off = 0

    def reset(self, off=0):
        self.off = off

    def take(self, shape, dtype):
        nel = 1
        for s in shape[1:]:
            nel *= s
        sz = 2 if dtype == BF16 else 4
        nw = (nel * sz + 3) // 4
        nw = (nw + 7) // 8 * 8
        assert self.off + nw <= self.n, f"arena overflow {self.off}+{nw}>{self.n}"
        ap = self.t[:, self.off:self.off + nw]
        self.off += nw
        if dtype != F32:
            ap = ap.bitcast(dtype)
        ap = ap[:, 0:nel]
        fd = shape[1:]
        if len(fd) == 2:
            ap = ap.rearrange("p (a b) -> p a b", a=fd[0])
        elif len(fd) == 3:
            ap = ap.rearrange("p (a b c) -> p a b c", a=fd[0], b=fd[1])
        elif len(fd) == 4:
            ap = ap.rearrange("p (a b c d) -> p a b c d", a=fd[0], b=fd[1], c=fd[2])
        return ap


def build_program(dbg=None, stop_after=None, mode=None):
    mode = mode or MODE
    NPRE = 24 if mode == "prefix" else 0
    NT = NPRE * 256 + 2048
    NCH = NT // 128
    FAKE_CC = isinstance(dbg, dict) and bool(dbg.get("fake_cc"))
    nc = bass.Bass("TRN2", target_bir_lowering=False)

    def din(name, shape, dt=F32):
        return nc.dram_tensor(name, list(shape), dt, kind="ExternalInput").ap()

    xT_d = din("xT", [128, 8, NT])
    xown_d = din("xown", [2, 128, 8, 1024])
    pos_d = din("pos", [128, NCH], I32)
    invf_d = din("invf", [128, 32])
    sqk_d = din("sqk", [128, 16])
    cdt_d = din("cdt", [128, 8])
    wtr_d = din("wtr", [128, 8, 4])
    sel5_d = din("sel5", [128, 3, 8])
    maskT_d = din("maskT", [128, 128])
    t0mask_d = din("t0mask", [128, 128])
    ident_d = din("ident", [128, 128])
    w_in_d = din("w_in", [128, 8, 5120])
    g1_d = din("g1", [128, 8])
    gng_d = din("gng", [128, 1024])
    are_d = din("s5are", [128, 32])
    aim_d = din("s5aim", [128, 32])
    ldt_d = din("s5ldt", [128, 32])
    bre_d = din("s5bre", [128, 32, 16])
    bim_d = din("s5bim", [128, 32, 16])
    cre_d = din("s5cre", [128, 32, 16])
    cim_d = din("s5cim", [128, 32, 16])
    dcol_d = din("dcol", [128, 64])
    glub_d = din("glub", [128, 1024])
    g2_d = din("g2rep", [128, 1024])
    gf_d = din("gfrep", [128, 1024])
    gm_d = din("gm", [128, 8])
    gluw_d = din("gluw", [128, 8, 1024])
    wout_d = din("wout", [128, 16, 1024])
    wqT_d = din("wqT", [128, 8, 1024])
    wk_d = din("wk", [128, 8, 1024])
    wv_d = din("wv", [128, 8, 1024])
    wo_d = din("wo", [128, 8, 1024])
    memT_d = din("memT", [128, 8, 256])
    out_d = nc.dram_tensor("out", [2, 128, 8, 1024], F32, kind="ExternalOutput").ap()

    skind = dict(kind="ExternalOutput") if dbg else {}
    u_d = nc.dram_tensor("u_scr", [NT, 1024], BF16, **skind).ap()
    ccR_in = nc.dram_tensor("ccR_in", [128, 512], F32)
    ccR_out = nc.dram_tensor("ccR_out", [1024, 512], F32)
    ccS_in = nc.dram_tensor("ccS_in", [128, 64], F32)
    ccS_out = nc.dram_tensor("ccS_out", [1024, 64], F32)
    g5_d = nc.dram_tensor("g5_scr", [2048, 1024], BF16, **skind).ap()
    retT_d = nc.dram_tensor("retT_scr", [128, 8, 2048], BF16, **skind).ap()
    ssmT_d = nc.dram_tensor("ssmT_scr", [2, 8, 128, 8, 128], BF16, **skind).ap()

    with ExitStack() as st:
        P = Prog(nc)
        sb = lambda name, shape, dt=F32: st.enter_context(nc.sbuf_tensor("sb_" + name, list(shape), dt))
        ARW = (dbg.get("arw") if isinstance(dbg, dict) and dbg.get("arw") else None) or (49200 if mode == "prefix" else 47200)
        arena_t = sb("arena", [128, ARW])
        AR = Arena(arena_t, ARW)
        identf = sb("identf", [128, 128])
        identb = sb("identb", [128, 128], BF16)
        ones_f = sb("ones_f", [128, 128])
        ones_b = sb("ones_b", [128, 128], BF16)
        epsc = sb("epsc", [128, 1])
        maskT = sb("maskT", [128, 128])
        t0mask = sb("t0mask", [128, 128])
        invf = sb("invf", [128, 32])
        sqk = sb("sqk", [128, 16])
        cdt = sb("cdt", [128, 8])
        wtr = sb("wtr", [128, 8, 4])
        sel5 = sb("sel5", [128, 3, 8])
        A2r = sb("A2r", [128, 2, 32])
        A2i = sb("A2i", [128, 2, 32])
        g1 = sb("g1", [128, 8])
        gm = sb("gm", [128, 8])
        Rf = sb("Rf", [128, 8, 128])
        Rbf = sb("Rbf", [128, 8, 128], BF16)
        car_r = sb("car_r", [128, 32])
        car_i = sb("car_i", [128, 32])
        PS = [st.enter_context(nc.psum_tensor(f"PS{i}", [128, 1024], F32))[:] for i in range(4)]

        def dma(out, in_, r=(), w=(), eng="sp"):
            P.dma(lambda e, out=out, in_=in_: e.dma_start(out=out, in_=in_), r=r, w=w, eng=eng)

        def TT(eng, out, in0, in1, op, r, w):
            P.op(eng, lambda e, out=out, in0=in0, in1=in1, op=op: e.tensor_tensor(out=out, in0=in0, in1=in1, op=op), r=r, w=w)

        def TS(eng, out, in0, s1, s2, op0, op1, r, w):
            if op1 is None:
                P.op(eng, lambda e, out=out, in0=in0, s1=s1, op0=op0: e.tensor_scalar(out=out, in0=in0, scalar1=s1, scalar2=None, op0=op0), r=r, w=w)
            else:
                P.op(eng, lambda e, out=out, in0=in0, s1=s1, s2=s2, op0=op0, op1=op1: e.tensor_scalar(out=out, in0=in0, scalar1=s1, scalar2=s2, op0=op0, op1=op1), r=r, w=w)

        def STT(out, in0, scalar, in1, op0, op1, r, w):
            P.op("dve", lambda e, out=out, in0=in0, scalar=scalar, in1=in1, op0=op0, op1=op1: e.scalar_tensor_tensor(out=out, in0=in0, scalar=scalar, in1=in1, op0=op0, op1=op1), r=r, w=w)

        def ACT(out, in_, func, r, w, **kw):
            P.op("act", lambda e, out=out, in_=in_, func=func, kw=kw: e.activation(out=out, in_=in_, func=func, **kw), r=r, w=w)

        def CP(eng, out, in_, r, w):
            if eng == "act":
                ACT(out, in_, AF.Copy, r, w)
            else:
                P.op(eng, lambda e, out=out, in_=in_: e.tensor_copy(out=out, in_=in_), r=r, w=w)

        def RECIP(out, in_, r, w):
            P.op("dve", lambda e, out=out, in_=in_: e.reciprocal(out=out, in_=in_), r=r, w=w)

        def RED(out, in_, op, r, w, axis=AX.X):
            P.op("dve", lambda e, out=out, in_=in_, op=op, axis=axis: e.tensor_reduce(out=out, in_=in_, axis=axis, op=op), r=r, w=w)

        def MM(out, lhsT, rhs, start, stop, r, w):
            P.op("pe", lambda e, out=out, lhsT=lhsT, rhs=rhs, start=start, stop=stop: e.matmul(out, lhsT=lhsT, rhs=rhs, start=start, stop=stop), r=r, w=w)

        def TR(out, in_, ident, r, w):
            P.op("pe", lambda e, out=out, in_=in_, ident=ident: e.transpose(out=out, in_=in_, identity=ident), r=r, w=w)

        def MEMSET(eng, ap, val, w):
            P.op(eng, lambda e, ap=ap, val=val: e.memset(ap, val), w=w)

        dma(identf[:], ident_d, w=["identf"])
        dma(maskT[:], maskT_d, w=["maskT"])
        dma(t0mask[:], t0mask_d, w=["t0mask"])
        dma(invf[:], invf_d, w=["invf"])
        dma(sqk[:], sqk_d, w=["sqk"])
        dma(cdt[:], cdt_d, w=["cdt"])
        dma(wtr[:], wtr_d, w=["wtr"])
        dma(sel5[:], sel5_d, w=["sel5"])
        dma(g1[:], g1_d, w=["g1"])
        dma(gm[:], gm_d, w=["gm"])
        CP("dve", identb[:], identf[:], ["identf"], ["identb"])
        MEMSET("pool", ones_f[:], 1.0, ["ones_f"])
        MEMSET("pool", ones_b[:], 1.0, ["ones_b"])
        MEMSET("pool", epsc[:], EPS, ["epsc"])
        MEMSET("pool", Rf[:], 0.0, ["Rf"])
        MEMSET("pool", Rbf[:], 0.0, ["Rbf"])
        MEMSET("pool", car_r[:], 0.0, ["car_r"])
        MEMSET("pool", car_i[:], 0.0, ["car_i"])

        def sincos(ang, shp, s_out, c_out, tmps, key):
            kf, ki, rr, mm = tmps
            rk = [key]
            TS("dve", kf, ang, 1.0 / TWO_PI, None, ALU.mult, None, rk, [key + "kf"])
            CP("dve", ki, kf, [key + "kf"], [key + "ki"])
            CP("dve", kf, ki, [key + "ki"], [key + "kf"])
            STT(rr, kf, -6.28125, ang, ALU.mult, ALU.add, [key + "kf"] + rk, [key + "rr"])
            STT(rr, kf, -(TWO_PI - 6.28125), rr, ALU.mult, ALU.add, [key + "kf", key + "rr"], [key + "rr"])

            def wrap(t):
                TS("dve", mm, t, -math.pi, None, ALU.is_lt, None, [key + "rr"], [key + "mm"])
                STT(t, mm, TWO_PI, t, ALU.mult, ALU.add, [key + "mm", key + "rr"], [key + "rr"])
                TS("dve", mm, t, math.pi, None, ALU.is_gt, None, [key + "rr"], [key + "mm"])
                STT(t, mm, -TWO_PI, t, ALU.mult, ALU.add, [key + "mm", key + "rr"], [key + "rr"])
            wrap(rr)
            ACT(s_out, rr, AF.Sin, [key + "rr"], [key + "s"])
            TS("dve", rr, rr, math.pi / 2.0, None, ALU.add, None, [key + "rr", key + "s"], [key + "rr"])
            wrap(rr)
            ACT(c_out, rr, AF.Sin, [key + "rr"], [key + "c"])

        def norm_fm(xt, N, gcol, sq, rs1, rs2, hT, psum_ap, kx, ksq, krs, kh, kps, gkey):
            ACT(sq, xt, AF.Square, [kx], [ksq])
            for kt in range(8):
                MM(psum_ap, ones_b[:], sq[:, kt, :], kt == 0, kt == 7, [ksq, "ones_b"], [kps])
            ACT(rs1, psum_ap, AF.Sqrt, [kps], [krs + "1"], bias=EPS, scale=1.0 / 1024.0)
            RECIP(rs2, rs1, [krs + "1"], [krs])
            for kt in range(8):
                STT(hT[:, kt, :], xt[:, kt, :], gcol[:, kt:kt + 1], rs2, ALU.mult, ALU.mult, [kx, krs, gkey], [kh])

        AR.reset()
        w_in = AR.take([128, 8, 5120], BF16)
        cosT = AR.take([128, NCH, 32], F32)
        sinT = AR.take([128, NCH, 32], F32)
        gng = AR.take([128, 1024], F32)
        xTb = [AR.take([128, 8, 256], F32) for _ in range(2)]
        sq = AR.take([128, 8, 256], BF16)
        posi = AR.take([128, NCH], I32)
        posf = AR.take([128, NCH], F32)
        Rc = AR.take([128, 4, 128], F32)
        mark = AR.off
        tki = AR.take([128, NCH * 32], I32)
        tmm = AR.take([128, NCH * 32], F32)
        trr_t = AR.take([128, NCH * 32], F32)
        AR.reset(mark)
        Gt = AR.take([128, 8, 4, 128], F32)
        AR.reset(mark)
        hTb = [AR.take([128, 8, 256], BF16) for _ in range(2)]
        rs1 = AR.take([128, 256], F32)
        rs2 = AR.take([128, 256], F32)
        qk = AR.take([128, 16, 2, 32], F32)
        rt = [AR.take([128, 16, 32], F32) for _ in range(4)]
        qkt = [AR.take([128, 16, 64], BF16) for _ in range(2)]
        v_bf = [AR.take([128, 1024], BF16) for _ in range(2)]
        gr_bf = [AR.take([128, 1024], BF16) for _ in range(2)]
        g5_bf = [AR.take([128, 1024], BF16) for _ in range(2)]
        u_bf = [AR.take([128, 1024], BF16) for _ in range(2)]
        kT = [AR.take([128, 4, 128], BF16) for _ in range(2)]
        qm = [AR.take([128, 8, 128], BF16) for _ in range(2)]
        PT = AR.take([128, 8, 128], BF16)
        o_sb = AR.take([128, 8, 128], F32)
        osq = AR.take([128, 8, 128], F32)
        ret = [AR.take([128, 1024], BF16) for _ in range(2)]
        retT_blk = AR.take([128, 8, 256], BF16)
        stt = AR.take([128, 48], F32)
        ang = xTb[0].rearrange("p a b -> p (a b)")[:, 0:NCH * 32]
        tkf = xTb[1].rearrange("p a b -> p (a b)")[:, 0:NCH * 32]
        trr = trr_t[:, 0:NCH * 32]

        dma(gng, gng_d, w=["gng"])
        MEMSET("pool", qm[0], 0.0, ["qm0"])
        MEMSET("pool", qm[1], 0.0, ["qm1"])
        dma(posi, pos_d, w=["posi"])
        CP("dve", posf, posi, ["posi"], ["posf"])
        TT("dve", ang.rearrange("p (c i) -> p c i", i=32), posf.unsqueeze(2).to_broadcast([128, NCH, 32]),
           invf[:].unsqueeze(1).to_broadcast([128, NCH, 32]), ALU.mult, ["posf", "invf", "xT0"], ["xT0"])
        sincos(ang, None, sinT.rearrange("p c i -> p (c i)"), cosT.rearrange("p c i -> p (c i)"),
               (tkf, tki, trr, tmm), "xT0")
        P.barrier()
        for cb in (1, 2, 3, 6, 7, 0, 4, 5, 8, 9):
            cs_ = slice(cb * 512, (cb + 1) * 512)
            dma(w_in[:, :, cs_], w_in_d[:, :, cs_], w=[f"w_in{cb}"], eng="pool")
            TT("dve", w_in[:, :, cs_], w_in[:, :, cs_], g1[:].unsqueeze(2).to_broadcast([128, 8, 512]), ALU.mult, [f"w_in{cb}", "g1"], [f"w_in{cb}"])

        qk2 = qk.rearrange("p h t i -> p (h t i)")
        qkt2 = [q_.rearrange("p h d -> p (h d)") for q_ in qkt]
        o2 = o_sb.rearrange("p h e -> p (h e)")
        PSUM_R2 = (mode == "coll") and not (isinstance(dbg, dict) and dbg.get("psum_new"))
        if PSUM_R2:
            KVp = PS[3].rearrange("p (h e) -> p h e", h=8)
            TRb = PS[0][:, 512:1024].bitcast(BF16)
            kvi = lambda h: h
            kvs = lambda par: slice(par, 8, 2)
        else:
            KVp = PS[3][:, 0:512].rearrange("p (b e) -> p b e", b=4)
            TRb = PS[3][:, 512:1024].bitcast(BF16)
            kvi = lambda h: h // 2
            kvs = lambda par: slice(0, 4, 1)
        SCp = PS[1]
        Op = PS[2]
        pj_i = [0]

        pj_n = [2]

        def pj():
            if PSUM_R2:
                return PS[0][:, 0:512], "PJ0"
            i = pj_i[0] % pj_n[0]
            pj_i[0] += 1
            return PS[i // 2][:, (i % 2) * 512:(i % 2 + 1) * 512], f"PJ{i}"

        nblk = NOWN_BLK
        LVL = dbg.get('own_lvl', 9) if isinstance(dbg, dict) else 9
        if mode == "coll":
            sa_iter = [("light", b_) for b_ in range(nblk)] + [("xchg", -1)] + [("full", b_) for b_ in range(nblk)]
            light_cbs = [1, 2, 3]
        else:
            sa_iter = [("light", b_) for b_ in range(NPRE)] + [("full", NPRE + b_) for b_ in range(nblk)]
            light_cbs = [1, 2, 3, 6, 7]
        def do_xchg():
            P.barrier()
            dma(ccR_in.ap()[0:64, :].rearrange("p (b e) -> p b e", b=4), Rf[0:64, 0::2, :], r=["Rf"], w=["ccR_in"])
            dma(ccR_in.ap()[64:128, :].rearrange("p (b e) -> p b e", b=4), Rf[64:128, 1::2, :], r=["Rf"], w=["ccR_in"])
            if FAKE_CC:
                dma(ccR_out.ap()[0:128, :], ccR_in.ap(), r=["ccR_in"], w=["ccR_out"])
            else:
                P.op("pool", lambda e: e.collective_compute("AllGather", ALU.bypass, replica_groups=[list(range(8))],
                                                            ins=[ccR_in.ap().opt()], outs=[ccR_out.ap().opt()]),
                     r=["ccR_in"], w=["ccR_out"])
            dma(Gt.rearrange("p r b e -> p r (b e)"), ccR_out.ap().rearrange("(r p) c -> p r c", p=128), r=["ccR_out"], w=["Gt"])
            TT("dve", Gt, Gt, wtr[:].unsqueeze(3).to_broadcast([128, 8, 4, 128]), ALU.mult, ["Gt", "wtr"], ["Gt"])
            RED(Rc, Gt.rearrange("p r b e -> p b e r"), ALU.add, ["Gt"], ["Rc"])
            CP("act", Rf[0:64, 0::2, :], Rc[0:64, :, :], ["Rc", "Rf"], ["Rf"])
            CP("act", Rf[64:128, 1::2, :], Rc[64:128, :, :], ["Rc", "Rf"], ["Rf"])
            CP("pool", Rbf[:], Rf[:], ["Rf"], ["Rbf"])
            P.barrier()


        def load_block(blk):
            dma(xTb[blk % 2], xT_d[:, :, blk * 256:(blk + 1) * 256], w=[f"xT{blk % 2}"])

        def norm_block(blk):
            xb = xTb[blk % 2]
            kx = f"xT{blk % 2}"
            bb = blk % 2
            ACT(sq, xb, AF.Square, [kx], ["sq"])
            ACT(hTb[bb], xb, AF.Copy, [kx], [f"hT{bb}"])
            pa, pk = pj()
            for ci in range(2):
                for kt in range(8):
                    MM(pa[:, ci:ci + 1], sq[:, kt, ci * 128:(ci + 1) * 128], ones_b[:, 0:1], kt == 0, kt == 7, ["sq", "ones_b"], [pk])
            ACT(rs1[:, 2 * bb:2 * bb + 2], pa[:, 0:2], AF.Sqrt, [pk], [f"rsa{bb}"], bias=EPS, scale=1.0 / 1024.0)
            RECIP(rs2[:, 2 * bb:2 * bb + 2], rs1[:, 2 * bb:2 * bb + 2], [f"rsa{bb}"], [f"rsb{bb}"])

        def phaseA(pas, blk, ci):
            own = pas == "full"
            c = blk * 2 + ci
            d = c % 2
            hT = hTb[blk % 2]
            hk = f"hT{blk % 2}"
            hs = lambda kt: hT[:, kt, ci * 128:(ci + 1) * 128]
            cbs = [0, 1, 2, 3, 4, 5, 6, 7, 8, 9] if own else light_cbs
            for cb in cbs:
                pa, pk = pj()
                for kt in range(8):
                    MM(pa, hs(kt), w_in[:, kt, cb * 512:(cb + 1) * 512], kt == 0, kt == 7, [hk, f"w_in{cb}"], [pk])
                rcol = rs2[:, 2 * (blk % 2) + ci:2 * (blk % 2) + ci + 1]
                rk = f"rsb{blk % 2}"
                if cb == 0:
                    ACT(qk2[:, 0:512], pa, AF.Copy, [pk, rk], ["qk_q"], scale=rcol)
                elif cb == 1:
                    ACT(qk2[:, 512:1024], pa, AF.Copy, [pk, rk], ["qk_k"], scale=rcol)
                elif cb in (2, 3):
                    TS("dve", v_bf[d][:, (cb - 2) * 512:(cb - 1) * 512], pa, rcol, None, ALU.mult, None, [pk, rk], [f"v_bf{d}"])
                elif cb in (4, 5):
                    ACT(gr_bf[d][:, (cb - 4) * 512:(cb - 3) * 512], pa, AF.Silu, [pk, rk], [f"gr_bf{d}"], scale=rcol)
                elif cb in (6, 7):
                    TS("dve", u_bf[d][:, (cb - 6) * 512:(cb - 5) * 512], pa, rcol, None, ALU.mult, None, [pk, rk], [f"u_bf{d}"])
                else:
                    ACT(g5_bf[d][:, (cb - 8) * 512:(cb - 7) * 512], pa, AF.Silu, [pk, rk], [f"g5_bf{d}"], scale=rcol)
            if own or 6 in light_cbs:
                dma(u_d[c * 128:(c + 1) * 128, :], u_bf[d], r=[f"u_bf{d}"])
            if own:
                oc = c - 2 * NPRE
                dma(g5_d[oc * 128:(oc + 1) * 128, :], g5_bf[d], r=[f"g5_bf{d}"])
            h0 = 0 if own else 8
            nh = 16 - h0
            cosb = cosT[:, c, :].unsqueeze(1).to_broadcast([128, nh, 32])
            sinb = sinT[:, c, :].unsqueeze(1).to_broadcast([128, nh, 32])
            x1 = qk[:, h0:16, 0, :]
            x2 = qk[:, h0:16, 1, :]
            rq = ["qk_q", "qk_k"] if own else ["qk_k"]
            qk3 = qk[:, h0:16, :, :].rearrange("p h t i -> p h (t i)")
            TT("pool", qk3, qk3, sqk[:, h0:16].unsqueeze(2).to_broadcast([128, nh, 64]), ALU.mult, rq + ["sqk"], rq)
            q4 = qkt[d].rearrange("p h (t i) -> p h t i", t=2)
            TT("dve", rt[0][:, h0:16, :], x1, cosb, ALU.mult, rq, ["rt0"])
            TT("dve", rt[1][:, h0:16, :], x2, sinb, ALU.mult, rq, ["rt1"])
            TT("dve", q4[:, h0:16, 0, :], rt[0][:, h0:16, :], rt[1][:, h0:16, :], ALU.subtract, ["rt0", "rt1"], [f"qkt{d}"])
            TT("pool", rt[2][:, h0:16, :], x1, sinb, ALU.mult, rq, ["rt2"])
            TT("pool", rt[3][:, h0:16, :], x2, cosb, ALU.mult, rq, ["rt3"])
            TT("pool", q4[:, h0:16, 1, :], rt[2][:, h0:16, :], rt[3][:, h0:16, :], ALU.add, ["rt2", "rt3"], [f"qkt{d}"])
            if own:
                for b in range(8):
                    TR(TRb[:, b * 128:(b + 1) * 128], qkt2[d][:, b * 128:(b + 1) * 128], identb[:], [f"qkt{d}", "identb"], ["TRb"])
                CP("act", kT[d].rearrange("p b t -> p (b t)"), TRb[:, 512:1024], ["TRb"], [f"kT{d}"])
                CP("act", qm[d][0:64, 0::2, :], TRb[0:64, 0:512].rearrange("p (b t) -> p b t", b=4), ["TRb"], [f"qm{d}"])
                CP("act", qm[d][64:128, 1::2, :], TRb[64:128, 0:512].rearrange("p (b t) -> p b t", b=4), ["TRb"], [f"qm{d}"])

        def phaseB(pas, blk, ci):
            own = pas == "full"
            c = blk * 2 + ci
            d = c % 2
            for h in range(8):
                MM(KVp[(h % 2) * 64:(h % 2) * 64 + 64, kvi(h), :], qkt2[d][:, 512 + h * 64:512 + (h + 1) * 64],
                   v_bf[d][:, h * 128:(h + 1) * 128], True, True, [f"qkt{d}", f"v_bf{d}"], ["KVp"])
            if own:
                for h in range(8):
                    MM(SCp[:, h * 128:(h + 1) * 128], kT[d][:, h // 2, :], qm[d][:, h, :], True, True, [f"kT{d}", f"qm{d}"], ["SCp"])
                TT("dve", PT, SCp.rearrange("p (h i) -> p h i", h=8), maskT[:].unsqueeze(1).to_broadcast([128, 8, 128]),
                   ALU.mult, ["SCp", "maskT"], ["PT"])
                for h in range(8):
                    MM(Op[:, h * 128:(h + 1) * 128], PT[:, h, :], v_bf[d][:, h * 128:(h + 1) * 128], True, False, ["PT", f"v_bf{d}"], ["Op"])
                    MM(Op[:, h * 128:(h + 1) * 128], qm[d][:, h, :], Rbf[:, h, :], False, True, [f"qm{d}", "Rbf"], ["Op"])
            for par in range(2):
                rows = slice(par * 64, par * 64 + 64)
                TT("dve", Rf[rows, par::2, :], KVp[rows, kvs(par), :], Rf[rows, par::2, :], ALU.add, ["KVp", "Rf"], ["Rf"])
                TT("dve", Rf[rows, par::2, :], Rf[rows, par::2, :], cdt[rows, par::2].unsqueeze(2).to_broadcast([64, 4, 128]), ALU.mult, ["Rf", "cdt"], ["Rf"])
                CP("act", Rbf[rows, par::2, :], Rf[rows, par::2, :], ["Rf"], ["Rbf"])
            if own:
                CP("act", o2, Op, ["Op"], ["o_sb"])
                ACT(osq.rearrange("p h e -> p (h e)"), Op, AF.Square, ["Op"], ["osq"])
                RED(stt[:, 0:8], o_sb, ALU.add, ["o_sb"], ["st_s"])
                RED(stt[:, 8:16], osq, ALU.add, ["osq"], ["st_q"])
                TS("dve", stt[:, 16:24], stt[:, 0:8], 1.0 / 128.0, None, ALU.mult, None, ["st_s"], ["st_m"])
                TT("dve", stt[:, 24:32], stt[:, 16:24], stt[:, 16:24], ALU.mult, ["st_m"], ["st_m2"])
                STT(stt[:, 32:40], stt[:, 8:16], 1.0 / 128.0, stt[:, 24:32], ALU.mult, ALU.subtract, ["st_q", "st_m2"], ["st_v"])
                ACT(stt[:, 40:48], stt[:, 32:40], AF.Sqrt, ["st_v"], ["st_sd"], bias=EPS, scale=1.0)
                RECIP(stt[:, 32:40], stt[:, 40:48], ["st_sd"], ["st_rs"])
                TT("pool", o_sb, o_sb, stt[:, 16:24].unsqueeze(2).to_broadcast([128, 8, 128]), ALU.subtract, ["o_sb", "st_m"], ["o_sb"])
                TT("pool", o_sb, o_sb, stt[:, 32:40].unsqueeze(2).to_broadcast([128, 8, 128]), ALU.mult, ["o_sb", "st_rs"], ["o_sb"])
                TT("dve", o2, o2, gng, ALU.mult, ["o_sb", "gng"], ["o_sb"])
                TT("dve", ret[d], o2, gr_bf[d], ALU.mult, ["o_sb", f"gr_bf{d}"], [f"ret{d}"])

        def phaseC(pas, blk, ci):
            if pas != "full":
                return
            c = blk * 2 + ci
            d = c % 2
            for kt in range(8):
                TR(TRb[:, kt * 128:(kt + 1) * 128], ret[d][:, kt * 128:(kt + 1) * 128], identb[:], [f"ret{d}", "identb"], ["TRb"])
            CP("act", retT_blk[:, :, ci * 128:(ci + 1) * 128], TRb.rearrange("p (k t) -> p k t", k=8), ["TRb"], ["retT_blk"])
            if ci == 1:
                ob = blk - NPRE
                dma(retT_d[:, :, ob * 256:(ob + 1) * 256], retT_blk, r=["retT_blk"])

        pendB = None
        pendC = None
        real = [(p_, b_) for p_, b_ in sa_iter if p_ != "xchg"]
        has_x = any(p_ == "xchg" for p_, _ in sa_iter)
        load_block(real[0][1])
        norm_block(real[0][1])
        ri = 0
        prev_pas = None
        for pas, blk in sa_iter:
            if pas == "light" and not PSUM_R2:
                pj_n[0] = 4
            if pas == "full" and prev_pas == "light" and mode != "coll":
                if pendB is not None:
                    phaseB(*pendB)
                    pendB = None
                P.barrier()
            if pas == "full":
                pj_n[0] = 2
            prev_pas = pas
            if pas == "xchg":
                if pendB is not None:
                    phaseB(*pendB)
                    pendB = None
                do_xchg()
                load_block(real[ri][1])
                norm_block(real[ri][1])
                continue
            nxt = real[ri + 1] if ri + 1 < len(real) else None
            if nxt is not None and has_x and nxt[0] != pas:
                nxt = None
            for ci in range(2):
                if ci == 0 and nxt is not None:
                    load_block(nxt[1])
                phaseA(pas, blk, ci)
                if ci == 0 and nxt is not None:
                    norm_block(nxt[1])
                if pendC is not None:
                    phaseC(*pendC)
                    pendC = None
                if pendB is not None:
                    phaseB(*pendB)
                    pendC = pendB
                pendB = (pas, blk, ci)
            ri += 1
        if pendC is not None:
            phaseC(*pendC)
        if pendB is not None:
            phaseB(*pendB)
            phaseC(*pendB)
        P.barrier()
        if stop_after == "SA":
            return finish(nc, P, st, out_d)

        AR.reset()
        T0 = AR.take([128, 64, 128], BF16)
        I2Sre = AR.take([128, 64, 64], BF16)
        I2Sim = AR.take([128, 64, 64], BF16)
        S2Ore = AR.take([128, 32, 128], BF16)
        S2Oim = AR.take([128, 32, 128], BF16)
        TIre = AR.take([128, 32, 16], F32)
        TIim = AR.take([128, 32, 16], F32)
        TOre = AR.take([128, 32, 8], F32)
        TOim = AR.take([128, 32, 8], F32)
        P8 = AR.take([128, 32], F32)
        Q8 = AR.take([128, 32], F32)
        P128 = AR.take([128, 32], F32)
        Q128 = AR.take([128, 32], F32)
        base_sb = AR.off
        are = AR.take([128, 32], F32)
        aim = AR.take([128, 32], F32)
        ldt = AR.take([128, 32], F32)
        bre = AR.take([128, 32, 16], F32)
        bim = AR.take([128, 32, 16], F32)
        cre = AR.take([128, 32, 16], F32)
        cim = AR.take([128, 32, 16], F32)
        dtt = AR.take([128, 32], F32)
        tA = [AR.take([128, 32], F32) for _ in range(12)]
        tI = AR.take([128, 32], I32)
        PWr = AR.take([128, 9, 32], F32)
        PWi = AR.take([128, 9, 32], F32)
        NWr = AR.take([128, 9, 32], F32)
        NWi = AR.take([128, 9, 32], F32)
        E8r = AR.take([128, 17, 32], F32)
        E8i = AR.take([128, 17, 32], F32)
        E16r = AR.take([128, 9, 32], F32)
        E16i = AR.take([128, 9, 32], F32)
        Bbr = AR.take([128, 32, 16], F32)
        Bbi = AR.take([128, 32, 16], F32)
        Gre = AR.take([128, 32, 8, 16], F32)
        Gim = AR.take([128, 32, 8, 16], F32)
        Sre = AR.take([128, 32, 8, 16], F32)
        Sim = AR.take([128, 32, 8, 16], F32)
        Ire = AR.take([128, 32, 8, 16], F32)
        Iim = AR.take([128, 32, 8, 16], F32)
        tB = [AR.take([128, 32, 16], F32) for _ in range(2)]
        T0f = AR.take([128, 4, 128], F32)
        dcol = AR.take([128, 64], F32)

        K0 = ["S0B"]
        dma(dcol, dcol_d, w=K0)
        for t_, d_ in ((are, are_d), (aim, aim_d), (ldt, ldt_d), (bre, bre_d), (bim, bim_d), (cre, cre_d), (cim, cim_d)):
            dma(t_, d_, w=K0)
        P.barrier()
        V = lambda out, in0, in1, op: TT("dve", out, in0, in1, op, K0, K0)
        VS = lambda out, in0, s1, op: TS("dve", out, in0, s1, None, op, None, K0, K0)
        VC = lambda out, in_: CP("dve", out, in_, K0, K0)
        VM = lambda ap, val: MEMSET("dve", ap, val, K0)
        ACT(dtt, ldt, AF.Exp, K0, K0)
        V(tA[0], are, dtt, ALU.mult)
        ACT(tA[1], tA[0], AF.Exp, K0, K0)
        ACT(tA[2], tA[0], AF.Exp, K0, K0, scale=-1.0)
        V(tA[3], aim, dtt, ALU.mult)
        sincos(tA[3], None, tA[4], tA[5], (tA[6], tI, tA[7], tA[8]), "S0B")
        P.barrier()

        def cmul(orr, oi, xr, xi, yr, yi, tmp):
            u1, u2 = tmp
            V(u1, xr, yr, ALU.mult)
            V(u2, xi, yi, ALU.mult)
            V(orr, u1, u2, ALU.subtract)
            V(u1, xr, yi, ALU.mult)
            V(u2, xi, yr, ALU.mult)
            V(oi, u1, u2, ALU.add)

        tsm = (tA[10], tA[11])
        VM(PWr[:, 0, :], 1.0)
        VM(PWi[:, 0, :], 0.0)
        V(PWr[:, 1, :], tA[1], tA[5], ALU.mult)
        V(PWi[:, 1, :], tA[1], tA[4], ALU.mult)
        VM(NWr[:, 0, :], 1.0)
        VM(NWi[:, 0, :], 0.0)
        V(NWr[:, 1, :], tA[2], tA[5], ALU.mult)
        V(tA[9], tA[2], tA[4], ALU.mult)
        VS(NWi[:, 1, :], tA[9], -1.0, ALU.mult)
        for k in range(1, 8):
            cmul(PWr[:, k + 1, :], PWi[:, k + 1, :], PWr[:, k, :], PWi[:, k, :], PWr[:, 1, :], PWi[:, 1, :], tsm)
            cmul(NWr[:, k + 1, :], NWi[:, k + 1, :], NWr[:, k, :], NWi[:, k, :], NWr[:, 1, :], NWi[:, 1, :], tsm)
        VM(E8r[:, 0, :], 1.0)
        VM(E8i[:, 0, :], 0.0)
        VC(E8r[:, 1, :], PWr[:, 8, :])
        VC(E8i[:, 1, :], PWi[:, 8, :])
        def tmpn(n):
            f_ = lambda t: t.rearrange("p g j -> p (g j)")[:, 0:n * 32].rearrange("p (k g) -> p k g", k=n)
            return (f_(tB[0]), f_(tB[1]))

        def pow_double(Er, Ei, top):
            cmul(Er[:, 2, :], Ei[:, 2, :], Er[:, 1, :], Ei[:, 1, :], Er[:, 1, :], Ei[:, 1, :], tsm)
            n = 2
            while n < top:
                br = Er[:, n, :].unsqueeze(1).to_broadcast([128, n, 32])
                bi = Ei[:, n, :].unsqueeze(1).to_broadcast([128, n, 32])
                cmul(Er[:, n + 1:2 * n + 1, :], Ei[:, n + 1:2 * n + 1, :], Er[:, 1:n + 1, :], Ei[:, 1:n + 1, :], br, bi, tmpn(n))
                n *= 2

        pow_double(E8r, E8i, 16)
        VM(E16r[:, 0, :], 1.0)
        VM(E16i[:, 0, :], 0.0)
        VC(E16r[:, 1, :], E8r[:, 16, :])
        VC(E16i[:, 1, :], E8i[:, 16, :])
        pow_double(E16r, E16i, 8)
        VC(P8, E8r[:, 1, :])
        VC(Q8, E8i[:, 1, :])
        VC(P128, E16r[:, 8, :])
        VC(Q128, E16i[:, 8, :])
        cmul(A2r[:, 0, :], A2i[:, 0, :], P128, Q128, P128, Q128, tsm)
        cmul(A2r[:, 1, :], A2i[:, 1, :], A2r[:, 0, :], A2i[:, 0, :], A2r[:, 0, :], A2i[:, 0, :], tsm)
        P.barrier()
        for r_ in range(16):
            CP("act", TIre[:, :, r_], E8r[:, 15 - r_, :], [], [f"TIr{r_}"])
            CP("pool", TIim[:, :, r_], E8i[:, 15 - r_, :], [], [f"TIi{r_}"])
        for q_ in range(8):
            CP("act", TOre[:, :, q_], E16r[:, 7 - q_, :], [], [f"TOr{q_}"])
            CP("pool", TOim[:, :, q_], E16i[:, 7 - q_, :], [], [f"TOi{q_}"])
        V(tA[0], are, are, ALU.mult)
        V(tA[1], aim, aim, ALU.mult)
        V(tA[0], tA[0], tA[1], ALU.add)
        RECIP(tA[0], tA[0], K0, K0)
        VS(tA[1], PWr[:, 1, :], -1.0, ALU.add)
        V(tA[2], tA[1], are, ALU.mult)
        V(tA[3], PWi[:, 1, :], aim, ALU.mult)
        V(tA[2], tA[2], tA[3], ALU.add)
        V(tA[2], tA[2], tA[0], ALU.mult)
        V(tA[3], PWi[:, 1, :], are, ALU.mult)
        V(tA[4], tA[1], aim, ALU.mult)
        V(tA[3], tA[3], tA[4], ALU.subtract)
        V(tA[3], tA[3], tA[0], ALU.mult)
        bc16 = lambda ap: ap.unsqueeze(2).to_broadcast([128, 32, 16])
        cmul(Bbr, Bbi, bre, bim, bc16(tA[2]), bc16(tA[3]), (tB[0], tB[1]))
        for s_ in range(8):
            cmul(Gre[:, :, s_, :], Gim[:, :, s_, :], Bbr, Bbi, bc16(NWr[:, s_ + 1, :]), bc16(NWi[:, s_ + 1, :]), (tB[0], tB[1]))
            cmul(Ire[:, :, s_, :], Iim[:, :, s_, :], Bbr, Bbi, bc16(PWr[:, 7 - s_, :]), bc16(PWi[:, 7 - s_, :]), (tB[0], tB[1]))
            cmul(Sre[:, :, s_, :], Sim[:, :, s_, :], cre, cim, bc16(PWr[:, s_ + 1, :]), bc16(PWi[:, s_ + 1, :]), (tB[0], tB[1]))
        Sim2 = Sim.rearrange("p g t j -> p (g t j)")
        VS(Sim2, Sim2, -1.0, ALU.mult)
        P.barrier()
        CP("act", S2Ore.rearrange("p g x -> p (g x)"), Sre.rearrange("p g t j -> p (g t j)"), K0, ["S2O"])
        CP("act", S2Oim.rearrange("p g x -> p (g x)"), Sim2, K0, ["S2O"])
        Gre3 = Gre.rearrange("p g s j -> p g (s j)")
        Gim3 = Gim.rearrange("p g s j -> p g (s j)")
        Sre3 = Sre.rearrange("p g t j -> p g (t j)")
        Sim3 = Sim.rearrange("p g t j -> p g (t j)")
        Ire3 = Ire.rearrange("p g s j -> p g (s j)")
        Iim3 = Iim.rearrange("p g s j -> p g (s j)")
        for gb in range(16):
            bank = PS[gb % 2][:, 0:512]
            bk = f"PSb{gb % 2}"
            for i in range(4):
                g = gb * 4 + i
                gh, gp = g // 32, g % 32
                pb = gh * 64
                MM(bank[:, i * 128:(i + 1) * 128], Gre3[pb:pb + 64, gp, :], Sre3[pb:pb + 64, gp, :], True, False, K0, [bk])
                MM(bank[:, i * 128:(i + 1) * 128], Gim3[pb:pb + 64, gp, :], Sim3[pb:pb + 64, gp, :], False, True, K0, [bk])
            TT("dve", T0f, bank.rearrange("p (g x) -> p g x", g=4),
               t0mask[:].unsqueeze(1).to_broadcast([128, 4, 128]), ALU.mult, [bk, "t0mask"] + K0, K0)
            for i in range(4):
                g = gb * 4 + i
                STT(T0[:, g, :], identf[:], dcol[:, g:g + 1], T0f[:, i, :], ALU.mult, ALU.add, K0 + ["identf"], K0 + ["T0"])
        for src3, dst, nm in ((Ire3, I2Sre, "I2Sre"), (Iim3, I2Sim, "I2Sim")):
            for gb in range(8):
                bank = PS[2 + gb % 2][:, 0:512]
                bk = f"PSc{gb % 2}"
                for i in range(8):
                    g = gb * 8 + i
                    gh, gp = g // 32, g % 32
                    pb = gh * 64
                    TR(bank[:, i * 64:(i + 1) * 64], src3[pb:pb + 64, gp, :], identf[pb:pb + 64, pb:pb + 64], ["identf", "S0B"], [bk])
                CP("act", dst[:, gb * 8:(gb + 1) * 8, :], bank.rearrange("p (g n) -> p g n", g=8), [bk], [nm])
        P.barrier()
        if stop_after == "S0b":
            return finish(nc, P, st, out_d)

        AR.reset(base_sb)
        u_raw = AR.take([128, 8, 1024], BF16)
        U = u_raw.rearrange("p s c -> p (s c)").rearrange("p (g m) -> p g m", g=64)
        u_tm2 = AR.take([128, 64, 8, 16], BF16)
        inc_raw = AR.take([128, 4096], F32)
        Incre = inc_raw.rearrange("p (g m) -> p g m", g=32)
        yg_bf = inc_raw.bitcast(BF16).rearrange("p (s c) -> p s c", s=8)
        Incim = AR.take([128, 32, 128], F32)
        gluw = AR.take([128, 8, 1024], BF16)
        glub = AR.take([128, 1024], F32)
        g5s = [AR.take([128, 1024], BF16) for _ in range(2)]
        ct = [AR.take([128, 32], F32) for _ in range(12)]
        un_base = AR.off
        LA = AR.take([128, 32, 8, 16], F32)
        LB = AR.take([128, 32, 8, 16], F32)
        LS = [AR.take([128, 32, 8], F32) for _ in range(4)]
        LE = [AR.take([128, 32], F32) for _ in range(4)]
        AR.reset(un_base)
        Xre_bf = AR.take([128, 32, 128], BF16)
        Xim_bf = AR.take([128, 32, 128], BF16)
        Ysb = [AR.take([128, 8, 128], F32) for _ in range(2)]
        gt1 = [AR.take([128, 1024], F32) for _ in range(2)]
        gt2 = [AR.take([128, 1024], F32) for _ in range(2)]
        AR.reset(un_base)
        ygT = [AR.take([128, 8, 128], BF16) for _ in range(2)]
        zb = [AR.take([128, 1024], F32) for _ in range(2)]
        gate = [AR.take([128, 1024], F32) for _ in range(2)]
        tt_ = [AR.take([128, 1024], F32) for _ in range(2)]
        ssm = [AR.take([128, 1024], BF16) for _ in range(2)]
        ssmT = [AR.take([128, 8, 128], BF16) for _ in range(2)]
        AR.reset(un_base)

        dma(gluw, gluw_d, w=["gluw"], eng="pool")
        dma(glub, glub_d, w=["glub"])

        TRU = PS[0][:, 0:512].bitcast(BF16)
        PIre = PS[1][:, 0:512]
        PIim = PS[1][:, 512:1024]
        TRU2 = [PS[0][:, 0:512].bitcast(BF16), PS[0][:, 512:1024].bitcast(BF16)]
        PI2 = [(PS[1][:, 0:512], PS[1][:, 512:1024]), (PS[2][:, 0:512], PS[2][:, 512:1024])]
        PY = PS[2]
        PYT = PS[3]
        IRK = [f"IR{m}" for m in range(128)]
        IIK = [f"II{m}" for m in range(128)]

        Gs = AR.take([128, 8, 64], F32)
        Gs2 = AR.take([128, 8, 64], F32)
        Sd = AR.take([128, 3, 64], F32)
        xt_ = [AR.take([128, 32], F32) for _ in range(4)]
        NPS = NPRE // 4
        if mode == "coll":
            sb_iter = [("light", 0), ("light", 1), ("xchg", -1), ("full", 0), ("full", 1)]
        else:
            sb_iter = [("light", sc_) for sc_ in range(NPS)] + [("full", NPS), ("full", NPS + 1)]
        for pas, sc in sb_iter:
            if pas == "xchg":
                P.barrier()
                dma(ccS_in.ap()[:, 0:32], car_r[:], r=["car_r"], w=["ccS_in"])
                dma(ccS_in.ap()[:, 32:64], car_i[:], r=["car_i"], w=["ccS_in"])
                if FAKE_CC:
                    dma(ccS_out.ap()[0:128, :], ccS_in.ap(), r=["ccS_in"], w=["ccS_out"])
                else:
                    P.op("pool", lambda e: e.collective_compute("AllGather", ALU.bypass, replica_groups=[list(range(8))],
                                                                ins=[ccS_in.ap().opt()], outs=[ccS_out.ap().opt()]),
                         r=["ccS_in"], w=["ccS_out"])
                dma(Gs, ccS_out.ap().rearrange("(r p) c -> p r c", p=128), r=["ccS_out"], w=["Gs"])
                KX = ["XCH"]
                for d_ in range(3):
                    TT("dve", Gs2, Gs, sel5[:, d_, :].unsqueeze(2).to_broadcast([128, 8, 64]), ALU.mult, ["Gs", "sel5"] + KX, KX)
                    RED(Sd[:, d_, :], Gs2.rearrange("p r c -> p c r"), ALU.add, KX, KX)
                XV = lambda out, in0, in1, op: TT("dve", out, in0, in1, op, KX + ["car_r", "car_i"], KX)
                XV(car_r[:], Sd[:, 0, 0:32], Sd[:, 0, 0:32], ALU.bypass) if False else CP("dve", car_r[:], Sd[:, 0, 0:32], KX + ["car_r"], KX + ["car_r"])
                CP("dve", car_i[:], Sd[:, 0, 32:64], KX + ["car_i"], KX + ["car_i"])
                for d_ in (1, 2):
                    sr, si = Sd[:, d_, 0:32], Sd[:, d_, 32:64]
                    ar_, ai_ = A2r[:, d_ - 1, :], A2i[:, d_ - 1, :]
                    XV(xt_[0], sr, ar_, ALU.mult)
                    XV(xt_[1], si, ai_, ALU.mult)
                    XV(xt_[0], xt_[0], xt_[1], ALU.subtract)
                    TT("dve", car_r[:], car_r[:], xt_[0], ALU.add, KX + ["car_r"], KX + ["car_r"])
                    XV(xt_[2], sr, ai_, ALU.mult)
                    XV(xt_[3], si, ar_, ALU.mult)
                    XV(xt_[2], xt_[2], xt_[3], ALU.add)
                    TT("dve", car_i[:], car_i[:], xt_[2], ALU.add, KX + ["car_i"], KX + ["car_i"])
                P.barrier()
                continue
            own = pas == "full"
            if own:
                P.barrier()
            dma(u_raw, u_d[sc * 1024:(sc + 1) * 1024, :].rearrange("(m s) c -> m s c", s=8), w=["u_raw", "U"])
            src4 = u_raw.rearrange("p s (g j) -> p g s j", j=16)
            CP("dve", u_tm2[:, 0:32, :, :], src4[:, 0:32, :, :], ["u_raw"], ["u_tm2a"])
            CP("act", u_tm2[:, 32:64, :, :], src4[:, 32:64, :, :], ["u_raw"], ["u_tm2b"])
            u3 = u_tm2.rearrange("p g s j -> p g (s j)")
            for gb in range(8):
                tru = TRU2[gb % 2]
                tk = f"TRU{gb % 2}"
                for i in range(8):
                    g = gb * 8 + i
                    TR(tru[:, i * 128:(i + 1) * 128], u3[:, g, :], identb[:], ["u_tm2a", "u_tm2b", "identb"], [tk])
                CP("act", U[:, gb * 8:(gb + 1) * 8, :], tru.rearrange("p (g m) -> p g m", g=8),
                   [tk], ["U", "u_raw"])
            for b4 in range(8):
                pre, pim = PI2[b4 % 2]
                kre, kim = f"PIre{b4 % 2}", f"PIim{b4 % 2}"
                for i in range(4):
                    gp = b4 * 4 + i
                    for gh in range(2):
                        g = gh * 32 + gp
                        MM(pre[gh * 64:(gh + 1) * 64, i * 128:(i + 1) * 128], I2Sre[:, g, :], U[:, g, :], True, True, ["U", "I2Sre"], [kre])
                        MM(pim[gh * 64:(gh + 1) * 64, i * 128:(i + 1) * 128], I2Sim[:, g, :], U[:, g, :], True, True, ["U", "I2Sim"], [kim])
                CP("act", Incre[:, b4 * 4:(b4 + 1) * 4, :], pre.rearrange("p (g m) -> p g m", g=4), [kre], IRK)
                CP("dve", Incim[:, b4 * 4:(b4 + 1) * 4, :], pim.rearrange("p (g m) -> p g m", g=4), [kim], IIK)
            if not own:
                I4r = Incre.rearrange("p g (q r) -> p g q r", r=16)
                I4i = Incim.rearrange("p g (q r) -> p g q r", r=16)
                tir = TIre.unsqueeze(2).to_broadcast([128, 32, 8, 16])
                tii = TIim.unsqueeze(2).to_broadcast([128, 32, 8, 16])
                TT("dve", LA, I4r, tir, ALU.mult, IRK + ["TI"], ["LA"])
                TT("dve", LB, I4i, tii, ALU.mult, IIK + ["TI"], ["LB"])
                TT("dve", LA, LA, LB, ALU.subtract, ["LA", "LB"], ["LA"])
                RED(LS[0], LA, ALU.add, ["LA"], ["LS0"])
                TT("dve", LA, I4i, tir, ALU.mult, IIK + ["TI", "LS0"], ["LA"])
                TT("dve", LB, I4r, tii, ALU.mult, IRK + ["TI", "LA"], ["LB"])
                TT("dve", LA, LA, LB, ALU.add, ["LA", "LB"], ["LA"])
                RED(LS[1], LA, ALU.add, ["LA"], ["LS1"])
                TT("dve", LS[2], LS[0], TOre, ALU.mult, ["LS0", "TO"], ["LS2"])
                TT("dve", LS[3], LS[1], TOim, ALU.mult, ["LS1", "TO"], ["LS3"])
                TT("dve", LS[2], LS[2], LS[3], ALU.subtract, ["LS2", "LS3"], ["LS2"])
                RED(LE[0], LS[2], ALU.add, ["LS2"], ["LE0"])
                TT("dve", LS[2], LS[1], TOre, ALU.mult, ["LS1", "TO", "LE0"], ["LS2"])
                TT("dve", LS[3], LS[0], TOim, ALU.mult, ["LS0", "TO", "LS2"], ["LS3"])
                TT("dve", LS[2], LS[2], LS[3], ALU.add, ["LS2", "LS3"], ["LS2"])
                RED(LE[1], LS[2], ALU.add, ["LS2"], ["LE1"])
                TT("dve", LE[2], car_r[:], P128, ALU.mult, ["car_r", "P128"], ["LE2"])
                TT("dve", LE[3], car_i[:], Q128, ALU.mult, ["car_i", "Q128"], ["LE3"])
                TT("dve", LE[2], LE[2], LE[3], ALU.subtract, ["LE2", "LE3"], ["LE2"])
                TT("dve", LE[0], LE[0], LE[2], ALU.add, ["LE0", "LE2"], ["LE0"])
                TT("dve", LE[2], car_i[:], P128, ALU.mult, ["car_i", "P128", "LE0"], ["LE2"])
                TT("dve", LE[3], car_r[:], Q128, ALU.mult, ["car_r", "Q128", "LE0"], ["LE3"])
                TT("dve", LE[2], LE[2], LE[3], ALU.add, ["LE2", "LE3"], ["LE2"])
                TT("dve", car_i[:], LE[1], LE[2], ALU.add, ["LE1", "LE2"], ["car_i"])
                CP("dve", car_r[:], LE[0], ["LE0"], ["car_r"])
                continue
            CP("dve", Xre_bf[:, :, 0], car_r[:], ["car_r"], ["Xre0"])
            CP("dve", Xim_bf[:, :, 0], car_i[:], ["car_i"], ["Xim0"])
            for m in range(128):
                if m == 0:
                    xr, xi, kr, ki = car_r[:], car_i[:], "car_r", "car_i"
                else:
                    xr, xi, kr, ki = Incre[:, :, m - 1], Incim[:, :, m - 1], IRK[m - 1], IIK[m - 1]
                pp = (m % 2) * 6
                c0, c1, c2, c3, c4, c5 = ct[pp:pp + 6]
                n0 = [f"ct{pp + j}" for j in range(6)]
                TT("dve", c0, xr, P8, ALU.mult, [kr, "P8"], [n0[0]])
                TT("dve", c1, xi, Q8, ALU.mult, [ki, "Q8"], [n0[1]])
                TT("dve", c3, xi, P8, ALU.mult, [ki, "P8"], [n0[3]])
                TT("dve", c4, xr, Q8, ALU.mult, [kr, "Q8"], [n0[4]])
                TT("dve", c2, c0, c1, ALU.subtract, [n0[0], n0[1]], [n0[2]])
                TT("dve", c5, c3, c4, ALU.add, [n0[3], n0[4]], [n0[5]])
                TT("dve", Incre[:, :, m], c2, Incre[:, :, m], ALU.add, [n0[2], IRK[m]], [IRK[m]])
                TT("dve", Incim[:, :, m], c5, Incim[:, :, m], ALU.add, [n0[5], IIK[m]], [IIK[m]])
            CP("dve", car_r[:], Incre[:, :, 127], [IRK[127], "Xre0"], ["car_r"])
            CP("dve", car_i[:], Incim[:, :, 127], [IIK[127], "Xim0"], ["car_i"])
            CP("act", Xre_bf[:, :, 1:128], Incre[:, :, 0:127], IRK, ["Xre"])
            CP("dve", Xim_bf[:, :, 1:128], Incim[:, :, 0:127], IIK, ["Xim"])
            PYb = [PS[1], PS[2]]
            PYTb = [PS[3], PS[0]]

            def y_mm(gb):
                bq = gb % 2
                for i in range(8):
                    g = gb * 8 + i
                    gh, gp = g // 32, g % 32
                    pb = gh * 64
                    oy = PYb[bq][:, i * 128:(i + 1) * 128]
                    MM(oy, T0[:, g, :], U[:, g, :], True, False, ["T0", "U"], [f"PY{bq}"])
                    MM(oy, S2Ore[pb:pb + 64, gp, :], Xre_bf[pb:pb + 64, gp, :], False, False, ["S2O", "Xre", "Xre0"], [f"PY{bq}"])
                    MM(oy, S2Oim[pb:pb + 64, gp, :], Xim_bf[pb:pb + 64, gp, :], False, True, ["S2O", "Xim", "Xim0"], [f"PY{bq}"])

            def y_ew(gb):
                bq = gb % 2
                pyt = PYTb[bq]
                ky, kt_ = f"PY{bq}", f"PYT{bq}"
                CP("act", Ysb[bq].rearrange("p g m -> p (g m)"), PYb[bq], [ky], [f"Ysb{bq}"])
                for i in range(8):
                    TR(pyt[:, i * 128:(i + 1) * 128], Ysb[bq][:, i, :], identf[:], [f"Ysb{bq}", "identf"], [kt_])
                ACT(gt1[bq], pyt, AF.Square, [kt_], [f"gt1{bq}"])
                TS("dve", gt1[bq], gt1[bq], KG * 0.044715, KG, ALU.mult, ALU.add, [f"gt1{bq}"], [f"gt1{bq}"])
                TT("dve", gt1[bq], gt1[bq], pyt, ALU.mult, [f"gt1{bq}", kt_], [f"gt1{bq}"])
                ACT(gt2[bq], gt1[bq], AF.Sigmoid, [f"gt1{bq}"], [f"gt2{bq}"])
                TT("dve", yg_bf[:, :, gb * 128:(gb + 1) * 128].rearrange("p t (g j) -> p g t j", j=16),
                   pyt.rearrange("p (g t j) -> p g t j", g=8, t=8), gt2[bq].rearrange("p (g t j) -> p g t j", g=8, t=8),
                   ALU.mult, [kt_, f"gt2{bq}"], ["yg_bf"])

            y_mm(0)
            for gb in range(8):
                if gb + 1 < 8:
                    y_mm(gb + 1)
                y_ew(gb)
            P.barrier()
            oc = sc - NPS
            g5v = g5_d[oc * 1024:(oc + 1) * 1024, :].rearrange("(m s) c -> m s c", s=8)
            TRa = [PS[0][:, 0:512].bitcast(BF16), PS[0][:, 512:1024].bitcast(BF16)]
            TRs = [PS[3][:, 0:512].bitcast(BF16), PS[3][:, 512:1024].bitcast(BF16)]
            PZb = [PS[1], PS[2]]

            def tail1(s_):
                bq = s_ % 2
                dma(g5s[bq], g5v[:, s_, :], w=[f"g5s{bq}"])
                for kt in range(8):
                    TR(TRa[bq][:, kt * 128:(kt + 1) * 128], yg_bf[:, s_, kt * 128:(kt + 1) * 128], identb[:], ["yg_bf", "identb"], [f"TRa{bq}"])
                CP("act", ygT[bq].rearrange("p k m -> p (k m)"), TRa[bq], [f"TRa{bq}"], [f"ygT{bq}"])
                for cb in range(2):
                    pz = PZb[bq][:, cb * 512:(cb + 1) * 512]
                    for kt in range(8):
                        MM(pz, ygT[bq][:, kt, :], gluw[:, kt, cb * 512:(cb + 1) * 512], kt == 0, kt == 7, [f"ygT{bq}", "gluw"], [f"PZ{bq}"])

            def tail2(s_):
                bq = s_ % 2
                TT("dve", zb[bq], PZb[bq], glub, ALU.add, [f"PZ{bq}", "glub"], [f"zb{bq}"])
                ACT(gate[bq], zb[bq], AF.Sigmoid, [f"zb{bq}"], [f"gate{bq}"])
                TT("dve", tt_[bq], yg_bf[:, s_, :], gate[bq], ALU.mult, ["yg_bf", f"gate{bq}"], [f"tt_{bq}"])
                TT("pool", ssm[bq], tt_[bq], g5s[bq], ALU.mult, [f"tt_{bq}", f"g5s{bq}"], [f"ssm{bq}"])
                for kt in range(8):
                    TR(TRs[bq][:, kt * 128:(kt + 1) * 128], ssm[bq][:, kt * 128:(kt + 1) * 128], identb[:], [f"ssm{bq}", "identb"], [f"TRs{bq}"])
                CP("act", ssmT[bq].rearrange("p k m -> p (k m)"), TRs[bq], [f"TRs{bq}"], [f"ssmT{bq}"])
                dma(ssmT_d[oc, s_], ssmT[bq], r=[f"ssmT{bq}"])

            tail1(0)
            for s_ in range(8):
                if s_ + 1 < 8:
                    tail1(s_ + 1)
                tail2(s_)
        P.barrier()
        if stop_after == "SB":
            return finish(nc, P, st, out_d)

        AR.reset()
        Wqk = AR.take([128, 8, 1024], BF16)
        VO = AR.take([128, 8, 1024], BF16)
        wout = AR.take([128, 16, 1024], BF16)
        g2r = AR.take([128, 1024], F32)
        gfr = AR.take([128, 1024], F32)
        base_sc = AR.off
        wqT = AR.take([128, 8, 1024], BF16)
        wk = AR.take([128, 8, 1024], BF16)
        wv = AR.take([128, 8, 1024], BF16)
        wo = AR.take([128, 8, 1024], BF16)
        memT = AR.take([128, 8, 256], F32)
        msq = AR.take([128, 8, 256], BF16)
        mrs1 = AR.take([128, 256], F32)
        mrs2 = AR.take([128, 256], F32)
        mnT = AR.take([128, 8, 256], BF16)
        kaT = AR.take([128, 8, 256], BF16)
        vaT = AR.take([128, 8, 256], BF16)
        for t_, d_, k_ in ((wk, wk_d, "wk"), (wv, wv_d, "wv"), (wqT, wqT_d, "wqT"), (wo, wo_d, "wo")):
            dma(t_, d_, w=[k_], eng="pool")
        dma(wout[:, 0:8, :], wout_d[:, 0:8, :], w=["wout"], eng="pool")
        dma(wout[:, 8:16, :], wout_d[:, 8:16, :], w=["wout"], eng="pool")
        dma(g2r, g2_d, w=["g2r"])
        dma(gfr, gf_d, w=["gfr"])
        dma(memT, memT_d, w=["memT"])
        norm_fm(memT, 256, gm[:], msq, mrs1, mrs2, mnT, PS[0][:, 0:256], "memT", "msq", "mrs", "mnT", "PSm", "gm")
        for r_ in range(8):
            for wt, dst, nm, pi in ((wk, kaT, "kaT", 1), (wv, vaT, "vaT", 2)):
                pp_ = PS[pi][:, (r_ % 2) * 512:(r_ % 2) * 512 + 256]
                pk = f"PSkv{pi}_{r_ % 2}"
                for kt in range(8):
                    MM(pp_, wt[:, kt, r_ * 128:(r_ + 1) * 128], mnT[:, kt, :], kt == 0, kt == 7, [nm[:2] if False else ("wk" if pi == 1 else "wv"), "mnT"], [pk])
                CP("act" if pi == 1 else "dve", dst[:, r_, :], pp_, [pk], [nm])
        for dt_ in range(8):
            pw_ = PS[dt_ % 2]
            pk = f"PSw{dt_ % 2}"
            for h in range(4):
                for hf in range(2):
                    MM(pw_[:, h * 256:(h + 1) * 256], wqT[:, 2 * h + hf, dt_ * 128:(dt_ + 1) * 128], kaT[:, 2 * h + hf, :],
                       hf == 0, hf == 1, ["wqT", "kaT"], [pk])
            ACT(Wqk[:, dt_, :], pw_, AF.Copy, [pk], ["Wqk"], scale=1.0 / 16.0)
        for h in range(4):
            for mh in range(2):
                idx = 2 * h + mh
                pw_ = PS[2 + idx % 2]
                pk = f"PSv{idx % 2}"
                for cb in range(2):
                    for hf in range(2):
                        MM(pw_[:, cb * 512:(cb + 1) * 512], vaT[:, 2 * h + hf, mh * 128:(mh + 1) * 128], wo[:, 2 * h + hf, cb * 512:(cb + 1) * 512],
                           hf == 0, hf == 1, ["vaT", "wo"], [pk])
                CP("dve", VO[:, idx, :], pw_, [pk], ["VO"])
        P.barrier()

        AR.reset(base_sc)
        retT_sc = [AR.take([128, 8, 1024], BF16) for _ in range(2)]
        ssmT_s = [AR.take([128, 8, 128], BF16) for _ in range(3)]
        x_t = [AR.take([128, 1024], F32) for _ in range(3)]
        x1 = [AR.take([128, 1024], F32) for _ in range(3)]
        junk = AR.take([128, 1024], F32)
        h2 = [AR.take([128, 1024], BF16) for _ in range(2)]
        h2T = AR.take([128, 8, 128], BF16)
        Pex = AR.take([128, 4, 256], BF16)
        Pn = [AR.take([128, 4, 256], BF16) for _ in range(2)]
        PTt = AR.take([128, 8, 128], BF16)
        x2 = AR.take([128, 1024], F32)
        osb = [AR.take([128, 1024], F32) for _ in range(2)]
        sst = [AR.take([128, 32], F32) for _ in range(3)]
        TRc = PS[3][:, 0:512].bitcast(BF16)
        TRd = PS[3][:, 512:1024].bitcast(BF16)
        tiles = [(sco, s_) for sco in range(2) for s_ in range(8)]
        dma(retT_sc[0], retT_d[:, :, 0:1024], w=["retT_sc0"])
        dma(retT_sc[1], retT_d[:, :, 1024:2048], w=["retT_sc1"])

        def sc_load(t):
            sco, s_ = tiles[t]
            b3 = t % 3
            dma(ssmT_s[b3], ssmT_d[sco, s_], w=[f"ssmT_s{b3}"])
            dma(x_t[b3], xown_d[sco, :, s_, :], w=[f"x_t{b3}"])

        def sc_s1(t):
            sco, s_ = tiles[t]
            b3 = t % 3
            st_ = sst[t % 3]
            sk = f"sst{t % 3}"
            for cb in range(2):
                px = PS[0][:, cb * 512:(cb + 1) * 512]
                for kt in range(8):
                    MM(px, retT_sc[sco][:, kt, s_::8], wout[:, kt, cb * 512:(cb + 1) * 512], kt == 0, False, [f"retT_sc{sco}", "wout"], [f"PX{cb}"])
                for kt in range(8):
                    MM(px, ssmT_s[b3][:, kt, :], wout[:, 8 + kt, cb * 512:(cb + 1) * 512], False, kt == 7, [f"ssmT_s{b3}", "wout"], [f"PX{cb}"])
            TT("dve", x1[b3], PS[0], x_t[b3], ALU.add, ["PX0", "PX1", f"x_t{b3}"], [f"x1_{b3}"])
            ACT(junk, x1[b3], AF.Square, [f"x1_{b3}"], ["junk", sk + "a"], accum_out=st_[:, 0:1])
            ACT(st_[:, 1:2], st_[:, 0:1], AF.Ln, [sk + "a"], [sk + "b"], bias=epsc[:, 0:1], scale=1.0 / 1024.0)
            ACT(st_[:, 2:3], st_[:, 1:2], AF.Exp, [sk + "b"], [sk + "c"], scale=-0.5)
            STT(h2[t % 2], x1[b3], st_[:, 2:3], g2r, ALU.mult, ALU.mult, [f"x1_{b3}", sk + "c", "g2r"], [f"h2_{t % 2}"])

        def sc_s2(t):
            st_ = sst[t % 3]
            sk = f"sst{t % 3}"
            hh = h2[t % 2]
            for kt in range(8):
                TR(TRc[:, kt * 128:(kt + 1) * 128], hh[:, kt * 128:(kt + 1) * 128], identb[:], [f"h2_{t % 2}", "identb"], ["TRc"])
            CP("act", h2T.rearrange("p k t -> p (k t)"), TRc, ["TRc"], ["h2T"])
            for cb in range(2):
                pq = PS[1][:, cb * 512:(cb + 1) * 512]
                for kt in range(8):
                    MM(pq, h2T[:, kt, :], Wqk[:, kt, cb * 512:(cb + 1) * 512], kt == 0, kt == 7, ["h2T", "Wqk"], [f"PQ{cb}"])
            RED(st_[:, 4:8], PS[1].rearrange("p (h m) -> p h m", h=4), ALU.max, ["PQ0", "PQ1"], [sk + "mx"])
            TS("dve", st_[:, 8:12], st_[:, 4:8], -1.0, None, ALU.mult, None, [sk + "mx"], [sk + "nmx"])
            for h in range(4):
                ACT(Pex[:, h, :], PS[1][:, h * 256:(h + 1) * 256], AF.Exp, ["PQ0", "PQ1", sk + "nmx"], ["Pex", sk + f"sm{h}"],
                    bias=st_[:, 8 + h:9 + h], scale=1.0, accum_out=st_[:, 12 + h:13 + h])
            RECIP(st_[:, 16:20], st_[:, 12:16], [sk + f"sm{h}" for h in range(4)], [sk + "rsm"])
            TT("pool", Pn[t % 2], Pex, st_[:, 16:20].unsqueeze(2).to_broadcast([128, 4, 256]), ALU.mult, ["Pex", sk + "rsm"], [f"Pn{t % 2}"])

        def sc_s3(t):
            sco, s_ = tiles[t]
            b3 = t % 3
            st_ = sst[t % 3]
            sk = f"sst{t % 3}"
            b = t % 2
            Pn2 = Pn[b].rearrange("p h m -> p (h m)")
            for idx in range(8):
                TR(TRd[:, idx * 128:(idx + 1) * 128], Pn2[:, idx * 128:(idx + 1) * 128], identb[:], [f"Pn{b}", "identb"], ["TRd"])
            CP("act", PTt.rearrange("p k t -> p (k t)"), TRd, ["TRd"], ["PTt"])
            for cb in range(2):
                po = PS[2][:, cb * 512:(cb + 1) * 512]
                for idx in range(8):
                    MM(po, PTt[:, idx, :], VO[:, idx, cb * 512:(cb + 1) * 512], idx == 0, idx == 7, ["PTt", "VO"], [f"PO{cb}"])
            TT("dve", x2, PS[2], x1[b3], ALU.add, ["PO0", "PO1", f"x1_{b3}"], ["x2"])
            ACT(junk, x2, AF.Square, ["x2"], ["junk", sk + "d"], accum_out=st_[:, 20:21])
            ACT(st_[:, 21:22], st_[:, 20:21], AF.Ln, [sk + "d"], [sk + "e"], bias=epsc[:, 0:1], scale=1.0 / 1024.0)
            ACT(st_[:, 22:23], st_[:, 21:22], AF.Exp, [sk + "e"], [sk + "f"], scale=-0.5)
            STT(osb[b], x2, st_[:, 22:23], gfr, ALU.mult, ALU.mult, ["x2", sk + "f", "gfr"], [f"osb{b}"])
            dma(out_d[sco, :, s_, :], osb[b], r=[f"osb{b}"])

        NTL = len(tiles)
        sc_load(0)
        for i in range(NTL + 2):
            if i + 1 < NTL:
                sc_load(i + 1)
            if i < NTL:
                sc_s1(i)
            if 0 <= i - 2 < NTL:
                sc_s3(i - 2)
            if 0 <= i - 1 < NTL:
                sc_s2(i - 1)
        return finish(nc, P, st, out_d)


def finish(nc, P, st, out_d):
    P.barrier()
    P.op("sp", None)
    P.emit(st)
    return nc


def _const_tables():
    H = 8
    log_g = np.log1p(-np.exp2(-5.0 - np.arange(H, dtype=np.float64)))
    i = np.arange(128, dtype=np.float64)
    sqk = np.zeros((128, 16), np.float64)
    sqk[:, 0:8] = np.exp(log_g[None, :] * (i[:, None] + 1.0))
    sqk[:, 8:16] = np.exp(-log_g[None, :] * (i[:, None] + 1.0)) * 0.125
    cdt = np.broadcast_to(np.exp(log_g * 128.0)[None, :], (128, 8)).copy()
    j = np.arange(128)
    maskT = (j[None, :] >= j[:, None]).astype(np.float32)
    s_idx = np.arange(128) // 16
    t0mask = (s_idx[None, :] >= s_idx[:, None]).astype(np.float32)
    invf = (10000.0 ** (-np.arange(32, dtype=np.float32) / np.float32(32))).astype(np.float32)
    invf = np.broadcast_to(invf[None, :], (128, 32)).copy()
    return (sqk.astype(np.float32), cdt.astype(np.float32), maskT, t0mask, invf, np.eye(128, dtype=np.float32))


def _kt(w):
    K, N = w.shape
    return np.ascontiguousarray(w.reshape(K // 128, 128, N).transpose(1, 0, 2))


def make_in_maps(inputs, mode=None):
    mode = mode or MODE
    NPRE_T = 6144 if mode == "prefix" else 0
    NT = NPRE_T + 2048
    f = lambda a: np.asarray(a, dtype=np.float32)
    x = f(inputs["x"])
    mem = f(inputs["mem"])
    pos = np.asarray(inputs["positions"]).astype(np.int32)
    sqk, cdt, maskT, t0mask, invf, ident = _const_tables()
    rep = lambda v: np.ascontiguousarray(np.broadcast_to(f(v).reshape(1, -1), (128, f(v).size)))
    cols = lambda v: np.ascontiguousarray(f(v).reshape(8, 128).T)

    def s5lay(a):
        a = f(a)
        rest = a.shape[2:]
        a = a.reshape((2, 32, 64) + rest)
        a = np.moveaxis(a, 2, 1)
        return np.ascontiguousarray(a.reshape((128, 32) + rest))

    a_re = s5lay(inputs["s5_a_re"][0])
    a_im = s5lay(inputs["s5_a_im"][0])
    ldt = s5lay(np.broadcast_to(f(inputs["s5_log_dt"][0])[:, None], (64, 64)))
    b_re = s5lay(inputs["s5_b_re"][0])
    b_im = s5lay(inputs["s5_b_im"][0])
    c_re = s5lay(np.transpose(f(inputs["s5_c_re"][0]), (0, 2, 1)))
    c_im = s5lay(np.transpose(f(inputs["s5_c_im"][0]), (0, 2, 1)))
    shared = dict(
        invf=invf, sqk=sqk, cdt=cdt, maskT=maskT, t0mask=t0mask, ident=ident,
        w_in=_kt(f(inputs["w_in"][0])), g1=cols(inputs["norm1_g"][0]), gng=rep(inputs["ret_gn_g"][0]),
        s5are=a_re, s5aim=a_im, s5ldt=ldt, s5bre=b_re, s5bim=b_im, s5cre=c_re, s5cim=c_im,
        dcol=np.ascontiguousarray(f(inputs["s5_d"][0]).reshape(64, 16).T[np.arange(128) % 16, :]), glub=rep(inputs["s5_glu_b"][0]), g2rep=rep(inputs["norm2_g"][0]),
        gfrep=rep(inputs["norm_f_g"]), gm=cols(inputs["norm_mem_g"][0]),
        gluw=_kt(f(inputs["s5_glu_w"][0])), wout=_kt(f(inputs["w_out"][0])),
        wqT=_kt(np.ascontiguousarray(f(inputs["xa_wq"][0]).T)), wk=_kt(f(inputs["xa_wk"][0])),
        wv=_kt(f(inputs["xa_wv"][0])), wo=_kt(f(inputs["xa_wo"][0])),
    )
    log_g = np.log1p(-np.exp2(-5.0 - np.arange(8, dtype=np.float64)))
    maps = []
    for core in range(8):
        b, j = core // 4, core % 4
        m = dict(shared)
        t1 = (j + 1) * 2048
        t0 = t1 - NT
        lo = max(t0, 0)
        xl = np.zeros((NT, 1024), np.float32)
        pl = np.zeros((NT,), np.int32)
        xl[lo - t0:, :] = x[b, lo:t1, :]
        pl[lo - t0:] = pos[b, lo:t1]
        m["xT"] = _kt(np.ascontiguousarray(xl.T))
        m["xown"] = np.ascontiguousarray(x[b, j * 2048:(j + 1) * 2048, :].reshape(2, 128, 8, 1024))
        m["pos"] = np.ascontiguousarray(pl.reshape(NT // 128, 128).T)
        m["memT"] = _kt(np.ascontiguousarray(mem[b].T))
        wtr = np.zeros((128, 8, 4), np.float64)
        sel5 = np.zeros((128, 3, 8), np.float32)
        for rank in range(8):
            rb, ri = rank // 4, rank % 4
            if rb == b and ri < j:
                dist = j - ri
                sel5[:, dist - 1, rank] = 1.0
                for blk in range(4):
                    for par in range(2):
                        h = 2 * blk + par
                        wtr[par * 64:(par + 1) * 64, rank, blk] = np.exp(log_g[h] * 2048.0 * (dist - 1))
        m["wtr"] = wtr.astype(np.float32)
        m["sel5"] = sel5
        maps.append(m)
    return maps


def kernel(**inputs):
    nc = build_program()
    maps = make_in_maps(inputs)
    res = run_bass_kernel_spmd(nc, maps, core_ids=list(range(8)))
    out = np.zeros((2, 8192, 1024), np.float32)
    for core in range(8):
        b, j = core // 4, core % 4
        out[b, j * 2048:(j + 1) * 2048, :] = np.asarray(res.results[core]["out"]).reshape(2048, 1024)
    return out
```

```python
import os
import math
from contextlib import ExitStack
import numpy as np
import concourse.bass as bass
import concourse.mybir as mybir
from concourse.bass_utils import run_bass_kernel_spmd

F32 = mybir.dt.float32
BF16 = mybir.dt.bfloat16
I32 = mybir.dt.int32
ALU = mybir.AluOpType
AF = mybir.ActivationFunctionType
AX = mybir.AxisListType

SEG = 6000
NDMA = 24
EPS = 1e-6
NOWN_BLK = 8
MODE = "prefix"
TWO_PI = 2.0 * math.pi
KG = 2.0 * math.sqrt(2.0 / math.pi)


class Prog:
    ENGS = ("pe", "act", "dve", "pool", "sp")

    def __init__(self, nc):
        self.nc = nc
        self.ops = []
        self.last_w = {}
        self.readers = {}
        self.pending_barrier = {e: set() for e in self.ENGS}
        self._bar_at = 0

    def op(self, eng, fn, r=(), w=(), dma=False):
        oid = len(self.ops)
        deps = set()
        for k in r:
            if k in self.last_w:
                deps.add(self.last_w[k])
        for k in w:
            if k in self.last_w:
                deps.add(self.last_w[k])
            for rd in self.readers.get(k, ()):
                deps.add(rd)
        deps |= self.pending_barrier[eng]
        self.pending_barrier[eng] = set()
        deps.discard(oid)
        last_per_eng = {}
        keep = set()
        for d_ in deps:
            od = self.ops[d_]
            if od["dma"]:
                keep.add(d_)
            else:
                e_ = od["eng"]
                if e_ not in last_per_eng or d_ > last_per_eng[e_]:
                    last_per_eng[e_] = d_
        deps = keep | set(last_per_eng.values())
        self.ops.append(dict(eng=eng, fn=fn, deps=deps, dma=dma))
        for k in r:
            self.readers.setdefault(k, []).append(oid)
        for k in w:
            self.last_w[k] = oid
            self.readers[k] = []
        return oid

    def dma(self, fn, r=(), w=(), eng="sp"):
        return self.op(eng, fn, r, w, dma=True)

    def barrier(self):
        last = {}
        for i, o in enumerate(self.ops):
            last[o["eng"]] = i
        dm = [i for i in range(self._bar_at, len(self.ops)) if self.ops[i]["dma"]]
        s = set(last.values()) | set(dm)
        for e in self.ENGS:
            self.pending_barrier[e] |= s
        self._bar_at = len(self.ops)

    def emit(self, stack):
        nc = self.nc
        ops = self.ops
        n = len(ops)
        needed = [False] * n
        for i, o in enumerate(ops):
            nd = set()
            for d in o["deps"]:
                od = ops[d]
                if od["eng"] == "pe" and o["eng"] == "pe":
                    continue
                if od["eng"] == o["eng"] and not od["dma"] and od["fn"] is None:
                    continue
                nd.add(d)
            o["deps"] = nd
            for d in nd:
                needed[d] = True
        cnt = {e: 0 for e in self.ENGS}
        nseg = {e: 0 for e in self.ENGS}
        ndma = {"sp": 0, "pool": 0}
        dma_ops = {"sp": [], "pool": []}
        for i, o in enumerate(ops):
            if o["dma"]:
                q = o["eng"]
                nd = ndma[q]
                o["sig"] = ("dma_" + q, nd % NDMA, 16 * (nd // NDMA + 1), nd)
                dma_ops[q].append(o)
                ndma[q] += 1
            elif needed[i]:
                e = o["eng"]
                c = cnt[e]
                o["sig"] = (e, c // SEG, c % SEG + 1)
                cnt[e] += 1
                nseg[e] = c // SEG + 1
            else:
                o["sig"] = None
        sems = {}
        for e in self.ENGS:
            for s in range(nseg[e]):
                sems[(e, s)] = stack.enter_context(nc.semaphore(f"s_{e}_{s}"))
        for q in ("sp", "pool"):
            for k in range(min(NDMA, ndma[q])):
                sems[("dma_" + q, k)] = stack.enter_context(nc.semaphore(f"s_dma_{q}_{k}"))
        block = stack.enter_context(nc.Block())

        def make(ename):
            def body(eng):
                waited = {}
                for i, o in enumerate(ops):
                    if o["eng"] != ename:
                        continue
                    req = {}
                    for d in o["deps"]:
                        sg = ops[d]["sig"]
                        key = (sg[0], sg[1])
                        req[key] = max(req.get(key, 0), sg[2])
                    if o["dma"]:
                        idx = o["sig"][3]
                        if idx >= NDMA:
                            prev = dma_ops[o["eng"]][idx - NDMA]["sig"]
                            key = (prev[0], prev[1])
                            req[key] = max(req.get(key, 0), prev[2])
                    for key in sorted(req.keys(), key=lambda k: (k[0], k[1])):
                        val = req[key]
                        if waited.get(key, 0) >= val:
                            continue
                        if not key[0].startswith("dma"):
                            if any(k[0] == key[0] and k[1] > key[1] for k in waited):
                                continue
                        eng.wait_ge(sems[key], val)
                        waited[key] = val
                    if o["fn"] is None:
                        continue
                    ins = o["fn"](eng)
                    sg = o["sig"]
                    if sg is not None:
                        if sg[0].startswith("dma"):
                            ins.then_inc(sems[(sg[0], sg[1])], 16)
                        else:
                            ins.then_inc(sems[(sg[0], sg[1])], 1)
            return body

        block.tensor(make("pe"))
        block.scalar(make("act"))
        block.vector(make("dve"))
        block.gpsimd(make("pool"))
        block.sync(make("sp"))


class Arena:
    def __init__(self, t, nwords):
        self.t = t
        self.n = nwords
        self.off = 0

    def reset(self, off=0):
        self.off = off

    def take(self, shape, dtype):
        nel = 1
        for s in shape[1:]:
            nel *= s
        sz = 2 if dtype == BF16 else 4
        nw = (nel * sz + 3) // 4
        nw = (nw + 7) // 8 * 8
        assert self.off + nw <= self.n, f"arena overflow {self.off}+{nw}>{self.n}"
        ap = self.t[:, self.off:self.off + nw]
        self.off += nw
        if dtype != F32:
            ap = ap.bitcast(dtype)
        ap = ap[:, 0:nel]
        fd = shape[1:]
        if len(fd) == 2:
            ap = ap.rearrange("p (a b) -> p a b", a=fd[0])
        elif len(fd) == 3:
            ap = ap.rearrange("p (a b c) -> p a b c", a=fd[0], b=fd[1])
        elif len(fd) == 4:
            ap = ap.rearrange("p (a b c d) -> p a b c d", a=fd[0], b=fd[1], c=fd[2])
        return ap


def build_program(dbg=None, stop_after=None, mode=None):
    mode = mode or MODE
    NPRE = 24 if mode == "prefix" else 0
    NT = NPRE * 256 + 2048
    NCH = NT // 128
    FAKE_CC = isinstance(dbg, dict) and bool(dbg.get("fake_cc"))
    nc = bass.Bass("TRN2", target_bir_lowering=False)

    def din(name, shape, dt=F32):
        return nc.dram_tensor(name, list(shape), dt, kind="ExternalInput").ap()

    xT_d = din("xT", [128, 8, NT])
    xown_d = din("xown", [2, 128, 8, 1024])
    pos_d = din("pos", [128, NCH], I32)
    invf_d = din("invf", [128, 32])
    sqk_d = din("sqk", [128, 16])
    cdt_d = din("cdt", [128, 8])
    wtr_d = din("wtr", [128, 8, 4])
    sel5_d = din("sel5", [128, 3, 8])
    maskT_d = din("maskT", [128, 128])
    t0mask_d = din("t0mask", [128, 128])
    ident_d = din("ident", [128, 128])
    w_in_d = din("w_in", [128, 8, 5120])
    g1_d = din("g1", [128, 8])
    gng_d = din("gng", [128, 1024])
    are_d = din("s5are", [128, 32])
    aim_d = din("s5aim", [128, 32])
    ldt_d = din("s5ldt", [128, 32])
    bre_d = din("s5bre", [128, 32, 16])
    bim_d = din("s5bim", [128, 32, 16])
    cre_d = din("s5cre", [128, 32, 16])
    cim_d = din("s5cim", [128, 32, 16])
    dcol_d = din("dcol", [128, 64])
    glub_d = din("glub", [128, 1024])
    g2_d = din("g2rep", [128, 1024])
    gf_d = din("gfrep", [128, 1024])
    gm_d = din("gm", [128, 8])
    gluw_d = din("gluw", [128, 8, 1024])
    wout_d = din("wout", [128, 16, 1024])
    wqT_d = din("wqT", [128, 8, 1024])
    wk_d = din("wk", [128, 8, 1024])
    wv_d = din("wv", [128, 8, 1024])
    wo_d = din("wo", [128, 8, 1024])
    memT_d = din("memT", [128, 8, 256])
    out_d = nc.dram_tensor("out", [2, 128, 8, 1024], F32, kind="ExternalOutput").ap()

    skind = dict(kind="ExternalOutput") if dbg else {}
    u_d = nc.dram_tensor("u_scr", [NT, 1024], BF16, **skind).ap()
    ccR_in = nc.dram_tensor("ccR_in", [128, 512], F32)
    ccR_out = nc.dram_tensor("ccR_out", [1024, 512], F32)
    ccS_in = nc.dram_tensor("ccS_in", [128, 64], F32)
    ccS_out = nc.dram_tensor("ccS_out", [1024, 64], F32)
    g5_d = nc.dram_tensor("g5_scr", [2048, 1024], BF16, **skind).ap()
    retT_d = nc.dram_tensor("retT_scr", [128, 8, 2048], BF16, **skind).ap()
    ssmT_d = nc.dram_tensor("ssmT_scr", [2, 8, 128, 8, 128], BF16, **skind).ap()

    with ExitStack() as st:
        P = Prog(nc)
        sb = lambda name, shape, dt=F32: st.enter_context(nc.sbuf_tensor("sb_" + name, list(shape), dt))
        ARW = (dbg.get("arw") if isinstance(dbg, dict) and dbg.get("arw") else None) or (49200 if mode == "prefix" else 47200)
        arena_t = sb("arena", [128, ARW])
        AR = Arena(arena_t, ARW)
        identf = sb("identf", [128, 128])
        identb = sb("identb", [128, 128], BF16)
        ones_f = sb("ones_f", [128, 128])
        ones_b = sb("ones_b", [128, 128], BF16)
        epsc = sb("epsc", [128, 1])
        maskT = sb("maskT", [128, 128])
        t0mask = sb("t0mask", [128, 128])
        invf = sb("invf", [128, 32])
        sqk = sb("sqk", [128, 16])
        cdt = sb("cdt", [128, 8])
        wtr = sb("wtr", [128, 8, 4])
        sel5 = sb("sel5", [128, 3, 8])
        A2r = sb("A2r", [128, 2, 32])
        A2i = sb("A2i", [128, 2, 32])
        g1 = sb("g1", [128, 8])
        gm = sb("gm", [128, 8])
        Rf = sb("Rf", [128, 8, 128])
        Rbf = sb("Rbf", [128, 8, 128], BF16)
        car_r = sb("car_r", [128, 32])
        car_i = sb("car_i", [128, 32])
        PS = [st.enter_context(nc.psum_tensor(f"PS{i}", [128, 1024], F32))[:] for i in range(4)]

        def dma(out, in_, r=(), w=(), eng="sp"):
            P.dma(lambda e, out=out, in_=in_: e.dma_start(out=out, in_=in_), r=r, w=w, eng=eng)

        def TT(eng, out, in0, in1, op, r, w):
            P.op(eng, lambda e, out=out, in0=in0, in1=in1, op=op: e.tensor_tensor(out=out, in0=in0, in1=in1, op=op), r=r, w=w)

        def TS(eng, out, in0, s1, s2, op0, op1, r, w):
            if op1 is None:
                P.op(eng, lambda e, out=out, in0=in0, s1=s1, op0=op0: e.tensor_scalar(out=out, in0=in0, scalar1=s1, scalar2=None, op0=op0), r=r, w=w)
            else:
                P.op(eng, lambda e, out=out, in0=in0, s1=s1, s2=s2, op0=op0, op1=op1: e.tensor_scalar(out=out, in0=in0, scalar1=s1, scalar2=s2, op0=op0, op1=op1), r=r, w=w)

        def STT(out, in0, scalar, in1, op0, op1, r, w):
            P.op("dve", lambda e, out=out, in0=in0, scalar=scalar, in1=in1, op0=op0, op1=op1: e.scalar_tensor_tensor(out=out, in0=in0, scalar=scalar, in1=in1, op0=op0, op1=op1), r=r, w=w)

        def ACT(out, in_, func, r, w, **kw):
            P.op("act", lambda e, out=out, in_=in_, func=func, kw=kw: e.activation(out=out, in_=in_, func=func, **kw), r=r, w=w)

        def CP(eng, out, in_, r, w):
            if eng == "act":
                ACT(out, in_, AF.Copy, r, w)
            else:
                P.op(eng, lambda e, out=out, in_=in_: e.tensor_copy(out=out, in_=in_), r=r, w=w)

        def RECIP(out, in_, r, w):
            P.op("dve", lambda e, out=out, in_=in_: e.reciprocal(out=out, in_=in_), r=r, w=w)

        def RED(out, in_, op, r, w, axis=AX.X):
            P.op("dve", lambda e, out=out, in_=in_, op=op, axis=axis: e.tensor_reduce(out=out, in_=in_, axis=axis, op=op), r=r, w=w)

        def MM(out, lhsT, rhs, start, stop, r, w):
            P.op("pe", lambda e, out=out, lhsT=lhsT, rhs=rhs, start=start, stop=stop: e.matmul(out, lhsT=lhsT, rhs=rhs, start=start, stop=stop), r=r, w=w)

        def TR(out, in_, ident, r, w):
            P.op("pe", lambda e, out=out, in_=in_, ident=ident: e.transpose(out=out, in_=in_, identity=ident), r=r, w=w)

        def MEMSET(eng, ap, val, w):
            P.op(eng, lambda e, ap=ap, val=val: e.memset(ap, val), w=w)

        dma(identf[:], ident_d, w=["identf"])
        dma(maskT[:], maskT_d, w=["maskT"])
        dma(t0mask[:], t0mask_d, w=["t0mask"])
        dma(invf[:], invf_d, w=["invf"])
        dma(sqk[:], sqk_d, w=["sqk"])
        dma(cdt[:], cdt_d, w=["cdt"])
        dma(wtr[:], wtr_d, w=["wtr"])
        dma(sel5[:], sel5_d, w=["sel5"])
        dma(g1[:], g1_d, w=["g1"])
        dma(gm[:], gm_d, w=["gm"])
        CP("dve", identb[:], identf[:], ["identf"], ["identb"])
        MEMSET("pool", ones_f[:], 1.0, ["ones_f"])
        MEMSET("pool", ones_b[:], 1.0, ["ones_b"])
        MEMSET("pool", epsc[:], EPS, ["epsc"])
        MEMSET("pool", Rf[:], 0.0, ["Rf"])
        MEMSET("pool", Rbf[:], 0.0, ["Rbf"])
        MEMSET("pool", car_r[:], 0.0, ["car_r"])
        MEMSET("pool", car_i[:], 0.0, ["car_i"])

        def sincos(ang, shp, s_out, c_out, tmps, key):
            kf, ki, rr, mm = tmps
            rk = [key]
            TS("dve", kf, ang, 1.0 / TWO_PI, None, ALU.mult, None, rk, [key + "kf"])
            CP("dve", ki, kf, [key + "kf"], [key + "ki"])
            CP("dve", kf, ki, [key + "ki"], [key + "kf"])
            STT(rr, kf, -6.28125, ang, ALU.mult, ALU.add, [key + "kf"] + rk, [key + "rr"])
            STT(rr, kf, -(TWO_PI - 6.28125), rr, ALU.mult, ALU.add, [key + "kf", key + "rr"], [key + "rr"])

            def wrap(t):
                TS("dve", mm, t, -math.pi, None, ALU.is_lt, None, [key + "rr"], [key + "mm"])
                STT(t, mm, TWO_PI, t, ALU.mult, ALU.add, [key + "mm", key + "rr"], [key + "rr"])
                TS("dve", mm, t, math.pi, None, ALU.is_gt, None, [key + "rr"], [key + "mm"])
                STT(t, mm, -TWO_PI, t, ALU.mult, ALU.add, [key + "mm", key + "rr"], [key + "rr"])
            wrap(rr)
            ACT(s_out, rr, AF.Sin, [key + "rr"], [key + "s"])
            TS("dve", rr, rr, math.pi / 2.0, None, ALU.add, None, [key + "rr", key + "s"], [key + "rr"])
            wrap(rr)
            ACT(c_out, rr, AF.Sin, [key + "rr"], [key + "c"])

        def norm_fm(xt, N, gcol, sq, rs1, rs2, hT, psum_ap, kx, ksq, krs, kh, kps, gkey):
            ACT(sq, xt, AF.Square, [kx], [ksq])
            for kt in range(8):
                MM(psum_ap, ones_b[:], sq[:, kt, :], kt == 0, kt == 7, [ksq, "ones_b"], [kps])
            ACT(rs1, psum_ap, AF.Sqrt, [kps], [krs + "1"], bias=EPS, scale=1.0 / 1024.0)
            RECIP(rs2, rs1, [krs + "1"], [krs])
            for kt in range(8):
                STT(hT[:, kt, :], xt[:, kt, :], gcol[:, kt:kt + 1], rs2, ALU.mult, ALU.mult, [kx, krs, gkey], [kh])

        AR.reset()
        w_in = AR.take([128, 8, 5120], BF16)
        cosT = AR.take([128, NCH, 32], F32)
        sinT = AR.take([128, NCH, 32], F32)
        gng = AR.take([128, 1024], F32)
        xTb = [AR.take([128, 8, 256], F32) for _ in range(2)]
        sq = AR.take([128, 8, 256], BF16)
        posi = AR.take([128, NCH], I32)
        posf = AR.take([128, NCH], F32)
        Rc = AR.take([128, 4, 128], F32)
        mark = AR.off
        tki = AR.take([128, NCH * 32], I32)
        tmm = AR.take([128, NCH * 32], F32)
        trr_t = AR.take([128, NCH * 32], F32)
        AR.reset(mark)
        Gt = AR.take([128, 8, 4, 128], F32)
        AR.reset(mark)
        hTb = [AR.take([128, 8, 256], BF16) for _ in range(2)]
        rs1 = AR.take([128, 256], F32)
        rs2 = AR.take([128, 256], F32)
        qk = AR.take([128, 16, 2, 32], F32)
        rt = [AR.take([128, 16, 32], F32) for _ in range(4)]
        qkt = [AR.take([128, 16, 64], BF16) for _ in range(2)]
        v_bf = [AR.take([128, 1024], BF16) for _ in range(2)]
        gr_bf = [AR.take([128, 1024], BF16) for _ in range(2)]
        g5_bf = [AR.take([128, 1024], BF16) for _ in range(2)]
        u_bf = [AR.take([128, 1024], BF16) for _ in range(2)]
        kT = [AR.take([128, 4, 128], BF16) for _ in range(2)]
        qm = [AR.take([128, 8, 128], BF16) for _ in range(2)]
        PT = AR.take([128, 8, 128], BF16)
        o_sb = AR.take([128, 8, 128], F32)
        osq = AR.take([128, 8, 128], F32)
        ret = [AR.take([128, 1024], BF16) for _ in range(2)]
        retT_blk = AR.take([128, 8, 256], BF16)
        stt = AR.take([128, 48], F32)
        ang = xTb[0].rearrange("p a b -> p (a b)")[:, 0:NCH * 32]
        tkf = xTb[1].rearrange("p a b -> p (a b)")[:, 0:NCH * 32]
        trr = trr_t[:, 0:NCH * 32]

        dma(gng, gng_d, w=["gng"])
        MEMSET("pool", qm[0], 0.0, ["qm0"])
        MEMSET("pool", qm[1], 0.0, ["qm1"])
        dma(posi, pos_d, w=["posi"])
        CP("dve", posf, posi, ["posi"], ["posf"])
        TT("dve", ang.rearrange("p (c i) -> p c i", i=32), posf.unsqueeze(2).to_broadcast([128, NCH, 32]),
           invf[:].unsqueeze(1).to_broadcast([128, NCH, 32]), ALU.mult, ["posf", "invf", "xT0"], ["xT0"])
        sincos(ang, None, sinT.rearrange("p c i -> p (c i)"), cosT.rearrange("p c i -> p (c i)"),
               (tkf, tki, trr, tmm), "xT0")
        P.barrier()
        for cb in (1, 2, 3, 6, 7, 0, 4, 5, 8, 9):
            cs_ = slice(cb * 512, (cb + 1) * 512)
            dma(w_in[:, :, cs_], w_in_d[:, :, cs_], w=[f"w_in{cb}"], eng="pool")
            TT("dve", w_in[:, :, cs_], w_in[:, :, cs_], g1[:].unsqueeze(2).to_broadcast([128, 8, 512]), ALU.mult, [f"w_in{cb}", "g1"], [f"w_in{cb}"])

        qk2 = qk.rearrange("p h t i -> p (h t i)")
        qkt2 = [q_.rearrange("p h d -> p (h d)") for q_ in qkt]
        o2 = o_sb.rearrange("p h e -> p (h e)")
        PSUM_R2 = (mode == "coll") and not (isinstance(dbg, dict) and dbg.get("psum_new"))
        if PSUM_R2:
            KVp = PS[3].rearrange("p (h e) -> p h e", h=8)
            TRb = PS[0][:, 512:1024].bitcast(BF16)
            kvi = lambda h: h
            kvs = lambda par: slice(par, 8, 2)
        else:
            KVp = PS[3][:, 0:512].rearrange("p (b e) -> p b e", b=4)
            TRb = PS[3][:, 512:1024].bitcast(BF16)
            kvi = lambda h: h // 2
            kvs = lambda par: slice(0, 4, 1)
        SCp = PS[1]
        Op = PS[2]
        pj_i = [0]

        pj_n = [2]

        def pj():
            if PSUM_R2:
                return PS[0][:, 0:512], "PJ0"
            i = pj_i[0] % pj_n[0]
            pj_i[0] += 1
            return PS[i // 2][:, (i % 2) * 512:(i % 2 + 1) * 512], f"PJ{i}"

        nblk = NOWN_BLK
        LVL = dbg.get('own_lvl', 9) if isinstance(dbg, dict) else 9
        if mode == "coll":
            sa_iter = [("light", b_) for b_ in range(nblk)] + [("xchg", -1)] + [("full", b_) for b_ in range(nblk)]
            light_cbs = [1, 2, 3]
        else:
            sa_iter = [("light", b_) for b_ in range(NPRE)] + [("full", NPRE + b_) for b_ in range(nblk)]
            light_cbs = [1, 2, 3, 6, 7]
        def do_xchg():
            P.barrier()
            dma(ccR_in.ap()[0:64, :].rearrange("p (b e) -> p b e", b=4), Rf[0:64, 0::2, :], r=["Rf"], w=["ccR_in"])
            dma(ccR_in.ap()[64:128, :].rearrange("p (b e) -> p b e", b=4), Rf[64:128, 1::2, :], r=["Rf"], w=["ccR_in"])
            if FAKE_CC:
                dma(ccR_out.ap()[0:128, :], ccR_in.ap(), r=["ccR_in"], w=["ccR_out"])
            else:
                P.op("pool", lambda e: e.collective_compute("AllGather", ALU.bypass, replica_groups=[list(range(8))],
                                                            ins=[ccR_in.ap().opt()], outs=[ccR_out.ap().opt()]),
                     r=["ccR_in"], w=["ccR_out"])
            dma(Gt.rearrange("p r b e -> p r (b e)"), ccR_out.ap().rearrange("(r p) c -> p r c", p=128), r=["ccR_out"], w=["Gt"])
            TT("dve", Gt, Gt, wtr[:].unsqueeze(3).to_broadcast([128, 8, 4, 128]), ALU.mult, ["Gt", "wtr"], ["Gt"])
            RED(Rc, Gt.rearrange("p r b e -> p b e r"), ALU.add, ["Gt"], ["Rc"])
            CP("act", Rf[0:64, 0::2, :], Rc[0:64, :, :], ["Rc", "Rf"], ["Rf"])
            CP("act", Rf[64:128, 1::2, :], Rc[64:128, :, :], ["Rc", "Rf"], ["Rf"])
            CP("pool", Rbf[:], Rf[:], ["Rf"], ["Rbf"])
            P.barrier()


        def load_block(blk):
            dma(xTb[blk % 2], xT_d[:, :, blk * 256:(blk + 1) * 256], w=[f"xT{blk % 2}"])

        def norm_block(blk):
            xb = xTb[blk % 2]
            kx = f"xT{blk % 2}"
            bb = blk % 2
            ACT(sq, xb, AF.Square, [kx], ["sq"])
            ACT(hTb[bb], xb, AF.Copy, [kx], [f"hT{bb}"])
            pa, pk = pj()
            for ci in range(2):
                for kt in range(8):
                    MM(pa[:, ci:ci + 1], sq[:, kt, ci * 128:(ci + 1) * 128], ones_b[:, 0:1], kt == 0, kt == 7, ["sq", "ones_b"], [pk])
            ACT(rs1[:, 2 * bb:2 * bb + 2], pa[:, 0:2], AF.Sqrt, [pk], [f"rsa{bb}"], bias=EPS, scale=1.0 / 1024.0)
            RECIP(rs2[:, 2 * bb:2 * bb + 2], rs1[:, 2 * bb:2 * bb + 2], [f"rsa{bb}"], [f"rsb{bb}"])

        def phaseA(pas, blk, ci):
            own = pas == "full"
            c = blk * 2 + ci
            d = c % 2
            hT = hTb[blk % 2]
            hk = f"hT{blk % 2}"
            hs = lambda kt: hT[:, kt, ci * 128:(ci + 1) * 128]
            cbs = [0, 1, 2, 3, 4, 5, 6, 7, 8, 9] if own else light_cbs
            for cb in cbs:
                pa, pk = pj()
                for kt in range(8):
                    MM(pa, hs(kt), w_in[:, kt, cb * 512:(cb + 1) * 512], kt == 0, kt == 7, [hk, f"w_in{cb}"], [pk])
                rcol = rs2[:, 2 * (blk % 2) + ci:2 * (blk % 2) + ci + 1]
                rk = f"rsb{blk % 2}"
                if cb == 0:
                    ACT(qk2[:, 0:512], pa, AF.Copy, [pk, rk], ["qk_q"], scale=rcol)
                elif cb == 1:
                    ACT(qk2[:, 512:1024], pa, AF.Copy, [pk, rk], ["qk_k"], scale=rcol)
                elif cb in (2, 3):
                    TS("dve", v_bf[d][:, (cb - 2) * 512:(cb - 1) * 512], pa, rcol, None, ALU.mult, None, [pk, rk], [f"v_bf{d}"])
                elif cb in (4, 5):
                    ACT(gr_bf[d][:, (cb - 4) * 512:(cb - 3) * 512], pa, AF.Silu, [pk, rk], [f"gr_bf{d}"], scale=rcol)
                elif cb in (6, 7):
                    TS("dve", u_bf[d][:, (cb - 6) * 512:(cb - 5) * 512], pa, rcol, None, ALU.mult, None, [pk, rk], [f"u_bf{d}"])
                else:
                    ACT(g5_bf[d][:, (cb - 8) * 512:(cb - 7) * 512], pa, AF.Silu, [pk, rk], [f"g5_bf{d}"], scale=rcol)
            if own or 6 in light_cbs:
                dma(u_d[c * 128:(c + 1) * 128, :], u_bf[d], r=[f"u_bf{d}"])
            if own:
                oc = c - 2 * NPRE
                dma(g5_d[oc * 128:(oc + 1) * 128, :], g5_bf[d], r=[f"g5_bf{d}"])
            h0 = 0 if own else 8
            nh = 16 - h0
            cosb = cosT[:, c, :].unsqueeze(1).to_broadcast([128, nh, 32])
            sinb = sinT[:, c, :].unsqueeze(1).to_broadcast([128, nh, 32])
            x1 = qk[:, h0:16, 0, :]
            x2 = qk[:, h0:16, 1, :]
            rq = ["qk_q", "qk_k"] if own else ["qk_k"]
            qk3 = qk[:, h0:16, :, :].rearrange("p h t i -> p h (t i)")
            TT("pool", qk3, qk3, sqk[:, h0:16].unsqueeze(2).to_broadcast([128, nh, 64]), ALU.mult, rq + ["sqk"], rq)
            q4 = qkt[d].rearrange("p h (t i) -> p h t i", t=2)
            TT("dve", rt[0][:, h0:16, :], x1, cosb, ALU.mult, rq, ["rt0"])
            TT("dve", rt[1][:, h0:16, :], x2, sinb, ALU.mult, rq, ["rt1"])
            TT("dve", q4[:, h0:16, 0, :], rt[0][:, h0:16, :], rt[1][:, h0:16, :], ALU.subtract, ["rt0", "rt1"], [f"qkt{d}"])
            TT("pool", rt[2][:, h0:16, :], x1, sinb, ALU.mult, rq, ["rt2"])
            TT("pool", rt[3][:, h0:16, :], x2, cosb, ALU.mult, rq, ["rt3"])
            TT("pool", q4[:, h0:16, 1, :], rt[2][:, h0:16, :], rt[3][:, h0:16, :], ALU.add, ["rt2", "rt3"], [f"qkt{d}"])
            if own:
                for b in range(8):
                    TR(TRb[:, b * 128:(b + 1) * 128], qkt2[d][:, b * 128:(b + 1) * 128], identb[:], [f"qkt{d}", "identb"], ["TRb"])
                CP("act", kT[d].rearrange("p b t -> p (b t)"), TRb[:, 512:1024], ["TRb"], [f"kT{d}"])
                CP("act", qm[d][0:64, 0::2, :], TRb[0:64, 0:512].rearrange("p (b t) -> p b t", b=4), ["TRb"], [f"qm{d}"])
                CP("act", qm[d][64:128, 1::2, :], TRb[64:128, 0:512].rearrange("p (b t) -> p b t", b=4), ["TRb"], [f"qm{d}"])

        def phaseB(pas, blk, ci):
            own = pas == "full"
            c = blk * 2 + ci
            d = c % 2
            for h in range(8):
                MM(KVp[(h % 2) * 64:(h % 2) * 64 + 64, kvi(h), :], qkt2[d][:, 512 + h * 64:512 + (h + 1) * 64],
                   v_bf[d][:, h * 128:(h + 1) * 128], True, True, [f"qkt{d}", f"v_bf{d}"], ["KVp"])
            if own:
                for h in range(8):
                    MM(SCp[:, h * 128:(h + 1) * 128], kT[d][:, h // 2, :], qm[d][:, h, :], True, True, [f"kT{d}", f"qm{d}"], ["SCp"])
                TT("dve", PT, SCp.rearrange("p (h i) -> p h i", h=8), maskT[:].unsqueeze(1).to_broadcast([128, 8, 128]),
                   ALU.mult, ["SCp", "maskT"], ["PT"])
                for h in range(8):
                    MM(Op[:, h * 128:(h + 1) * 128], PT[:, h, :], v_bf[d][:, h * 128:(h + 1) * 128], True, False, ["PT", f"v_bf{d}"], ["Op"])
                    MM(Op[:, h * 128:(h + 1) * 128], qm[d][:, h, :], Rbf[:, h, :], False, True, [f"qm{d}", "Rbf"], ["Op"])
            for par in range(2):
                rows = slice(par * 64, par * 64 + 64)
                TT("dve", Rf[rows, par::2, :], KVp[rows, kvs(par), :], Rf[rows, par::2, :], ALU.add, ["KVp", "Rf"], ["Rf"])
                TT("dve", Rf[rows, par::2, :], Rf[rows, par::2, :], cdt[rows, par::2].unsqueeze(2).to_broadcast([64, 4, 128]), ALU.mult, ["Rf", "cdt"], ["Rf"])
                CP("act", Rbf[rows, par::2, :], Rf[rows, par::2, :], ["Rf"], ["Rbf"])
            if own:
                CP("act", o2, Op, ["Op"], ["o_sb"])
                ACT(osq.rearrange("p h e -> p (h e)"), Op, AF.Square, ["Op"], ["osq"])
                RED(stt[:, 0:8], o_sb, ALU.add, ["o_sb"], ["st_s"])
                RED(stt[:, 8:16], osq, ALU.add, ["osq"], ["st_q"])
                TS("dve", stt[:, 16:24], stt[:, 0:8], 1.0 / 128.0, None, ALU.mult, None, ["st_s"], ["st_m"])
                TT("dve", stt[:, 24:32], stt[:, 16:24], stt[:, 16:24], ALU.mult, ["st_m"], ["st_m2"])
                STT(stt[:, 32:40], stt[:, 8:16], 1.0 / 128.0, stt[:, 24:32], ALU.mult, ALU.subtract, ["st_q", "st_m2"], ["st_v"])
                ACT(stt[:, 40:48], stt[:, 32:40], AF.Sqrt, ["st_v"], ["st_sd"], bias=EPS, scale=1.0)
                RECIP(stt[:, 32:40], stt[:, 40:48], ["st_sd"], ["st_rs"])
                TT("pool", o_sb, o_sb, stt[:, 16:24].unsqueeze(2).to_broadcast([128, 8, 128]), ALU.subtract, ["o_sb", "st_m"], ["o_sb"])
                TT("pool", o_sb, o_sb, stt[:, 32:40].unsqueeze(2).to_broadcast([128, 8, 128]), ALU.mult, ["o_sb", "st_rs"], ["o_sb"])
                TT("dve", o2, o2, gng, ALU.mult, ["o_sb", "gng"], ["o_sb"])
                TT("dve", ret[d], o2, gr_bf[d], ALU.mult, ["o_sb", f"gr_bf{d}"], [f"ret{d}"])

        def phaseC(pas, blk, ci):
            if pas != "full":
                return
            c = blk * 2 + ci
            d = c % 2
            for kt in range(8):
                TR(TRb[:, kt * 128:(kt + 1) * 128], ret[d][:, kt * 128:(kt + 1) * 128], identb[:], [f"ret{d}", "identb"], ["TRb"])
            CP("act", retT_blk[:, :, ci * 128:(ci + 1) * 128], TRb.rearrange("p (k t) -> p k t", k=8), ["TRb"], ["retT_blk"])
            if ci == 1:
                ob = blk - NPRE
                dma(retT_d[:, :, ob * 256:(ob + 1) * 256], retT_blk, r=["retT_blk"])

        pendB = None
        pendC = None
        real = [(p_, b_) for p_, b_ in sa_iter if p_ != "xchg"]
        has_x = any(p_ == "xchg" for p_, _ in sa_iter)
        load_block(real[0][1])
        norm_block(real[0][1])
        ri = 0
        prev_pas = None
        for pas, blk in sa_iter:
            if pas == "light" and not PSUM_R2:
                pj_n[0] = 4
            if pas == "full" and prev_pas == "light" and mode != "coll":
                if pendB is not None:
                    phaseB(*pendB)
                    pendB = None
                P.barrier()
            if pas == "full":
                pj_n[0] = 2
            prev_pas = pas
            if pas == "xchg":
                if pendB is not None:
                    phaseB(*pendB)
                    pendB = None
                do_xchg()
                load_block(real[ri][1])
                norm_block(real[ri][1])
                continue
            nxt = real[ri + 1] if ri + 1 < len(real) else None
            if nxt is not None and has_x and nxt[0] != pas:
                nxt = None
            for ci in range(2):
                if ci == 0 and nxt is not None:
                    load_block(nxt[1])
                phaseA(pas, blk, ci)
                if ci == 0 and nxt is not None:
                    norm_block(nxt[1])
                if pendC is not None:
                    phaseC(*pendC)
                    pendC = None
                if pendB is not None:
                    phaseB(*pendB)
                    pendC = pendB
                pendB = (pas, blk, ci)
            ri += 1
        if pendC is not None:
            phaseC(*pendC)
        if pendB is not None:
            phaseB(*pendB)
            phaseC(*pendB)
        P.barrier()
        if stop_after == "SA":
            return finish(nc, P, st, out_d)

        AR.reset()
        T0 = AR.take([128, 64, 128], BF16)
        I2Sre = AR.take([128, 64, 64], BF16)
        I2Sim = AR.take([128, 64, 64], BF16)
        S2Ore = AR.take([128, 32, 128], BF16)
        S2Oim = AR.take([128, 32, 128], BF16)
        TIre = AR.take([128, 32, 16], F32)
        TIim = AR.take([128, 32, 16], F32)
        TOre = AR.take([128, 32, 8], F32)
        TOim = AR.take([128, 32, 8], F32)
        P8 = AR.take([128, 32], F32)
        Q8 = AR.take([128, 32], F32)
        P128 = AR.take([128, 32], F32)
        Q128 = AR.take([128, 32], F32)
        base_sb = AR.off
        are = AR.take([128, 32], F32)
        aim = AR.take([128, 32], F32)
        ldt = AR.take([128, 32], F32)
        bre = AR.take([128, 32, 16], F32)
        bim = AR.take([128, 32, 16], F32)
        cre = AR.take([128, 32, 16], F32)
        cim = AR.take([128, 32, 16], F32)
        dtt = AR.take([128, 32], F32)
        tA = [AR.take([128, 32], F32) for _ in range(12)]
        tI = AR.take([128, 32], I32)
        PWr = AR.take([128, 9, 32], F32)
        PWi = AR.take([128, 9, 32], F32)
        NWr = AR.take([128, 9, 32], F32)
        NWi = AR.take([128, 9, 32], F32)
        E8r = AR.take([128, 17, 32], F32)
        E8i = AR.take([128, 17, 32], F32)
        E16r = AR.take([128, 9, 32], F32)
        E16i = AR.take([128, 9, 32], F32)
        Bbr = AR.take([128, 32, 16], F32)
        Bbi = AR.take([128, 32, 16], F32)
        Gre = AR.take([128, 32, 8, 16], F32)
        Gim = AR.take([128, 32, 8, 16], F32)
        Sre = AR.take([128, 32, 8, 16], F32)
        Sim = AR.take([128, 32, 8, 16], F32)
        Ire = AR.take([128, 32, 8, 16], F32)
        Iim = AR.take([128, 32, 8, 16], F32)
        tB = [AR.take([128, 32, 16], F32) for _ in range(2)]
        T0f = AR.take([128, 4, 128], F32)
        dcol = AR.take([128, 64], F32)

        K0 = ["S0B"]
        dma(dcol, dcol_d, w=K0)
        for t_, d_ in ((are, are_d), (aim, aim_d), (ldt, ldt_d), (bre, bre_d), (bim, bim_d), (cre, cre_d), (cim, cim_d)):
            dma(t_, d_, w=K0)
        P.barrier()
        V = lambda out, in0, in1, op: TT("dve", out, in0, in1, op, K0, K0)
        VS = lambda out, in0, s1, op: TS("dve", out, in0, s1, None, op, None, K0, K0)
        VC = lambda out, in_: CP("dve", out, in_, K0, K0)
        VM = lambda ap, val: MEMSET("dve", ap, val, K0)
        ACT(dtt, ldt, AF.Exp, K0, K0)
        V(tA[0], are, dtt, ALU.mult)
        ACT(tA[1], tA[0], AF.Exp, K0, K0)
        ACT(tA[2], tA[0], AF.Exp, K0, K0, scale=-1.0)
        V(tA[3], aim, dtt, ALU.mult)
        sincos(tA[3], None, tA[4], tA[5], (tA[6], tI, tA[7], tA[8]), "S0B")
        P.barrier()

        def cmul(orr, oi, xr, xi, yr, yi, tmp):
            u1, u2 = tmp
            V(u1, xr, yr, ALU.mult)
            V(u2, xi, yi, ALU.mult)
            V(orr, u1, u2, ALU.subtract)
            V(u1, xr, yi, ALU.mult)
            V(u2, xi, yr, ALU.mult)
            V(oi, u1, u2, ALU.add)

        tsm = (tA[10], tA[11])
        VM(PWr[:, 0, :], 1.0)
        VM(PWi[:, 0, :], 0.0)
        V(PWr[:, 1, :], tA[1], tA[5], ALU.mult)
        V(PWi[:, 1, :], tA[1], tA[4], ALU.mult)
        VM(NWr[:, 0, :], 1.0)
        VM(NWi[:, 0, :], 0.0)
        V(NWr[:, 1, :], tA[2], tA[5], ALU.mult)
        V(tA[9], tA[2], tA[4], ALU.mult)
        VS(NWi[:, 1, :], tA[9], -1.0, ALU.mult)
        def tmpn(n):
            f_ = lambda t: t.rearrange("p g j -> p (g j)")[:, 0:n * 32].rearrange("p (k g) -> p k g", k=n)
            return (f_(tB[0]), f_(tB[1]))

        def pow_double(Er, Ei, top):
            cmul(Er[:, 2, :], Ei[:, 2, :], Er[:, 1, :], Ei[:, 1, :], Er[:, 1, :], Ei[:, 1, :], tsm)
            n = 2
            while n < top:
                br = Er[:, n, :].unsqueeze(1).to_broadcast([128, n, 32])
                bi = Ei[:, n, :].unsqueeze(1).to_broadcast([128, n, 32])
                cmul(Er[:, n + 1:2 * n + 1, :], Ei[:, n + 1:2 * n + 1, :], Er[:, 1:n + 1, :], Ei[:, 1:n + 1, :], br, bi, tmpn(n))
                n *= 2

        pow_double(PWr, PWi, 8)
        pow_double(NWr, NWi, 8)
        VM(E8r[:, 0, :], 1.0)
        VM(E8i[:, 0, :], 0.0)
        VC(E8r[:, 1, :], PWr[:, 8, :])
        VC(E8i[:, 1, :], PWi[:, 8, :])
        def tmpn(n):
            f_ = lambda t: t.rearrange("p g j -> p (g j)")[:, 0:n * 32].rearrange("p (k g) -> p k g", k=n)
            return (f_(tB[0]), f_(tB[1]))

        def pow_double(Er, Ei, top):
            cmul(Er[:, 2, :], Ei[:, 2, :], Er[:, 1, :], Ei[:, 1, :], Er[:, 1, :], Ei[:, 1, :], tsm)
            n = 2
            while n < top:
                br = Er[:, n, :].unsqueeze(1).to_broadcast([128, n, 32])
                bi = Ei[:, n, :].unsqueeze(1).to_broadcast([128, n, 32])
                cmul(Er[:, n + 1:2 * n + 1, :], Ei[:, n + 1:2 * n + 1, :], Er[:, 1:n + 1, :], Ei[:, 1:n + 1, :], br, bi, tmpn(n))
                n *= 2

        pow_double(E8r, E8i, 16)
        VM(E16r[:, 0, :], 1.0)
        VM(E16i[:, 0, :], 0.0)
        VC(E16r[:, 1, :], E8r[:, 16, :])
        VC(E16i[:, 1, :], E8i[:, 16, :])
        pow_double(E16r, E16i, 8)
        VC(P8, E8r[:, 1, :])
        VC(Q8, E8i[:, 1, :])
        VC(P128, E16r[:, 8, :])
        VC(Q128, E16i[:, 8, :])
        cmul(A2r[:, 0, :], A2i[:, 0, :], P128, Q128, P128, Q128, tsm)
        cmul(A2r[:, 1, :], A2i[:, 1, :], A2r[:, 0, :], A2i[:, 0, :], A2r[:, 0, :], A2i[:, 0, :], tsm)
        P.barrier()
        for r_ in range(16):
            CP("act", TIre[:, :, r_], E8r[:, 15 - r_, :], [], [f"TIr{r_}"])
            CP("pool", TIim[:, :, r_], E8i[:, 15 - r_, :], [], [f"TIi{r_}"])
        for q_ in range(8):
            CP("act", TOre[:, :, q_], E16r[:, 7 - q_, :], [], [f"TOr{q_}"])
            CP("pool", TOim[:, :, q_], E16i[:, 7 - q_, :], [], [f"TOi{q_}"])
        V(tA[0], are, are, ALU.mult)
        V(tA[1], aim, aim, ALU.mult)
        V(tA[0], tA[0], tA[1], ALU.add)
        RECIP(tA[0], tA[0], K0, K0)
        VS(tA[1], PWr[:, 1, :], -1.0, ALU.add)
        V(tA[2], tA[1], are, ALU.mult)
        V(tA[3], PWi[:, 1, :], aim, ALU.mult)
        V(tA[2], tA[2], tA[3], ALU.add)
        V(tA[2], tA[2], tA[0], ALU.mult)
        V(tA[3], PWi[:, 1, :], are, ALU.mult)
        V(tA[4], tA[1], aim, ALU.mult)
        V(tA[3], tA[3], tA[4], ALU.subtract)
        V(tA[3], tA[3], tA[0], ALU.mult)
        bc16 = lambda ap: ap.unsqueeze(2).to_broadcast([128, 32, 16])
        cmul(Bbr, Bbi, bre, bim, bc16(tA[2]), bc16(tA[3]), (tB[0], tB[1]))
        for s_ in range(8):
            cmul(Gre[:, :, s_, :], Gim[:, :, s_, :], Bbr, Bbi, bc16(NWr[:, s_ + 1, :]), bc16(NWi[:, s_ + 1, :]), (tB[0], tB[1]))
            cmul(Ire[:, :, s_, :], Iim[:, :, s_, :], Bbr, Bbi, bc16(PWr[:, 7 - s_, :]), bc16(PWi[:, 7 - s_, :]), (tB[0], tB[1]))
            cmul(Sre[:, :, s_, :], Sim[:, :, s_, :], cre, cim, bc16(PWr[:, s_ + 1, :]), bc16(PWi[:, s_ + 1, :]), (tB[0], tB[1]))
        Sim2 = Sim.rearrange("p g t j -> p (g t j)")
        VS(Sim2, Sim2, -1.0, ALU.mult)
        P.barrier()
        CP("act", S2Ore.rearrange("p g x -> p (g x)"), Sre.rearrange("p g t j -> p (g t j)"), K0, ["S2O"])
        CP("act", S2Oim.rearrange("p g x -> p (g x)"), Sim2, K0, ["S2O"])
        Gre3 = Gre.rearrange("p g s j -> p g (s j)")
        Gim3 = Gim.rearrange("p g s j -> p g (s j)")
        Sre3 = Sre.rearrange("p g t j -> p g (t j)")
        Sim3 = Sim.rearrange("p g t j -> p g (t j)")
        Ire3 = Ire.rearrange("p g s j -> p g (s j)")
        Iim3 = Iim.rearrange("p g s j -> p g (s j)")
        for gb in range(16):
            bank = PS[gb % 2][:, 0:512]
            bk = f"PSb{gb % 2}"
            for i in range(4):
                g = gb * 4 + i
                gh, gp = g // 32, g % 32
                pb = gh * 64
                MM(bank[:, i * 128:(i + 1) * 128], Gre3[pb:pb + 64, gp, :], Sre3[pb:pb + 64, gp, :], True, False, K0, [bk])
                MM(bank[:, i * 128:(i + 1) * 128], Gim3[pb:pb + 64, gp, :], Sim3[pb:pb + 64, gp, :], False, True, K0, [bk])
            TT("dve", T0f, bank.rearrange("p (g x) -> p g x", g=4),
               t0mask[:].unsqueeze(1).to_broadcast([128, 4, 128]), ALU.mult, [bk, "t0mask"] + K0, K0)
            for i in range(4):
                g = gb * 4 + i
                STT(T0[:, g, :], identf[:], dcol[:, g:g + 1], T0f[:, i, :], ALU.mult, ALU.add, K0 + ["identf"], K0 + ["T0"])
        for src3, dst, nm in ((Ire3, I2Sre, "I2Sre"), (Iim3, I2Sim, "I2Sim")):
            for gb in range(8):
                bank = PS[2 + gb % 2][:, 0:512]
                bk = f"PSc{gb % 2}"
                for i in range(8):
                    g = gb * 8 + i
                    gh, gp = g // 32, g % 32
                    pb = gh * 64
                    TR(bank[:, i * 64:(i + 1) * 64], src3[pb:pb + 64, gp, :], identf[pb:pb + 64, pb:pb + 64], ["identf", "S0B"], [bk])
                CP("act", dst[:, gb * 8:(gb + 1) * 8, :], bank.rearrange("p (g n) -> p g n", g=8), [bk], [nm])
        P.barrier()
        if stop_after == "S0b":
            return finish(nc, P, st, out_d)

        AR.reset(base_sb)
        u_raw = AR.take([128, 8, 1024], BF16)
        U = u_raw.rearrange("p s c -> p (s c)").rearrange("p (g m) -> p g m", g=64)
        u_tm2 = AR.take([128, 64, 8, 16], BF16)
        inc_raw = AR.take([128, 4096], F32)
        Incre = inc_raw.rearrange("p (g m) -> p g m", g=32)
        yg_bf = inc_raw.bitcast(BF16).rearrange("p (s c) -> p s c", s=8)
        Incim = AR.take([128, 32, 128], F32)
        gluw = AR.take([128, 8, 1024], BF16)
        glub = AR.take([128, 1024], F32)
        g5s = [AR.take([128, 1024], BF16) for _ in range(2)]
        ct = [AR.take([128, 32], F32) for _ in range(12)]
        un_base = AR.off
        LA = AR.take([128, 32, 8, 16], F32)
        LB = AR.take([128, 32, 8, 16], F32)
        LS = [AR.take([128, 32, 8], F32) for _ in range(4)]
        LE = [AR.take([128, 32], F32) for _ in range(4)]
        AR.reset(un_base)
        Xre_bf = AR.take([128, 32, 128], BF16)
        Xim_bf = AR.take([128, 32, 128], BF16)
        Ysb = [AR.take([128, 8, 128], F32) for _ in range(2)]
        gt1 = [AR.take([128, 1024], F32) for _ in range(2)]
        gt2 = [AR.take([128, 1024], F32) for _ in range(2)]
        AR.reset(un_base)
        ygT = [AR.take([128, 8, 128], BF16) for _ in range(2)]
        zb = [AR.take([128, 1024], F32) for _ in range(2)]
        gate = [AR.take([128, 1024], F32) for _ in range(2)]
        tt_ = [AR.take([128, 1024], F32) for _ in range(2)]
        ssm = [AR.take([128, 1024], BF16) for _ in range(2)]
        ssmT = [AR.take([128, 8, 128], BF16) for _ in range(2)]
        AR.reset(un_base)

        dma(gluw, gluw_d, w=["gluw"], eng="pool")
        dma(glub, glub_d, w=["glub"])

        TRU = PS[0][:, 0:512].bitcast(BF16)
        PIre = PS[1][:, 0:512]
        PIim = PS[1][:, 512:1024]
        TRU2 = [PS[0][:, 0:512].bitcast(BF16), PS[0][:, 512:1024].bitcast(BF16)]
        PI2 = [(PS[1][:, 0:512], PS[1][:, 512:1024]), (PS[2][:, 0:512], PS[2][:, 512:1024])]
        PY = PS[2]
        PYT = PS[3]
        IRK = [f"IR{m}" for m in range(128)]
        IIK = [f"II{m}" for m in range(128)]

        Gs = AR.take([128, 8, 64], F32)
        Gs2 = AR.take([128, 8, 64], F32)
        Sd = AR.take([128, 3, 64], F32)
        xt_ = [AR.take([128, 32], F32) for _ in range(4)]
        NPS = NPRE // 4
        if mode == "coll":
            sb_iter = [("light", 0), ("light", 1), ("xchg", -1), ("full", 0), ("full", 1)]
        else:
            sb_iter = [("light", sc_) for sc_ in range(NPS)] + [("full", NPS), ("full", NPS + 1)]
        for pas, sc in sb_iter:
            if pas == "xchg":
                P.barrier()
                dma(ccS_in.ap()[:, 0:32], car_r[:], r=["car_r"], w=["ccS_in"])
                dma(ccS_in.ap()[:, 32:64], car_i[:], r=["car_i"], w=["ccS_in"])
                if FAKE_CC:
                    dma(ccS_out.ap()[0:128, :], ccS_in.ap(), r=["ccS_in"], w=["ccS_out"])
                else:
                    P.op("pool", lambda e: e.collective_compute("AllGather", ALU.bypass, replica_groups=[list(range(8))],
                                                                ins=[ccS_in.ap().opt()], outs=[ccS_out.ap().opt()]),
                         r=["ccS_in"], w=["ccS_out"])
                dma(Gs, ccS_out.ap().rearrange("(r p) c -> p r c", p=128), r=["ccS_out"], w=["Gs"])
                KX = ["XCH"]
                for d_ in range(3):
                    TT("dve", Gs2, Gs, sel5[:, d_, :].unsqueeze(2).to_broadcast([128, 8, 64]), ALU.mult, ["Gs", "sel5"] + KX, KX)
                    RED(Sd[:, d_, :], Gs2.rearrange("p r c -> p c r"), ALU.add, KX, KX)
                XV = lambda out, in0, in1, op: TT("dve", out, in0, in1, op, KX + ["car_r", "car_i"], KX)
                XV(car_r[:], Sd[:, 0, 0:32], Sd[:, 0, 0:32], ALU.bypass) if False else CP("dve", car_r[:], Sd[:, 0, 0:32], KX + ["car_r"], KX + ["car_r"])
                CP("dve", car_i[:], Sd[:, 0, 32:64], KX + ["car_i"], KX + ["car_i"])
                for d_ in (1, 2):
                    sr, si = Sd[:, d_, 0:32], Sd[:, d_, 32:64]
                    ar_, ai_ = A2r[:, d_ - 1, :], A2i[:, d_ - 1, :]
                    XV(xt_[0], sr, ar_, ALU.mult)
                    XV(xt_[1], si, ai_, ALU.mult)
                    XV(xt_[0], xt_[0], xt_[1], ALU.subtract)
                    TT("dve", car_r[:], car_r[:], xt_[0], ALU.add, KX + ["car_r"], KX + ["car_r"])
                    XV(xt_[2], sr, ai_, ALU.mult)
                    XV(xt_[3], si, ar_, ALU.mult)
                    XV(xt_[2], xt_[2], xt_[3], ALU.add)
                    TT("dve", car_i[:], car_i[:], xt_[2], ALU.add, KX + ["car_i"], KX + ["car_i"])
                P.barrier()
                continue
            own = pas == "full"
            if own:
                P.barrier()
            dma(u_raw, u_d[sc * 1024:(sc + 1) * 1024, :].rearrange("(m s) c -> m s c", s=8), w=["u_raw", "U"])
            src4 = u_raw.rearrange("p s (g j) -> p g s j", j=16)
            CP("dve", u_tm2[:, 0:32, :, :], src4[:, 0:32, :, :], ["u_raw"], ["u_tm2a"])
            CP("act", u_tm2[:, 32:64, :, :], src4[:, 32:64, :, :], ["u_raw"], ["u_tm2b"])
            u3 = u_tm2.rearrange("p g s j -> p g (s j)")
            for gb in range(8):
                tru = TRU2[gb % 2]
                tk = f"TRU{gb % 2}"
                for i in range(8):
                    g = gb * 8 + i
                    TR(tru[:, i * 128:(i + 1) * 128], u3[:, g, :], identb[:], ["u_tm2a", "u_tm2b", "identb"], [tk])
                CP("act", U[:, gb * 8:(gb + 1) * 8, :], tru.rearrange("p (g m) -> p g m", g=8),
                   [tk], ["U", "u_raw"])
            for b4 in range(8):
                pre, pim = PI2[b4 % 2]
                kre, kim = f"PIre{b4 % 2}", f"PIim{b4 % 2}"
                for i in range(4):
                    gp = b4 * 4 + i
                    for gh in range(2):
                        g = gh * 32 + gp
                        MM(pre[gh * 64:(gh + 1) * 64, i * 128:(i + 1) * 128], I2Sre[:, g, :], U[:, g, :], True, True, ["U", "I2Sre"], [kre])
                        MM(pim[gh * 64:(gh + 1) * 64, i * 128:(i + 1) * 128], I2Sim[:, g, :], U[:, g, :], True, True, ["U", "I2Sim"], [kim])
                CP("act", Incre[:, b4 * 4:(b4 + 1) * 4, :], pre.rearrange("p (g m) -> p g m", g=4), [kre], IRK)
                CP("dve", Incim[:, b4 * 4:(b4 + 1) * 4, :], pim.rearrange("p (g m) -> p g m", g=4), [kim], IIK)
            if not own:
                I4r = Incre.rearrange("p g (q r) -> p g q r", r=16)
                I4i = Incim.rearrange("p g (q r) -> p g q r", r=16)
                tir = TIre.unsqueeze(2).to_broadcast([128, 32, 8, 16])
                tii = TIim.unsqueeze(2).to_broadcast([128, 32, 8, 16])
                TT("dve", LA, I4r, tir, ALU.mult, IRK + ["TI"], ["LA"])
                TT("dve", LB, I4i, tii, ALU.mult, IIK + ["TI"], ["LB"])
                TT("dve", LA, LA, LB, ALU.subtract, ["LA", "LB"], ["LA"])
                RED(LS[0], LA, ALU.add, ["LA"], ["LS0"])
                TT("dve", LA, I4i, tir, ALU.mult, IIK + ["TI", "LS0"], ["LA"])
                TT("dve", LB, I4r, tii, ALU.mult, IRK + ["TI", "LA"], ["LB"])
                TT("dve", LA, LA, LB, ALU.add, ["LA", "LB"], ["LA"])
                RED(LS[1], LA, ALU.add, ["LA"], ["LS1"])
                TT("dve", LS[2], LS[0], TOre, ALU.mult, ["LS0", "TO"], ["LS2"])
                TT("dve", LS[3], LS[1], TOim, ALU.mult, ["LS1", "TO"], ["LS3"])
                TT("dve", LS[2], LS[2], LS[3], ALU.subtract, ["LS2", "LS3"], ["LS2"])
                RED(LE[0], LS[2], ALU.add, ["LS2"], ["LE0"])
                TT("dve", LS[2], LS[1], TOre, ALU.mult, ["LS1", "TO", "LE0"], ["LS2"])
                TT("dve", LS[3], LS[0], TOim, ALU.mult, ["LS0", "TO", "LS2"], ["LS3"])
                TT("dve", LS[2], LS[2], LS[3], ALU.add, ["LS2", "LS3"], ["LS2"])
                RED(LE[1], LS[2], ALU.add, ["LS2"], ["LE1"])
                TT("dve", LE[2], car_r[:], P128, ALU.mult, ["car_r", "P128"], ["LE2"])
                TT("dve", LE[3], car_i[:], Q128, ALU.mult, ["car_i", "Q128"], ["LE3"])
                TT("dve", LE[2], LE[2], LE[3], ALU.subtract, ["LE2", "LE3"], ["LE2"])
                TT("dve", LE[0], LE[0], LE[2], ALU.add, ["LE0", "LE2"], ["LE0"])
                TT("dve", LE[2], car_i[:], P128, ALU.mult, ["car_i", "P128", "LE0"], ["LE2"])
                TT("dve", LE[3], car_r[:], Q128, ALU.mult, ["car_r", "Q128", "LE0"], ["LE3"])
                TT("dve", LE[2], LE[2], LE[3], ALU.add, ["LE2", "LE3"], ["LE2"])
                TT("dve", car_i[:], LE[1], LE[2], ALU.add, ["LE1", "LE2"], ["car_i"])
                CP("dve", car_r[:], LE[0], ["LE0"], ["car_r"])
                continue
            CP("dve", Xre_bf[:, :, 0], car_r[:], ["car_r"], ["Xre0"])
            CP("dve", Xim_bf[:, :, 0], car_i[:], ["car_i"], ["Xim0"])
            for m in range(128):
                if m == 0:
                    xr, xi, kr, ki = car_r[:], car_i[:], "car_r", "car_i"
                else:
                    xr, xi, kr, ki = Incre[:, :, m - 1], Incim[:, :, m - 1], IRK[m - 1], IIK[m - 1]
                pp = (m % 2) * 6
                c0, c1, c2, c3, c4, c5 = ct[pp:pp + 6]
                n0 = [f"ct{pp + j}" for j in range(6)]
                TT("dve", c0, xr, P8, ALU.mult, [kr, "P8"], [n0[0]])
                TT("dve", c1, xi, Q8, ALU.mult, [ki, "Q8"], [n0[1]])
                TT("dve", c3, xi, P8, ALU.mult, [ki, "P8"], [n0[3]])
                TT("dve", c4, xr, Q8, ALU.mult, [kr, "Q8"], [n0[4]])
                TT("dve", c2, c0, c1, ALU.subtract, [n0[0], n0[1]], [n0[2]])
                TT("dve", c5, c3, c4, ALU.add, [n0[3], n0[4]], [n0[5]])
                TT("dve", Incre[:, :, m], c2, Incre[:, :, m], ALU.add, [n0[2], IRK[m]], [IRK[m]])
                TT("dve", Incim[:, :, m], c5, Incim[:, :, m], ALU.add, [n0[5], IIK[m]], [IIK[m]])
            CP("dve", car_r[:], Incre[:, :, 127], [IRK[127], "Xre0"], ["car_r"])
            CP("dve", car_i[:], Incim[:, :, 127], [IIK[127], "Xim0"], ["car_i"])
            CP("act", Xre_bf[:, :, 1:128], Incre[:, :, 0:127], IRK, ["Xre"])
            CP("dve", Xim_bf[:, :, 1:128], Incim[:, :, 0:127], IIK, ["Xim"])
            PYb = [PS[1], PS[2]]
            PYTb = [PS[3], PS[0]]

            def y_mm(gb):
                bq = gb % 2
                for i in range(8):
                    g = gb * 8 + i
                    gh, gp = g // 32, g % 32
                    pb = gh * 64
                    oy = PYb[bq][:, i * 128:(i + 1) * 128]
                    MM(oy, T0[:, g, :], U[:, g, :], True, False, ["T0", "U"], [f"PY{bq}"])
                    MM(oy, S2Ore[pb:pb + 64, gp, :], Xre_bf[pb:pb + 64, gp, :], False, False, ["S2O", "Xre", "Xre0"], [f"PY{bq}"])
                    MM(oy, S2Oim[pb:pb + 64, gp, :], Xim_bf[pb:pb + 64, gp, :], False, True, ["S2O", "Xim", "Xim0"], [f"PY{bq}"])

            def y_ew(gb):
                bq = gb % 2
                pyt = PYTb[bq]
                ky, kt_ = f"PY{bq}", f"PYT{bq}"
                CP("act", Ysb[bq].rearrange("p g m -> p (g m)"), PYb[bq], [ky], [f"Ysb{bq}"])
                for i in range(8):
                    TR(pyt[:, i * 128:(i + 1) * 128], Ysb[bq][:, i, :], identf[:], [f"Ysb{bq}", "identf"], [kt_])
                ACT(gt1[bq], pyt, AF.Square, [kt_], [f"gt1{bq}"])
                TS("dve", gt1[bq], gt1[bq], KG * 0.044715, KG, ALU.mult, ALU.add, [f"gt1{bq}"], [f"gt1{bq}"])
                TT("dve", gt1[bq], gt1[bq], pyt, ALU.mult, [f"gt1{bq}", kt_], [f"gt1{bq}"])
                ACT(gt2[bq], gt1[bq], AF.Sigmoid, [f"gt1{bq}"], [f"gt2{bq}"])
                TT("dve", yg_bf[:, :, gb * 128:(gb + 1) * 128].rearrange("p t (g j) -> p g t j", j=16),
                   pyt.rearrange("p (g t j) -> p g t j", g=8, t=8), gt2[bq].rearrange("p (g t j) -> p g t j", g=8, t=8),
                   ALU.mult, [kt_, f"gt2{bq}"], ["yg_bf"])

            y_mm(0)
            for gb in range(8):
                if gb + 1 < 8:
                    y_mm(gb + 1)
                y_ew(gb)
            P.barrier()
            oc = sc - NPS
            g5v = g5_d[oc * 1024:(oc + 1) * 1024, :].rearrange("(m s) c -> m s c", s=8)
            TRa = [PS[0][:, 0:512].bitcast(BF16), PS[0][:, 512:1024].bitcast(BF16)]
            TRs = [PS[3][:, 0:512].bitcast(BF16), PS[3][:, 512:1024].bitcast(BF16)]
            PZb = [PS[1], PS[2]]

            def tail1(s_):
                bq = s_ % 2
                dma(g5s[bq], g5v[:, s_, :], w=[f"g5s{bq}"])
                for kt in range(8):
                    TR(TRa[bq][:, kt * 128:(kt + 1) * 128], yg_bf[:, s_, kt * 128:(kt + 1) * 128], identb[:], ["yg_bf", "identb"], [f"TRa{bq}"])
                CP("act", ygT[bq].rearrange("p k m -> p (k m)"), TRa[bq], [f"TRa{bq}"], [f"ygT{bq}"])
                for cb in range(2):
                    pz = PZb[bq][:, cb * 512:(cb + 1) * 512]
                    for kt in range(8):
                        MM(pz, ygT[bq][:, kt, :], gluw[:, kt, cb * 512:(cb + 1) * 512], kt == 0, kt == 7, [f"ygT{bq}", "gluw"], [f"PZ{bq}"])

            def tail2(s_):
                bq = s_ % 2
                TT("dve", zb[bq], PZb[bq], glub, ALU.add, [f"PZ{bq}", "glub"], [f"zb{bq}"])
                ACT(gate[bq], zb[bq], AF.Sigmoid, [f"zb{bq}"], [f"gate{bq}"])
                TT("dve", tt_[bq], yg_bf[:, s_, :], gate[bq], ALU.mult, ["yg_bf", f"gate{bq}"], [f"tt_{bq}"])
                TT("pool", ssm[bq], tt_[bq], g5s[bq], ALU.mult, [f"tt_{bq}", f"g5s{bq}"], [f"ssm{bq}"])
                for kt in range(8):
                    TR(TRs[bq][:, kt * 128:(kt + 1) * 128], ssm[bq][:, kt * 128:(kt + 1) * 128], identb[:], [f"ssm{bq}", "identb"], [f"TRs{bq}"])
                CP("act", ssmT[bq].rearrange("p k m -> p (k m)"), TRs[bq], [f"TRs{bq}"], [f"ssmT{bq}"])
                dma(ssmT_d[oc, s_], ssmT[bq], r=[f"ssmT{bq}"])

            tail1(0)
            for s_ in range(8):
                if s_ + 1 < 8:
                    tail1(s_ + 1)
                tail2(s_)
        P.barrier()
        if stop_after == "SB":
            return finish(nc, P, st, out_d)

        AR.reset()
        Wqk = AR.take([128, 8, 1024], BF16)
        VO = AR.take([128, 8, 1024], BF16)
        wout = AR.take([128, 16, 1024], BF16)
        g2r = AR.take([128, 1024], F32)
        gfr = AR.take([128, 1024], F32)
        base_sc = AR.off
        wqT = AR.take([128, 8, 1024], BF16)
        wk = AR.take([128, 8, 1024], BF16)
        wv = AR.take([128, 8, 1024], BF16)
        wo = AR.take([128, 8, 1024], BF16)
        memT = AR.take([128, 8, 256], F32)
        msq = AR.take([128, 8, 256], BF16)
        mrs1 = AR.take([128, 256], F32)
        mrs2 = AR.take([128, 256], F32)
        mnT = AR.take([128, 8, 256], BF16)
        kaT = AR.take([128, 8, 256], BF16)
        vaT = AR.take([128, 8, 256], BF16)
        for t_, d_, k_ in ((wk, wk_d, "wk"), (wv, wv_d, "wv"), (wqT, wqT_d, "wqT"), (wo, wo_d, "wo")):
            dma(t_, d_, w=[k_], eng="pool")
        dma(wout[:, 0:8, :], wout_d[:, 0:8, :], w=["wout"], eng="pool")
        dma(wout[:, 8:16, :], wout_d[:, 8:16, :], w=["wout"], eng="pool")
        dma(g2r, g2_d, w=["g2r"])
        dma(gfr, gf_d, w=["gfr"])
        dma(memT, memT_d, w=["memT"])
        norm_fm(memT, 256, gm[:], msq, mrs1, mrs2, mnT, PS[0][:, 0:256], "memT", "msq", "mrs", "mnT", "PSm", "gm")
        for r_ in range(8):
            for wt, dst, nm, pi in ((wk, kaT, "kaT", 1), (wv, vaT, "vaT", 2)):
                pp_ = PS[pi][:, (r_ % 2) * 512:(r_ % 2) * 512 + 256]
                pk = f"PSkv{pi}_{r_ % 2}"
                for kt in range(8):
                    MM(pp_, wt[:, kt, r_ * 128:(r_ + 1) * 128], mnT[:, kt, :], kt == 0, kt == 7, [nm[:2] if False else ("wk" if pi == 1 else "wv"), "mnT"], [pk])
                CP("act" if pi == 1 else "dve", dst[:, r_, :], pp_, [pk], [nm])
        for dt_ in range(8):
            pw_ = PS[dt_ % 2]
            pk = f"PSw{dt_ % 2}"
            for h in range(4):
                for hf in range(2):
                    MM(pw_[:, h * 256:(h + 1) * 256], wqT[:, 2 * h + hf, dt_ * 128:(dt_ + 1) * 128], kaT[:, 2 * h + hf, :],
                       hf == 0, hf == 1, ["wqT", "kaT"], [pk])
            ACT(Wqk[:, dt_, :], pw_, AF.Copy, [pk], ["Wqk"], scale=1.0 / 16.0)
        for h in range(4):
            for mh in range(2):
                idx = 2 * h + mh
                pw_ = PS[2 + idx % 2]
                pk = f"PSv{idx % 2}"
                for cb in range(2):
                    for hf in range(2):
                        MM(pw_[:, cb * 512:(cb + 1) * 512], vaT[:, 2 * h + hf, mh * 128:(mh + 1) * 128], wo[:, 2 * h + hf, cb * 512:(cb + 1) * 512],
                           hf == 0, hf == 1, ["vaT", "wo"], [pk])
                CP("dve", VO[:, idx, :], pw_, [pk], ["VO"])
        P.barrier()

        AR.reset(base_sc)
        retT_sc = [AR.take([128, 8, 1024], BF16) for _ in range(2)]
        ssmT_s = [AR.take([128, 8, 128], BF16) for _ in range(3)]
        x_t = [AR.take([128, 1024], F32) for _ in range(3)]
        x1 = [AR.take([128, 1024], F32) for _ in range(3)]
        junk = AR.take([128, 1024], F32)
        h2 = [AR.take([128, 1024], BF16) for _ in range(2)]
        h2T = AR.take([128, 8, 128], BF16)
        Pex = AR.take([128, 4, 256], BF16)
        Pn = [AR.take([128, 4, 256], BF16) for _ in range(2)]
        PTt = AR.take([128, 8, 128], BF16)
        x2 = AR.take([128, 1024], F32)
        osb = [AR.take([128, 1024], F32) for _ in range(2)]
        sst = [AR.take([128, 32], F32) for _ in range(3)]
        TRc = PS[3][:, 0:512].bitcast(BF16)
        TRd = PS[3][:, 512:1024].bitcast(BF16)
        tiles = [(sco, s_) for sco in range(2) for s_ in range(8)]
        dma(retT_sc[0], retT_d[:, :, 0:1024], w=["retT_sc0"])
        dma(retT_sc[1], retT_d[:, :, 1024:2048], w=["retT_sc1"])

        def sc_load(t):
            sco, s_ = tiles[t]
            b3 = t % 3
            dma(ssmT_s[b3], ssmT_d[sco, s_], w=[f"ssmT_s{b3}"])
            dma(x_t[b3], xown_d[sco, :, s_, :], w=[f"x_t{b3}"])

        def sc_s1(t):
            sco, s_ = tiles[t]
            b3 = t % 3
            st_ = sst[t % 3]
            sk = f"sst{t % 3}"
            for cb in range(2):
                px = PS[0][:, cb * 512:(cb + 1) * 512]
                for kt in range(8):
                    MM(px, retT_sc[sco][:, kt, s_::8], wout[:, kt, cb * 512:(cb + 1) * 512], kt == 0, False, [f"retT_sc{sco}", "wout"], [f"PX{cb}"])
                for kt in range(8):
                    MM(px, ssmT_s[b3][:, kt, :], wout[:, 8 + kt, cb * 512:(cb + 1) * 512], False, kt == 7, [f"ssmT_s{b3}", "wout"], [f"PX{cb}"])
            TT("dve", x1[b3], PS[0], x_t[b3], ALU.add, ["PX0", "PX1", f"x_t{b3}"], [f"x1_{b3}"])
            ACT(junk, x1[b3], AF.Square, [f"x1_{b3}"], ["junk", sk + "a"], accum_out=st_[:, 0:1])
            ACT(st_[:, 1:2], st_[:, 0:1], AF.Ln, [sk + "a"], [sk + "b"], bias=epsc[:, 0:1], scale=1.0 / 1024.0)
            ACT(st_[:, 2:3], st_[:, 1:2], AF.Exp, [sk + "b"], [sk + "c"], scale=-0.5)
            STT(h2[t % 2], x1[b3], st_[:, 2:3], g2r, ALU.mult, ALU.mult, [f"x1_{b3}", sk + "c", "g2r"], [f"h2_{t % 2}"])

        def sc_s2(t):
            st_ = sst[t % 3]
            sk = f"sst{t % 3}"
            hh = h2[t % 2]
            for kt in range(8):
                TR(TRc[:, kt * 128:(kt + 1) * 128], hh[:, kt * 128:(kt + 1) * 128], identb[:], [f"h2_{t % 2}", "identb"], ["TRc"])
            CP("act", h2T.rearrange("p k t -> p (k t)"), TRc, ["TRc"], ["h2T"])
            for cb in range(2):
                pq = PS[1][:, cb * 512:(cb + 1) * 512]
                for kt in range(8):
                    MM(pq, h2T[:, kt, :], Wqk[:, kt, cb * 512:(cb + 1) * 512], kt == 0, kt == 7, ["h2T", "Wqk"], [f"PQ{cb}"])
            RED(st_[:, 4:8], PS[1].rearrange("p (h m) -> p h m", h=4), ALU.max, ["PQ0", "PQ1"], [sk + "mx"])
            TS("dve", st_[:, 8:12], st_[:, 4:8], -1.0, None, ALU.mult, None, [sk + "mx"], [sk + "nmx"])
            for h in range(4):
                ACT(Pex[:, h, :], PS[1][:, h * 256:(h + 1) * 256], AF.Exp, ["PQ0", "PQ1", sk + "nmx"], ["Pex", sk + f"sm{h}"],
                    bias=st_[:, 8 + h:9 + h], scale=1.0, accum_out=st_[:, 12 + h:13 + h])
            RECIP(st_[:, 16:20], st_[:, 12:16], [sk + f"sm{h}" for h in range(4)], [sk + "rsm"])
            TT("pool", Pn[t % 2], Pex, st_[:, 16:20].unsqueeze(2).to_broadcast([128, 4, 256]), ALU.mult, ["Pex", sk + "rsm"], [f"Pn{t % 2}"])

        def sc_s3(t):
            sco, s_ = tiles[t]
            b3 = t % 3
            st_ = sst[t % 3]
            sk = f"sst{t % 3}"
            b = t % 2
            Pn2 = Pn[b].rearrange("p h m -> p (h m)")
            for idx in range(8):
                TR(TRd[:, idx * 128:(idx + 1) * 128], Pn2[:, idx * 128:(idx + 1) * 128], identb[:], [f"Pn{b}", "identb"], ["TRd"])
            CP("act", PTt.rearrange("p k t -> p (k t)"), TRd, ["TRd"], ["PTt"])
            for cb in range(2):
                po = PS[2][:, cb * 512:(cb + 1) * 512]
                for idx in range(8):
                    MM(po, PTt[:, idx, :], VO[:, idx, cb * 512:(cb + 1) * 512], idx == 0, idx == 7, ["PTt", "VO"], [f"PO{cb}"])
            TT("dve", x2, PS[2], x1[b3], ALU.add, ["PO0", "PO1", f"x1_{b3}"], ["x2"])
            ACT(junk, x2, AF.Square, ["x2"], ["junk", sk + "d"], accum_out=st_[:, 20:21])
            ACT(st_[:, 21:22], st_[:, 20:21], AF.Ln, [sk + "d"], [sk + "e"], bias=epsc[:, 0:1], scale=1.0 / 1024.0)
            ACT(st_[:, 22:23], st_[:, 21:22], AF.Exp, [sk + "e"], [sk + "f"], scale=-0.5)
            STT(osb[b], x2, st_[:, 22:23], gfr, ALU.mult, ALU.mult, ["x2", sk + "f", "gfr"], [f"osb{b}"])
            dma(out_d[sco, :, s_, :], osb[b], r=[f"osb{b}"])

        NTL = len(tiles)
        sc_load(0)
        for i in range(NTL + 2):
            if i + 1 < NTL:
                sc_load(i + 1)
            if i < NTL:
                sc_s1(i)
            if 0 <= i - 2 < NTL:
                sc_s3(i - 2)
            if 0 <= i - 1 < NTL:
                sc_s2(i - 1)
        return finish(nc, P, st, out_d)


def finish(nc, P, st, out_d):
    P.barrier()
    P.op("sp", None)
    P.emit(st)
    return nc


def _const_tables():
    H = 8
    log_g = np.log1p(-np.exp2(-5.0 - np.arange(H, dtype=np.float64)))
    i = np.arange(128, dtype=np.float64)
    sqk = np.zeros((128, 16), np.float64)
    sqk[:, 0:8] = np.exp(log_g[None, :] * (i[:, None] + 1.0))
    sqk[:, 8:16] = np.exp(-log_g[None, :] * (i[:, None] + 1.0)) * 0.125
    cdt = np.broadcast_to(np.exp(log_g * 128.0)[None, :], (128, 8)).copy()
    j = np.arange(128)
    maskT = (j[None, :] >= j[:, None]).astype(np.float32)
    s_idx = np.arange(128) // 16
    t0mask = (s_idx[None, :] >= s_idx[:, None]).astype(np.float32)
    invf = (10000.0 ** (-np.arange(32, dtype=np.float32) / np.float32(32))).astype(np.float32)
    invf = np.broadcast_to(invf[None, :], (128, 32)).copy()
    return (sqk.astype(np.float32), cdt.astype(np.float32), maskT, t0mask, invf, np.eye(128, dtype=np.float32))


def _kt(w):
    K, N = w.shape
    return np.ascontiguousarray(w.reshape(K // 128, 128, N).transpose(1, 0, 2))


def make_in_maps(inputs, mode=None):
    mode = mode or MODE
    NPRE_T = 6144 if mode == "prefix" else 0
    NT = NPRE_T + 2048
    f = lambda a: np.asarray(a, dtype=np.float32)
    x = f(inputs["x"])
    mem = f(inputs["mem"])
    pos = np.asarray(inputs["positions"]).astype(np.int32)
    sqk, cdt, maskT, t0mask, invf, ident = _const_tables()
    rep = lambda v: np.ascontiguousarray(np.broadcast_to(f(v).reshape(1, -1), (128, f(v).size)))
    cols = lambda v: np.ascontiguousarray(f(v).reshape(8, 128).T)

    def s5lay(a):
        a = f(a)
        rest = a.shape[2:]
        a = a.reshape((2, 32, 64) + rest)
        a = np.moveaxis(a, 2, 1)
        return np.ascontiguousarray(a.reshape((128, 32) + rest))

    a_re = s5lay(inputs["s5_a_re"][0])
    a_im = s5lay(inputs["s5_a_im"][0])
    ldt = s5lay(np.broadcast_to(f(inputs["s5_log_dt"][0])[:, None], (64, 64)))
    b_re = s5lay(inputs["s5_b_re"][0])
    b_im = s5lay(inputs["s5_b_im"][0])
    c_re = s5lay(np.transpose(f(inputs["s5_c_re"][0]), (0, 2, 1)))
    c_im = s5lay(np.transpose(f(inputs["s5_c_im"][0]), (0, 2, 1)))
    shared = dict(
        invf=invf, sqk=sqk, cdt=cdt, maskT=maskT, t0mask=t0mask, ident=ident,
        w_in=_kt(f(inputs["w_in"][0])), g1=cols(inputs["norm1_g"][0]), gng=rep(inputs["ret_gn_g"][0]),
        s5are=a_re, s5aim=a_im, s5ldt=ldt, s5bre=b_re, s5bim=b_im, s5cre=c_re, s5cim=c_im,
        dcol=np.ascontiguousarray(f(inputs["s5_d"][0]).reshape(64, 16).T[np.arange(128) % 16, :]), glub=rep(inputs["s5_glu_b"][0]), g2rep=rep(inputs["norm2_g"][0]),
        gfrep=rep(inputs["norm_f_g"]), gm=cols(inputs["norm_mem_g"][0]),
        gluw=_kt(f(inputs["s5_glu_w"][0])), wout=_kt(f(inputs["w_out"][0])),
        wqT=_kt(np.ascontiguousarray(f(inputs["xa_wq"][0]).T)), wk=_kt(f(inputs["xa_wk"][0])),
        wv=_kt(f(inputs["xa_wv"][0])), wo=_kt(f(inputs["xa_wo"][0])),
    )
    log_g = np.log1p(-np.exp2(-5.0 - np.arange(8, dtype=np.float64)))
    maps = []
    for core in range(8):
        b, j = core // 4, core % 4
        m = dict(shared)
        t1 = (j + 1) * 2048
        t0 = t1 - NT
        lo = max(t0, 0)
        xl = np.zeros((NT, 1024), np.float32)
        pl = np.zeros((NT,), np.int32)
        xl[lo - t0:, :] = x[b, lo:t1, :]
        pl[lo - t0:] = pos[b, lo:t1]
        m["xT"] = _kt(np.ascontiguousarray(xl.T))
        m["xown"] = np.ascontiguousarray(x[b, j * 2048:(j + 1) * 2048, :].reshape(2, 128, 8, 1024))
        m["pos"] = np.ascontiguousarray(pl.reshape(NT // 128, 128).T)
        m["memT"] = _kt(np.ascontiguousarray(mem[b].T))
        wtr = np.zeros((128, 8, 4), np.float64)
        sel5 = np.zeros((128, 3, 8), np.float32)
        for rank in range(8):
            rb, ri = rank // 4, rank % 4
            if rb == b and ri < j:
                dist = j - ri
                sel5[:, dist - 1, rank] = 1.0
                for blk in range(4):
                    for par in range(2):
                        h = 2 * blk + par
                        wtr[par * 64:(par + 1) * 64, rank, blk] = np.exp(log_g[h] * 2048.0 * (dist - 1))
        m["wtr"] = wtr.astype(np.float32)
        m["sel5"] = sel5
        maps.append(m)
    return maps


def kernel(**inputs):
    nc = build_program()
    maps = make_in_maps(inputs)
    res = run_bass_kernel_spmd(nc, maps, core_ids=list(range(8)))
    out = np.zeros((2, 8192, 1024), np.float32)
    for core in range(8):
        b, j = core // 4, core % 4
        out[b, j * 2048:(j + 1) * 2048, :] = np.asarray(res.results[core]["out"]).reshape(2048, 1024)
    return out
```
